# Optimizing a Trainium2 kernel written in Bass

```python
import math
import jax, jax.numpy as jnp
from jax import lax
import numpy as np

D_MODEL = 1024
BATCH = 2
SEQ = 8192
DEPTH = 1

GRID_W = 64
CTX_LEN = 256
S5_WIDTH = 512
S5_GROUP = 16
S5_GROUPS = S5_WIDTH // S5_GROUP
S5_STATE = 64
DT_MIN = 1e-3
DT_MAX = 1e-1
RET_WIDTH = D_MODEL - S5_WIDTH
RET_HEADS = 4
RET_HEAD_DIM = RET_WIDTH // RET_HEADS
RET_CHUNK = 128
ROPE_THETA = 10000.0
IN_COLS = S5_WIDTH + 4 * RET_WIDTH
D_FF = 2816
CONV_W = 3
NORM_EPS = 1e-6

kernel_name = "hybrid_s5_retention_convglu_dit_layer"


def rms_norm(t, w):
    tf = t.astype(jnp.float32)
    y = tf * lax.rsqrt(jnp.mean(tf * tf, axis=-1, keepdims=True) + NORM_EPS)
    return (y * w.astype(jnp.float32)).astype(t.dtype)


def adaln(cond, w_mod, b_mod):
    m = jax.nn.silu(cond) @ w_mod + b_mod
    return [t.reshape(-1, 1, D_MODEL) for t in jnp.split(m, 6, axis=-1)]


def modulate(h, shift, scale):
    return h * (1.0 + scale) + shift


def rope_2d(t):
    n_tok = t.shape[1]
    rows = n_tok // GRID_W
    row = jnp.repeat(jnp.arange(rows, dtype=jnp.float32), GRID_W)
    col = jnp.tile(jnp.arange(GRID_W, dtype=jnp.float32), rows)
    n_freq = RET_HEAD_DIM // 4
    inv_freq = ROPE_THETA ** (-jnp.arange(n_freq, dtype=jnp.float32) / n_freq)
    ang = jnp.concatenate([row[:, None] * inv_freq, col[:, None] * inv_freq], axis=-1)
    cos = jnp.cos(ang)[None, :, None, :]
    sin = jnp.sin(ang)[None, :, None, :]
    tf = t.astype(jnp.float32)
    t1, t2 = tf[..., 0::2], tf[..., 1::2]
    out = jnp.stack([t1 * cos - t2 * sin, t1 * sin + t2 * cos], axis=-1).reshape(t.shape)
    return out.astype(t.dtype)


def split_projection(p, rotate):
    b, n_tok, _ = p.shape
    u = p[..., :S5_WIDTH].reshape(b, n_tok, S5_GROUPS, S5_GROUP)
    q, k, v, g = jnp.split(p[..., S5_WIDTH:], 4, axis=-1)
    q = q.reshape(b, n_tok, RET_HEADS, RET_HEAD_DIM)
    k = k.reshape(b, n_tok, RET_HEADS, RET_HEAD_DIM) * (RET_HEAD_DIM ** -0.5)
    v = v.reshape(b, n_tok, RET_HEADS, RET_HEAD_DIM)
    if rotate:
        q = rope_2d(q)
        k = rope_2d(k)
    return (u, q.transpose(0, 2, 1, 3), k.transpose(0, 2, 1, 3), v.transpose(0, 2, 1, 3), g)


def s5_discretize(lam_re, lam_im, log_step, b_re, b_im):
    lam = lax.complex(lam_re.astype(jnp.float32), lam_im.astype(jnp.float32))
    step = jnp.exp(log_step.astype(jnp.float32))[:, None]
    lam_bar = jnp.exp(lam * step)
    b_mat = lax.complex(b_re.astype(jnp.float32), b_im.astype(jnp.float32))
    b_bar = ((lam_bar - 1.0) / lam)[..., None] * b_mat
    return lam_bar, b_bar


def _linear_combine(left, right):
    a_l, b_l = left
    a_r, b_r = right
    return a_r * a_l, a_r * b_l + b_r


def s5_scan(u, lam_bar, b_bar, h0, reverse):
    bu = jnp.einsum('gnp,blgp->blgn', b_bar, u.astype(jnp.float32).astype(jnp.complex64))
    if reverse:
        bu = jnp.flip(bu, axis=1)
    bu = bu.at[:, 0].add(lam_bar * h0)
    a = jnp.broadcast_to(lam_bar, bu.shape)
    _, h = lax.associative_scan(_linear_combine, (a, bu), axis=1)
    if reverse:
        h = jnp.flip(h, axis=1)
    return h


def s5_readout(u, h_f, h_b, c_re, c_im, d, w_glu, b_glu):
    b, n_tok = u.shape[0], u.shape[1]
    c_mat = lax.complex(c_re.astype(jnp.float32), c_im.astype(jnp.float32))
    y = jnp.real(jnp.einsum('gpn,blgn->blgp', c_mat, h_f + h_b))
    y = y + d.astype(jnp.float32).reshape(S5_GROUPS, S5_GROUP) * u.astype(jnp.float32)
    y = jax.nn.gelu(y.reshape(b, n_tok, S5_WIDTH))
    return y * jax.nn.sigmoid(y @ w_glu.astype(jnp.float32) + b_glu.astype(jnp.float32))


def retention_chunkwise(q, k, v, log_decay, r0, strict):
    b, h, n_tok, dk = q.shape
    dv = v.shape[-1]
    n_chunks = n_tok // RET_CHUNK
    ld = log_decay.astype(jnp.float32)
    qc = q.astype(jnp.float32).reshape(b, h, n_chunks, RET_CHUNK, dk)
    kc = k.astype(jnp.float32).reshape(b, h, n_chunks, RET_CHUNK, dk)
    vc = v.astype(jnp.float32).reshape(b, h, n_chunks, RET_CHUNK, dv)
    pos = jnp.arange(RET_CHUNK, dtype=jnp.float32)
    diff = pos[:, None] - pos[None, :]
    keep = diff > 0 if strict else diff >= 0
    intra_decay = jnp.where(keep, jnp.exp(ld[:, None, None] * jnp.maximum(diff, 0.0)), 0.0)
    scores = jnp.einsum('bhncd,bhnmd->bhncm', qc, kc) * intra_decay[None, :, None]
    intra = jnp.einsum('bhncm,bhnme->bhnce', scores, vc)
    zeta = jnp.exp(ld[:, None] * (RET_CHUNK - 1.0 - pos))
    chunk_kv = jnp.einsum('bhnmd,bhnme->nbhde', kc * zeta[None, :, None, :, None], vc)
    chunk_decay = jnp.exp(ld * RET_CHUNK)[None, :, None, None]

    def step(state, kv):
        return chunk_decay * state + kv, state

    _, r_prev = lax.scan(step, r0.astype(jnp.float32), chunk_kv)
    xi = jnp.exp(ld[:, None] * (pos + 1.0))
    cross = jnp.einsum('bhncd,nbhde->bhnce', qc * xi[None, :, None, :, None], r_prev)
    return (intra + cross).reshape(b, h, n_tok, dv)


def retention_final_state(k, v, log_decay):
    n_tok = k.shape[2]
    ld = log_decay.astype(jnp.float32)
    w = jnp.exp(ld[:, None] * (n_tok - 1.0 - jnp.arange(n_tok, dtype=jnp.float32)))
    return jnp.einsum('bhld,bhle->bhde', k.astype(jnp.float32) * w[None, :, :, None], v.astype(jnp.float32))


def retention_mixer(q, k, v, g, ld_f, ld_b, r0_f, r0_b):
    out_f = retention_chunkwise(q, k, v, ld_f, r0_f, strict=False)
    out_b = jnp.flip(retention_chunkwise(jnp.flip(q, 2), jnp.flip(k, 2), jnp.flip(v, 2), ld_b, r0_b, strict=True), 2)
    y = out_f + out_b
    mu = jnp.mean(y, axis=-1, keepdims=True)
    var = jnp.mean((y - mu) ** 2, axis=-1, keepdims=True)
    y = (y - mu) * lax.rsqrt(var + NORM_EPS)
    b, h, n_tok, dv = y.shape
    y = y.transpose(0, 2, 1, 3).reshape(b, n_tok, h * dv)
    return jax.nn.silu(g.astype(jnp.float32)) * y


def conv_ffn(h, w_up, conv_w, conv_b, w_down):
    a, g = jnp.split(h @ w_up, 2, axis=-1)
    n_tok = g.shape[1]
    half = CONV_W // 2
    gp = jnp.pad(g, ((0, 0), (half, half), (0, 0)))
    g_conv = conv_b + gp[:, 0:n_tok] * conv_w[0]
    for j in range(1, CONV_W):
        g_conv = g_conv + gp[:, j:j + n_tok] * conv_w[j]
    return (jax.nn.gelu(g_conv) * a) @ w_down


def setup_inputs(seed: int = 0) -> dict:
    key = jax.random.key(seed)
    ks = jax.random.split(key, 32)
    f32 = jnp.float32

    def nrm(k, shape, s):
        return s * jax.random.normal(k, shape, f32)

    gshape = (DEPTH, S5_GROUPS, S5_STATE)
    lam_re = -0.5 * jnp.ones(gshape, f32)
    lam_im = math.pi * jnp.broadcast_to(jnp.arange(S5_STATE, dtype=f32), gshape)

    def log_dt(k):
        return math.log(DT_MIN) + jax.random.uniform(k, (DEPTH, S5_GROUPS), f32) * (math.log(DT_MAX) - math.log(DT_MIN))

    base_decay = jnp.log(1.0 - 2.0 ** (-5.0 - jnp.arange(RET_HEADS, dtype=f32)))
    return {
        "x": nrm(ks[0], (BATCH, SEQ, D_MODEL), 1.0),
        "c": nrm(ks[1], (BATCH, D_MODEL), 1.0),
        "ctx": nrm(ks[2], (BATCH, CTX_LEN, D_MODEL), 1.0),
        "c_ctx": nrm(ks[3], (D_MODEL,), 1.0),
        "w_mod": nrm(ks[4], (DEPTH, D_MODEL, 6 * D_MODEL), 0.5 * D_MODEL ** -0.5),
        "b_mod": nrm(ks[5], (DEPTH, 6 * D_MODEL), 0.01),
        "norm1_w": 1.0 + nrm(ks[6], (DEPTH, D_MODEL), 0.02),
        "w_in": nrm(ks[7], (DEPTH, D_MODEL, IN_COLS), D_MODEL ** -0.5),
        "s5_lambda_re_f": lam_re + nrm(ks[8], gshape, 0.01),
        "s5_lambda_im_f": lam_im + nrm(ks[9], gshape, 0.01),
        "s5_log_step_f": log_dt(ks[10]),
        "s5_lambda_re_b": lam_re + nrm(ks[11], gshape, 0.01),
        "s5_lambda_im_b": lam_im + nrm(ks[12], gshape, 0.01),
        "s5_log_step_b": log_dt(ks[13]),
        "s5_b_re": nrm(ks[14], (DEPTH, S5_GROUPS, S5_STATE, S5_GROUP), (2.0 * S5_GROUP) ** -0.5),
        "s5_b_im": nrm(ks[15], (DEPTH, S5_GROUPS, S5_STATE, S5_GROUP), (2.0 * S5_GROUP) ** -0.5),
        "s5_c_re": nrm(ks[16], (DEPTH, S5_GROUPS, S5_GROUP, S5_STATE), 0.5),
        "s5_c_im": nrm(ks[17], (DEPTH, S5_GROUPS, S5_GROUP, S5_STATE), 0.5),
        "s5_d": nrm(ks[18], (DEPTH, S5_WIDTH), 0.5),
        "s5_w_glu": nrm(ks[19], (DEPTH, S5_WIDTH, S5_WIDTH), S5_WIDTH ** -0.5),
        "s5_b_glu": nrm(ks[20], (DEPTH, S5_WIDTH), 0.01),
        "ret_log_decay_f": base_decay * jnp.exp(nrm(ks[21], (DEPTH, RET_HEADS), 0.05)),
        "ret_log_decay_b": base_decay * jnp.exp(nrm(ks[22], (DEPTH, RET_HEADS), 0.05)),
        "w_out": nrm(ks[23], (DEPTH, D_MODEL, D_MODEL), D_MODEL ** -0.5),
        "norm2_w": 1.0 + nrm(ks[24], (DEPTH, D_MODEL), 0.02),
        "w_up": nrm(ks[25], (DEPTH, D_MODEL, 2 * D_FF), D_MODEL ** -0.5),
        "conv_w": nrm(ks[26], (DEPTH, CONV_W, D_FF), CONV_W ** -0.5),
        "conv_b": nrm(ks[27], (DEPTH, D_FF), 0.01),
        "w_down": nrm(ks[28], (DEPTH, D_FF, D_MODEL), D_FF ** -0.5),
        "final_norm_w": 1.0 + nrm(ks[29], (D_MODEL,), 0.02),
    }


def reference(x, c, ctx, c_ctx, w_mod, b_mod, norm1_w, w_in,
              s5_lambda_re_f, s5_lambda_im_f, s5_log_step_f,
              s5_lambda_re_b, s5_lambda_im_b, s5_log_step_b,
              s5_b_re, s5_b_im, s5_c_re, s5_c_im, s5_d, s5_w_glu, s5_b_glu,
              ret_log_decay_f, ret_log_decay_b, w_out,
              norm2_w, w_up, conv_w, conv_b, w_down, final_norm_w):
    batch = x.shape[0]
    zero_s5 = jnp.zeros((batch, S5_GROUPS, S5_STATE), jnp.complex64)
    zero_ret = jnp.zeros((batch, RET_HEADS, RET_HEAD_DIM, RET_HEAD_DIM), jnp.float32)
    for layer in range(DEPTH):
        mx = adaln(c, w_mod[layer], b_mod[layer])
        mc = adaln(c_ctx, w_mod[layer], b_mod[layer])
        hx = modulate(rms_norm(x, norm1_w[layer]), mx[0], mx[1])
        hc = modulate(rms_norm(ctx, norm1_w[layer]), mc[0], mc[1])
        ux, qx, kx, vx, gx = split_projection(hx @ w_in[layer], rotate=True)
        uc, qc, kc, vc, gc = split_projection(hc @ w_in[layer], rotate=False)
        lam_f, bbar_f = s5_discretize(s5_lambda_re_f[layer], s5_lambda_im_f[layer], s5_log_step_f[layer], s5_b_re[layer], s5_b_im[layer])
        lam_b, bbar_b = s5_discretize(s5_lambda_re_b[layer], s5_lambda_im_b[layer], s5_log_step_b[layer], s5_b_re[layer], s5_b_im[layer])
        hc_f = s5_scan(uc, lam_f, bbar_f, zero_s5, reverse=False)
        hc_b = s5_scan(uc, lam_b, bbar_b, zero_s5, reverse=True)
        rc_f = retention_final_state(kc, vc, ret_log_decay_f[layer])
        rc_b = retention_final_state(jnp.flip(kc, 2), jnp.flip(vc, 2), ret_log_decay_b[layer])
        hx_f = s5_scan(ux, lam_f, bbar_f, hc_f[:, -1], reverse=False)
        hx_b = s5_scan(ux, lam_b, bbar_b, hc_b[:, 0], reverse=True)
        s5_x = s5_readout(ux, hx_f, hx_b, s5_c_re[layer], s5_c_im[layer], s5_d[layer], s5_w_glu[layer], s5_b_glu[layer])
        ret_x = retention_mixer(qx, kx, vx, gx, ret_log_decay_f[layer], ret_log_decay_b[layer], rc_f, rc_b)
        mix_x = jnp.concatenate([s5_x, ret_x], axis=-1).astype(x.dtype) @ w_out[layer]
        x = x + mx[2] * mix_x
        hx2 = modulate(rms_norm(x, norm2_w[layer]), mx[3], mx[4])
        x = x + mx[5] * conv_ffn(hx2, w_up[layer], conv_w[layer], conv_b[layer], w_down[layer])
        if layer < DEPTH - 1:
            s5_c = s5_readout(uc, hc_f, hc_b, s5_c_re[layer], s5_c_im[layer], s5_d[layer], s5_w_glu[layer], s5_b_glu[layer])
            ret_c = retention_mixer(qc, kc, vc, gc, ret_log_decay_f[layer], ret_log_decay_b[layer], zero_ret, zero_ret)
            ctx = ctx + mc[2] * (jnp.concatenate([s5_c, ret_c], axis=-1).astype(ctx.dtype) @ w_out[layer])
            hc2 = modulate(rms_norm(ctx, norm2_w[layer]), mc[3], mc[4])
            ctx = ctx + mc[5] * conv_ffn(hc2, w_up[layer], conv_w[layer], conv_b[layer], w_down[layer])
    return rms_norm(x, final_norm_w)
```

```python
import contextlib
import numpy as np
import concourse.bass as bass
import concourse.mybir as mybir
from concourse.bass_utils import run_bass_kernel_spmd

F32 = mybir.dt.float32
BF16 = mybir.dt.bfloat16
I32 = mybir.dt.int32
AF = mybir.ActivationFunctionType
ALU = mybir.AluOpType

D = 1024
T = 2048
NT = 16
NCORES = 8
SB_BASE = 16512
SB_TOP = 229344


class Res:
    __slots__ = ("name", "last_w", "readers")

    def __init__(self, name):
        self.name = name
        self.last_w = None
        self.readers = []


class Op:
    __slots__ = ("eng", "fn", "deps", "needs_inc", "sig", "kind", "group", "inc")

    def __init__(self, eng, fn, kind, group, inc):
        self.eng = eng
        self.fn = fn
        self.deps = []
        self.needs_inc = False
        self.sig = None
        self.kind = kind
        self.group = group
        self.inc = inc


class Prog:
    ENGS = ("pe", "act", "dve", "pool", "sp")

    def __init__(self, nc):
        self.nc = nc
        self.ops = []
        self.res = {}

    def _r(self, x):
        if isinstance(x, Res):
            return x
        if x not in self.res:
            self.res[x] = Res(x)
        return self.res[x]

    def _add(self, op, reads, writes):
        reads = [self._r(x) for x in reads]
        writes = [self._r(x) for x in writes]
        deps = set()
        for x in reads:
            if x.last_w is not None:
                deps.add(x.last_w)
            if x.name.startswith("ps"):
                for rd in x.readers:
                    if rd.eng != op.eng:
                        deps.add(rd)
        for x in writes:
            if x.last_w is not None:
                deps.add(x.last_w)
            deps.update(x.readers)
        deps.discard(op)
        for d in deps:
            if d.kind == "c" and op.kind == "c" and d.eng == "pe" and op.eng == "pe":
                continue
            op.deps.append(d)
            d.needs_inc = True
        for x in reads:
            x.readers.append(op)
        for x in writes:
            x.last_w = op
            x.readers = []
        self.ops.append(op)
        return op

    def c(self, eng, fn, reads=(), writes=()):
        return self._add(Op(eng, fn, "c", None, 1), reads, writes)

    def dma(self, eng, fn, group, reads=(), writes=(), inc=16):
        op = Op(eng, fn, "d", group, inc)
        op.needs_inc = True
        return self._add(op, reads, writes)

    def emit(self, final_wait_groups=()):
        nc = self.nc
        sems = {}
        with contextlib.ExitStack() as st:
            cnt = {}
            for op in self.ops:
                if op.kind == "c":
                    if op.needs_inc:
                        k = "E_" + op.eng
                        cnt[k] = cnt.get(k, 0) + 1
                        op.sig = (k, cnt[k])
                else:
                    k = "D_" + op.group
                    cnt[k] = cnt.get(k, 0) + op.inc
                    op.sig = (k, cnt[k])
            for k in cnt:
                sems[k] = st.enter_context(nc.semaphore(k))
            engobj = {"pe": "tensor", "act": "scalar", "dve": "vector", "pool": "gpsimd", "sp": "sync"}
            finals = [("D_" + g, cnt["D_" + g]) for g in final_wait_groups]
            with nc.Block() as block:
                for e in self.ENGS:
                    ops = [o for o in self.ops if o.eng == e]

                    def body(eng, ops=ops, e=e):
                        seen = {}
                        for op in ops:
                            need = {}
                            for d in op.deps:
                                s, v = d.sig
                                if v > need.get(s, 0):
                                    need[s] = v
                            for s, v in need.items():
                                if seen.get(s, 0) >= v:
                                    continue
                                eng.wait_ge(sems[s], v)
                                seen[s] = v
                            ins = op.fn(eng)
                            if op.needs_inc:
                                ins.then_inc(sems[op.sig[0]], op.inc)
                        if e == "sp":
                            for s, v in finals:
                                eng.wait_ge(sems[s], v)
                    if ops or e == "sp":
                        getattr(block, engobj[e])(body)
        return cnt


class Arena:
    def __init__(self, nc):
        self.nc = nc
        self.off = SB_BASE
        self.n = 0

    def alloc(self, name, shape, dtype, at=None):
        esz = 4 if dtype in (F32, I32) else 2
        per = int(np.prod(shape[1:])) * esz
        per = (per + 63) // 64 * 64
        if at is None:
            at = self.off
            self.off += per
            assert self.off <= SB_TOP, (name, self.off)
        self.n += 1
        return self.nc.alloc_sbuf_tensor_at(f"{name}_{self.n}", list(shape), dtype, offset=at)


class _Stop(Exception):
    pass


def build(debug=False, stop=None):
    nc = bass.Bass("TRN2", target_bir_lowering=False)
    P = Prog(nc)
    A = Arena(nc)

    def din(name, shape, dt=F32):
        return nc.dram_tensor(name, list(shape), dt, kind="ExternalInput").ap()

    x_d = din("x_loc", [T, D])
    cvec_d = din("cvec", [128, 16])
    wmod_d = din("w_mod", [D, 6 * D])
    bmod_d = din("b_mod2", [2, 6 * D])
    n1w_d = din("n1w", [128, 8])
    n2w_d = din("n2w", [128, 8])
    win_d = din("w_in", [D, 2560])
    segf_d = din("segf", [128, 1])
    ctx_d = din("ctxb", [256, D])
    ldv_d = din("ldv", [128, 8])
    msk_d = din("msk", [128, 8])
    out_d = nc.dram_tensor("out", [T, D], F32, kind="ExternalOutput").ap()
    dbg = {}

    def dout(name, shape):
        dbg[name] = nc.dram_tensor(name, list(shape), F32, kind="ExternalOutput").ap()
        return dbg[name]

    def stage(name):
        if stop == name:
            raise _Stop()

    stopped = False
    try:
        _body(nc, P, A, debug, stage, dout, dbg, din, x_d, cvec_d, wmod_d, bmod_d, n1w_d, n2w_d, win_d, segf_d, ctx_d, ldv_d, msk_d, out_d)
    except _Stop:
        stopped = True
    if stopped:
        xfin = nc.alloc_sbuf_tensor_at("xfin", [128, D], F32, offset=SB_TOP - 4096)
        rfin = ["xfin"] + [n for n in P.res]
        for i in range(NT):
            P.dma("sp", lambda e, i=i: e.dma_start(out=xfin[:, :], in_=x_d[i * 128:(i + 1) * 128, :]), "xinF", reads=[], writes=rfin if i == 0 else ["xfin"])
            P.dma("sp", lambda e, i=i: e.dma_start(out=out_d[i * 128:(i + 1) * 128, :], in_=xfin[:, :]), "xout", reads=["xfin"])
    has_dbg = any(o.kind == "d" and o.group == "dbg" for o in P.ops)
    P.emit(final_wait_groups=["xout"] + (["dbg"] if has_dbg else []))
    return nc, dbg


def _body(nc, P, A, debug, stage, dout, dbg, din, x_d, cvec_d, wmod_d, bmod_d, n1w_d, n2w_d, win_d, segf_d, ctx_d, ldv_d, msk_d, out_d):

    ident_f = A.alloc("ident_f", [128, 128], F32)
    ident_b = A.alloc("ident_b", [128, 128], BF16)
    ones_f = A.alloc("ones_f", [128, 128], F32)
    P.c("pool", lambda e: e.memset(ones_f[:, :], 1.0), writes=["ones_f"])
    negpi = A.alloc("negpi", [128, 1], F32)
    epst = A.alloc("epst", [128, 1], F32)
    P.c("pool", lambda e: e.memset(negpi[:, :], -float(np.pi)), writes=["negpi"])
    P.c("pool", lambda e: e.memset(epst[:, :], 1e-6), writes=["epst"])
    P.c("pool", lambda e: e.memset(ident_f[:, :], 0.0), writes=["ident_f"])
    P.c("pool", lambda e: e.affine_select(out=ident_f[:, :], in_=ones_f[:, :], pattern=[[1, 128]],
                                          compare_op=ALU.is_equal, fill=0.0, base=0, channel_multiplier=-1),
        reads=["ones_f"], writes=["ident_f"])
    P.c("dve", lambda e: e.tensor_copy(out=ident_b[:, :], in_=ident_f[:, :]), reads=["ident_f"], writes=["ident_b"])

    cvec = A.alloc("cvec", [128, 16], F32)
    s_bf = A.alloc("s_bf", [128, 16], BF16)
    n1w = A.alloc("n1w", [128, 8], F32)
    n2w = A.alloc("n2w", [128, 8], F32)
    modT = A.alloc("modT", [128, 32, 2], F32)
    g1x = A.alloc("g1x", [128, 8], F32)
    g1c = A.alloc("g1c", [128, 8], F32)
    g2x = A.alloc("g2x", [128, 8], F32)
    bc2 = A.alloc("bc2", [128, D], F32)
    bc5 = A.alloc("bc5", [128, D], F32)
    segf = A.alloc("segf", [128, 1], F32)
    colf = A.alloc("colf", [128, 1], F32)
    rowb = A.alloc("rowb", [128, 1], F32)
    rowv = A.alloc("rowv", [128, 16], F32)
    invf = A.alloc("invf", [128, 32], F32)
    big0 = A.off
    wm = A.alloc("wm", [128, 8, 2048], BF16)
    bmod = A.alloc("bmod", [2, 6 * D], F32)
    modrow = A.alloc("modrow", [2, 6 * D], F32)
    ang = A.alloc("ang", [128, 16, 64], F32)
    tq = A.alloc("tq", [128, 1024], F32)
    ti = A.alloc("ti", [128, 1024], I32)
    tf = A.alloc("tf", [128, 1024], F32)
    assert A.off - big0 == 96 * 1024, A.off - big0
    r2_base = A.off
    scratchB_off = A.off
    A.off += 37 * 1024 + 512
    P.dma("sp", lambda e: e.dma_start(out=cvec[:, :], in_=cvec_d[:, :]), "small", writes=["cvec"])
    P.dma("sp", lambda e: e.dma_start(out=bmod[:, :], in_=bmod_d[:, :]), "small2", writes=["bmod"])
    P.dma("sp", lambda e: e.dma_start(out=n1w[:, :], in_=n1w_d[:, :]), "small3", writes=["n1w"])
    P.dma("sp", lambda e: e.dma_start(out=n2w[:, :], in_=n2w_d[:, :]), "small4", writes=["n2w"])
    P.c("act", lambda e: e.activation(out=s_bf[:, :], in_=cvec[:, :], func=AF.Silu), reads=["cvec"], writes=["s_bf"])

    wm_src = wmod_d.rearrange("(k p) n -> p k n", p=128)
    psA = [nc.alloc_psum_tensor(f"ps{i}", [128, 512], F32) for i in range(7)]
    s3 = s_bf[:, :].rearrange("p (k w) -> p k w", w=2)
    wmB = nc.alloc_sbuf_tensor_at("wmB", [128, 8, 2048], BF16, offset=scratchB_off)
    wms = [(wm, "wm"), (wmB, "wmB"), (wm, "wm")]
    def wm_dma(cb):
        wt_, wn_ = wms[cb]
        P.dma("pool", lambda e: e.dma_start(out=wt_[:, :, :], in_=wm_src[:, :, cb * 2048:(cb + 1) * 2048]), f"wmd{cb}", writes=[wn_])

    wm_dma(0)
    wm_dma(1)
    for nb in range(12):
        ps = psA[nb % 2]
        wt_, wn_ = wms[nb // 4]
        if nb == 4:
            wm_dma(2)
        for k in range(8):
            P.c("pe", lambda e, ps=ps, k=k, nb=nb, wt_=wt_: e.matmul(ps[0:2, :], lhsT=s3[:, k, :], rhs=wt_[:, k, (nb % 4) * 512:(nb % 4 + 1) * 512],
                                                                  start=(k == 0), stop=(k == 7)),
                reads=["s_bf", wn_], writes=[f"ps{nb % 2}"])
        P.c("dve", lambda e, ps=ps, nb=nb: e.tensor_tensor(out=modrow[:, nb * 512:(nb + 1) * 512], in0=ps[0:2, :],
                                                         in1=bmod[:, nb * 512:(nb + 1) * 512], op=ALU.add),
            reads=[f"ps{nb % 2}", "bmod"], writes=["modrow"])
    psT = psA[2]
    chunks = list(range(0, 16)) + list(range(24, 40))
    for j, ch in enumerate(chunks):
        P.c("pe", lambda e, j=j, ch=ch: e.transpose(psT[:, 2 * j:2 * j + 2], modrow[:, ch * 128:(ch + 1) * 128], ident_f[0:2, 0:2]),
            reads=["modrow", "ident_f"], writes=["ps2"])
    P.c("dve", lambda e: e.tensor_copy(out=modT[:, :, :].rearrange("p a b -> p (a b)"), in_=psT[:, 0:64]),
        reads=["ps2"], writes=["modT"])
    P.c("dve", lambda e: e.scalar_tensor_tensor(out=g1x[:, :], in0=modT[:, 8:16, 0], scalar=1.0, in1=n1w[:, :],
                                                op0=ALU.add, op1=ALU.mult), reads=["modT", "n1w"], writes=["g1x"])
    P.c("dve", lambda e: e.scalar_tensor_tensor(out=g1c[:, :], in0=modT[:, 8:16, 1], scalar=1.0, in1=n1w[:, :],
                                                op0=ALU.add, op1=ALU.mult), reads=["modT", "n1w"], writes=["g1c"])
    P.c("dve", lambda e: e.scalar_tensor_tensor(out=g2x[:, :], in0=modT[:, 24:32, 0], scalar=1.0, in1=n2w[:, :],
                                                op0=ALU.add, op1=ALU.mult), reads=["modT", "n2w"], writes=["g2x"])

    if debug:
        d_mod = dout("d_mod", [2, 6 * D])
        P.dma("sp", lambda e: e.dma_start(out=d_mod[:, :], in_=modrow[:, :]), "dbg", reads=["modrow"])
        d_g1x = dout("d_g1x", [128, 8])
        P.dma("sp", lambda e: e.dma_start(out=d_g1x[:, :], in_=g1x[:, :]), "dbg", reads=["g1x"])

    stage("A")
    for (bc, base, nm) in ((bc2, 2048, "bc2"), (bc5, 5120, "bc5")):
        for hh in range(2):
            ps = psA[3 + hh]
            P.c("pe", lambda e, ps=ps, base=base, hh=hh: e.matmul(ps[:, :], lhsT=ones_f[0:1, :], rhs=modrow[0:1, base + hh * 512: base + (hh + 1) * 512],
                                                                start=True, stop=True), reads=["ones_f", "modrow"], writes=[f"ps{3 + hh}"])
            P.c("act", lambda e, ps=ps, bc=bc, hh=hh: e.activation(out=bc[:, hh * 512:(hh + 1) * 512], in_=ps[:, :], func=AF.Copy),
                reads=[f"ps{3 + hh}"], writes=[nm])

    P.dma("sp", lambda e: e.dma_start(out=segf[:, :], in_=segf_d[:, :]), "small5", writes=["segf"])
    for hb in range(2):
        P.c("pool", lambda e, hb=hb: e.iota(colf[hb * 64:(hb + 1) * 64, :], pattern=[[0, 1]], base=0, channel_multiplier=1,
                                           allow_small_or_imprecise_dtypes=True), writes=["colf"])
        P.c("pool", lambda e, hb=hb: e.memset(rowb[hb * 64:(hb + 1) * 64, :], float(hb)), writes=["rowb"])
    P.c("pool", lambda e: e.iota(rowv[:, :], pattern=[[2, 16]], base=0, channel_multiplier=0, allow_small_or_imprecise_dtypes=True),
        writes=["rowv"])
    for j in range(32):
        P.c("pool", lambda e, j=j: e.memset(invf[:, j:j + 1], float(np.float32(10000.0) ** (-np.float32(j) / np.float32(32)))),
            writes=["invf"])
    P.c("dve", lambda e: e.scalar_tensor_tensor(out=rowb[:, :], in0=segf[:, :], scalar=32.0, in1=rowb[:, :], op0=ALU.mult, op1=ALU.add),
        reads=["segf", "rowb"], writes=["rowb"])
    P.c("dve", lambda e: e.tensor_scalar(out=rowv[:, :], in0=rowv[:, :], scalar1=rowb[:, 0:1], scalar2=None, op0=ALU.add),
        reads=["rowv", "rowb"], writes=["rowv"])
    ropeC = A.alloc("ropeC", [128, 16, 64], F32)
    ropeS = A.alloc("ropeS", [128, 16, 64], F32)
    ropeCk = A.alloc("ropeCk", [128, 16, 64], F32)
    ropeSk = A.alloc("ropeSk", [128, 16, 64], F32)
    for i in range(16):
        P.c("dve", lambda e, i=i: e.tensor_scalar(out=ang[:, i, 0:32], in0=invf[:, :], scalar1=rowv[:, i:i + 1], scalar2=None, op0=ALU.mult),
            reads=["invf", "rowv"], writes=["ang"])
        P.c("dve", lambda e, i=i: e.tensor_scalar(out=ang[:, i, 32:64], in0=invf[:, :], scalar1=colf[:, 0:1], scalar2=None, op0=ALU.mult),
            reads=["invf", "colf"], writes=["ang"])
    angf = ang[:, :, :].rearrange("p a b -> p (a b)")
    TWO_PI = 2.0 * np.pi
    for (dst, off, nm) in ((ropeS, 0.5, "ropeS"), (ropeC, 0.75, "ropeC")):
        dflat = dst[:, :, :].rearrange("p a b -> p (a b)")
        P.c("dve", lambda e, off=off: e.tensor_scalar(out=tq[:, :], in0=angf, scalar1=1.0 / TWO_PI, scalar2=off, op0=ALU.mult, op1=ALU.add),
            reads=["ang"], writes=["tq"])
        P.c("dve", lambda e: e.tensor_copy(out=ti[:, :], in_=tq[:, :]), reads=["tq"], writes=["ti"])
        P.c("dve", lambda e: e.tensor_copy(out=tf[:, :], in_=ti[:, :]), reads=["ti"], writes=["tf"])
        P.c("dve", lambda e: e.tensor_tensor(out=tq[:, :], in0=tq[:, :], in1=tf[:, :], op=ALU.subtract), reads=["tq", "tf"], writes=["tq"])
        P.c("dve", lambda e: e.tensor_scalar(out=tf[:, :], in0=tq[:, :], scalar1=0.0, scalar2=None, op0=ALU.is_lt), reads=["tq"], writes=["tf"])
        P.c("dve", lambda e: e.tensor_tensor(out=tq[:, :], in0=tq[:, :], in1=tf[:, :], op=ALU.add), reads=["tq", "tf"], writes=["tq"])
        P.c("act", lambda e, dflat=dflat: e.activation(out=dflat, in_=tq[:, :], func=AF.Sin, scale=TWO_PI, bias=negpi[:, 0:1]),
            reads=["tq", "negpi"], writes=[nm])
    KS = float(128.0 ** -0.5)
    P.c("dve", lambda e: e.tensor_scalar(out=ropeCk[:, :, :], in0=ropeC[:, :, :], scalar1=KS, scalar2=None, op0=ALU.mult), reads=["ropeC"], writes=["ropeCk"])
    P.c("dve", lambda e: e.tensor_scalar(out=ropeSk[:, :, :], in0=ropeS[:, :, :], scalar1=KS, scalar2=None, op0=ALU.mult), reads=["ropeS"], writes=["ropeSk"])

    win = A.alloc("win", [128, 8, 2560], BF16)
    win_src = win_d.rearrange("(k p) n -> p k n", p=128)
    for cb in range(2):
        P.dma("pool", lambda e, cb=cb: e.dma_start(out=win[:, :, cb * 1280:(cb + 1) * 1280], in_=win_src[:, :, cb * 1280:(cb + 1) * 1280]),
              f"win{cb}", writes=["win"])

    A2 = Arena(nc)
    A2.off = big0
    qT = A2.alloc("qT", [128, 4, T], BF16)
    kT = A2.alloc("kT", [128, 4, T], BF16)
    k_tm = A2.alloc("k_tm", [128, NT, 512], BF16)
    v_tm = A2.alloc("v_tm", [128, NT, 512], BF16)
    gate = A2.alloc("gate", [128, NT, 512], BF16)
    uT = A2.alloc("uT", [128, 4, T], BF16)
    assert A2.off <= big0 + 96 * 1024
    stage_a_res = [P._r(n) for n in ("wm", "modrow", "bmod", "ang", "tq", "ti", "tf")]
    for nm in ("qT", "kT", "k_tm", "v_tm", "gate", "uT"):
        r_ = P._r(nm)
        for o in stage_a_res:
            r_.readers.extend(o.readers)
            if o.last_w is not None:
                r_.readers.append(o.last_w)

    r2_end = A.off
    SBA = Arena(nc); SBA.off = scratchB_off
    xts = [SBA.alloc(f"xt{i}", [128, D], F32) for i in range(2)]
    xsl = [SBA.alloc(f"xs{i}", [128, D], F32) for i in range(2)]
    ssl = [SBA.alloc(f"ss{i}", [128, 4], F32) for i in range(2)]
    hxT = [SBA.alloc(f"hxT{i}", [128, 8, 512], BF16) for i in range(2)]
    qrot = SBA.alloc("qrot", [128, 512], BF16)
    t1 = SBA.alloc("t1", [128, 256], F32)
    t2 = SBA.alloc("t2", [128, 256], F32)
    assert SBA.off <= scratchB_off + 37 * 1024 + 512, SBA.off - scratchB_off
    for nm_ in ("xt0", "xt1", "xs0", "xs1", "ss0a", "ss0b", "ss0c", "ss1a", "ss1b", "ss1c", "hxT0", "hxT1", "qrot", "t1", "t2"):
        r_ = P._r(nm_)
        o = P._r("wmB")
        r_.readers.extend(o.readers)
        if o.last_w is not None:
            r_.readers.append(o.last_w)
    psX = [psA[0], psA[1]]
    psP = [psA[2], psA[3], psA[4], psA[5]]
    psU = psA[6]
    psTq = nc.alloc_psum_tensor("psTq", [128, 1024], BF16)

    def norm_front(src_ap, pp):
        xt, xs_, s_ = xts[pp], xsl[pp], ssl[pp]
        P.dma("sp", lambda e: e.dma_start(out=xt[:, :], in_=src_ap), f"xin{pp}", writes=[f"xt{pp}"])
        P.c("act", lambda e: e.activation(out=xs_[:, :], in_=xt[:, :], func=AF.Square, accum_out=s_[:, 0:1]),
            reads=[f"xt{pp}"], writes=[f"xs{pp}", f"ss{pp}a"])
        P.c("act", lambda e: e.activation(out=s_[:, 1:2], in_=s_[:, 0:1], func=AF.Sqrt, scale=1.0 / D, bias=epst[:, 0:1]),
            reads=[f"ss{pp}a", "epst"], writes=[f"ss{pp}b"])
        P.c("dve", lambda e: e.reciprocal(out=s_[:, 2:3], in_=s_[:, 1:2]), reads=[f"ss{pp}b"], writes=[f"ss{pp}c"])
        P.c("dve", lambda e: e.tensor_scalar(out=xs_[:, :], in0=xt[:, :], scalar1=s_[:, 2:3], scalar2=None, op0=ALU.mult),
            reads=[f"xt{pp}", f"ss{pp}c"], writes=[f"xs{pp}"])

    def norm_back(pp, gvec, shvec_fn, dstT, col0, tag):
        xs_ = xsl[pp]
        for k in range(8):
            ps = psX[k // 4]
            P.c("pe", lambda e, ps=ps, k=k: e.transpose(ps[:, (k % 4) * 128:(k % 4 + 1) * 128], xs_[:, k * 128:(k + 1) * 128], ident_f[:, :]),
                reads=[f"xs{pp}", "ident_f"], writes=[f"ps{k // 4}"])
        for k in range(8):
            ps = psX[k // 4]
            P.c("act", lambda e, ps=ps, k=k: e.activation(out=dstT[:, k, col0:col0 + 128], in_=ps[:, (k % 4) * 128:(k % 4 + 1) * 128],
                                                        func=AF.Identity, scale=gvec[:, k:k + 1], bias=shvec_fn(k)),
                reads=[f"ps{k // 4}", "g1x", "g1c", "g2x", "modT"], writes=[tag])

    def rope(ps, Ct, St, i, dst_ap):
        pv = ps[:, :].rearrange("p (h j w) -> p h j w", h=4, w=2)
        dv = dst_ap.rearrange("p (h j w) -> p h j w", h=4, w=2)
        Cb = Ct[:, i, :].unsqueeze(1).broadcast_to([128, 4, 64])
        Sb = St[:, i, :].unsqueeze(1).broadcast_to([128, 4, 64])
        a = t1[:, :].rearrange("p (h j) -> p h j", h=4)
        b = t2[:, :].rearrange("p (h j) -> p h j", h=4)
        return pv, dv, Cb, Sb, a, b

    norm_front(x_d[0:128, :], 0)
    for grp in range(4):
        hT = hxT[grp % 2]
        for ti_ in range(4):
            i = grp * 4 + ti_
            if i + 1 < NT:
                norm_front(x_d[(i + 1) * 128:(i + 2) * 128, :], (i + 1) % 2)
            norm_back(i % 2, g1x, lambda k: modT[:, k, 0:1], hT, ti_ * 128, f"hxT{grp % 2}")
            for nb in range(4):
                ps = psP[nb]
                for k in range(8):
                    P.c("pe", lambda e, ps=ps, k=k, nb=nb, hT=hT, ti_=ti_: e.matmul(
                        ps[:, :], lhsT=hT[:, k, ti_ * 128:(ti_ + 1) * 128], rhs=win[:, k, 512 + nb * 512: 1024 + nb * 512],
                        start=(k == 0), stop=(k == 7)), reads=[f"hxT{grp % 2}", "win"], writes=[f"ps{2 + nb}"])
            for (nb, Ct, St, dstT, nm) in ((0, ropeC, ropeS, qT, "q"), (1, ropeCk, ropeSk, kT, "k")):
                ps = psP[nb]
                dst_ap = qrot[:, :] if nb == 0 else k_tm[:, i, :]
                dres = "qrot" if nb == 0 else "k_tm"
                pv, dv, Cb, Sb, a, b = rope(ps, Ct, St, i, dst_ap)
                rd = [f"ps{2 + nb}", "ropeC", "ropeS", "ropeCk", "ropeSk"]
                P.c("dve", lambda e, pv=pv, Cb=Cb, a=a: e.tensor_tensor(out=a, in0=pv[:, :, :, 0], in1=Cb, op=ALU.mult), reads=rd, writes=["t1"])
                P.c("dve", lambda e, pv=pv, Sb=Sb, b=b: e.tensor_tensor(out=b, in0=pv[:, :, :, 1], in1=Sb, op=ALU.mult), reads=rd, writes=["t2"])
                P.c("dve", lambda e, dv=dv, a=a, b=b: e.tensor_tensor(out=dv[:, :, :, 0], in0=a, in1=b, op=ALU.subtract),
                    reads=["t1", "t2"], writes=[dres])
                P.c("dve", lambda e, pv=pv, Sb=Sb, a=a: e.tensor_tensor(out=a, in0=pv[:, :, :, 0], in1=Sb, op=ALU.mult), reads=rd + [dres], writes=["t1"])
                P.c("dve", lambda e, pv=pv, Cb=Cb, b=b: e.tensor_tensor(out=b, in0=pv[:, :, :, 1], in1=Cb, op=ALU.mult), reads=rd + [dres], writes=["t2"])
                P.c("dve", lambda e, dv=dv, a=a, b=b: e.tensor_tensor(out=dv[:, :, :, 1], in0=a, in1=b, op=ALU.add),
                    reads=["t1", "t2"], writes=[dres])
                for h in range(4):
                    P.c("pe", lambda e, h=h, dst_ap=dst_ap: e.transpose(psTq[:, h * 128:(h + 1) * 128], dst_ap[:, h * 128:(h + 1) * 128], ident_b[:, :]),
                        reads=[dres, "ident_b"], writes=["psTq"])
                P.c("act", lambda e, dstT=dstT, i=i: e.activation(out=dstT[:, :, i * 128:(i + 1) * 128],
                                                                 in_=psTq[:, 0:512].rearrange("p (h t) -> p h t", h=4), func=AF.Copy),
                    reads=["psTq"], writes=[nm + "T"])
            P.c("act", lambda e, i=i: e.activation(out=v_tm[:, i, :], in_=psP[2][:, :], func=AF.Copy), reads=["ps4"], writes=["v_tm"])
            P.c("act", lambda e, i=i: e.activation(out=gate[:, i, :], in_=psP[3][:, :], func=AF.Silu), reads=["ps5"], writes=["gate"])
        for fc in range(4):
            for k in range(8):
                P.c("pe", lambda e, fc=fc, k=k, hT=hT: e.matmul(psU[:, :], lhsT=win[:, k, fc * 128:(fc + 1) * 128], rhs=hT[:, k, :],
                                                              start=(k == 0), stop=(k == 7)), reads=[f"hxT{grp % 2}", "win"], writes=["ps6"])
            P.c("dve", lambda e, fc=fc, grp=grp: e.tensor_copy(out=uT[:, fc, grp * 512:(grp + 1) * 512], in_=psU[:, :]),
                reads=["ps6"], writes=["uT"])

    stage("B")

    def alias(new_names, old_names):
        olds = [P._r(n) for n in old_names]
        for nm in new_names:
            r_ = P._r(nm)
            for o in olds:
                r_.readers.extend(o.readers)
                if o.last_w is not None:
                    r_.readers.append(o.last_w)

    spare_off = A.off
    kc_tm = A.alloc("kc_tm", [128, 2, 512], BF16)
    vc_tm = A.alloc("vc_tm", [128, 2, 512], BF16)
    ucT = A.alloc("ucT", [128, 4, 256], BF16)
    KSC = float(128.0 ** -0.5)
    hT = hxT[0]
    for j in range(2):
        norm_front(ctx_d[j * 128:(j + 1) * 128, :], j)
        norm_back(j, g1c, lambda k: modT[:, k, 1:2], hT, j * 128, "hxT0")
        for (nb, dst, nm) in ((1, kc_tm, "kc_tm"), (2, vc_tm, "vc_tm")):
            ps = psP[nb]
            for k in range(8):
                P.c("pe", lambda e, ps=ps, k=k, nb=nb, j=j: e.matmul(ps[:, :], lhsT=hT[:, k, j * 128:(j + 1) * 128],
                                                                  rhs=win[:, k, 512 + nb * 512: 1024 + nb * 512], start=(k == 0), stop=(k == 7)),
                    reads=["hxT0", "win"], writes=[f"ps{2 + nb}"])
            P.c("act", lambda e, ps=ps, dst=dst, j=j, nb=nb: e.activation(out=dst[:, j, :], in_=ps[:, :], func=AF.Copy, scale=(KSC if nb == 1 else 1.0)),
                reads=[f"ps{2 + nb}"], writes=[nm])
    for fc in range(4):
        for k in range(8):
            P.c("pe", lambda e, fc=fc, k=k: e.matmul(psU[:, 0:256], lhsT=win[:, k, fc * 128:(fc + 1) * 128], rhs=hT[:, k, 0:256],
                                                   start=(k == 0), stop=(k == 7)), reads=["hxT0", "win"], writes=["ps6"])
        P.c("dve", lambda e, fc=fc: e.tensor_copy(out=ucT[:, fc, :], in_=psU[:, 0:256]), reads=["ps6"], writes=["ucT"])

    stage("C")
    ldv = A.alloc("ldv", [128, 8], F32)
    msk = A.alloc("msk", [128, 8], F32)
    P.dma("sp", lambda e: e.dma_start(out=ldv[:, :], in_=ldv_d[:, :]), "small6", writes=["ldv"])
    P.dma("sp", lambda e: e.dma_start(out=msk[:, :], in_=msk_d[:, :]), "small7", writes=["msk"])
    AR = Arena(nc)
    AR.off = r2_base
    old_r2 = ["ropeC", "ropeS", "ropeCk", "ropeSk", "win", "xt0", "xt1", "xs0", "xs1", "ss0a", "ss0b", "ss0c", "ss1a", "ss1b", "ss1c", "hxT0", "hxT1", "qrot", "t1", "t2",
              "invf", "rowv", "colf", "rowb"]
    mixT = AR.alloc("mixT", [128, 8, T], BF16)
    Rb_store = AR.alloc("Rb_store", [128, NT, 512], BF16)
    xg_off = AR.off
    xg = AR.alloc("xg", [128, 4, 1024], F32)
    Dm = AR.alloc("Dm", [128, 4, 128], F32)
    XiF = AR.alloc("XiF", [128, 4, 128], F32)
    XiB = AR.alloc("XiB", [128, 4, 128], F32)
    dm = AR.alloc("dm", [128, 128], F32)
    tabs_off = AR.off
    dpos = AR.alloc("dpos", [128, 128], F32)
    dneg = AR.alloc("dneg", [128, 128], F32)
    mge = AR.alloc("mge", [128, 128], F32)
    mlt = AR.alloc("mlt", [128, 128], F32)
    e1 = AR.alloc("e1", [128, 128], F32)
    e2 = AR.alloc("e2", [128, 128], F32)
    tfree = AR.alloc("tfree", [128, 128], F32)
    pidx = AR.alloc("pidx", [128, 2], F32)
    zarg = AR.alloc("zarg", [128, 8], F32)
    Zeta = AR.alloc("Zeta", [128, 8], F32)
    G128 = AR.alloc("G128", [128, 8], F32)
    G2048 = AR.alloc("G2048", [128, 8], F32)
    nld = AR.alloc("nld", [128, 8], F32)
    ld128 = AR.alloc("ld128", [128, 8], F32)
    Rf = AR.alloc("Rf", [128, 4, 128], F32)
    Rb = AR.alloc("Rb", [128, 4, 128], F32)
    Rcf_off = AR.off
    Rcf = AR.alloc("Rcf", [128, 4, 128], F32)
    Rcb_off = AR.off
    Rcb = AR.alloc("Rcb", [128, 4, 128], F32)
    Rfbf_off = AR.off
    Rf_bf = AR.alloc("Rf_bf", [128, 4, 128], BF16)
    vzs = [AR.alloc(f"vz{i}", [128, 4, 128], BF16) for i in range(2)]
    scm = AR.alloc("scm", [128, 4, 128], BF16)
    qf = AR.alloc("qf", [128, 4, 128], BF16)
    qb = AR.alloc("qb", [128, 4, 128], BF16)
    yn = AR.alloc("yn", [128, 4, 128], F32)
    hacc_t = yn
    retx = AR.alloc("retx", [128, 512], BF16)
    bst = AR.alloc("bst", [128, 4, 6], F32)
    junk2 = AR.alloc("junk2", [128, 128], BF16)
    mv = AR.alloc("mv", [128, 4, 2], F32)
    rs = AR.alloc("rs", [128, 8], F32)
    assert AR.off <= r2_end, (AR.off, r2_end)
    new_r2 = ["mixT", "Rb_store", "xg", "Dm", "XiF", "XiB", "dm", "dpos", "dneg", "mge", "mlt", "e1", "e2", "tfree", "pidx", "zarg", "Zeta",
              "G128", "G2048", "nld", "ld128", "Rf", "Rb", "Rcf", "Rcb", "Rf_bf", "hacc_t", "vz0", "vz1", "scm", "qf", "qb", "yn", "retx", "bst", "bst2", "bst3", "junk2", "mv", "rs"]
    alias(new_r2, old_r2)

    P.c("pool", lambda e: e.iota(pidx[:, 0:1], pattern=[[0, 1]], base=0, channel_multiplier=1, allow_small_or_imprecise_dtypes=True), writes=["pidx"])
    P.c("pool", lambda e: e.iota(tfree[:, :], pattern=[[1, 128]], base=0, channel_multiplier=0, allow_small_or_imprecise_dtypes=True), writes=["tfree"])
    P.c("pool", lambda e: e.iota(dm[:, :], pattern=[[1, 128]], base=0, channel_multiplier=-1, allow_small_or_imprecise_dtypes=True), writes=["dm"])
    P.c("dve", lambda e: e.tensor_scalar(out=pidx[:, 1:2], in0=pidx[:, 0:1], scalar1=-1.0, scalar2=127.0, op0=ALU.mult, op1=ALU.add),
        reads=["pidx"], writes=["pidx"])
    P.c("dve", lambda e: e.tensor_scalar(out=zarg[:, 0:4], in0=ldv[:, 0:4], scalar1=pidx[:, 1:2], scalar2=None, op0=ALU.mult), reads=["ldv", "pidx"], writes=["zarg"])
    P.c("dve", lambda e: e.tensor_scalar(out=zarg[:, 4:8], in0=ldv[:, 4:8], scalar1=pidx[:, 0:1], scalar2=None, op0=ALU.mult), reads=["ldv", "pidx"], writes=["zarg"])
    P.c("act", lambda e: e.activation(out=Zeta[:, :], in_=zarg[:, :], func=AF.Exp), reads=["zarg"], writes=["Zeta"])
    P.c("act", lambda e: e.activation(out=G128[:, :], in_=ldv[:, :], func=AF.Exp, scale=128.0), reads=["ldv"], writes=["G128"])
    P.c("act", lambda e: e.activation(out=G2048[:, :], in_=ldv[:, :], func=AF.Exp, scale=2048.0), reads=["ldv"], writes=["G2048"])
    P.c("dve", lambda e: e.tensor_scalar(out=nld[:, :], in0=ldv[:, :], scalar1=-1.0, scalar2=None, op0=ALU.mult), reads=["ldv"], writes=["nld"])
    P.c("dve", lambda e: e.tensor_scalar(out=ld128[:, :], in0=ldv[:, :], scalar1=128.0, scalar2=None, op0=ALU.mult), reads=["ldv"], writes=["ld128"])
    P.c("dve", lambda e: e.tensor_scalar(out=dpos[:, :], in0=dm[:, :], scalar1=0.0, scalar2=None, op0=ALU.max), reads=["dm"], writes=["dpos"])
    P.c("dve", lambda e: e.tensor_scalar(out=dneg[:, :], in0=dm[:, :], scalar1=-1.0, scalar2=0.0, op0=ALU.mult, op1=ALU.max), reads=["dm"], writes=["dneg"])
    P.c("dve", lambda e: e.tensor_scalar(out=mge[:, :], in0=dm[:, :], scalar1=0.0, scalar2=None, op0=ALU.is_ge), reads=["dm"], writes=["mge"])
    P.c("dve", lambda e: e.tensor_scalar(out=mlt[:, :], in0=dm[:, :], scalar1=0.0, scalar2=None, op0=ALU.is_lt), reads=["dm"], writes=["mlt"])
    for h in range(4):
        P.c("act", lambda e, h=h: e.activation(out=XiF[:, h, :], in_=tfree[:, :], func=AF.Exp, scale=ldv[:, h:h + 1], bias=ldv[:, h:h + 1]),
            reads=["tfree", "ldv"], writes=["XiF"])
        P.c("act", lambda e, h=h: e.activation(out=XiB[:, h, :], in_=tfree[:, :], func=AF.Exp, scale=nld[:, 4 + h:5 + h], bias=ld128[:, 4 + h:5 + h]),
            reads=["tfree", "nld", "ld128"], writes=["XiB"])
        P.c("act", lambda e, h=h: e.activation(out=e1[:, :], in_=dpos[:, :], func=AF.Exp, scale=ldv[:, h:h + 1]), reads=["dpos", "ldv"], writes=["e1"])
        P.c("act", lambda e, h=h: e.activation(out=e2[:, :], in_=dneg[:, :], func=AF.Exp, scale=ldv[:, 4 + h:5 + h]), reads=["dneg", "ldv"], writes=["e2"])
        P.c("dve", lambda e: e.tensor_tensor(out=e1[:, :], in0=e1[:, :], in1=mge[:, :], op=ALU.mult), reads=["e1", "mge"], writes=["e1"])
        P.c("dve", lambda e: e.tensor_tensor(out=e2[:, :], in0=e2[:, :], in1=mlt[:, :], op=ALU.mult), reads=["e2", "mlt"], writes=["e2"])
        P.c("dve", lambda e, h=h: e.tensor_tensor(out=Dm[:, h, :], in0=e1[:, :], in1=e2[:, :], op=ALU.add), reads=["e1", "e2"], writes=["Dm"])

    stage("Dtab")
    psS, psO, psKV = psA[0], psA[1], psA[2]

    vz_extra = [nc.alloc_sbuf_tensor_at("vz2", [128, 4, 128], BF16, offset=tabs_off), nc.alloc_sbuf_tensor_at("vz3", [128, 4, 128], BF16, offset=tabs_off + 1024)]
    vz_tab = [[vzs[0], vz_extra[0]], [vzs[1], vz_extra[1]]]
    vz_nm = [["vz0", "vz2"], ["vz1", "vz3"]]
    kv_bank = [[(psA[2], "ps2"), (psA[4], "ps4")], [(psA[3], "ps3"), (psA[5], "ps5")]]

    def kv_front(ksrc, vsrc, d, kres, vres, lane, par):
        zb = Zeta[:, d * 4:(d + 1) * 4].unsqueeze(2).broadcast_to([128, 4, 128])
        vz_, vzn = vz_tab[lane % 2][par], vz_nm[lane % 2][par]
        pk_, pkn = kv_bank[lane % 2][par]
        if lane % 2 == 0:
            for h in range(4):
                P.c("act", lambda e, h=h: e.activation(out=vz_[:, h, :], in_=vsrc[:, h * 128:(h + 1) * 128], func=AF.Identity, scale=Zeta[:, d * 4 + h:d * 4 + h + 1]),
                    reads=[vres, "Zeta"], writes=[vzn])
        else:
            P.c("pool", lambda e: e.tensor_tensor(out=vz_[:, :, :], in0=vsrc.rearrange("p (h e) -> p h e", h=4), in1=zb, op=ALU.mult),
                reads=[vres, "Zeta"], writes=[vzn])
        for h in range(4):
            P.c("pe", lambda e, h=h: e.matmul(pk_[:, h * 128:(h + 1) * 128], lhsT=ksrc[:, h * 128:(h + 1) * 128], rhs=vz_[:, h, :], start=True, stop=True),
                reads=[kres, vzn], writes=[pkn])

    def kv_back(Rst, rname, d, lane, par):
        pk_, pkn = kv_bank[lane % 2][par]
        for h in range(4):
            P.c("dve", lambda e, h=h: e.scalar_tensor_tensor(out=Rst[:, h, :], in0=Rst[:, h, :], scalar=G128[:, d * 4 + h:d * 4 + h + 1],
                                                            in1=pk_[:, h * 128:(h + 1) * 128], op0=ALU.mult, op1=ALU.add),
                reads=[rname, "G128", pkn], writes=[rname])

    def kv_step(Rst, rname, ksrc, vsrc, d, kres, vres, lane=0):
        kv_front(ksrc, vsrc, d, kres, vres, lane, 0)
        kv_back(Rst, rname, d, lane, 0)

    def zero(t_, nm):
        P.c("pool", lambda e: e.memset(t_[:, :, :], 0.0), writes=[nm])

    alias(["vz2", "vz3"], ["dpos", "dneg", "mge", "mlt", "e1", "e2"])
    zero(Rf, "Rf"); zero(Rb, "Rb"); zero(Rcf, "Rcf"); zero(Rcb, "Rcb")
    def fronts_A(n_):
        kv_front(k_tm[:, n_, :], v_tm[:, n_, :], 0, "k_tm", "v_tm", 0, n_ % 2)
        kv_front(k_tm[:, NT - 1 - n_, :], v_tm[:, NT - 1 - n_, :], 1, "k_tm", "v_tm", 1, n_ % 2)

    fronts_A(0)
    for n_ in range(NT):
        if n_ + 1 < NT:
            fronts_A(n_ + 1)
        kv_back(Rf, "Rf", 0, 0, n_ % 2)
        kv_back(Rb, "Rb", 1, 1, n_ % 2)
    for n_ in range(2):
        kv_step(Rcf, "Rcf", kc_tm[:, n_, :], vc_tm[:, n_, :], 0, "kc_tm", "vc_tm", 0)
        kv_step(Rcb, "Rcb", kc_tm[:, 1 - n_, :], vc_tm[:, 1 - n_, :], 1, "kc_tm", "vc_tm", 1)

    stage("DpassA")
    xa_in = nc.dram_tensor("xa_in", [128, 1024], F32)
    xa_out = nc.dram_tensor("xa_out", [512, 1024], F32)
    P.dma("sp", lambda e: e.dma_start(out=xa_in[:, 0:512], in_=Rf[:, :, :].rearrange("p h e -> p (h e)")), "xa_st", reads=["Rf"], writes=["xa_in"])
    P.dma("sp", lambda e: e.dma_start(out=xa_in[:, 512:1024], in_=Rb[:, :, :].rearrange("p h e -> p (h e)")), "xa_st", reads=["Rb"], writes=["xa_in"])
    P.dma("pool", lambda e: e.collective_compute("AllGather", ALU.bypass, replica_groups=[[0, 1, 2, 3], [4, 5, 6, 7]],
                                                  ins=[xa_in.ap().opt()], outs=[xa_out.ap().opt()]),
          "ccA", reads=["xa_in"], writes=["xa_out"], inc=1)
    P.dma("sp", lambda e: e.dma_start(out=xg[:, :, :], in_=xa_out.ap().rearrange("(r p) n -> p r n", p=128)), "xa_ld", reads=["xa_out"], writes=["xg"])

    stage("Dxchg")

    def horner(acc, aname, d, order, mcol0):
        gb = G2048[:, d * 4:(d + 1) * 4].unsqueeze(2).broadcast_to([128, 4, 128])
        for n_, i in enumerate(order):
            P.c("dve", lambda e: e.tensor_tensor(out=hacc_t[:, :, :], in0=acc[:, :, :], in1=gb, op=ALU.mult), reads=[aname, "G2048"], writes=["hacc_t"])
            P.c("dve", lambda e, i=i: e.tensor_tensor(out=hacc_t[:, :, :], in0=hacc_t[:, :, :],
                                                     in1=xg[:, i, d * 512:(d + 1) * 512].rearrange("p (h e) -> p h e", h=4), op=ALU.add),
                reads=["hacc_t", "xg"], writes=["hacc_t"])
            P.c("dve", lambda e: e.tensor_tensor(out=hacc_t[:, :, :], in0=hacc_t[:, :, :], in1=acc[:, :, :], op=ALU.subtract),
                reads=["hacc_t", aname], writes=["hacc_t"])
            P.c("dve", lambda e, n_=n_: e.scalar_tensor_tensor(out=acc[:, :, :], in0=hacc_t[:, :, :], scalar=msk[:, mcol0 + n_:mcol0 + n_ + 1],
                                                              in1=acc[:, :, :], op0=ALU.mult, op1=ALU.add),
                reads=["hacc_t", aname, "msk"], writes=[aname])

    horner(Rcf, "Rcf", 0, [0, 1, 2], 0)
    horner(Rcb, "Rcb", 1, [3, 2, 1], 3)

    stage("Dhorner")
    alias(["yn"], ["hacc_t"])
    P.c("dve", lambda e: e.tensor_copy(out=Rb[:, :, :], in_=Rcb[:, :, :]), reads=["Rcb"], writes=["Rb"])
    P.c("dve", lambda e: e.tensor_copy(out=Rf[:, :, :], in_=Rcf[:, :, :]), reads=["Rcf"], writes=["Rf"])
    Rf_store = nc.alloc_sbuf_tensor_at("Rf_store", [128, NT, 512], BF16, offset=xg_off)
    alias(["Rf_store"], ["xg"])
    def fronts_B(n_):
        kv_front(k_tm[:, NT - 1 - n_, :], v_tm[:, NT - 1 - n_, :], 1, "k_tm", "v_tm", 0, n_ % 2)
        kv_front(k_tm[:, n_, :], v_tm[:, n_, :], 0, "k_tm", "v_tm", 1, n_ % 2)

    fronts_B(0)
    for n_ in range(NT):
        ib = NT - 1 - n_
        P.c("act", lambda e, ib=ib: e.activation(out=Rb_store[:, ib, :], in_=Rb[:, :, :].rearrange("p h e -> p (h e)"), func=AF.Copy),
            reads=["Rb"], writes=["Rb_store"])
        P.c("act", lambda e, n_=n_: e.activation(out=Rf_store[:, n_, :], in_=Rf[:, :, :].rearrange("p h e -> p (h e)"), func=AF.Copy),
            reads=["Rf"], writes=["Rf_store"])
        if n_ < NT - 1:
            if n_ + 1 < NT - 1:
                fronts_B(n_ + 1)
            kv_back(Rb, "Rb", 1, 0, n_ % 2)
            kv_back(Rf, "Rf", 0, 1, n_ % 2)
    stage("DBb")
    scm_l = [scm, nc.alloc_sbuf_tensor_at("scm1", [128, 4, 128], BF16, offset=tabs_off)]
    qf_l = [qf, nc.alloc_sbuf_tensor_at("qf1", [128, 4, 128], BF16, offset=tabs_off + 1024)]
    qb_l = [qb, nc.alloc_sbuf_tensor_at("qb1", [128, 4, 128], BF16, offset=tabs_off + 2048)]
    yn_l = [yn, nc.alloc_sbuf_tensor_at("yn1", [128, 4, 128], F32, offset=Rcf_off)]
    retx_l = [retx, nc.alloc_sbuf_tensor_at("retx1", [128, 512], BF16, offset=Rfbf_off)]
    bst_l = [bst, nc.alloc_sbuf_tensor_at("bst1", [128, 4, 6], F32, offset=Rcb_off)]
    mv_l = [mv, nc.alloc_sbuf_tensor_at("mv1", [128, 4, 2], F32, offset=Rcb_off + 128)]
    rs_l = [rs, nc.alloc_sbuf_tensor_at("rs1", [128, 8], F32, offset=Rcb_off + 192)]
    alias(["scm1", "qf1", "qb1"], ["dpos", "dneg", "mge", "mlt", "e1", "e2", "vz2", "vz3"])
    alias(["yn1"], ["Rcf"])
    alias(["bst1", "bst21", "bst31", "mv1", "rs1", "rs41"], ["Rcb"])
    psS_l = [(psA[0], "ps0"), (psA[6], "ps6")]
    psO_l = [(psA[1], "ps1"), (psA[4], "ps4")]

    def ret_front(i):
        pp = i % 2
        sfx = "" if pp == 0 else "1"
        tsl = slice(i * 128, (i + 1) * 128)
        pS, pSn = psS_l[pp]
        pO, pOn = psO_l[pp]
        scm_, qf_, qb_, yn_ = scm_l[pp], qf_l[pp], qb_l[pp], yn_l[pp]
        for h in range(4):
            P.c("pe", lambda e, h=h: e.matmul(pS[:, h * 128:(h + 1) * 128], lhsT=kT[:, h, tsl], rhs=qT[:, h, tsl], start=True, stop=True),
                reads=["kT", "qT"], writes=[pSn])
        P.c("dve", lambda e: e.tensor_tensor(out=scm_[:, :, :], in0=pS[:, :].rearrange("p (h t) -> p h t", h=4), in1=Dm[:, :, :], op=ALU.mult),
            reads=[pSn, "Dm"], writes=["scm" + sfx])
        P.c("pool", lambda e: e.tensor_tensor(out=qf_[:, :, :], in0=qT[:, :, tsl], in1=XiF[:, :, :], op=ALU.mult), reads=["qT", "XiF"], writes=["qf" + sfx])
        P.c("pool", lambda e: e.tensor_tensor(out=qb_[:, :, :], in0=qT[:, :, tsl], in1=XiB[:, :, :], op=ALU.mult), reads=["qT", "XiB"], writes=["qb" + sfx])
        for h in range(4):
            osl = slice(h * 128, (h + 1) * 128)
            P.c("pe", lambda e, h=h, osl=osl: e.matmul(pO[:, osl], lhsT=scm_[:, h, :], rhs=v_tm[:, i, osl], start=True, stop=False),
                reads=["scm" + sfx, "v_tm"], writes=[pOn])
            P.c("pe", lambda e, h=h, osl=osl: e.matmul(pO[:, osl], lhsT=qf_[:, h, :], rhs=Rf_store[:, i, osl], start=False, stop=False),
                reads=["qf" + sfx, "Rf_store"], writes=[pOn])
            P.c("pe", lambda e, h=h, osl=osl: e.matmul(pO[:, osl], lhsT=qb_[:, h, :], rhs=Rb_store[:, i, osl], start=False, stop=True),
                reads=["qb" + sfx, "Rb_store"], writes=[pOn])
        P.c("act", lambda e: e.activation(out=yn_[:, :, :], in_=pO[:, :].rearrange("p (h e) -> p h e", h=4), func=AF.Copy), reads=[pOn], writes=["yn" + sfx])
        if debug and i in (0, 15):
            dd = dout(f"d_rety{i}", [128, 512])
            P.dma("sp", lambda e, dd=dd: e.dma_start(out=dd[:, :], in_=yn_[:, :, :].rearrange("p h e -> p (h e)")), "dbg", reads=["yn" + sfx])

    def ret_back(i):
        pp = i % 2
        sfx = "" if pp == 0 else "1"
        tsl = slice(i * 128, (i + 1) * 128)
        yn_, retx_, bst_, mv_, rs_ = yn_l[pp], retx_l[pp], bst_l[pp], mv_l[pp], rs_l[pp]
        ynn, rxn = "yn" + sfx, "retx" + sfx
        P.c("dve", lambda e: e.tensor_reduce(out=bst_[:, 0, 0:4], in_=yn_[:, :, :], axis=mybir.AxisListType.X, op=ALU.add), reads=[ynn], writes=["bst" + sfx])
        for h in range(4):
            P.c("act", lambda e, h=h: e.activation(out=junk2[:, :], in_=yn_[:, h, :], func=AF.Square, accum_out=bst_[:, 1, h:h + 1]),
                reads=[ynn], writes=["junk2", "bst2" + sfx])
        P.c("dve", lambda e: e.tensor_scalar(out=mv_[:, :, 0], in0=bst_[:, 0, 0:4], scalar1=1.0 / 128.0, scalar2=None, op0=ALU.mult), reads=["bst" + sfx], writes=["mv" + sfx])
        P.c("dve", lambda e: e.tensor_tensor(out=bst_[:, 2, 0:4], in0=mv_[:, :, 0], in1=mv_[:, :, 0], op=ALU.mult), reads=["mv" + sfx], writes=["bst3" + sfx])
        P.c("dve", lambda e: e.scalar_tensor_tensor(out=mv_[:, :, 1], in0=bst_[:, 1, 0:4], scalar=1.0 / 128.0, in1=bst_[:, 2, 0:4], op0=ALU.mult, op1=ALU.subtract),
            reads=["bst2" + sfx, "bst3" + sfx, "mv" + sfx], writes=["mv" + sfx])
        P.c("act", lambda e: e.activation(out=rs_[:, 0:4], in_=mv_[:, :, 1], func=AF.Sqrt, bias=epst[:, 0:1]), reads=["mv" + sfx, "epst"], writes=["rs" + sfx])
        P.c("dve", lambda e: e.reciprocal(out=rs_[:, 4:8], in_=rs_[:, 0:4]), reads=["rs" + sfx], writes=["rs4" + sfx])
        for h in range(4):
            P.c("dve", lambda e, h=h: e.tensor_scalar(out=yn_[:, h, :], in0=yn_[:, h, :], scalar1=mv_[:, h, 0:1], scalar2=rs_[:, 4 + h:5 + h],
                                                     op0=ALU.subtract, op1=ALU.mult), reads=[ynn, "mv" + sfx, "rs4" + sfx], writes=[ynn])
        P.c("dve", lambda e: e.tensor_tensor(out=retx_[:, :], in0=yn_[:, :, :].rearrange("p h e -> p (h e)"), in1=gate[:, i, :], op=ALU.mult),
            reads=[ynn, "gate"], writes=[rxn])
        for h in range(4):
            P.c("pe", lambda e, h=h: e.transpose(psTq[:, h * 128:(h + 1) * 128], retx_[:, h * 128:(h + 1) * 128], ident_b[:, :]),
                reads=[rxn, "ident_b"], writes=["psTq"])
        P.c("act", lambda e: e.activation(out=mixT[:, 4:8, tsl], in_=psTq[:, 0:512].rearrange("p (h t) -> p h t", h=4), func=AF.Copy),
            reads=["psTq"], writes=["mixT"])

    ret_front(0)
    for i in range(NT):
        if i + 1 < NT:
            ret_front(i + 1)
        ret_back(i)

    if debug:
        for (nm, tns) in (("d_rcf", Rcf), ("d_rcb", Rcb)):
            dd = dout(nm, [128, 512])
            P.dma("sp", lambda e, dd=dd, tns=tns: e.dma_start(out=dd[:, :], in_=tns[:, :, :].rearrange("p h e -> p (h e)")), "dbg", reads=["Rcf", "Rcb"])
        dd = nc.dram_tensor("d_mixT", [128, 8 * T], BF16, kind="ExternalOutput").ap()
        dbg["d_mixT"] = dd
        P.dma("sp", lambda e, dd=dd: e.dma_start(out=dd[:, :], in_=mixT[:, :, :].rearrange("p a b -> p (a b)")), "dbg", reads=["mixT"])

    stage("ret")
    d_in = {}
    for (nm, shp) in (("lamre", [128, 64]), ("lamim", [128, 64]), ("lstep", [128, 64]), ("bre", [128, 32, 16]), ("bim", [128, 32, 16]),
                      ("cre", [128, 32, 16]), ("cim", [128, 32, 16]), ("d128", [128, 32, 16]), ("bglu", [128, 4])):
        d_in[nm] = din("s5_" + nm, shp)
    d_in["wglu"] = din("w_glu", [512, 512])
    _s5(nc, P, debug, stage, dout, dbg, alias, big0, r2_base, r2_end, spare_off, uT, ucT, mixT, psA, psTq, ident_f, ident_b, ones_f, negpi, msk, d_in)
    stage("s5")
    wout_d = din("w_out", [D, D])
    fnw_d = din("fnw_b", [128, D])
    cw_d = din("cw", [128, 66])
    cb_d = din("cb", [128, 22])
    oh_d = din("oh", [128, 8])
    wup_d = din("w_up", [D, 5632])
    wdn_d = din("w_down", [2816, D])
    SP = Arena(nc); SP.off = spare_off
    fnw = SP.alloc("fnw", [128, D], F32)
    cw = SP.alloc("cw", [128, 22, 3], F32)
    cb = SP.alloc("cb", [128, 22], F32)
    oh = SP.alloc("oh", [128, 8], F32)
    hxe = SP.alloc("hxe", [128, 16], F32)
    xgC = SP.alloc("xgC", [128, 4, 16], F32)
    halo = SP.alloc("halo", [128, 16], F32)
    ss2 = SP.alloc("ss2", [128, 4], F32)
    assert SP.off <= spare_off + 6 * 1024, SP.off - spare_off
    alias(["fnw", "cw", "cb", "oh", "hxe", "xgC", "halo", "ss2a", "ss2b", "ss2c"], ["kc_tm", "vc_tm", "ucT", "e1lo", "e1hi", "e2lo", "e2hi", "s5tab"])
    P.dma("sp", lambda e: e.dma_start(out=fnw[:, :], in_=fnw_d[:, :]), "small8", writes=["fnw"])
    P.dma("sp", lambda e: e.dma_start(out=cw[:, :, :], in_=cw_d.rearrange("p (j w) -> p j w", w=3)), "small9", writes=["cw"])
    P.dma("sp", lambda e: e.dma_start(out=cb[:, :], in_=cb_d[:, :]), "small10", writes=["cb"])
    P.dma("sp", lambda e: e.dma_start(out=oh[:, :], in_=oh_d[:, :]), "small11", writes=["oh"])
    FS = Arena(nc); FS.off = big0
    x_mid = FS.alloc("x_mid", [128, NT, D], F32)
    wout = FS.alloc("wout", [128, 8, D], BF16)
    xf = [FS.alloc(f"xf{i}", [128, D], F32) for i in range(2)]
    assert FS.off <= big0 + 96 * 1024
    s_dead = ["qT", "kT", "k_tm", "v_tm", "gate", "uT", "XR", "QN", "s5tab", "L1lo", "L1hi", "e1lo", "e1hi", "e2lo", "e2hi", "tmpM", "Z", "Zc", "Ysb",
              "Fall", "Fctx", "hacc", "hacc_bf", "htmp", "xgB", "D2", "s5gT", "wglu", "sgt", "Xw0", "Xw1", "TAw0", "TAw1", "TBw0", "TBw1",
              "XAw0", "XAw1", "XBw0", "XBw1"] + [f"Rr{i}" for i in range(36)]
    alias(["x_mid", "wout", "xf0", "xf1"], s_dead)
    P.dma("pool", lambda e: e.dma_start(out=wout[:, :, :], in_=wout_d.rearrange("(k p) n -> p k n", p=128)), "wout", writes=["wout"])
    for k in range(8):
        P.c("dve", lambda e, k=k: e.tensor_tensor(out=wout[:, k, :], in0=wout[:, k, :], in1=bc2[:, :], op=ALU.mult), reads=["wout", "bc2"], writes=["wout"])
    for i in range(NT):
        xt_ = xf[i % 2]
        P.dma("sp", lambda e, xt_=xt_, i=i: e.dma_start(out=xt_[:, :], in_=x_d[i * 128:(i + 1) * 128, :]), f"xf{i % 2}", writes=[f"xf{i % 2}"])
        for hh in range(2):
            ps = psA[hh]
            for k in range(8):
                P.c("pe", lambda e, ps=ps, k=k, hh=hh, i=i: e.matmul(ps[:, :], lhsT=mixT[:, k, i * 128:(i + 1) * 128], rhs=wout[:, k, hh * 512:(hh + 1) * 512],
                                                                  start=(k == 0), stop=(k == 7)), reads=["mixT", "wout"], writes=[f"ps{hh}"])
            P.c("dve", lambda e, ps=ps, hh=hh, i=i, xt_=xt_: e.tensor_tensor(out=x_mid[:, i, hh * 512:(hh + 1) * 512], in0=ps[:, :],
                                                                           in1=xt_[:, hh * 512:(hh + 1) * 512], op=ALU.add),
                reads=[f"ps{hh}", f"xf{i % 2}"], writes=[f"x_mid{i}"])
    if debug:
        dd = dout("d_xmid", [128, NT * D])
        P.dma("sp", lambda e, dd=dd: e.dma_start(out=dd[:, :], in_=x_mid[:, :, :].rearrange("p a b -> p (a b)")), "dbg", reads=[f"x_mid{i}" for i in range(NT)])
    stage("F")

    GR = Arena(nc); GR.off = r2_base
    hx2T = GR.alloc("hx2T", [128, 8, T + 2], BF16)
    NJ = 4
    wa = GR.alloc("wa", [128, NJ, 8, 128], BF16)
    wg = GR.alloc("wg", [128, NJ, 8, 128], BF16)
    wdn = GR.alloc("wdn", [128, NJ, D], BF16)
    gbuf2 = [GR.alloc(f"gbuf{i}", [128, T + 2], F32) for i in range(2)]
    abuf2 = [GR.alloc(f"abuf{i}", [128, T], BF16) for i in range(2)]
    glb2 = [GR.alloc(f"glb{i}", [128, T], BF16) for i in range(2)]
    assert GR.off <= r2_end, (GR.off, r2_end)
    xs2 = nc.alloc_sbuf_tensor_at("xs2", [128, D], F32, offset=GR.off - 8192)
    junk3 = nc.alloc_sbuf_tensor_at("junk3", [128, D], BF16, offset=GR.off - 4096)
    gnames = ["hx2T"] + [f"{n}{i}" for n in ("wa", "wg", "wdn") for i in range(NJ)] + ["gbuf0", "gbuf1", "abuf0", "abuf1", "glb0", "glb1", "xs2", "junk3"]
    alias(gnames, ["mixT", "WRb", "WRf", "PV", "Mbf", "Eblk", "Macc", "L1lo", "L1hi"])
    HS = Arena(nc); HS.off = big0 + NT * D * 4
    hid = HS.alloc("hid", [128, NJ, T], BF16)
    ub2 = [HS.alloc(f"ub{i}", [128, T], F32) for i in range(2)]
    assert HS.off <= big0 + 96 * 1024
    alias([f"hid{i}" for i in range(NJ)] + ["ub0", "ub1"], ["wout", "xf0", "xf1"])

    xs2b = nc.alloc_sbuf_tensor_at("xs2b", [128, D], F32, offset=GR.off - 12288)
    ss2b = SP.alloc("ss2b", [128, 4], F32)
    xs2s = [(xs2, "xs2", ss2, "ss2"), (xs2b, "xs2b", ss2b, "ss2b")]

    def norm2_front(i, pp):
        xs_, xn_, s_, sn_ = xs2s[pp]
        P.c("act", lambda e: e.activation(out=junk3[:, :], in_=x_mid[:, i, :], func=AF.Square, accum_out=s_[:, 0:1]),
            reads=[f"x_mid{i}"], writes=["junk3", sn_ + "a"])
        P.c("act", lambda e: e.activation(out=s_[:, 1:2], in_=s_[:, 0:1], func=AF.Sqrt, scale=1.0 / D, bias=epst[:, 0:1]), reads=[sn_ + "a", "epst"], writes=[sn_ + "b"])
        P.c("dve", lambda e: e.reciprocal(out=s_[:, 2:3], in_=s_[:, 1:2]), reads=[sn_ + "b"], writes=[sn_ + "c"])
        P.c("dve", lambda e: e.tensor_scalar(out=xs_[:, :], in0=x_mid[:, i, :], scalar1=s_[:, 2:3], scalar2=None, op0=ALU.mult),
            reads=[f"x_mid{i}", sn_ + "c"], writes=[xn_])

    def norm2_back(i, pp):
        xs_, xn_, s_, sn_ = xs2s[pp]
        for k in range(8):
            ps = psA[k // 4]
            P.c("pe", lambda e, ps=ps, k=k: e.transpose(ps[:, (k % 4) * 128:(k % 4 + 1) * 128], xs_[:, k * 128:(k + 1) * 128], ident_f[:, :]),
                reads=[xn_, "ident_f"], writes=[f"ps{k // 4}"])
        for k in range(8):
            ps = psA[k // 4]
            P.c("act", lambda e, ps=ps, k=k: e.activation(out=hx2T[:, k, 1 + i * 128: 1 + (i + 1) * 128], in_=ps[:, (k % 4) * 128:(k % 4 + 1) * 128],
                                                        func=AF.Identity, scale=g2x[:, k:k + 1], bias=modT[:, 16 + k, 0:1]),
                reads=[f"ps{k // 4}", "g2x", "modT"], writes=["hx2T"])

    wup_src = wup_d.rearrange("(k p) n -> p k n", p=128)
    passes = [(0, 4), (4, 8), (8, 12), (12, 16), (16, 19), (19, 22)]

    def load_up(j, sl):
        P.dma("pool", lambda e: e.dma_start(out=wa[:, sl, :, :], in_=wup_src[:, :, j * 128:(j + 1) * 128]), f"wa{sl}", writes=[f"wa{sl}"])
        P.dma("pool", lambda e: e.dma_start(out=wg[:, sl, :, :], in_=wup_src[:, :, 2816 + j * 128: 2816 + (j + 1) * 128]), f"wg{sl}", writes=[f"wg{sl}"])

    def load_dn(j, sl):
        P.dma("pool", lambda e: e.dma_start(out=wdn[:, sl, :], in_=wdn_d[j * 128:(j + 1) * 128, :]), f"wdn{sl}", writes=[f"wdn{sl}"])

    def scale_dn(sl):
        P.c("dve", lambda e: e.tensor_tensor(out=wdn[:, sl, :], in0=wdn[:, sl, :], in1=bc5[:, :], op=ALU.mult), reads=[f"wdn{sl}", "bc5"], writes=[f"wdn{sl}"])

    def chunk_tail(t_):
        glb, ub, abuf, jj, gln, ubn, abn = t_
        P.c("act", lambda e: e.activation(out=glb[:, :], in_=ub[:, :], func=AF.Gelu_apprx_tanh), reads=[ubn], writes=[gln])
        P.c("dve", lambda e: e.tensor_tensor(out=hid[:, jj, :], in0=glb[:, :], in1=abuf[:, :], op=ALU.mult), reads=[gln, abn], writes=[f"hid{jj}"])

    for sl in range(passes[0][1]):
        load_up(sl, sl)
    for jj_ in range(passes[0][1] - passes[0][0]):
        load_dn(passes[0][0] + jj_, jj_)
    order = [0, NT - 1] + list(range(1, NT - 1))
    norm2_front(order[0], 0)
    for n_, i in enumerate(order):
        if n_ + 1 < len(order):
            norm2_front(order[n_ + 1], (n_ + 1) % 2)
        norm2_back(i, n_ % 2)
        if n_ == 1:
            P.c("dve", lambda e: e.tensor_copy(out=hxe[:, 0:8], in_=hx2T[:, :, 1]), reads=["hx2T"], writes=["hxe"])
            P.c("dve", lambda e: e.tensor_copy(out=hxe[:, 8:16], in_=hx2T[:, :, T]), reads=["hx2T"], writes=["hxe"])
            xc_in = nc.dram_tensor("xc_in", [128, 16], F32)
            xc_out = nc.dram_tensor("xc_out", [512, 16], F32)
            P.dma("sp", lambda e: e.dma_start(out=xc_in[:, :], in_=hxe[:, :]), "xc_st", reads=["hxe"], writes=["xc_in"])
            P.dma("pool", lambda e: e.collective_compute("AllGather", ALU.bypass, replica_groups=[[0, 1, 2, 3], [4, 5, 6, 7]],
                                                          ins=[xc_in.ap().opt()], outs=[xc_out.ap().opt()]),
                  "ccC", reads=["xc_in"], writes=["xc_out"], inc=1)
            P.dma("sp", lambda e: e.dma_start(out=xgC[:, :, :], in_=xc_out.ap().rearrange("(r p) n -> p r n", p=128)), "xc_ld", reads=["xc_out"], writes=["xgC"])
    P.c("pool", lambda e: e.memset(halo[:, :], 0.0), writes=["halo"])
    for r_ in range(4):
        P.c("dve", lambda e, r_=r_: e.scalar_tensor_tensor(out=halo[:, 0:8], in0=xgC[:, r_, 8:16], scalar=oh[:, r_:r_ + 1], in1=halo[:, 0:8],
                                                          op0=ALU.mult, op1=ALU.add), reads=["xgC", "oh", "halo"], writes=["halo"])
        P.c("dve", lambda e, r_=r_: e.scalar_tensor_tensor(out=halo[:, 8:16], in0=xgC[:, r_, 0:8], scalar=oh[:, 4 + r_:5 + r_], in1=halo[:, 8:16],
                                                          op0=ALU.mult, op1=ALU.add), reads=["xgC", "oh", "halo"], writes=["halo"])
    P.c("dve", lambda e: e.tensor_copy(out=hx2T[:, :, 0], in_=halo[:, 0:8]), reads=["halo"], writes=["hx2T"])
    P.c("dve", lambda e: e.tensor_copy(out=hx2T[:, :, T + 1], in_=halo[:, 8:16]), reads=["halo"], writes=["hx2T"])
    stage("G0")

    alias(["glb1"], ["xs2", "junk3"])
    alias(["glb0"], ["xs2"])
    alias(["abuf1"], ["xs2b"])
    cnum = 0
    pend_tail = []
    for pi, (j0, j1) in enumerate(passes):
        nj = j1 - j0
        if pi > 0:
            for jj in range(nj):
                load_dn(j0 + jj, jj)
        for jj in range(nj):
            j = j0 + jj
            pb = cnum % 2
            cnum += 1
            gbuf, abuf, glb, ub = gbuf2[pb], abuf2[pb], glb2[pb], ub2[pb]
            gbn, abn, gln, ubn = f"gbuf{pb}", f"abuf{pb}", f"glb{pb}", f"ub{pb}"
            for tg in range(4):
                psa, psg = (psA[2], psA[3]) if tg % 2 == 0 else (psA[4], psA[5])
                pan, pgn = ("ps2", "ps3") if tg % 2 == 0 else ("ps4", "ps5")
                csl = slice(1 + tg * 512, 1 + (tg + 1) * 512)
                for k in range(8):
                    P.c("pe", lambda e, psa=psa, k=k, jj=jj, csl=csl: e.matmul(psa[:, :], lhsT=wa[:, jj, k, :], rhs=hx2T[:, k, csl], start=(k == 0), stop=(k == 7)),
                        reads=[f"wa{jj}", "hx2T"], writes=[pan])
                for k in range(8):
                    P.c("pe", lambda e, psg=psg, k=k, jj=jj, csl=csl: e.matmul(psg[:, :], lhsT=wg[:, jj, k, :], rhs=hx2T[:, k, csl], start=(k == 0), stop=(k == 7)),
                        reads=[f"wg{jj}", "hx2T"], writes=[pgn])
                P.c("act", lambda e, psa=psa, tg=tg, abuf=abuf: e.activation(out=abuf[:, tg * 512:(tg + 1) * 512], in_=psa[:, :], func=AF.Copy), reads=[pan], writes=[abn])
                P.c("act", lambda e, psg=psg, csl=csl, gbuf=gbuf: e.activation(out=gbuf[:, csl], in_=psg[:, :], func=AF.Copy), reads=[pgn], writes=[gbn])
            for k in range(8):
                P.c("pe", lambda e, k=k, jj=jj: e.matmul(psA[6][:, 0:2], lhsT=wg[:, jj, k, :], rhs=hx2T[:, k, 0:T + 2:T + 1], start=(k == 0), stop=(k == 7)),
                    reads=[f"wg{jj}", "hx2T"], writes=["ps6"])
            P.c("act", lambda e, gbuf=gbuf: e.activation(out=gbuf[:, 0:T + 2:T + 1], in_=psA[6][:, 0:2], func=AF.Copy), reads=["ps6"], writes=[gbn])
            while len(pend_tail) > 0:
                chunk_tail(pend_tail.pop(0))
            if pi + 1 < len(passes) and jj < passes[pi + 1][1] - passes[pi + 1][0]:
                load_up(passes[pi + 1][0] + jj, jj)
            P.c("dve", lambda e, j=j, ub=ub, gbuf=gbuf: e.tensor_scalar(out=ub[:, :], in0=gbuf[:, 1:T + 1], scalar1=cw[:, j, 1:2], scalar2=cb[:, j:j + 1], op0=ALU.mult, op1=ALU.add),
                reads=[gbn, "cw", "cb"], writes=[ubn])
            P.c("dve", lambda e, j=j, ub=ub, gbuf=gbuf: e.scalar_tensor_tensor(out=ub[:, :], in0=gbuf[:, 0:T], scalar=cw[:, j, 0:1], in1=ub[:, :], op0=ALU.mult, op1=ALU.add),
                reads=[gbn, "cw", ubn], writes=[ubn])
            P.c("dve", lambda e, j=j, ub=ub, gbuf=gbuf: e.scalar_tensor_tensor(out=ub[:, :], in0=gbuf[:, 2:T + 2], scalar=cw[:, j, 2:3], in1=ub[:, :], op0=ALU.mult, op1=ALU.add),
                reads=[gbn, "cw", ubn], writes=[ubn])
            pend_tail.append((glb, ub, abuf, jj, gln, ubn, abn))
        while len(pend_tail) > 0:
            chunk_tail(pend_tail.pop(0))
        for jj in range(nj):
            scale_dn(jj)
        for i in range(NT):
            for hh in range(2):
                ps = psA[hh]
                for jj in range(nj):
                    P.c("pe", lambda e, ps=ps, jj=jj, i=i, hh=hh, nj=nj: e.matmul(ps[:, :], lhsT=hid[:, jj, i * 128:(i + 1) * 128], rhs=wdn[:, jj, hh * 512:(hh + 1) * 512],
                                                                               start=(jj == 0), stop=(jj == nj - 1)), reads=[f"hid{jj}", f"wdn{jj}"], writes=[f"ps{hh}"])
                P.c("dve", lambda e, ps=ps, i=i, hh=hh: e.tensor_tensor(out=x_mid[:, i, hh * 512:(hh + 1) * 512], in0=ps[:, :],
                                                                      in1=x_mid[:, i, hh * 512:(hh + 1) * 512], op=ALU.add),
                    reads=[f"ps{hh}", f"x_mid{i}"], writes=[f"x_mid{i}"])
    stage("G1")
    alias(["junk3"], ["glb1"])
    def fin_front(i):
        s_, sn_ = (ss2, "ss2") if i % 2 == 0 else (ss2b, "ss2b")
        P.c("act", lambda e: e.activation(out=junk3[:, :], in_=x_mid[:, i, :], func=AF.Square, accum_out=s_[:, 0:1]),
            reads=[f"x_mid{i}"], writes=["junk3", sn_ + "a"])
        P.c("act", lambda e: e.activation(out=s_[:, 1:2], in_=s_[:, 0:1], func=AF.Sqrt, scale=1.0 / D, bias=epst[:, 0:1]), reads=[sn_ + "a", "epst"], writes=[sn_ + "b"])
        P.c("dve", lambda e: e.reciprocal(out=s_[:, 2:3], in_=s_[:, 1:2]), reads=[sn_ + "b"], writes=[sn_ + "c"])

    fin_front(0)
    for i in range(NT):
        if i + 1 < NT:
            fin_front(i + 1)
        s_, sn_ = (ss2, "ss2") if i % 2 == 0 else (ss2b, "ss2b")
        P.c("dve", lambda e, i=i, s_=s_: e.scalar_tensor_tensor(out=x_mid[:, i, :], in0=x_mid[:, i, :], scalar=s_[:, 2:3], in1=fnw[:, :], op0=ALU.mult, op1=ALU.mult),
            reads=[f"x_mid{i}", sn_ + "c", "fnw"], writes=[f"x_mid{i}"])
        P.dma("sp", lambda e, i=i: e.dma_start(out=out_d[i * 128:(i + 1) * 128, :], in_=x_mid[:, i, :]), "xout", reads=[f"x_mid{i}"])
    stage("end")


def _s5(nc, P, debug, stage, dout, dbg, alias, big0, r2_base, r2_end, spare_off, uT, ucT, mixT, psA, psTq, ident_f, ident_b, ones_f, negpi, msk, d_in):
    TWO_PI = 2.0 * np.pi
    AS = Arena(nc)
    AS.off = big0
    S_END = big0 + 80 * 1024
    XR = AS.alloc("XR", [128, 32, 128], F32)
    QN = AS.alloc("QN", [128, 32, 128], F32)
    Macc = nc.alloc_sbuf_tensor_at("Macc", [128, 32, 128], F32, offset=r2_base)
    sm0 = AS.off
    V1 = AS.alloc("V1", [128, 13, 64], F32); V2 = AS.alloc("V2", [128, 13, 64], F32)
    bglu = AS.alloc("bglu", [128, 4], F32)
    dead0 = AS.off

    def f32t(name, w):
        return AS.alloc(name, [128, w], F32)
    lamre = f32t("lamre", 64); lamim = f32t("lamim", 64); lstep = f32t("lstep", 64)
    dtt = f32t("dtt", 64); rho = f32t("rho", 64); tht = f32t("tht", 64); mag = f32t("mag", 64)
    sn = f32t("sn", 64); cs = f32t("cs", 64); ta = f32t("ta", 64); tb = f32t("tb", 64); tc = f32t("tc", 64)
    tiI = AS.alloc("tiI", [128, 64], I32)
    kre = f32t("kre", 64); kim = f32t("kim", 64)
    PWre = AS.alloc("PWre", [128, 9, 64], F32); PWim = AS.alloc("PWim", [128, 9, 64], F32)
    NPre = AS.alloc("NPre", [128, 8, 64], F32); NPim = AS.alloc("NPim", [128, 8, 64], F32)
    HPre = AS.alloc("HPre", [128, 13, 64], F32); HPim = AS.alloc("HPim", [128, 13, 64], F32)
    bre = AS.alloc("bre", [128, 32, 16], F32); bim = AS.alloc("bim", [128, 32, 16], F32)
    cre = AS.alloc("cre", [128, 32, 16], F32); cim = AS.alloc("cim", [128, 32, 16], F32)
    Cr = AS.alloc("Cr", [128, 32, 16], F32)
    d128 = AS.alloc("d128", [128, 32, 16], F32)
    BBre = AS.alloc("BBre", [128, 64, 16], F32); BBim = AS.alloc("BBim", [128, 64, 16], F32)
    e1 = nc.alloc_sbuf_tensor_at("s5e1", [128, 32, 16], F32, offset=spare_off)
    e2 = nc.alloc_sbuf_tensor_at("s5e2", [128, 32, 16], F32, offset=spare_off + 2048)
    maskF = AS.alloc("maskF", [128, 128], F32); maskB = AS.alloc("maskB", [128, 128], F32)
    tmpM = AS.alloc("tmpM", [128, 128], F32)
    WRf = nc.alloc_sbuf_tensor_at("WRf", [128, 32, 128], F32, offset=r2_base + 8 * T * 2 + 56 * 1024 - 16 * 1024)
    assert AS.off <= S_END, (AS.off, S_END)
    AR2 = Arena(nc)
    AR2.off = r2_base + 8 * T * 2
    WR = AR2.alloc("WR", [128, 64, 128], BF16)
    PV = AR2.alloc("PV", [128, 64, 128], BF16)
    Mbf = AR2.alloc("Mbf", [128, 32, 128], BF16)
    Eblk = AR2.alloc("Eblk", [128, 64, 128], BF16)
    assert AR2.off <= r2_end, (AR2.off, r2_end)
    s5names = ["XR", "QN", "Macc", "s5tab", "WRb", "WRf", "PV", "Mbf", "Eblk", "s5tmp", "L1lo", "L1hi", "e1lo", "e2lo", "e1hi", "e2hi"]
    alias(s5names, ["qT", "kT", "k_tm", "v_tm", "gate", "Rb_store", "Rf_store", "scm1", "qf1", "qb1", "yn1", "retx1", "bst1", "bst21", "bst31", "mv1", "rs1", "rs41", "xg", "Dm", "XiF", "XiB", "dm", "dpos", "dneg", "mge", "mlt", "e1", "e2", "tfree",
                    "pidx", "zarg", "Zeta", "G128", "G2048", "nld", "ld128", "Rf", "Rb", "Rcf", "Rcb", "Rf_bf", "hacc_t", "vz0", "vz1", "scm", "qf", "qb", "yn",
                    "retx", "bst", "bst2", "bst3", "junk2", "mv", "rs", "rs4", "kc_tm", "vc_tm"])
    TAB = "s5tab"

    for (t_, nm) in ((lamre, "lamre"), (lamim, "lamim"), (lstep, "lstep"), (bre, "bre"), (bim, "bim"), (cre, "cre"), (cim, "cim"), (d128, "d128"), (bglu, "bglu")):
        src_ap = d_in[nm]
        if len(t_.shape) == 3:
            P.dma("sp", lambda e, t_=t_, src_ap=src_ap: e.dma_start(out=t_[:, :, :], in_=src_ap), "s5ld", writes=[TAB])
        else:
            P.dma("sp", lambda e, t_=t_, src_ap=src_ap: e.dma_start(out=t_[:, :], in_=src_ap), "s5ld", writes=[TAB])

    cnt = [0]

    def ew():
        cnt[0] += 1
        return "dve" if cnt[0] % 2 else "pool"

    def tt(out, a, b, op, eng=None, rd=(TAB,), wr=(TAB,)):
        P.c(eng or ew(), lambda e: e.tensor_tensor(out=out, in0=a, in1=b, op=op), reads=list(rd), writes=list(wr))

    def ts(out, a, s1, s2, op0, op1=None, eng="dve", rd=(TAB,), wr=(TAB,)):
        if op1 is None:
            P.c(eng, lambda e: e.tensor_scalar(out=out, in0=a, scalar1=s1, scalar2=None, op0=op0), reads=list(rd), writes=list(wr))
        else:
            P.c(eng, lambda e: e.tensor_scalar(out=out, in0=a, scalar1=s1, scalar2=s2, op0=op0, op1=op1), reads=list(rd), writes=list(wr))

    def sin_of(dst, src_, off):
        ts(ta[:, :], src_, 1.0 / TWO_PI, off, ALU.mult, ALU.add)
        P.c("dve", lambda e: e.tensor_copy(out=tiI[:, :], in_=ta[:, :]), reads=[TAB], writes=[TAB])
        P.c("dve", lambda e: e.tensor_copy(out=tb[:, :], in_=tiI[:, :]), reads=[TAB], writes=[TAB])
        tt(ta[:, :], ta[:, :], tb[:, :], ALU.subtract, eng="dve")
        ts(tb[:, :], ta[:, :], 0.0, None, ALU.is_lt)
        tt(ta[:, :], ta[:, :], tb[:, :], ALU.add, eng="dve")
        P.c("act", lambda e: e.activation(out=dst, in_=ta[:, :], func=AF.Sin, scale=TWO_PI, bias=negpi[:, 0:1]), reads=[TAB, "negpi"], writes=[TAB])

    def cmul(ore, oim, are, aim, bre_, bim_, w=64):
        t1, t2 = ta[:, 0:w], tb[:, 0:w]
        tt(t1, are, bre_, ALU.mult, eng="dve"); tt(t2, aim, bim_, ALU.mult, eng="dve"); tt(ore, t1, t2, ALU.subtract, eng="dve")
        tt(t1, are, bim_, ALU.mult, eng="dve"); tt(t2, aim, bre_, ALU.mult, eng="dve"); tt(oim, t1, t2, ALU.add, eng="dve")

    P.c("act", lambda e: e.activation(out=dtt[:, :], in_=lstep[:, :], func=AF.Exp), reads=[TAB], writes=[TAB])
    tt(rho[:, :], lamre[:, :], dtt[:, :], ALU.mult, eng="dve")
    tt(tht[:, :], lamim[:, :], dtt[:, :], ALU.mult, eng="dve")
    P.c("act", lambda e: e.activation(out=mag[:, :], in_=rho[:, :], func=AF.Exp), reads=[TAB], writes=[TAB])
    sin_of(sn[:, :], tht[:, :], 0.5)
    sin_of(cs[:, :], tht[:, :], 0.75)
    P.c("pool", lambda e: e.memset(PWre[:, 0, :], 1.0), reads=[TAB], writes=[TAB])
    P.c("pool", lambda e: e.memset(PWim[:, 0, :], 0.0), reads=[TAB], writes=[TAB])
    tt(PWre[:, 1, :], mag[:, :], cs[:, :], ALU.mult, eng="dve")
    tt(PWim[:, 1, :], mag[:, :], sn[:, :], ALU.mult, eng="dve")
    ts(tc[:, :], PWre[:, 1, :], -1.0, None, ALU.add)
    tt(kre[:, :], tc[:, :], lamre[:, :], ALU.mult, eng="dve"); tt(ta[:, :], PWim[:, 1, :], lamim[:, :], ALU.mult, eng="dve")
    tt(kre[:, :], kre[:, :], ta[:, :], ALU.add, eng="dve")
    tt(kim[:, :], PWim[:, 1, :], lamre[:, :], ALU.mult, eng="dve"); tt(ta[:, :], tc[:, :], lamim[:, :], ALU.mult, eng="dve")
    tt(kim[:, :], kim[:, :], ta[:, :], ALU.subtract, eng="dve")
    tt(ta[:, :], lamre[:, :], lamre[:, :], ALU.mult, eng="dve"); tt(tb[:, :], lamim[:, :], lamim[:, :], ALU.mult, eng="dve")
    tt(ta[:, :], ta[:, :], tb[:, :], ALU.add, eng="dve")
    P.c("dve", lambda e: e.reciprocal(out=tb[:, :], in_=ta[:, :]), reads=[TAB], writes=[TAB])
    tt(kre[:, :], kre[:, :], tb[:, :], ALU.mult, eng="dve"); tt(kim[:, :], kim[:, :], tb[:, :], ALU.mult, eng="dve")
    for d in range(2):
        cs_ = slice(d * 32, (d + 1) * 32)
        kr = kre[:, cs_].unsqueeze(2).broadcast_to([128, 32, 16]); ki = kim[:, cs_].unsqueeze(2).broadcast_to([128, 32, 16])
        tt(e1[:, :, :], kr, bre[:, :, :], ALU.mult); tt(e2[:, :, :], ki, bim[:, :, :], ALU.mult)
        tt(BBre[:, cs_, :], e1[:, :, :], e2[:, :, :], ALU.subtract, eng="dve")
        tt(e1[:, :, :], kr, bim[:, :, :], ALU.mult); tt(e2[:, :, :], ki, bre[:, :, :], ALU.mult)
        tt(BBim[:, cs_, :], e1[:, :, :], e2[:, :, :], ALU.add, eng="dve")
    P.c("dve", lambda e: e.tensor_copy(out=Cr[0:64, :, :], in_=cre[0:64, :, :]), reads=[TAB], writes=[TAB])
    ts(Cr[64:128, :, :], cim[64:128, :, :], -1.0, None, ALU.mult)
    for e_ in range(2, 9):
        cmul(PWre[:, e_, :], PWim[:, e_, :], PWre[:, e_ - 1, :], PWim[:, e_ - 1, :], PWre[:, 1, :], PWim[:, 1, :])
    P.c("pool", lambda e: e.memset(NPre[:, 0, :], 1.0), reads=[TAB], writes=[TAB])
    P.c("pool", lambda e: e.memset(NPim[:, 0, :], 0.0), reads=[TAB], writes=[TAB])
    tt(ta[:, :], PWre[:, 1, :], PWre[:, 1, :], ALU.mult, eng="dve"); tt(tb[:, :], PWim[:, 1, :], PWim[:, 1, :], ALU.mult, eng="dve")
    tt(ta[:, :], ta[:, :], tb[:, :], ALU.add, eng="dve")
    P.c("dve", lambda e: e.reciprocal(out=tc[:, :], in_=ta[:, :]), reads=[TAB], writes=[TAB])
    tt(NPre[:, 1, :], PWre[:, 1, :], tc[:, :], ALU.mult, eng="dve")
    tt(NPim[:, 1, :], PWim[:, 1, :], tc[:, :], ALU.mult, eng="dve")
    ts(NPim[:, 1, :], NPim[:, 1, :], -1.0, None, ALU.mult)
    for e_ in range(2, 8):
        cmul(NPre[:, e_, :], NPim[:, e_, :], NPre[:, e_ - 1, :], NPim[:, e_ - 1, :], NPre[:, 1, :], NPim[:, 1, :])
    P.c("dve", lambda e: e.tensor_copy(out=HPre[:, 0, :], in_=PWre[:, 8, :]), reads=[TAB], writes=[TAB])
    P.c("dve", lambda e: e.tensor_copy(out=HPim[:, 0, :], in_=PWim[:, 8, :]), reads=[TAB], writes=[TAB])
    for k in range(4):
        b0 = 3 * k
        cmul(HPre[:, b0 + 1, :], HPim[:, b0 + 1, :], HPre[:, b0, :], HPim[:, b0, :], HPre[:, b0, :], HPim[:, b0, :])
        cmul(HPre[:, b0 + 2, :], HPim[:, b0 + 2, :], HPre[:, b0 + 1, :], HPim[:, b0 + 1, :], HPre[:, b0, :], HPim[:, b0, :])
        cmul(HPre[:, b0 + 3, :], HPim[:, b0 + 3, :], HPre[:, b0 + 1, :], HPim[:, b0 + 1, :], HPre[:, b0 + 1, :], HPim[:, b0 + 1, :])
    P.c("dve", lambda e: e.tensor_copy(out=V1[0:64, :, :], in_=HPre[0:64, :, :]), reads=[TAB], writes=[TAB])
    ts(V1[64:128, :, :], HPim[64:128, :, :], -1.0, None, ALU.mult)
    P.c("dve", lambda e: e.tensor_copy(out=V2[0:64, :, :], in_=HPim[0:64, :, :]), reads=[TAB], writes=[TAB])
    P.c("dve", lambda e: e.tensor_copy(out=V2[64:128, :, :], in_=HPre[64:128, :, :]), reads=[TAB], writes=[TAB])
    P.c("pool", lambda e: e.iota(maskF[:, :], pattern=[[1, 8], [0, 16]], base=0, channel_multiplier=0, allow_small_or_imprecise_dtypes=True),
        reads=[TAB], writes=[TAB])
    P.c("pool", lambda e: e.iota(tiI[:, 0:1], pattern=[[0, 1]], base=0, channel_multiplier=1), reads=[TAB], writes=[TAB])
    P.c("dve", lambda e: e.tensor_single_scalar(out=tiI[:, 1:2], in_=tiI[:, 0:1], scalar=4, op=ALU.arith_shift_right), reads=[TAB], writes=[TAB])
    P.c("dve", lambda e: e.tensor_copy(out=ta[:, 0:1], in_=tiI[:, 1:2]), reads=[TAB], writes=[TAB])
    ts(maskB[:, :], maskF[:, :], ta[:, 0:1], None, ALU.is_le)
    ts(maskF[:, :], maskF[:, :], ta[:, 0:1], None, ALU.is_ge)
    if debug:
        for (nm, t_, w) in (("d_PWre", PWre, 9 * 64), ("d_PWim", PWim, 9 * 64), ("d_HPre", HPre, 13 * 64), ("d_HPim", HPim, 13 * 64),
                            ("d_BBre", BBre, 64 * 16), ("d_BBim", BBim, 64 * 16), ("d_NPre", NPre, 8 * 64)):
            dd = dout(nm, [128, w])
            P.dma("sp", lambda e, dd=dd, t_=t_: e.dma_start(out=dd[:, :], in_=t_[:, :, :].rearrange("p a b -> p (a b)")), "dbg", reads=[TAB])
        dd = dout("d_maskF", [128, 128])
        P.dma("sp", lambda e, dd=dd: e.dma_start(out=dd[:, :], in_=maskF[:, :]), "dbg", reads=[TAB])
    stage("S5tab")

    XR4 = XR[:, :, :].rearrange("p g (s q) -> p g s q", s=8)
    QN4 = QN[:, :, :].rearrange("p g (s q) -> p g s q", s=8)
    WRf4 = WRf[:, :, :].rearrange("p g (s q) -> p g s q", s=8)
    LO, HI = slice(0, 64), slice(64, 128)
    psM = [psA[0], psA[1]]
    psPV = psA[2]

    P.c("dve", lambda e: e.tensor_copy(out=e1[HI, :, :], in_=BBre[HI, 0:32, :]), reads=[TAB], writes=["e1hi"])
    P.c("dve", lambda e: e.tensor_copy(out=e2[HI, :, :], in_=BBre[HI, 32:64, :]), reads=[TAB], writes=["e2hi"])
    P.c("dve", lambda e: e.tensor_copy(out=BBre[HI, :, :], in_=BBim[HI, :, :]), reads=[TAB, "e1hi", "e2hi"], writes=[TAB])
    P.c("dve", lambda e: e.tensor_scalar(out=BBim[HI, 0:32, :], in0=e1[HI, :, :], scalar1=-1.0, scalar2=None, op0=ALU.mult), reads=[TAB, "e1hi"], writes=[TAB])
    P.c("dve", lambda e: e.tensor_scalar(out=BBim[HI, 32:64, :], in0=e2[HI, :, :], scalar1=-1.0, scalar2=None, op0=ALU.mult), reads=[TAB, "e2hi"], writes=[TAB])
    P.c("dve", lambda e: e.tensor_copy(out=cim[HI, :, :], in_=cre[HI, :, :]), reads=[TAB], writes=[TAB])

    def ctab(dst, dname, Are, Aim, e_idx, d, X1, X2, eng, ta_, tan):
        cs_ = slice(d * 32, (d + 1) * 32)
        ar = Are[:, e_idx, cs_].unsqueeze(2).broadcast_to([128, 32, 16])
        ai = Aim[:, e_idx, cs_].unsqueeze(2).broadcast_to([128, 32, 16])
        P.c(eng, lambda e: e.tensor_tensor(out=dst, in0=ar, in1=X1, op=ALU.mult), reads=[TAB], writes=[dname])
        P.c(eng, lambda e: e.tensor_tensor(out=ta_, in0=ai, in1=X2, op=ALU.mult), reads=[TAB], writes=[tan])
        P.c(eng, lambda e: e.tensor_tensor(out=dst, in0=dst, in1=ta_, op=ALU.subtract), reads=[dname, tan], writes=[dname])

    for d in range(2):
        cs_ = slice(d * 32, (d + 1) * 32)
        for s_ in range(8):
            eb = (7 - s_) if d == 0 else s_
            ctab(XR4[:, :, s_, :], "L1lo", PWre, PWim, eb, d, BBre[:, cs_, :], BBim[:, cs_, :], "dve", e1[:, :, :], "e1lo")
            ew_ = (s_ + 1) if d == 0 else (8 - s_)
            ctab(WRf4[:, :, s_, :], "WRf", PWre, PWim, ew_, d, Cr[:, :, :], cim[:, :, :], "dve", e1[:, :, :], "e1lo")
            en = (7 - s_) if d == 0 else s_
            ctab(QN4[:, :, s_, :], "L1hi", NPre, NPim, en, d, Cr[:, :, :], cim[:, :, :], "pool", e2[:, :, :], "e2lo")
        P.c("act", lambda e, cs_=cs_: e.activation(out=WR[:, cs_, :], in_=WRf[:, :, :], func=AF.Copy), reads=["WRf"], writes=["WRb"])
        for g in range(32):
            ps = psM[g % 2]
            P.c("pe", lambda e, ps=ps, g=g: e.matmul(ps[:, 0:128], lhsT=XR[:, g, :], rhs=QN[:, g, :], start=True, stop=True),
                reads=["L1lo", "L1hi"], writes=[f"ps{g % 2}"])
            if d == 0:
                P.c("dve", lambda e, ps=ps, g=g: e.tensor_tensor(out=Macc[:, g, :], in0=ps[:, 0:128], in1=maskF[:, :], op=ALU.mult),
                    reads=[f"ps{g % 2}", TAB], writes=["Macc"])
            else:
                P.c("dve", lambda e, ps=ps, g=g: e.tensor_tensor(out=tmpM[:, :], in0=ps[:, 0:128], in1=maskB[:, :], op=ALU.mult),
                    reads=[f"ps{g % 2}", TAB], writes=["tmpM"])
                P.c("dve", lambda e, g=g: e.tensor_tensor(out=Macc[:, g, :], in0=Macc[:, g, :], in1=tmpM[:, :], op=ALU.add),
                    reads=["tmpM", "Macc"], writes=["Macc"])
        for g4 in range(8):
            for j in range(4):
                g = g4 * 4 + j
                P.c("pe", lambda e, g=g, j=j: e.transpose(psPV[:, j * 128:(j + 1) * 128], XR[:, g, :], ident_f[:, :]),
                    reads=["L1lo", "L1hi", "ident_f"], writes=["ps2"])
            P.c("act", lambda e, g4=g4, d=d: e.activation(out=PV[:, d * 32 + g4 * 4: d * 32 + g4 * 4 + 4, :],
                                                        in_=psPV[:, :].rearrange("p (j m) -> p j m", j=4), func=AF.Copy),
                reads=["ps2"], writes=["PV"])
    idb = ident_f[:, :].rearrange("p (t q) -> p t q", t=8).unsqueeze(1).broadcast_to([128, 32, 8, 16])
    d4 = d128[:, :, :].unsqueeze(2).broadcast_to([128, 32, 8, 16])
    P.c("dve", lambda e: e.tensor_tensor(out=QN4, in0=idb, in1=d4, op=ALU.mult), reads=[TAB, "ident_f", "L1lo", "L1hi"], writes=["L1lo", "L1hi"])
    P.c("dve", lambda e: e.tensor_tensor(out=Mbf[:, :, :], in0=Macc[:, :, :], in1=QN[:, :, :], op=ALU.add), reads=["Macc", "L1lo", "L1hi"], writes=["Mbf"])
    if debug:
        dd = nc.dram_tensor("d_Mbf", [128, 32 * 128], BF16, kind="ExternalOutput").ap(); dbg["d_Mbf"] = dd
        P.dma("sp", lambda e, dd=dd: e.dma_start(out=dd[:, :], in_=Mbf[:, :, :].rearrange("p a b -> p (a b)")), "dbg", reads=["Mbf"])
        dd = nc.dram_tensor("d_PV", [128, 64 * 128], BF16, kind="ExternalOutput").ap(); dbg["d_PV"] = dd
        P.dma("sp", lambda e, dd=dd: e.dma_start(out=dd[:, :], in_=PV[:, :, :].rearrange("p a b -> p (a b)")), "dbg", reads=["PV"])
        dd = nc.dram_tensor("d_WR", [128, 64 * 128], BF16, kind="ExternalOutput").ap(); dbg["d_WR"] = dd
        P.dma("sp", lambda e, dd=dd: e.dma_start(out=dd[:, :], in_=WR[:, :, :].rearrange("p a b -> p (a b)")), "dbg", reads=["WRb"])
    stage("S5l1")

    B1 = Arena(nc); B1.off = big0
    Z = B1.alloc("Z", [128, 32, 256], BF16)
    Zc = B1.alloc("Zc", [128, 32, 32], BF16)
    Ysb = B1.alloc("Ysb", [128, 8, 256], BF16)
    WV = 4
    Xw = [B1.alloc(f"Xw{i}", [128, WV, 288], BF16) for i in range(2)]
    TAw = [B1.alloc(f"TAw{i}", [128, WV, 80], BF16) for i in range(2)]
    TBw = [B1.alloc(f"TBw{i}", [128, WV, 80], BF16) for i in range(2)]
    Fall = B1.alloc("Fall", [128, 64], F32)
    Fctx = B1.alloc("Fctx", [128, 64], F32)
    hacc = B1.alloc("hacc", [128, 64], F32)
    hacc_bf = B1.alloc("hacc_bf", [128, 64], BF16)
    htmp = B1.alloc("htmp", [128, 64], F32)
    xgB = B1.alloc("xgB", [128, 4, 64], F32)
    D2 = B1.alloc("D2", [128, 64], F32)
    assert B1.off <= big0 + 32 * 1024, B1.off - big0
    B2 = Arena(nc); B2.off = dead0
    s5gT = B2.alloc("s5gT", [128, 4, T], BF16)
    wglu = B2.alloc("wglu", [128, 4, 512], BF16)
    NR = 36
    Rring = [B2.alloc(f"Rr{i}", [128, 128], BF16) for i in range(NR)]
    sgt = B2.alloc("sgt", [128, 512], BF16)
    XAw = [B2.alloc(f"XAw{i}", [128, WV, 256], BF16) for i in range(2)]
    XBw = [B2.alloc(f"XBw{i}", [128, WV, 256], BF16) for i in range(2)]
    assert B2.off <= S_END, (B2.off, S_END)
    post = ["Z", "Zc", "Ysb", "Fall", "Fctx", "hacc", "hacc_bf", "htmp", "xgB", "D2", "s5gT", "wglu", "sgt"] + \
           [f"{n}{i}" for n in ("Xw", "TAw", "TBw", "XAw", "XBw") for i in range(2)] + [f"Rr{i}" for i in range(NR)]
    alias(post, ["L1lo", "L1hi", "e1lo", "e1hi", "e2lo", "e2hi", "tmpM", TAB])
    P.dma("pool", lambda e: e.dma_start(out=wglu[:, :, :], in_=d_in["wglu"].rearrange("(k p) n -> p k n", p=128)), "wglu", reads=[TAB], writes=["wglu"])
    P.c("dve", lambda e: e.tensor_tensor(out=D2[:, :], in0=ident_f[:, 0:64], in1=ident_f[:, 64:128], op=ALU.add), reads=["ident_f", TAB], writes=["D2"])
    P.c("pool", lambda e: e.memset(Eblk[:, :, :], 0.0), reads=[], writes=["Eblk", "WRf"])
    for a_ in range(8):
        for b_ in range(8):
            P.c("pool", lambda e, a_=a_, b_=b_: e.affine_select(out=Eblk[:, a_ * 8 + b_, 16 * b_:16 * b_ + 16], in_=ones_f[:, 0:16], pattern=[[1, 16]],
                                                              compare_op=ALU.is_equal, fill=0.0, base=16 * a_, channel_multiplier=-1),
                reads=["ones_f"], writes=["Eblk"])
    psZ = [psA[3], psA[4]]
    ev = [0]

    def evac(out, in_, reads, writes, eng=None):
        ev[0] += 1
        if eng is None:
            eng = "act" if ev[0] % 2 else "dve"
        if eng == "act":
            P.c("act", lambda e: e.activation(out=out, in_=in_, func=AF.Copy), reads=reads, writes=writes)
        else:
            P.c("dve", lambda e: e.tensor_copy(out=out, in_=in_), reads=reads, writes=writes)

    for g in range(32):
        fc, g8 = g // 8, g % 8
        ps = psZ[g % 2]
        for s_ in range(8):
            P.c("pe", lambda e, ps=ps, fc=fc, g8=g8, s_=s_: e.matmul(ps[:, 0:256], lhsT=Eblk[:, g8 * 8 + s_, :], rhs=uT[:, fc, s_::8],
                                                                    start=(s_ == 0), stop=(s_ == 7)), reads=["Eblk", "uT"], writes=[f"ps{3 + g % 2}"])
        for s_ in range(8):
            P.c("pe", lambda e, ps=ps, fc=fc, g8=g8, s_=s_: e.matmul(ps[:, 256:288], lhsT=Eblk[:, g8 * 8 + s_, :], rhs=ucT[:, fc, s_::8],
                                                                    start=(s_ == 0), stop=(s_ == 7)), reads=["Eblk", "ucT"], writes=[f"ps{3 + g % 2}"])
        en_ = "act" if g % 2 else "dve"
        evac(Z[:, g, :], ps[:, 0:256], [f"ps{3 + g % 2}"], ["Z"], eng=en_)
        evac(Zc[:, g, :], ps[:, 256:288], [f"ps{3 + g % 2}"], ["Zc"], eng=en_)

    rr = [0]
    reng = ["dve", "pool"]

    def build_R(dst, dname, hp_idx, q):
        for half, Vt in ((0, V1), (1, V2)):
            rr[0] += 1
            eng = reng[rr[0] % 2]
            o_ = dst[:, half * 64:(half + 1) * 64]
            sc = Vt[:, hp_idx, q:q + 1]
            if eng == "act":
                P.c("act", lambda e, o_=o_, sc=sc: e.activation(out=o_, in_=D2[:, :], func=AF.Identity, scale=sc), reads=["D2", TAB], writes=[dname])
            elif eng == "dve":
                P.c("dve", lambda e, o_=o_, sc=sc: e.tensor_scalar(out=o_, in0=D2[:, :], scalar1=sc, scalar2=None, op0=ALU.mult), reads=["D2", TAB], writes=[dname])
            else:
                P.c("pool", lambda e, o_=o_, sc=sc: e.tensor_scalar(out=o_, in0=D2[:, :], scalar1=sc, scalar2=0.0, op0=ALU.mult, op1=ALU.add),
                    reads=["D2", TAB], writes=[dname])

    ring = [0]

    def get_R(hp_idx, q):
        slot = ring[0] % NR
        ring[0] += 1
        build_R(Rring[slot][:, :], f"Rr{slot}", hp_idx, q)
        return Rring[slot], f"Rr{slot}"

    def tree_wave(w):
        par = w % 2
        qs = [w * WV + c_ for c_ in range(WV)]
        d = qs[0] // 32
        bankL = [psA[1 + 2 * par], psA[2 + 2 * par]]
        bnL = [f"ps{1 + 2 * par}", f"ps{2 + 2 * par}"]
        psC, pcn = psA[5 + par], f"ps{5 + par}"
        for c_, q in enumerate(qs):
            g = q % 32
            bk, bn = bankL[c_ // 2], bnL[c_ // 2]
            P.c("pe", lambda e, bk=bk, q=q, g=g, c_=c_: e.matmul(bk[:, (c_ % 2) * 256:(c_ % 2 + 1) * 256], lhsT=PV[:, q, :], rhs=Z[:, g, :], start=True, stop=True),
                reads=["PV", "Z"], writes=[bn])
            P.c("pe", lambda e, q=q, g=g, c_=c_, psC=psC: e.matmul(psC[:, c_ * 32:(c_ + 1) * 32], lhsT=PV[:, q, :], rhs=Zc[:, g, :], start=True, stop=True),
                reads=["PV", "Zc"], writes=[pcn])
        yield
        xw, xwn = Xw[par], f"Xw{par}"
        for h_ in range(2):
            evac(xw[:, 2 * h_:2 * h_ + 2, 0:256], bankL[h_][:, :].rearrange("p (c n) -> p c n", c=2), [bnL[h_]], [xwn], eng="act")
        evac(xw[:, :, 256:288], psC[:, 0:WV * 32].rearrange("p (c n) -> p c n", c=WV), [pcn], [xwn], eng="act")
        yield
        cur, cname = xw, xwn
        loc_off, loc_n, ctx_off, ctx_n = 0, 256, 256, 32
        bufs = [(TAw[par], f"TAw{par}"), (TBw[par], f"TBw{par}")]
        tb_ = par * 256
        Rw_next = [[None] + [get_R(j - 1, q) for j in (1, 2, 3)] for q in qs]
        for k in range(4):
            nxt, nname = bufs[k % 2]
            lev = []
            for (off, n_, is_ctx) in ((loc_off, loc_n, False), (ctx_off, ctx_n, True)):
                if n_ <= 1:
                    continue
                rad = 4 if n_ >= 4 else n_
                no = n_ // rad
                base = 128 if not is_ctx else 384
                lev.append((off, n_, is_ctx, rad, no, base))
            Rw = Rw_next
            for c_, q in enumerate(qs):
                for (off, n_, is_ctx, rad, no, base) in lev:
                    ocol = base + c_ * no
                    bk_, bkn_ = (psC, pcn)
                    for jj in range(rad):
                        pw = (rad - 1 - jj) if d == 0 else jj
                        lt, ltn = (ident_b, "ident_b") if pw == 0 else Rw[c_][pw]
                        P.c("pe", lambda e, bk_=bk_, lt=lt, cur=cur, c_=c_, off=off, jj=jj, rad=rad, n_=n_, ocol=ocol, no=no: e.matmul(
                            bk_[:, ocol:ocol + no], lhsT=lt[:, :], rhs=cur[:, c_, off + jj:off + n_:rad], start=(jj == 0), stop=(jj == rad - 1)),
                            reads=[ltn, cname], writes=[bkn_])
            if k < 3:
                Rw_next = [[None] + [get_R(3 * (k + 1) + j - 1, q) for j in (1, 2, 3)] for q in qs]
            yield
            for (off, n_, is_ctx, rad, no, base) in lev:
                bk_, bkn_ = (psC, pcn)
                if no == 1:
                    dstF, dn = (Fctx, "Fctx") if is_ctx else (Fall, "Fall")
                    P.c("act", lambda e, bk_=bk_, dstF=dstF, base=base, q0=qs[0]: e.activation(out=dstF[:, q0:q0 + WV], in_=bk_[:, base:base + WV], func=AF.Copy),
                        reads=[bkn_], writes=[dn])
                else:
                    o2 = 64 if is_ctx else 0
                    evac(nxt[:, :, o2:o2 + no], bk_[:, base:base + WV * no].rearrange("p (c n) -> p c n", c=WV), [bkn_], [nname], eng="act")
            cur, cname = nxt, nname
            loc_off, loc_n = 0, (loc_n // 4 if loc_n > 1 else 0)
            ctx_off, ctx_n = 64, (ctx_n // (4 if ctx_n >= 4 else ctx_n) if ctx_n > 1 else 0)

    def lockstep(gens):
        gens = list(gens)
        while gens:
            for g_ in list(gens):
                try:
                    next(g_)
                except StopIteration:
                    gens.remove(g_)

    def rolling(gens, depth=2, stagger=2):
        it = iter(gens)
        active = []
        pending = next(it, None)
        while active or pending is not None:
            if pending is not None and len(active) < depth and (not active or active[-1][1] >= stagger):
                active.append([pending, 0])
                pending = next(it, None)
            for ent in list(active):
                try:
                    next(ent[0])
                    ent[1] += 1
                except StopIteration:
                    active.remove(ent)

    rolling([tree_wave(w) for w in range(64 // WV)])
    if debug:
        for (nm, t_) in (("d_Fall", Fall), ("d_Fctx", Fctx)):
            dd = dout(nm, [128, 64])
            P.dma("sp", lambda e, dd=dd, t_=t_: e.dma_start(out=dd[:, :], in_=t_[:, :]), "dbg", reads=["Fall", "Fctx"])
    stage("S5tree")

    xb_in = nc.dram_tensor("xb_in", [128, 64], F32)
    xb_out = nc.dram_tensor("xb_out", [512, 64], F32)
    P.dma("sp", lambda e: e.dma_start(out=xb_in[:, :], in_=Fall[:, :]), "xb_st", reads=["Fall"], writes=["xb_in"])
    P.dma("pool", lambda e: e.collective_compute("AllGather", ALU.bypass, replica_groups=[[0, 1, 2, 3], [4, 5, 6, 7]],
                                                  ins=[xb_in.ap().opt()], outs=[xb_out.ap().opt()]),
          "ccB", reads=["xb_in"], writes=["xb_out"], inc=1)
    P.dma("sp", lambda e: e.dma_start(out=xgB[:, :, :], in_=xb_out.ap().rearrange("(r p) n -> p r n", p=128)), "xb_ld", reads=["xb_out"], writes=["xgB"])
    P.c("dve", lambda e: e.tensor_copy(out=hacc[:, :], in_=Fctx[:, :]), reads=["Fctx"], writes=["hacc"])
    psH = psA[1]
    for n_ in range(3):
        P.c("act", lambda e: e.activation(out=hacc_bf[:, :], in_=hacc[:, :], func=AF.Copy), reads=["hacc"], writes=["hacc_bf"])
        for q in range(64):
            Rt, Rn = get_R(12, q)
            P.c("pe", lambda e, q=q, Rt=Rt: e.matmul(psH[:, q:q + 1], lhsT=Rt[:, :], rhs=hacc_bf[:, q:q + 1], start=True, stop=True),
                reads=[Rn, "hacc_bf"], writes=["ps1"])
        for (cs_, rank, mcol) in ((slice(0, 32), n_, n_), (slice(32, 64), 3 - n_, 3 + n_)):
            P.c("dve", lambda e, cs_=cs_, rank=rank: e.tensor_tensor(out=htmp[:, cs_], in0=psH[:, cs_], in1=xgB[:, rank, cs_], op=ALU.add),
                reads=["ps1", "xgB"], writes=["htmp"])
            P.c("dve", lambda e, cs_=cs_: e.tensor_tensor(out=htmp[:, cs_], in0=htmp[:, cs_], in1=hacc[:, cs_], op=ALU.subtract),
                reads=["htmp", "hacc"], writes=["htmp"])
            P.c("dve", lambda e, cs_=cs_, mcol=mcol: e.scalar_tensor_tensor(out=hacc[:, cs_], in0=htmp[:, cs_], scalar=msk[:, mcol:mcol + 1],
                                                                           in1=hacc[:, cs_], op0=ALU.mult, op1=ALU.add),
                reads=["htmp", "hacc", "msk"], writes=["hacc"])
    P.c("act", lambda e: e.activation(out=hacc_bf[:, :], in_=hacc[:, :], func=AF.Copy), reads=["hacc"], writes=["hacc_bf"])
    stage("S5xchg")

    psY = psA[0]
    psS2 = [psA[5], psA[6]]
    def ks_wave(fc, gp):
        w = fc * 4 + gp
        par = w % 2
        g0 = fc * 8 + gp * 2
        qs = [g0, g0 + 1, 32 + g0, 33 + g0]
        banks = [psA[1 + 2 * par], psA[2 + 2 * par]]
        bns = [f"ps{1 + 2 * par}", f"ps{2 + 2 * par}"]
        for c_, q in enumerate(qs):
            P.c("pe", lambda e, c_=c_, q=q, bk=banks[c_ // 2]: e.matmul(bk[:, (c_ % 2) * 256:(c_ % 2 + 1) * 256], lhsT=PV[:, q, :], rhs=Z[:, q % 32, :],
                                                   start=True, stop=True), reads=["PV", "Z"], writes=[bns[c_ // 2]])
        yield
        xa, xan, xb_, xbn = XAw[par], f"XAw{par}", XBw[par], f"XBw{par}"
        evac(xa[:, 0:2, 1:256], banks[0][:, :].rearrange("p (c n) -> p c n", c=2)[:, :, 0:255], [bns[0]], [xan], eng="act")
        evac(xa[:, 2:4, 0:255], banks[1][:, :].rearrange("p (c n) -> p c n", c=2)[:, :, 1:256], [bns[1]], [xan], eng="act")
        P.c("dve", lambda e, xa=xa, g0=g0: e.tensor_copy(out=xa[:, 0:2, 0], in_=hacc_bf[:, g0:g0 + 2]), reads=["hacc_bf"], writes=[xan])
        P.c("dve", lambda e, xa=xa, g0=g0: e.tensor_copy(out=xa[:, 2:4, 255], in_=hacc_bf[:, 32 + g0:34 + g0]), reads=["hacc_bf"], writes=[xan])
        yield
        cur, cn, oth, on = xa, xan, xb_, xbn
        Rw_next = [[get_R(j - 1, q) for j in (1, 2, 3)] for q in qs]
        for k in range(4):
            Rw = Rw_next
            for c_, q in enumerate(qs):
                pk, pkn = banks[c_ // 2], bns[c_ // 2]
                cb_ = (c_ % 2) * 256
                P.c("pe", lambda e, pk=pk, cb_=cb_, cur=cur, c_=c_: e.matmul(pk[:, cb_:cb_ + 256], lhsT=ident_b[:, :], rhs=cur[:, c_, :], start=True, stop=False),
                    reads=["ident_b", cn], writes=[pkn])
                for j in (1, 2, 3):
                    sh = j * (4 ** k)
                    Rt, Rn = Rw[c_][j - 1]
                    if c_ < 2:
                        P.c("pe", lambda e, pk=pk, cb_=cb_, Rt=Rt, cur=cur, c_=c_, sh=sh, j=j: e.matmul(
                            pk[:, cb_ + sh:cb_ + 256], lhsT=Rt[:, :], rhs=cur[:, c_, 0:256 - sh], start=False, stop=(j == 3)), reads=[Rn, cn], writes=[pkn])
                    else:
                        P.c("pe", lambda e, pk=pk, cb_=cb_, Rt=Rt, cur=cur, c_=c_, sh=sh, j=j: e.matmul(
                            pk[:, cb_:cb_ + 256 - sh], lhsT=Rt[:, :], rhs=cur[:, c_, sh:256], start=False, stop=(j == 3)), reads=[Rn, cn], writes=[pkn])
            if k < 3:
                Rw_next = [[get_R(3 * (k + 1) + j - 1, q) for j in (1, 2, 3)] for q in qs]
            yield
            evac(oth[:, 0:2, :], banks[0][:, :].rearrange("p (c n) -> p c n", c=2), [bns[0]], [on], eng="act")
            evac(oth[:, 2:4, :], banks[1][:, :].rearrange("p (c n) -> p c n", c=2), [bns[1]], [on], eng="act")
            cur, cn, oth, on = oth, on, cur, cn
            yield
        for c_ in range(2):
            g = g0 + c_
            ysl = slice(c_ * 256, (c_ + 1) * 256)
            P.c("pe", lambda e, g=g, ysl=ysl: e.matmul(psY[:, ysl], lhsT=Mbf[:, g, :], rhs=Z[:, g, :], start=True, stop=False), reads=["Mbf", "Z"], writes=["ps0"])
            P.c("pe", lambda e, g=g, ysl=ysl, cur=cur, c_=c_: e.matmul(psY[:, ysl], lhsT=WR[:, g, :], rhs=cur[:, c_, :], start=False, stop=False),
                reads=["WRb", cn], writes=["ps0"])
            P.c("pe", lambda e, g=g, ysl=ysl, cur=cur, c_=c_: e.matmul(psY[:, ysl], lhsT=WR[:, 32 + g, :], rhs=cur[:, 2 + c_, :], start=False, stop=True),
                reads=["WRb", cn], writes=["ps0"])
        evac(Ysb[:, 2 * gp:2 * gp + 2, :], psY[:, :].rearrange("p (c n) -> p c n", c=2), ["ps0"], ["Ysb"], eng="act")

    for fc in range(4):
        rolling([ks_wave(fc, gp) for gp in range(4)])
        for s_ in range(8):
            ps = psS2[s_ % 2]
            for g8 in range(8):
                P.c("pe", lambda e, ps=ps, s_=s_, g8=g8: e.matmul(ps[:, 0:256], lhsT=Eblk[:, s_ * 8 + g8, :], rhs=Ysb[:, g8, :],
                                                                 start=(g8 == 0), stop=(g8 == 7)), reads=["Eblk", "Ysb"], writes=[f"ps{5 + s_ % 2}"])
            P.c("act", lambda e, ps=ps, s_=s_, fc=fc: e.activation(out=s5gT[:, fc, s_::8], in_=ps[:, 0:256], func=AF.Gelu_apprx_tanh),
                reads=[f"ps{5 + s_ % 2}"], writes=["s5gT"])
    stage("S5y")
    psG = [psA[0], psA[1]]
    for tb_ in range(4):
        tsl = slice(tb_ * 512, (tb_ + 1) * 512)
        for oc in range(4):
            ps = psG[oc % 2]
            for kc in range(4):
                P.c("pe", lambda e, ps=ps, kc=kc, oc=oc, tsl=tsl: e.matmul(ps[:, :], lhsT=wglu[:, kc, oc * 128:(oc + 1) * 128], rhs=s5gT[:, kc, tsl],
                                                                          start=(kc == 0), stop=(kc == 3)), reads=["wglu", "s5gT"], writes=[f"ps{oc % 2}"])
            P.c("act", lambda e, ps=ps, oc=oc: e.activation(out=sgt[:, :], in_=ps[:, :], func=AF.Sigmoid, bias=bglu[:, oc:oc + 1]),
                reads=[f"ps{oc % 2}", TAB], writes=["sgt"])
            P.c("dve", lambda e, oc=oc, tsl=tsl: e.tensor_tensor(out=mixT[:, oc, tsl], in0=s5gT[:, oc, tsl], in1=sgt[:, :], op=ALU.mult),
                reads=["s5gT", "sgt", "Macc"], writes=["mixT", "Macc"])
    if debug:
        dd = nc.dram_tensor("d_s5gT", [128, 4 * T], BF16, kind="ExternalOutput").ap(); dbg["d_s5gT"] = dd
        P.dma("sp", lambda e, dd=dd: e.dma_start(out=dd[:, :], in_=s5gT[:, :, :].rearrange("p a b -> p (a b)")), "dbg", reads=["s5gT"])
        dd = nc.dram_tensor("d_mixT2", [128, 8 * T], BF16, kind="ExternalOutput").ap(); dbg["d_mixT2"] = dd
        P.dma("sp", lambda e, dd=dd: e.dma_start(out=dd[:, :], in_=mixT[:, :, :].rearrange("p a b -> p (a b)")), "dbg", reads=["mixT"])
    stage("S5glu")
    return None


def _s5_host(inputs):
    def dup(a):
        return np.ascontiguousarray(np.concatenate([a, a], 0)).astype(np.float32)
    lre = np.concatenate([inputs["s5_lambda_re_f"][0], inputs["s5_lambda_re_b"][0]], 0)
    lim = np.concatenate([inputs["s5_lambda_im_f"][0], inputs["s5_lambda_im_b"][0]], 0)
    lst = np.concatenate([inputs["s5_log_step_f"][0], inputs["s5_log_step_b"][0]], 0)
    return {
        "s5_lamre": dup(lre.T), "s5_lamim": dup(lim.T),
        "s5_lstep": np.ascontiguousarray(np.broadcast_to(lst[None, :], (128, 64))).astype(np.float32),
        "s5_bre": dup(inputs["s5_b_re"][0].transpose(1, 0, 2)), "s5_bim": dup(inputs["s5_b_im"][0].transpose(1, 0, 2)),
        "s5_cre": dup(inputs["s5_c_re"][0].transpose(2, 0, 1)), "s5_cim": dup(inputs["s5_c_im"][0].transpose(2, 0, 1)),
        "s5_d128": np.ascontiguousarray(np.broadcast_to(inputs["s5_d"][0].reshape(1, 32, 16), (128, 32, 16))).astype(np.float32),
        "s5_bglu": np.ascontiguousarray(inputs["s5_b_glu"][0].reshape(4, 128).T).astype(np.float32),
        "w_glu": np.ascontiguousarray(inputs["s5_w_glu"][0]).astype(np.float32),
    }


def make_inputs(inputs):
    x = np.asarray(inputs["x"], np.float32)
    per = []
    for r in range(NCORES):
        b, seg = r // 4, r % 4
        cv = np.stack([inputs["c"][b].reshape(8, 128).T, inputs["c_ctx"].reshape(8, 128).T], -1)
        m = {
            "x_loc": np.ascontiguousarray(x[b, seg * T:(seg + 1) * T]),
            "cvec": np.ascontiguousarray(cv.reshape(128, 16)).astype(np.float32),
            "w_mod": np.ascontiguousarray(inputs["w_mod"][0]),
            "b_mod2": np.ascontiguousarray(np.broadcast_to(inputs["b_mod"][0][None, :], (2, 6 * D))).astype(np.float32),
            "n1w": np.ascontiguousarray(inputs["norm1_w"][0].reshape(8, 128).T),
            "n2w": np.ascontiguousarray(inputs["norm2_w"][0].reshape(8, 128).T),
            "w_in": np.ascontiguousarray(inputs["w_in"][0]),
            "segf": np.full((128, 1), float(seg), np.float32),
            "ctxb": np.ascontiguousarray(inputs["ctx"][b]).astype(np.float32),
            "ldv": np.ascontiguousarray(np.broadcast_to(np.concatenate([inputs["ret_log_decay_f"][0], inputs["ret_log_decay_b"][0]])[None, :], (128, 8))).astype(np.float32),
            **_s5_host(inputs),
            "w_out": np.ascontiguousarray(inputs["w_out"][0]).astype(np.float32),
            "fnw_b": np.ascontiguousarray(np.broadcast_to(inputs["final_norm_w"][None, :], (128, D))).astype(np.float32),
            "cw": np.ascontiguousarray(inputs["conv_w"][0].reshape(3, 22, 128).transpose(2, 1, 0).reshape(128, 66)).astype(np.float32),
            "cb": np.ascontiguousarray(inputs["conv_b"][0].reshape(22, 128).T).astype(np.float32),
            "oh": np.ascontiguousarray(np.broadcast_to(np.array([float(r_ == seg - 1) for r_ in range(4)] + [float(r_ == seg + 1) for r_ in range(4)],
                                                                 np.float32)[None, :], (128, 8))),
            "w_up": np.ascontiguousarray(inputs["w_up"][0]).astype(np.float32),
            "w_down": np.ascontiguousarray(inputs["w_down"][0]).astype(np.float32),
            "msk": np.ascontiguousarray(np.broadcast_to(np.array([0 < seg, 1 < seg, 2 < seg, 3 > seg, 2 > seg, 1 > seg, 0, 0], np.float32)[None, :], (128, 8))),
        }
        per.append(m)
    return per


def kernel(**inputs):
    nc, _ = build(debug=False)
    per = make_inputs(inputs)
    res = run_bass_kernel_spmd(nc, per, core_ids=list(range(NCORES)))
    out = np.zeros((2, 4 * T, D), np.float32)
    for r in range(NCORES):
        b, seg = r // 4, r % 4
        out[b, seg * T:(seg + 1) * T] = res.results[r]["out"]
    return out
```

```python
import contextlib
import numpy as np
import concourse.bass as bass
import concourse.mybir as mybir
from concourse.bass_utils import run_bass_kernel_spmd

F32 = mybir.dt.float32
BF16 = mybir.dt.bfloat16
I32 = mybir.dt.int32
AF = mybir.ActivationFunctionType
ALU = mybir.AluOpType

D = 1024
T = 2048
NT = 16
NCORES = 8
SB_BASE = 16512
SB_TOP = 229344


class Res:
    __slots__ = ("name", "last_w", "readers")

    def __init__(self, name):
        self.name = name
        self.last_w = None
        self.readers = []


class Op:
    __slots__ = ("eng", "fn", "deps", "needs_inc", "sig", "kind", "group", "inc")

    def __init__(self, eng, fn, kind, group, inc):
        self.eng = eng
        self.fn = fn
        self.deps = []
        self.needs_inc = False
        self.sig = None
        self.kind = kind
        self.group = group
        self.inc = inc


class Prog:
    ENGS = ("pe", "act", "dve", "pool", "sp")

    def __init__(self, nc):
        self.nc = nc
        self.ops = []
        self.res = {}

    def _r(self, x):
        if isinstance(x, Res):
            return x
        if x not in self.res:
            self.res[x] = Res(x)
        return self.res[x]

    def _add(self, op, reads, writes):
        reads = [self._r(x) for x in reads]
        writes = [self._r(x) for x in writes]
        deps = set()
        for x in reads:
            if x.last_w is not None:
                deps.add(x.last_w)
            if x.name.startswith("ps"):
                for rd in x.readers:
                    if rd.eng != op.eng:
                        deps.add(rd)
        for x in writes:
            if x.last_w is not None:
                deps.add(x.last_w)
            deps.update(x.readers)
        deps.discard(op)
        for d in deps:
            if d.kind == "c" and op.kind == "c" and d.eng == "pe" and op.eng == "pe":
                continue
            op.deps.append(d)
            d.needs_inc = True
        for x in reads:
            x.readers.append(op)
        for x in writes:
            x.last_w = op
            x.readers = []
        self.ops.append(op)
        return op

    def c(self, eng, fn, reads=(), writes=()):
        return self._add(Op(eng, fn, "c", None, 1), reads, writes)

    def dma(self, eng, fn, group, reads=(), writes=(), inc=16):
        op = Op(eng, fn, "d", group, inc)
        op.needs_inc = True
        return self._add(op, reads, writes)

    def emit(self, final_wait_groups=()):
        nc = self.nc
        sems = {}
        with contextlib.ExitStack() as st:
            cnt = {}
            for op in self.ops:
                if op.kind == "c":
                    if op.needs_inc:
                        k = "E_" + op.eng
                        cnt[k] = cnt.get(k, 0) + 1
                        op.sig = (k, cnt[k])
                else:
                    k = "D_" + op.group
                    cnt[k] = cnt.get(k, 0) + op.inc
                    op.sig = (k, cnt[k])
            for k in cnt:
                sems[k] = st.enter_context(nc.semaphore(k))
            engobj = {"pe": "tensor", "act": "scalar", "dve": "vector", "pool": "gpsimd", "sp": "sync"}
            finals = [("D_" + g, cnt["D_" + g]) for g in final_wait_groups]
            with nc.Block() as block:
                for e in self.ENGS:
                    ops = [o for o in self.ops if o.eng == e]

                    def body(eng, ops=ops, e=e):
                        seen = {}
                        for op in ops:
                            need = {}
                            for d in op.deps:
                                s, v = d.sig
                                if v > need.get(s, 0):
                                    need[s] = v
                            for s, v in need.items():
                                if seen.get(s, 0) >= v:
                                    continue
                                eng.wait_ge(sems[s], v)
                                seen[s] = v
                            ins = op.fn(eng)
                            if op.needs_inc:
                                ins.then_inc(sems[op.sig[0]], op.inc)
                        if e == "sp":
                            for s, v in finals:
                                eng.wait_ge(sems[s], v)
                    if ops or e == "sp":
                        getattr(block, engobj[e])(body)
        return cnt


class Arena:
    def __init__(self, nc):
        self.nc = nc
        self.off = SB_BASE
        self.n = 0

    def alloc(self, name, shape, dtype, at=None):
        esz = 4 if dtype in (F32, I32) else 2
        per = int(np.prod(shape[1:])) * esz
        per = (per + 63) // 64 * 64
        if at is None:
            at = self.off
            self.off += per
            assert self.off <= SB_TOP, (name, self.off)
        self.n += 1
        return self.nc.alloc_sbuf_tensor_at(f"{name}_{self.n}", list(shape), dtype, offset=at)


class _Stop(Exception):
    pass


def build(debug=False, stop=None):
    nc = bass.Bass("TRN2", target_bir_lowering=False)
    P = Prog(nc)
    A = Arena(nc)

    def din(name, shape, dt=F32):
        return nc.dram_tensor(name, list(shape), dt, kind="ExternalInput").ap()

    x_d = din("x_loc", [T, D])
    cvec_d = din("cvec", [128, 16])
    wmod_d = din("w_mod", [D, 6 * D])
    bmod_d = din("b_mod2", [2, 6 * D])
    n1w_d = din("n1w", [128, 8])
    n2w_d = din("n2w", [128, 8])
    win_d = din("w_in", [D, 2560])
    segf_d = din("segf", [128, 1])
    ctx_d = din("ctxb", [256, D])
    ldv_d = din("ldv", [128, 8])
    msk_d = din("msk", [128, 8])
    out_d = nc.dram_tensor("out", [T, D], F32, kind="ExternalOutput").ap()
    dbg = {}

    def dout(name, shape):
        dbg[name] = nc.dram_tensor(name, list(shape), F32, kind="ExternalOutput").ap()
        return dbg[name]

    def stage(name):
        if stop == name:
            raise _Stop()

    stopped = False
    try:
        _body(nc, P, A, debug, stage, dout, dbg, din, x_d, cvec_d, wmod_d, bmod_d, n1w_d, n2w_d, win_d, segf_d, ctx_d, ldv_d, msk_d, out_d)
    except _Stop:
        stopped = True
    if stopped:
        xfin = nc.alloc_sbuf_tensor_at("xfin", [128, D], F32, offset=SB_TOP - 4096)
        rfin = ["xfin"] + [n for n in P.res]
        for i in range(NT):
            P.dma("sp", lambda e, i=i: e.dma_start(out=xfin[:, :], in_=x_d[i * 128:(i + 1) * 128, :]), "xinF", reads=[], writes=rfin if i == 0 else ["xfin"])
            P.dma("sp", lambda e, i=i: e.dma_start(out=out_d[i * 128:(i + 1) * 128, :], in_=xfin[:, :]), "xout", reads=["xfin"])
    has_dbg = any(o.kind == "d" and o.group == "dbg" for o in P.ops)
    P.emit(final_wait_groups=["xout"] + (["dbg"] if has_dbg else []))
    return nc, dbg


def _body(nc, P, A, debug, stage, dout, dbg, din, x_d, cvec_d, wmod_d, bmod_d, n1w_d, n2w_d, win_d, segf_d, ctx_d, ldv_d, msk_d, out_d):

    ident_f = A.alloc("ident_f", [128, 128], F32)
    ident_b = A.alloc("ident_b", [128, 128], BF16)
    ones_f = A.alloc("ones_f", [128, 128], F32)
    P.c("pool", lambda e: e.memset(ones_f[:, :], 1.0), writes=["ones_f"])
    negpi = A.alloc("negpi", [128, 1], F32)
    epst = A.alloc("epst", [128, 1], F32)
    P.c("pool", lambda e: e.memset(negpi[:, :], -float(np.pi)), writes=["negpi"])
    P.c("pool", lambda e: e.memset(epst[:, :], 1e-6), writes=["epst"])
    P.c("pool", lambda e: e.memset(ident_f[:, :], 0.0), writes=["ident_f"])
    P.c("pool", lambda e: e.affine_select(out=ident_f[:, :], in_=ones_f[:, :], pattern=[[1, 128]],
                                          compare_op=ALU.is_equal, fill=0.0, base=0, channel_multiplier=-1),
        reads=["ones_f"], writes=["ident_f"])
    P.c("dve", lambda e: e.tensor_copy(out=ident_b[:, :], in_=ident_f[:, :]), reads=["ident_f"], writes=["ident_b"])

    cvec = A.alloc("cvec", [128, 16], F32)
    s_bf = A.alloc("s_bf", [128, 16], BF16)
    n1w = A.alloc("n1w", [128, 8], F32)
    n2w = A.alloc("n2w", [128, 8], F32)
    modT = A.alloc("modT", [128, 32, 2], F32)
    g1x = A.alloc("g1x", [128, 8], F32)
    g1c = A.alloc("g1c", [128, 8], F32)
    g2x = A.alloc("g2x", [128, 8], F32)
    bc2 = A.alloc("bc2", [128, D], F32)
    bc5 = A.alloc("bc5", [128, D], F32)
    segf = A.alloc("segf", [128, 1], F32)
    colf = A.alloc("colf", [128, 1], F32)
    rowb = A.alloc("rowb", [128, 1], F32)
    rowv = A.alloc("rowv", [128, 16], F32)
    invf = A.alloc("invf", [128, 32], F32)
    big0 = A.off
    wm = A.alloc("wm", [128, 8, 2048], BF16)
    bmod = A.alloc("bmod", [2, 6 * D], F32)
    modrow = A.alloc("modrow", [2, 6 * D], F32)
    ang = A.alloc("ang", [128, 16, 64], F32)
    tq = A.alloc("tq", [128, 1024], F32)
    ti = A.alloc("ti", [128, 1024], I32)
    tf = A.alloc("tf", [128, 1024], F32)
    assert A.off - big0 == 96 * 1024, A.off - big0
    r2_base = A.off
    scratchB_off = A.off
    A.off += 37 * 1024 + 512
    P.dma("sp", lambda e: e.dma_start(out=cvec[:, :], in_=cvec_d[:, :]), "small", writes=["cvec"])
    P.dma("sp", lambda e: e.dma_start(out=bmod[:, :], in_=bmod_d[:, :]), "small2", writes=["bmod"])
    P.dma("sp", lambda e: e.dma_start(out=n1w[:, :], in_=n1w_d[:, :]), "small3", writes=["n1w"])
    P.dma("sp", lambda e: e.dma_start(out=n2w[:, :], in_=n2w_d[:, :]), "small4", writes=["n2w"])
    P.c("act", lambda e: e.activation(out=s_bf[:, :], in_=cvec[:, :], func=AF.Silu), reads=["cvec"], writes=["s_bf"])

    wm_src = wmod_d.rearrange("(k p) n -> p k n", p=128)
    psA = [nc.alloc_psum_tensor(f"ps{i}", [128, 512], F32) for i in range(7)]
    s3 = s_bf[:, :].rearrange("p (k w) -> p k w", w=2)
    wmB = nc.alloc_sbuf_tensor_at("wmB", [128, 8, 2048], BF16, offset=scratchB_off)
    wms = [(wm, "wm"), (wmB, "wmB"), (wm, "wm")]
    def wm_dma(cb):
        wt_, wn_ = wms[cb]
        P.dma("pool", lambda e: e.dma_start(out=wt_[:, :, :], in_=wm_src[:, :, cb * 2048:(cb + 1) * 2048]), f"wmd{cb}", writes=[wn_])

    wm_dma(0)
    wm_dma(1)
    for nb in range(12):
        ps = psA[nb % 2]
        wt_, wn_ = wms[nb // 4]
        if nb == 4:
            wm_dma(2)
        for k in range(8):
            P.c("pe", lambda e, ps=ps, k=k, nb=nb, wt_=wt_: e.matmul(ps[0:2, :], lhsT=s3[:, k, :], rhs=wt_[:, k, (nb % 4) * 512:(nb % 4 + 1) * 512],
                                                                  start=(k == 0), stop=(k == 7)),
                reads=["s_bf", wn_], writes=[f"ps{nb % 2}"])
        P.c("dve", lambda e, ps=ps, nb=nb: e.tensor_tensor(out=modrow[:, nb * 512:(nb + 1) * 512], in0=ps[0:2, :],
                                                         in1=bmod[:, nb * 512:(nb + 1) * 512], op=ALU.add),
            reads=[f"ps{nb % 2}", "bmod"], writes=["modrow"])
    psT = psA[2]
    chunks = list(range(0, 16)) + list(range(24, 40))
    for j, ch in enumerate(chunks):
        P.c("pe", lambda e, j=j, ch=ch: e.transpose(psT[:, 2 * j:2 * j + 2], modrow[:, ch * 128:(ch + 1) * 128], ident_f[0:2, 0:2]),
            reads=["modrow", "ident_f"], writes=["ps2"])
    P.c("dve", lambda e: e.tensor_copy(out=modT[:, :, :].rearrange("p a b -> p (a b)"), in_=psT[:, 0:64]),
        reads=["ps2"], writes=["modT"])
    P.c("dve", lambda e: e.scalar_tensor_tensor(out=g1x[:, :], in0=modT[:, 8:16, 0], scalar=1.0, in1=n1w[:, :],
                                                op0=ALU.add, op1=ALU.mult), reads=["modT", "n1w"], writes=["g1x"])
    P.c("dve", lambda e: e.scalar_tensor_tensor(out=g1c[:, :], in0=modT[:, 8:16, 1], scalar=1.0, in1=n1w[:, :],
                                                op0=ALU.add, op1=ALU.mult), reads=["modT", "n1w"], writes=["g1c"])
    P.c("dve", lambda e: e.scalar_tensor_tensor(out=g2x[:, :], in0=modT[:, 24:32, 0], scalar=1.0, in1=n2w[:, :],
                                                op0=ALU.add, op1=ALU.mult), reads=["modT", "n2w"], writes=["g2x"])

    if debug:
        d_mod = dout("d_mod", [2, 6 * D])
        P.dma("sp", lambda e: e.dma_start(out=d_mod[:, :], in_=modrow[:, :]), "dbg", reads=["modrow"])
        d_g1x = dout("d_g1x", [128, 8])
        P.dma("sp", lambda e: e.dma_start(out=d_g1x[:, :], in_=g1x[:, :]), "dbg", reads=["g1x"])

    stage("A")
    for (bc, base, nm) in ((bc2, 2048, "bc2"), (bc5, 5120, "bc5")):
        for hh in range(2):
            ps = psA[3 + hh]
            P.c("pe", lambda e, ps=ps, base=base, hh=hh: e.matmul(ps[:, :], lhsT=ones_f[0:1, :], rhs=modrow[0:1, base + hh * 512: base + (hh + 1) * 512],
                                                                start=True, stop=True), reads=["ones_f", "modrow"], writes=[f"ps{3 + hh}"])
            P.c("act", lambda e, ps=ps, bc=bc, hh=hh: e.activation(out=bc[:, hh * 512:(hh + 1) * 512], in_=ps[:, :], func=AF.Copy),
                reads=[f"ps{3 + hh}"], writes=[nm])

    P.dma("sp", lambda e: e.dma_start(out=segf[:, :], in_=segf_d[:, :]), "small5", writes=["segf"])
    for hb in range(2):
        P.c("pool", lambda e, hb=hb: e.iota(colf[hb * 64:(hb + 1) * 64, :], pattern=[[0, 1]], base=0, channel_multiplier=1,
                                           allow_small_or_imprecise_dtypes=True), writes=["colf"])
        P.c("pool", lambda e, hb=hb: e.memset(rowb[hb * 64:(hb + 1) * 64, :], float(hb)), writes=["rowb"])
    P.c("pool", lambda e: e.iota(rowv[:, :], pattern=[[2, 16]], base=0, channel_multiplier=0, allow_small_or_imprecise_dtypes=True),
        writes=["rowv"])
    for j in range(32):
        P.c("pool", lambda e, j=j: e.memset(invf[:, j:j + 1], float(np.float32(10000.0) ** (-np.float32(j) / np.float32(32)))),
            writes=["invf"])
    P.c("dve", lambda e: e.scalar_tensor_tensor(out=rowb[:, :], in0=segf[:, :], scalar=32.0, in1=rowb[:, :], op0=ALU.mult, op1=ALU.add),
        reads=["segf", "rowb"], writes=["rowb"])
    P.c("dve", lambda e: e.tensor_scalar(out=rowv[:, :], in0=rowv[:, :], scalar1=rowb[:, 0:1], scalar2=None, op0=ALU.add),
        reads=["rowv", "rowb"], writes=["rowv"])
    ropeC = A.alloc("ropeC", [128, 16, 64], F32)
    ropeS = A.alloc("ropeS", [128, 16, 64], F32)
    ropeCk = A.alloc("ropeCk", [128, 16, 64], F32)
    ropeSk = A.alloc("ropeSk", [128, 16, 64], F32)
    for i in range(16):
        P.c("dve", lambda e, i=i: e.tensor_scalar(out=ang[:, i, 0:32], in0=invf[:, :], scalar1=rowv[:, i:i + 1], scalar2=None, op0=ALU.mult),
            reads=["invf", "rowv"], writes=["ang"])
        P.c("dve", lambda e, i=i: e.tensor_scalar(out=ang[:, i, 32:64], in0=invf[:, :], scalar1=colf[:, 0:1], scalar2=None, op0=ALU.mult),
            reads=["invf", "colf"], writes=["ang"])
    angf = ang[:, :, :].rearrange("p a b -> p (a b)")
    TWO_PI = 2.0 * np.pi
    for (dst, off, nm) in ((ropeS, 0.5, "ropeS"), (ropeC, 0.75, "ropeC")):
        dflat = dst[:, :, :].rearrange("p a b -> p (a b)")
        P.c("dve", lambda e, off=off: e.tensor_scalar(out=tq[:, :], in0=angf, scalar1=1.0 / TWO_PI, scalar2=off, op0=ALU.mult, op1=ALU.add),
            reads=["ang"], writes=["tq"])
        P.c("dve", lambda e: e.tensor_copy(out=ti[:, :], in_=tq[:, :]), reads=["tq"], writes=["ti"])
        P.c("dve", lambda e: e.tensor_copy(out=tf[:, :], in_=ti[:, :]), reads=["ti"], writes=["tf"])
        P.c("dve", lambda e: e.tensor_tensor(out=tq[:, :], in0=tq[:, :], in1=tf[:, :], op=ALU.subtract), reads=["tq", "tf"], writes=["tq"])
        P.c("dve", lambda e: e.tensor_scalar(out=tf[:, :], in0=tq[:, :], scalar1=0.0, scalar2=None, op0=ALU.is_lt), reads=["tq"], writes=["tf"])
        P.c("dve", lambda e: e.tensor_tensor(out=tq[:, :], in0=tq[:, :], in1=tf[:, :], op=ALU.add), reads=["tq", "tf"], writes=["tq"])
        P.c("act", lambda e, dflat=dflat: e.activation(out=dflat, in_=tq[:, :], func=AF.Sin, scale=TWO_PI, bias=negpi[:, 0:1]),
            reads=["tq", "negpi"], writes=[nm])
    KS = float(128.0 ** -0.5)
    P.c("dve", lambda e: e.tensor_scalar(out=ropeCk[:, :, :], in0=ropeC[:, :, :], scalar1=KS, scalar2=None, op0=ALU.mult), reads=["ropeC"], writes=["ropeCk"])
    P.c("dve", lambda e: e.tensor_scalar(out=ropeSk[:, :, :], in0=ropeS[:, :, :], scalar1=KS, scalar2=None, op0=ALU.mult), reads=["ropeS"], writes=["ropeSk"])

    win = A.alloc("win", [128, 8, 2560], BF16)
    win_src = win_d.rearrange("(k p) n -> p k n", p=128)
    for cb in range(2):
        P.dma("pool", lambda e, cb=cb: e.dma_start(out=win[:, :, cb * 1280:(cb + 1) * 1280], in_=win_src[:, :, cb * 1280:(cb + 1) * 1280]),
              f"win{cb}", writes=["win"])

    A2 = Arena(nc)
    A2.off = big0
    qT = A2.alloc("qT", [128, 4, T], BF16)
    kT = A2.alloc("kT", [128, 4, T], BF16)
    k_tm = A2.alloc("k_tm", [128, NT, 512], BF16)
    v_tm = A2.alloc("v_tm", [128, NT, 512], BF16)
    gate = A2.alloc("gate", [128, NT, 512], BF16)
    uT = A2.alloc("uT", [128, 4, T], BF16)
    assert A2.off <= big0 + 96 * 1024
    stage_a_res = [P._r(n) for n in ("wm", "modrow", "bmod", "ang", "tq", "ti", "tf")]
    for nm in ("qT", "kT", "k_tm", "v_tm", "gate", "uT"):
        r_ = P._r(nm)
        for o in stage_a_res:
            r_.readers.extend(o.readers)
            if o.last_w is not None:
                r_.readers.append(o.last_w)

    r2_end = A.off
    SBA = Arena(nc); SBA.off = scratchB_off
    xts = [SBA.alloc(f"xt{i}", [128, D], F32) for i in range(2)]
    xsl = [SBA.alloc(f"xs{i}", [128, D], F32) for i in range(2)]
    ssl = [SBA.alloc(f"ss{i}", [128, 4], F32) for i in range(2)]
    hxT = [SBA.alloc(f"hxT{i}", [128, 8, 512], BF16) for i in range(2)]
    qrot = SBA.alloc("qrot", [128, 512], BF16)
    t1 = SBA.alloc("t1", [128, 256], F32)
    t2 = SBA.alloc("t2", [128, 256], F32)
    assert SBA.off <= scratchB_off + 37 * 1024 + 512, SBA.off - scratchB_off
    for nm_ in ("xt0", "xt1", "xs0", "xs1", "ss0a", "ss0b", "ss0c", "ss1a", "ss1b", "ss1c", "hxT0", "hxT1", "qrot", "t1", "t2"):
        r_ = P._r(nm_)
        o = P._r("wmB")
        r_.readers.extend(o.readers)
        if o.last_w is not None:
            r_.readers.append(o.last_w)
    psX = [psA[0], psA[1]]
    psP = [psA[2], psA[3], psA[4], psA[5]]
    psU = psA[6]
    psTq = nc.alloc_psum_tensor("psTq", [128, 1024], BF16)

    def norm_front(src_ap, pp):
        xt, xs_, s_ = xts[pp], xsl[pp], ssl[pp]
        P.dma("sp", lambda e: e.dma_start(out=xt[:, :], in_=src_ap), f"xin{pp}", writes=[f"xt{pp}"])
        P.c("act", lambda e: e.activation(out=xs_[:, :], in_=xt[:, :], func=AF.Square, accum_out=s_[:, 0:1]),
            reads=[f"xt{pp}"], writes=[f"xs{pp}", f"ss{pp}a"])
        P.c("act", lambda e: e.activation(out=s_[:, 1:2], in_=s_[:, 0:1], func=AF.Sqrt, scale=1.0 / D, bias=epst[:, 0:1]),
            reads=[f"ss{pp}a", "epst"], writes=[f"ss{pp}b"])
        P.c("dve", lambda e: e.reciprocal(out=s_[:, 2:3], in_=s_[:, 1:2]), reads=[f"ss{pp}b"], writes=[f"ss{pp}c"])
        P.c("dve", lambda e: e.tensor_scalar(out=xs_[:, :], in0=xt[:, :], scalar1=s_[:, 2:3], scalar2=None, op0=ALU.mult),
            reads=[f"xt{pp}", f"ss{pp}c"], writes=[f"xs{pp}"])

    def norm_back(pp, gvec, shvec_fn, dstT, col0, tag):
        xs_ = xsl[pp]
        for k in range(8):
            ps = psX[k // 4]
            P.c("pe", lambda e, ps=ps, k=k: e.transpose(ps[:, (k % 4) * 128:(k % 4 + 1) * 128], xs_[:, k * 128:(k + 1) * 128], ident_f[:, :]),
                reads=[f"xs{pp}", "ident_f"], writes=[f"ps{k // 4}"])
        for k in range(8):
            ps = psX[k // 4]
            P.c("act", lambda e, ps=ps, k=k: e.activation(out=dstT[:, k, col0:col0 + 128], in_=ps[:, (k % 4) * 128:(k % 4 + 1) * 128],
                                                        func=AF.Identity, scale=gvec[:, k:k + 1], bias=shvec_fn(k)),
                reads=[f"ps{k // 4}", "g1x", "g1c", "g2x", "modT"], writes=[tag])

    def rope(ps, Ct, St, i, dst_ap):
        pv = ps[:, :].rearrange("p (h j w) -> p h j w", h=4, w=2)
        dv = dst_ap.rearrange("p (h j w) -> p h j w", h=4, w=2)
        Cb = Ct[:, i, :].unsqueeze(1).broadcast_to([128, 4, 64])
        Sb = St[:, i, :].unsqueeze(1).broadcast_to([128, 4, 64])
        a = t1[:, :].rearrange("p (h j) -> p h j", h=4)
        b = t2[:, :].rearrange("p (h j) -> p h j", h=4)
        return pv, dv, Cb, Sb, a, b

    norm_front(x_d[0:128, :], 0)
    for grp in range(4):
        hT = hxT[grp % 2]
        for ti_ in range(4):
            i = grp * 4 + ti_
            if i + 1 < NT:
                norm_front(x_d[(i + 1) * 128:(i + 2) * 128, :], (i + 1) % 2)
            norm_back(i % 2, g1x, lambda k: modT[:, k, 0:1], hT, ti_ * 128, f"hxT{grp % 2}")
            for nb in range(4):
                ps = psP[nb]
                for k in range(8):
                    P.c("pe", lambda e, ps=ps, k=k, nb=nb, hT=hT, ti_=ti_: e.matmul(
                        ps[:, :], lhsT=hT[:, k, ti_ * 128:(ti_ + 1) * 128], rhs=win[:, k, 512 + nb * 512: 1024 + nb * 512],
                        start=(k == 0), stop=(k == 7)), reads=[f"hxT{grp % 2}", "win"], writes=[f"ps{2 + nb}"])
            for (nb, Ct, St, dstT, nm) in ((0, ropeC, ropeS, qT, "q"), (1, ropeCk, ropeSk, kT, "k")):
                ps = psP[nb]
                dst_ap = qrot[:, :] if nb == 0 else k_tm[:, i, :]
                dres = "qrot" if nb == 0 else "k_tm"
                pv, dv, Cb, Sb, a, b = rope(ps, Ct, St, i, dst_ap)
                rd = [f"ps{2 + nb}", "ropeC", "ropeS", "ropeCk", "ropeSk"]
                P.c("dve", lambda e, pv=pv, Cb=Cb, a=a: e.tensor_tensor(out=a, in0=pv[:, :, :, 0], in1=Cb, op=ALU.mult), reads=rd, writes=["t1"])
                P.c("dve", lambda e, pv=pv, Sb=Sb, b=b: e.tensor_tensor(out=b, in0=pv[:, :, :, 1], in1=Sb, op=ALU.mult), reads=rd, writes=["t2"])
                P.c("dve", lambda e, dv=dv, a=a, b=b: e.tensor_tensor(out=dv[:, :, :, 0], in0=a, in1=b, op=ALU.subtract),
                    reads=["t1", "t2"], writes=[dres])
                P.c("dve", lambda e, pv=pv, Sb=Sb, a=a: e.tensor_tensor(out=a, in0=pv[:, :, :, 0], in1=Sb, op=ALU.mult), reads=rd + [dres], writes=["t1"])
                P.c("dve", lambda e, pv=pv, Cb=Cb, b=b: e.tensor_tensor(out=b, in0=pv[:, :, :, 1], in1=Cb, op=ALU.mult), reads=rd + [dres], writes=["t2"])
                P.c("dve", lambda e, dv=dv, a=a, b=b: e.tensor_tensor(out=dv[:, :, :, 1], in0=a, in1=b, op=ALU.add),
                    reads=["t1", "t2"], writes=[dres])
                for h in range(4):
                    P.c("pe", lambda e, h=h, dst_ap=dst_ap: e.transpose(psTq[:, h * 128:(h + 1) * 128], dst_ap[:, h * 128:(h + 1) * 128], ident_b[:, :]),
                        reads=[dres, "ident_b"], writes=["psTq"])
                P.c("act", lambda e, dstT=dstT, i=i: e.activation(out=dstT[:, :, i * 128:(i + 1) * 128],
                                                                 in_=psTq[:, 0:512].rearrange("p (h t) -> p h t", h=4), func=AF.Copy),
                    reads=["psTq"], writes=[nm + "T"])
            P.c("act", lambda e, i=i: e.activation(out=v_tm[:, i, :], in_=psP[2][:, :], func=AF.Copy), reads=["ps4"], writes=["v_tm"])
            P.c("act", lambda e, i=i: e.activation(out=gate[:, i, :], in_=psP[3][:, :], func=AF.Silu), reads=["ps5"], writes=["gate"])
        for fc in range(4):
            for k in range(8):
                P.c("pe", lambda e, fc=fc, k=k, hT=hT: e.matmul(psU[:, :], lhsT=win[:, k, fc * 128:(fc + 1) * 128], rhs=hT[:, k, :],
                                                              start=(k == 0), stop=(k == 7)), reads=[f"hxT{grp % 2}", "win"], writes=["ps6"])
            P.c("dve", lambda e, fc=fc, grp=grp: e.tensor_copy(out=uT[:, fc, grp * 512:(grp + 1) * 512], in_=psU[:, :]),
                reads=["ps6"], writes=["uT"])

    stage("B")

    def alias(new_names, old_names):
        olds = [P._r(n) for n in old_names]
        for nm in new_names:
            r_ = P._r(nm)
            for o in olds:
                r_.readers.extend(o.readers)
                if o.last_w is not None:
                    r_.readers.append(o.last_w)

    spare_off = A.off
    kc_tm = A.alloc("kc_tm", [128, 2, 512], BF16)
    vc_tm = A.alloc("vc_tm", [128, 2, 512], BF16)
    ucT = A.alloc("ucT", [128, 4, 256], BF16)
    KSC = float(128.0 ** -0.5)
    hT = hxT[0]
    for j in range(2):
        norm_front(ctx_d[j * 128:(j + 1) * 128, :], j)
        norm_back(j, g1c, lambda k: modT[:, k, 1:2], hT, j * 128, "hxT0")
        for (nb, dst, nm) in ((1, kc_tm, "kc_tm"), (2, vc_tm, "vc_tm")):
            ps = psP[nb]
            for k in range(8):
                P.c("pe", lambda e, ps=ps, k=k, nb=nb, j=j: e.matmul(ps[:, :], lhsT=hT[:, k, j * 128:(j + 1) * 128],
                                                                  rhs=win[:, k, 512 + nb * 512: 1024 + nb * 512], start=(k == 0), stop=(k == 7)),
                    reads=["hxT0", "win"], writes=[f"ps{2 + nb}"])
            P.c("act", lambda e, ps=ps, dst=dst, j=j, nb=nb: e.activation(out=dst[:, j, :], in_=ps[:, :], func=AF.Copy, scale=(KSC if nb == 1 else 1.0)),
                reads=[f"ps{2 + nb}"], writes=[nm])
    for fc in range(4):
        for k in range(8):
            P.c("pe", lambda e, fc=fc, k=k: e.matmul(psU[:, 0:256], lhsT=win[:, k, fc * 128:(fc + 1) * 128], rhs=hT[:, k, 0:256],
                                                   start=(k == 0), stop=(k == 7)), reads=["hxT0", "win"], writes=["ps6"])
        P.c("dve", lambda e, fc=fc: e.tensor_copy(out=ucT[:, fc, :], in_=psU[:, 0:256]), reads=["ps6"], writes=["ucT"])

    stage("C")
    ldv = A.alloc("ldv", [128, 8], F32)
    msk = A.alloc("msk", [128, 8], F32)
    P.dma("sp", lambda e: e.dma_start(out=ldv[:, :], in_=ldv_d[:, :]), "small6", writes=["ldv"])
    P.dma("sp", lambda e: e.dma_start(out=msk[:, :], in_=msk_d[:, :]), "small7", writes=["msk"])
    AR = Arena(nc)
    AR.off = r2_base
    old_r2 = ["ropeC", "ropeS", "ropeCk", "ropeSk", "win", "xt0", "xt1", "xs0", "xs1", "ss0a", "ss0b", "ss0c", "ss1a", "ss1b", "ss1c", "hxT0", "hxT1", "qrot", "t1", "t2",
              "invf", "rowv", "colf", "rowb"]
    mixT = AR.alloc("mixT", [128, 8, T], BF16)
    Rb_store = AR.alloc("Rb_store", [128, NT, 512], BF16)
    xg_off = AR.off
    xg = AR.alloc("xg", [128, 4, 1024], F32)
    Dm = AR.alloc("Dm", [128, 4, 128], F32)
    XiF = AR.alloc("XiF", [128, 4, 128], F32)
    XiB = AR.alloc("XiB", [128, 4, 128], F32)
    dm = AR.alloc("dm", [128, 128], F32)
    tabs_off = AR.off
    dpos = AR.alloc("dpos", [128, 128], F32)
    dneg = AR.alloc("dneg", [128, 128], F32)
    mge = AR.alloc("mge", [128, 128], F32)
    mlt = AR.alloc("mlt", [128, 128], F32)
    e1 = AR.alloc("e1", [128, 128], F32)
    e2 = AR.alloc("e2", [128, 128], F32)
    tfree = AR.alloc("tfree", [128, 128], F32)
    pidx = AR.alloc("pidx", [128, 2], F32)
    zarg = AR.alloc("zarg", [128, 8], F32)
    Zeta = AR.alloc("Zeta", [128, 8], F32)
    G128 = AR.alloc("G128", [128, 8], F32)
    G2048 = AR.alloc("G2048", [128, 8], F32)
    nld = AR.alloc("nld", [128, 8], F32)
    ld128 = AR.alloc("ld128", [128, 8], F32)
    Rf = AR.alloc("Rf", [128, 4, 128], F32)
    Rb = AR.alloc("Rb", [128, 4, 128], F32)
    Rcf_off = AR.off
    Rcf = AR.alloc("Rcf", [128, 4, 128], F32)
    Rcb_off = AR.off
    Rcb = AR.alloc("Rcb", [128, 4, 128], F32)
    Rfbf_off = AR.off
    Rf_bf = AR.alloc("Rf_bf", [128, 4, 128], BF16)
    vzs = [AR.alloc(f"vz{i}", [128, 4, 128], BF16) for i in range(2)]
    scm = AR.alloc("scm", [128, 4, 128], BF16)
    qf = AR.alloc("qf", [128, 4, 128], BF16)
    qb = AR.alloc("qb", [128, 4, 128], BF16)
    yn = AR.alloc("yn", [128, 4, 128], F32)
    hacc_t = yn
    retx = AR.alloc("retx", [128, 512], BF16)
    bst = AR.alloc("bst", [128, 4, 6], F32)
    junk2 = AR.alloc("junk2", [128, 128], BF16)
    mv = AR.alloc("mv", [128, 4, 2], F32)
    rs = AR.alloc("rs", [128, 8], F32)
    assert AR.off <= r2_end, (AR.off, r2_end)
    new_r2 = ["mixT", "Rb_store", "xg", "Dm", "XiF", "XiB", "dm", "dpos", "dneg", "mge", "mlt", "e1", "e2", "tfree", "pidx", "zarg", "Zeta",
              "G128", "G2048", "nld", "ld128", "Rf", "Rb", "Rcf", "Rcb", "Rf_bf", "hacc_t", "vz0", "vz1", "scm", "qf", "qb", "yn", "retx", "bst", "bst2", "bst3", "junk2", "mv", "rs"]
    alias(new_r2, old_r2)

    P.c("pool", lambda e: e.iota(pidx[:, 0:1], pattern=[[0, 1]], base=0, channel_multiplier=1, allow_small_or_imprecise_dtypes=True), writes=["pidx"])
    P.c("pool", lambda e: e.iota(tfree[:, :], pattern=[[1, 128]], base=0, channel_multiplier=0, allow_small_or_imprecise_dtypes=True), writes=["tfree"])
    P.c("pool", lambda e: e.iota(dm[:, :], pattern=[[1, 128]], base=0, channel_multiplier=-1, allow_small_or_imprecise_dtypes=True), writes=["dm"])
    P.c("dve", lambda e: e.tensor_scalar(out=pidx[:, 1:2], in0=pidx[:, 0:1], scalar1=-1.0, scalar2=127.0, op0=ALU.mult, op1=ALU.add),
        reads=["pidx"], writes=["pidx"])
    P.c("dve", lambda e: e.tensor_scalar(out=zarg[:, 0:4], in0=ldv[:, 0:4], scalar1=pidx[:, 1:2], scalar2=None, op0=ALU.mult), reads=["ldv", "pidx"], writes=["zarg"])
    P.c("dve", lambda e: e.tensor_scalar(out=zarg[:, 4:8], in0=ldv[:, 4:8], scalar1=pidx[:, 0:1], scalar2=None, op0=ALU.mult), reads=["ldv", "pidx"], writes=["zarg"])
    P.c("act", lambda e: e.activation(out=Zeta[:, :], in_=zarg[:, :], func=AF.Exp), reads=["zarg"], writes=["Zeta"])
    P.c("act", lambda e: e.activation(out=G128[:, :], in_=ldv[:, :], func=AF.Exp, scale=128.0), reads=["ldv"], writes=["G128"])
    P.c("act", lambda e: e.activation(out=G2048[:, :], in_=ldv[:, :], func=AF.Exp, scale=2048.0), reads=["ldv"], writes=["G2048"])
    P.c("dve", lambda e: e.tensor_scalar(out=nld[:, :], in0=ldv[:, :], scalar1=-1.0, scalar2=None, op0=ALU.mult), reads=["ldv"], writes=["nld"])
    P.c("dve", lambda e: e.tensor_scalar(out=ld128[:, :], in0=ldv[:, :], scalar1=128.0, scalar2=None, op0=ALU.mult), reads=["ldv"], writes=["ld128"])
    P.c("dve", lambda e: e.tensor_scalar(out=dpos[:, :], in0=dm[:, :], scalar1=0.0, scalar2=None, op0=ALU.max), reads=["dm"], writes=["dpos"])
    P.c("dve", lambda e: e.tensor_scalar(out=dneg[:, :], in0=dm[:, :], scalar1=-1.0, scalar2=0.0, op0=ALU.mult, op1=ALU.max), reads=["dm"], writes=["dneg"])
    P.c("dve", lambda e: e.tensor_scalar(out=mge[:, :], in0=dm[:, :], scalar1=0.0, scalar2=None, op0=ALU.is_ge), reads=["dm"], writes=["mge"])
    P.c("dve", lambda e: e.tensor_scalar(out=mlt[:, :], in0=dm[:, :], scalar1=0.0, scalar2=None, op0=ALU.is_lt), reads=["dm"], writes=["mlt"])
    for h in range(4):
        P.c("act", lambda e, h=h: e.activation(out=XiF[:, h, :], in_=tfree[:, :], func=AF.Exp, scale=ldv[:, h:h + 1], bias=ldv[:, h:h + 1]),
            reads=["tfree", "ldv"], writes=["XiF"])
        P.c("act", lambda e, h=h: e.activation(out=XiB[:, h, :], in_=tfree[:, :], func=AF.Exp, scale=nld[:, 4 + h:5 + h], bias=ld128[:, 4 + h:5 + h]),
            reads=["tfree", "nld", "ld128"], writes=["XiB"])
        P.c("act", lambda e, h=h: e.activation(out=e1[:, :], in_=dpos[:, :], func=AF.Exp, scale=ldv[:, h:h + 1]), reads=["dpos", "ldv"], writes=["e1"])
        P.c("act", lambda e, h=h: e.activation(out=e2[:, :], in_=dneg[:, :], func=AF.Exp, scale=ldv[:, 4 + h:5 + h]), reads=["dneg", "ldv"], writes=["e2"])
        P.c("dve", lambda e: e.tensor_tensor(out=e1[:, :], in0=e1[:, :], in1=mge[:, :], op=ALU.mult), reads=["e1", "mge"], writes=["e1"])
        P.c("dve", lambda e: e.tensor_tensor(out=e2[:, :], in0=e2[:, :], in1=mlt[:, :], op=ALU.mult), reads=["e2", "mlt"], writes=["e2"])
        P.c("dve", lambda e, h=h: e.tensor_tensor(out=Dm[:, h, :], in0=e1[:, :], in1=e2[:, :], op=ALU.add), reads=["e1", "e2"], writes=["Dm"])

    stage("Dtab")
    psS, psO, psKV = psA[0], psA[1], psA[2]

    vz_extra = [nc.alloc_sbuf_tensor_at("vz2", [128, 4, 128], BF16, offset=tabs_off), nc.alloc_sbuf_tensor_at("vz3", [128, 4, 128], BF16, offset=tabs_off + 1024)]
    vz_tab = [[vzs[0], vz_extra[0]], [vzs[1], vz_extra[1]]]
    vz_nm = [["vz0", "vz2"], ["vz1", "vz3"]]
    kv_bank = [[(psA[2], "ps2"), (psA[4], "ps4")], [(psA[3], "ps3"), (psA[5], "ps5")]]

    def kv_front(ksrc, vsrc, d, kres, vres, lane, par):
        zb = Zeta[:, d * 4:(d + 1) * 4].unsqueeze(2).broadcast_to([128, 4, 128])
        vz_, vzn = vz_tab[lane % 2][par], vz_nm[lane % 2][par]
        pk_, pkn = kv_bank[lane % 2][par]
        if lane % 2 == 0:
            for h in range(4):
                P.c("act", lambda e, h=h: e.activation(out=vz_[:, h, :], in_=vsrc[:, h * 128:(h + 1) * 128], func=AF.Identity, scale=Zeta[:, d * 4 + h:d * 4 + h + 1]),
                    reads=[vres, "Zeta"], writes=[vzn])
        else:
            P.c("pool", lambda e: e.tensor_tensor(out=vz_[:, :, :], in0=vsrc.rearrange("p (h e) -> p h e", h=4), in1=zb, op=ALU.mult),
                reads=[vres, "Zeta"], writes=[vzn])
        for h in range(4):
            P.c("pe", lambda e, h=h: e.matmul(pk_[:, h * 128:(h + 1) * 128], lhsT=ksrc[:, h * 128:(h + 1) * 128], rhs=vz_[:, h, :], start=True, stop=True),
                reads=[kres, vzn], writes=[pkn])

    def kv_back(Rst, rname, d, lane, par):
        pk_, pkn = kv_bank[lane % 2][par]
        for h in range(4):
            P.c("dve", lambda e, h=h: e.scalar_tensor_tensor(out=Rst[:, h, :], in0=Rst[:, h, :], scalar=G128[:, d * 4 + h:d * 4 + h + 1],
                                                            in1=pk_[:, h * 128:(h + 1) * 128], op0=ALU.mult, op1=ALU.add),
                reads=[rname, "G128", pkn], writes=[rname])

    def kv_step(Rst, rname, ksrc, vsrc, d, kres, vres, lane=0):
        kv_front(ksrc, vsrc, d, kres, vres, lane, 0)
        kv_back(Rst, rname, d, lane, 0)

    def zero(t_, nm):
        P.c("pool", lambda e: e.memset(t_[:, :, :], 0.0), writes=[nm])

    alias(["vz2", "vz3"], ["dpos", "dneg", "mge", "mlt", "e1", "e2"])
    zero(Rf, "Rf"); zero(Rb, "Rb"); zero(Rcf, "Rcf"); zero(Rcb, "Rcb")
    def fronts_A(n_):
        kv_front(k_tm[:, n_, :], v_tm[:, n_, :], 0, "k_tm", "v_tm", 0, n_ % 2)
        kv_front(k_tm[:, NT - 1 - n_, :], v_tm[:, NT - 1 - n_, :], 1, "k_tm", "v_tm", 1, n_ % 2)

    fronts_A(0)
    for n_ in range(NT):
        if n_ + 1 < NT:
            fronts_A(n_ + 1)
        kv_back(Rf, "Rf", 0, 0, n_ % 2)
        kv_back(Rb, "Rb", 1, 1, n_ % 2)
    for n_ in range(2):
        kv_step(Rcf, "Rcf", kc_tm[:, n_, :], vc_tm[:, n_, :], 0, "kc_tm", "vc_tm", 0)
        kv_step(Rcb, "Rcb", kc_tm[:, 1 - n_, :], vc_tm[:, 1 - n_, :], 1, "kc_tm", "vc_tm", 1)

    stage("DpassA")
    xa_in = nc.dram_tensor("xa_in", [128, 1024], F32)
    xa_out = nc.dram_tensor("xa_out", [512, 1024], F32)
    P.dma("sp", lambda e: e.dma_start(out=xa_in[:, 0:512], in_=Rf[:, :, :].rearrange("p h e -> p (h e)")), "xa_st", reads=["Rf"], writes=["xa_in"])
    P.dma("sp", lambda e: e.dma_start(out=xa_in[:, 512:1024], in_=Rb[:, :, :].rearrange("p h e -> p (h e)")), "xa_st", reads=["Rb"], writes=["xa_in"])
    P.dma("pool", lambda e: e.collective_compute("AllGather", ALU.bypass, replica_groups=[[0, 1, 2, 3], [4, 5, 6, 7]],
                                                  ins=[xa_in.ap().opt()], outs=[xa_out.ap().opt()]),
          "ccA", reads=["xa_in"], writes=["xa_out"], inc=1)
    P.dma("sp", lambda e: e.dma_start(out=xg[:, :, :], in_=xa_out.ap().rearrange("(r p) n -> p r n", p=128)), "xa_ld", reads=["xa_out"], writes=["xg"])

    stage("Dxchg")

    def horner(acc, aname, d, order, mcol0):
        gb = G2048[:, d * 4:(d + 1) * 4].unsqueeze(2).broadcast_to([128, 4, 128])
        for n_, i in enumerate(order):
            P.c("dve", lambda e: e.tensor_tensor(out=hacc_t[:, :, :], in0=acc[:, :, :], in1=gb, op=ALU.mult), reads=[aname, "G2048"], writes=["hacc_t"])
            P.c("dve", lambda e, i=i: e.tensor_tensor(out=hacc_t[:, :, :], in0=hacc_t[:, :, :],
                                                     in1=xg[:, i, d * 512:(d + 1) * 512].rearrange("p (h e) -> p h e", h=4), op=ALU.add),
                reads=["hacc_t", "xg"], writes=["hacc_t"])
            P.c("dve", lambda e: e.tensor_tensor(out=hacc_t[:, :, :], in0=hacc_t[:, :, :], in1=acc[:, :, :], op=ALU.subtract),
                reads=["hacc_t", aname], writes=["hacc_t"])
            P.c("dve", lambda e, n_=n_: e.scalar_tensor_tensor(out=acc[:, :, :], in0=hacc_t[:, :, :], scalar=msk[:, mcol0 + n_:mcol0 + n_ + 1],
                                                              in1=acc[:, :, :], op0=ALU.mult, op1=ALU.add),
                reads=["hacc_t", aname, "msk"], writes=[aname])

    horner(Rcf, "Rcf", 0, [0, 1, 2], 0)
    horner(Rcb, "Rcb", 1, [3, 2, 1], 3)

    stage("Dhorner")
    alias(["yn"], ["hacc_t"])
    P.c("dve", lambda e: e.tensor_copy(out=Rb[:, :, :], in_=Rcb[:, :, :]), reads=["Rcb"], writes=["Rb"])
    P.c("dve", lambda e: e.tensor_copy(out=Rf[:, :, :], in_=Rcf[:, :, :]), reads=["Rcf"], writes=["Rf"])
    Rf_store = nc.alloc_sbuf_tensor_at("Rf_store", [128, NT, 512], BF16, offset=xg_off)
    alias(["Rf_store"], ["xg"])
    def fronts_B(n_):
        kv_front(k_tm[:, NT - 1 - n_, :], v_tm[:, NT - 1 - n_, :], 1, "k_tm", "v_tm", 0, n_ % 2)
        kv_front(k_tm[:, n_, :], v_tm[:, n_, :], 0, "k_tm", "v_tm", 1, n_ % 2)

    fronts_B(0)
    for n_ in range(NT):
        ib = NT - 1 - n_
        P.c("act", lambda e, ib=ib: e.activation(out=Rb_store[:, ib, :], in_=Rb[:, :, :].rearrange("p h e -> p (h e)"), func=AF.Copy),
            reads=["Rb"], writes=["Rb_store"])
        P.c("act", lambda e, n_=n_: e.activation(out=Rf_store[:, n_, :], in_=Rf[:, :, :].rearrange("p h e -> p (h e)"), func=AF.Copy),
            reads=["Rf"], writes=["Rf_store"])
        if n_ < NT - 1:
            if n_ + 1 < NT - 1:
                fronts_B(n_ + 1)
            kv_back(Rb, "Rb", 1, 0, n_ % 2)
            kv_back(Rf, "Rf", 0, 1, n_ % 2)
    stage("DBb")
    scm_l = [scm, nc.alloc_sbuf_tensor_at("scm1", [128, 4, 128], BF16, offset=tabs_off)]
    qf_l = [qf, nc.alloc_sbuf_tensor_at("qf1", [128, 4, 128], BF16, offset=tabs_off + 1024)]
    qb_l = [qb, nc.alloc_sbuf_tensor_at("qb1", [128, 4, 128], BF16, offset=tabs_off + 2048)]
    yn_l = [yn, nc.alloc_sbuf_tensor_at("yn1", [128, 4, 128], F32, offset=Rcf_off)]
    retx_l = [retx, nc.alloc_sbuf_tensor_at("retx1", [128, 512], BF16, offset=Rfbf_off)]
    bst_l = [bst, nc.alloc_sbuf_tensor_at("bst1", [128, 4, 6], F32, offset=Rcb_off)]
    mv_l = [mv, nc.alloc_sbuf_tensor_at("mv1", [128, 4, 2], F32, offset=Rcb_off + 128)]
    rs_l = [rs, nc.alloc_sbuf_tensor_at("rs1", [128, 8], F32, offset=Rcb_off + 192)]
    alias(["scm1", "qf1", "qb1"], ["dpos", "dneg", "mge", "mlt", "e1", "e2", "vz2", "vz3"])
    alias(["yn1"], ["Rcf"])
    alias(["bst1", "bst21", "bst31", "mv1", "rs1", "rs41"], ["Rcb"])
    psS_l = [(psA[0], "ps0"), (psA[6], "ps6")]
    psO_l = [(psA[1], "ps1"), (psA[4], "ps4")]

    def ret_front(i):
        pp = i % 2
        sfx = "" if pp == 0 else "1"
        tsl = slice(i * 128, (i + 1) * 128)
        pS, pSn = psS_l[pp]
        pO, pOn = psO_l[pp]
        scm_, qf_, qb_, yn_ = scm_l[pp], qf_l[pp], qb_l[pp], yn_l[pp]
        for h in range(4):
            P.c("pe", lambda e, h=h: e.matmul(pS[:, h * 128:(h + 1) * 128], lhsT=kT[:, h, tsl], rhs=qT[:, h, tsl], start=True, stop=True),
                reads=["kT", "qT"], writes=[pSn])
        P.c("dve", lambda e: e.tensor_tensor(out=scm_[:, :, :], in0=pS[:, :].rearrange("p (h t) -> p h t", h=4), in1=Dm[:, :, :], op=ALU.mult),
            reads=[pSn, "Dm"], writes=["scm" + sfx])
        P.c("pool", lambda e: e.tensor_tensor(out=qf_[:, :, :], in0=qT[:, :, tsl], in1=XiF[:, :, :], op=ALU.mult), reads=["qT", "XiF"], writes=["qf" + sfx])
        P.c("pool", lambda e: e.tensor_tensor(out=qb_[:, :, :], in0=qT[:, :, tsl], in1=XiB[:, :, :], op=ALU.mult), reads=["qT", "XiB"], writes=["qb" + sfx])
        for h in range(4):
            osl = slice(h * 128, (h + 1) * 128)
            P.c("pe", lambda e, h=h, osl=osl: e.matmul(pO[:, osl], lhsT=scm_[:, h, :], rhs=v_tm[:, i, osl], start=True, stop=False),
                reads=["scm" + sfx, "v_tm"], writes=[pOn])
            P.c("pe", lambda e, h=h, osl=osl: e.matmul(pO[:, osl], lhsT=qf_[:, h, :], rhs=Rf_store[:, i, osl], start=False, stop=False),
                reads=["qf" + sfx, "Rf_store"], writes=[pOn])
            P.c("pe", lambda e, h=h, osl=osl: e.matmul(pO[:, osl], lhsT=qb_[:, h, :], rhs=Rb_store[:, i, osl], start=False, stop=True),
                reads=["qb" + sfx, "Rb_store"], writes=[pOn])
        P.c("act", lambda e: e.activation(out=yn_[:, :, :], in_=pO[:, :].rearrange("p (h e) -> p h e", h=4), func=AF.Copy), reads=[pOn], writes=["yn" + sfx])
        if debug and i in (0, 15):
            dd = dout(f"d_rety{i}", [128, 512])
            P.dma("sp", lambda e, dd=dd: e.dma_start(out=dd[:, :], in_=yn_[:, :, :].rearrange("p h e -> p (h e)")), "dbg", reads=["yn" + sfx])

    def ret_back(i):
        pp = i % 2
        sfx = "" if pp == 0 else "1"
        tsl = slice(i * 128, (i + 1) * 128)
        yn_, retx_, bst_, mv_, rs_ = yn_l[pp], retx_l[pp], bst_l[pp], mv_l[pp], rs_l[pp]
        ynn, rxn = "yn" + sfx, "retx" + sfx
        P.c("dve", lambda e: e.tensor_reduce(out=bst_[:, 0, 0:4], in_=yn_[:, :, :], axis=mybir.AxisListType.X, op=ALU.add), reads=[ynn], writes=["bst" + sfx])
        for h in range(4):
            P.c("act", lambda e, h=h: e.activation(out=junk2[:, :], in_=yn_[:, h, :], func=AF.Square, accum_out=bst_[:, 1, h:h + 1]),
                reads=[ynn], writes=["junk2", "bst2" + sfx])
        P.c("dve", lambda e: e.tensor_scalar(out=mv_[:, :, 0], in0=bst_[:, 0, 0:4], scalar1=1.0 / 128.0, scalar2=None, op0=ALU.mult), reads=["bst" + sfx], writes=["mv" + sfx])
        P.c("dve", lambda e: e.tensor_tensor(out=bst_[:, 2, 0:4], in0=mv_[:, :, 0], in1=mv_[:, :, 0], op=ALU.mult), reads=["mv" + sfx], writes=["bst3" + sfx])
        P.c("dve", lambda e: e.scalar_tensor_tensor(out=mv_[:, :, 1], in0=bst_[:, 1, 0:4], scalar=1.0 / 128.0, in1=bst_[:, 2, 0:4], op0=ALU.mult, op1=ALU.subtract),
            reads=["bst2" + sfx, "bst3" + sfx, "mv" + sfx], writes=["mv" + sfx])
        P.c("act", lambda e: e.activation(out=rs_[:, 0:4], in_=mv_[:, :, 1], func=AF.Sqrt, bias=epst[:, 0:1]), reads=["mv" + sfx, "epst"], writes=["rs" + sfx])
        P.c("dve", lambda e: e.reciprocal(out=rs_[:, 4:8], in_=rs_[:, 0:4]), reads=["rs" + sfx], writes=["rs4" + sfx])
        for h in range(4):
            P.c("dve", lambda e, h=h: e.tensor_scalar(out=yn_[:, h, :], in0=yn_[:, h, :], scalar1=mv_[:, h, 0:1], scalar2=rs_[:, 4 + h:5 + h],
                                                     op0=ALU.subtract, op1=ALU.mult), reads=[ynn, "mv" + sfx, "rs4" + sfx], writes=[ynn])
        P.c("dve", lambda e: e.tensor_tensor(out=retx_[:, :], in0=yn_[:, :, :].rearrange("p h e -> p (h e)"), in1=gate[:, i, :], op=ALU.mult),
            reads=[ynn, "gate"], writes=[rxn])
        for h in range(4):
            P.c("pe", lambda e, h=h: e.transpose(psTq[:, h * 128:(h + 1) * 128], retx_[:, h * 128:(h + 1) * 128], ident_b[:, :]),
                reads=[rxn, "ident_b"], writes=["psTq"])
        P.c("act", lambda e: e.activation(out=mixT[:, 4:8, tsl], in_=psTq[:, 0:512].rearrange("p (h t) -> p h t", h=4), func=AF.Copy),
            reads=["psTq"], writes=["mixT"])

    ret_front(0)
    for i in range(NT):
        if i + 1 < NT:
            ret_front(i + 1)
        ret_back(i)

    if debug:
        for (nm, tns) in (("d_rcf", Rcf), ("d_rcb", Rcb)):
            dd = dout(nm, [128, 512])
            P.dma("sp", lambda e, dd=dd, tns=tns: e.dma_start(out=dd[:, :], in_=tns[:, :, :].rearrange("p h e -> p (h e)")), "dbg", reads=["Rcf", "Rcb"])
        dd = nc.dram_tensor("d_mixT", [128, 8 * T], BF16, kind="ExternalOutput").ap()
        dbg["d_mixT"] = dd
        P.dma("sp", lambda e, dd=dd: e.dma_start(out=dd[:, :], in_=mixT[:, :, :].rearrange("p a b -> p (a b)")), "dbg", reads=["mixT"])

    stage("ret")
    d_in = {}
    for (nm, shp) in (("lamre", [128, 64]), ("lamim", [128, 64]), ("lstep", [128, 64]), ("bre", [128, 32, 16]), ("bim", [128, 32, 16]),
                      ("cre", [128, 32, 16]), ("cim", [128, 32, 16]), ("d128", [128, 32, 16]), ("bglu", [128, 4])):
        d_in[nm] = din("s5_" + nm, shp)
    d_in["wglu"] = din("w_glu", [512, 512])
    _s5(nc, P, debug, stage, dout, dbg, alias, big0, r2_base, r2_end, spare_off, uT, ucT, mixT, psA, psTq, ident_f, ident_b, ones_f, negpi, msk, d_in)
    stage("s5")
    wout_d = din("w_out", [D, D])
    fnw_d = din("fnw_b", [128, D])
    cw_d = din("cw", [128, 66])
    cb_d = din("cb", [128, 22])
    oh_d = din("oh", [128, 8])
    wup_d = din("w_up", [D, 5632])
    wdn_d = din("w_down", [2816, D])
    SP = Arena(nc); SP.off = spare_off
    fnw = SP.alloc("fnw", [128, D], F32)
    cw = SP.alloc("cw", [128, 22, 3], F32)
    cb = SP.alloc("cb", [128, 22], F32)
    oh = SP.alloc("oh", [128, 8], F32)
    hxe = SP.alloc("hxe", [128, 16], F32)
    xgC = SP.alloc("xgC", [128, 4, 16], F32)
    halo = SP.alloc("halo", [128, 16], F32)
    ss2 = SP.alloc("ss2", [128, 4], F32)
    assert SP.off <= spare_off + 6 * 1024, SP.off - spare_off
    alias(["fnw", "cw", "cb", "oh", "hxe", "xgC", "halo", "ss2a", "ss2b", "ss2c"], ["kc_tm", "vc_tm", "ucT", "e1lo", "e1hi", "e2lo", "e2hi", "s5tab"])
    P.dma("sp", lambda e: e.dma_start(out=fnw[:, :], in_=fnw_d[:, :]), "small8", writes=["fnw"])
    P.dma("sp", lambda e: e.dma_start(out=cw[:, :, :], in_=cw_d.rearrange("p (j w) -> p j w", w=3)), "small9", writes=["cw"])
    P.dma("sp", lambda e: e.dma_start(out=cb[:, :], in_=cb_d[:, :]), "small10", writes=["cb"])
    P.dma("sp", lambda e: e.dma_start(out=oh[:, :], in_=oh_d[:, :]), "small11", writes=["oh"])
    FS = Arena(nc); FS.off = big0
    x_mid = FS.alloc("x_mid", [128, NT, D], F32)
    wout = FS.alloc("wout", [128, 8, D], BF16)
    xf = [FS.alloc(f"xf{i}", [128, D], F32) for i in range(2)]
    assert FS.off <= big0 + 96 * 1024
    s_dead = ["qT", "kT", "k_tm", "v_tm", "gate", "uT", "XR", "QN", "s5tab", "L1lo", "L1hi", "e1lo", "e1hi", "e2lo", "e2hi", "tmpM", "Z", "Zc", "Ysb",
              "Fall", "Fctx", "hacc", "hacc_bf", "htmp", "xgB", "D2", "s5gT", "wglu", "sgt", "Xw0", "Xw1", "TAw0", "TAw1", "TBw0", "TBw1",
              "XAw0", "XAw1", "XBw0", "XBw1"] + [f"Rr{i}" for i in range(36)]
    alias(["x_mid", "wout", "xf0", "xf1"], s_dead)
    P.dma("pool", lambda e: e.dma_start(out=wout[:, :, :], in_=wout_d.rearrange("(k p) n -> p k n", p=128)), "wout", writes=["wout"])
    for k in range(8):
        P.c("dve", lambda e, k=k: e.tensor_tensor(out=wout[:, k, :], in0=wout[:, k, :], in1=bc2[:, :], op=ALU.mult), reads=["wout", "bc2"], writes=["wout"])
    for i in range(NT):
        xt_ = xf[i % 2]
        P.dma("sp", lambda e, xt_=xt_, i=i: e.dma_start(out=xt_[:, :], in_=x_d[i * 128:(i + 1) * 128, :]), f"xf{i % 2}", writes=[f"xf{i % 2}"])
        for hh in range(2):
            ps = psA[hh]
            for k in range(8):
                P.c("pe", lambda e, ps=ps, k=k, hh=hh, i=i: e.matmul(ps[:, :], lhsT=mixT[:, k, i * 128:(i + 1) * 128], rhs=wout[:, k, hh * 512:(hh + 1) * 512],
                                                                  start=(k == 0), stop=(k == 7)), reads=["mixT", "wout"], writes=[f"ps{hh}"])
            P.c("dve", lambda e, ps=ps, hh=hh, i=i, xt_=xt_: e.tensor_tensor(out=x_mid[:, i, hh * 512:(hh + 1) * 512], in0=ps[:, :],
                                                                           in1=xt_[:, hh * 512:(hh + 1) * 512], op=ALU.add),
                reads=[f"ps{hh}", f"xf{i % 2}"], writes=[f"x_mid{i}"])
    if debug:
        dd = dout("d_xmid", [128, NT * D])
        P.dma("sp", lambda e, dd=dd: e.dma_start(out=dd[:, :], in_=x_mid[:, :, :].rearrange("p a b -> p (a b)")), "dbg", reads=[f"x_mid{i}" for i in range(NT)])
    stage("F")

    GR = Arena(nc); GR.off = r2_base
    hx2T = GR.alloc("hx2T", [128, 8, T + 2], BF16)
    NJ = 4
    wa = GR.alloc("wa", [128, NJ, 8, 128], BF16)
    wg = GR.alloc("wg", [128, NJ, 8, 128], BF16)
    wdn = GR.alloc("wdn", [128, NJ, D], BF16)
    gbuf2 = [GR.alloc(f"gbuf{i}", [128, T + 2], F32) for i in range(2)]
    abuf2 = [GR.alloc(f"abuf{i}", [128, T], BF16) for i in range(2)]
    glb2 = [GR.alloc(f"glb{i}", [128, T], BF16) for i in range(2)]
    assert GR.off <= r2_end, (GR.off, r2_end)
    xs2 = nc.alloc_sbuf_tensor_at("xs2", [128, D], F32, offset=GR.off - 8192)
    junk3 = nc.alloc_sbuf_tensor_at("junk3", [128, D], BF16, offset=GR.off - 4096)
    gnames = ["hx2T"] + [f"{n}{i}" for n in ("wa", "wg", "wdn") for i in range(NJ)] + ["gbuf0", "gbuf1", "abuf0", "abuf1", "glb0", "glb1", "xs2", "junk3"]
    alias(gnames, ["mixT", "WRb", "WRf", "PV", "Mbf", "Eblk", "Macc", "L1lo", "L1hi"])
    HS = Arena(nc); HS.off = big0 + NT * D * 4
    hid = HS.alloc("hid", [128, NJ, T], BF16)
    ub2 = [HS.alloc(f"ub{i}", [128, T], F32) for i in range(2)]
    assert HS.off <= big0 + 96 * 1024
    alias([f"hid{i}" for i in range(NJ)] + ["ub0", "ub1"], ["wout", "xf0", "xf1"])

    xs2b = nc.alloc_sbuf_tensor_at("xs2b", [128, D], F32, offset=GR.off - 12288)
    ss2b = SP.alloc("ss2b", [128, 4], F32)
    xs2s = [(xs2, "xs2", ss2, "ss2"), (xs2b, "xs2b", ss2b, "ss2b")]

    def norm2_front(i, pp):
        xs_, xn_, s_, sn_ = xs2s[pp]
        P.c("act", lambda e: e.activation(out=junk3[:, :], in_=x_mid[:, i, :], func=AF.Square, accum_out=s_[:, 0:1]),
            reads=[f"x_mid{i}"], writes=["junk3", sn_ + "a"])
        P.c("act", lambda e: e.activation(out=s_[:, 1:2], in_=s_[:, 0:1], func=AF.Sqrt, scale=1.0 / D, bias=epst[:, 0:1]), reads=[sn_ + "a", "epst"], writes=[sn_ + "b"])
        P.c("dve", lambda e: e.reciprocal(out=s_[:, 2:3], in_=s_[:, 1:2]), reads=[sn_ + "b"], writes=[sn_ + "c"])
        P.c("dve", lambda e: e.tensor_scalar(out=xs_[:, :], in0=x_mid[:, i, :], scalar1=s_[:, 2:3], scalar2=None, op0=ALU.mult),
            reads=[f"x_mid{i}", sn_ + "c"], writes=[xn_])

    def norm2_back(i, pp):
        xs_, xn_, s_, sn_ = xs2s[pp]
        for k in range(8):
            ps = psA[k // 4]
            P.c("pe", lambda e, ps=ps, k=k: e.transpose(ps[:, (k % 4) * 128:(k % 4 + 1) * 128], xs_[:, k * 128:(k + 1) * 128], ident_f[:, :]),
                reads=[xn_, "ident_f"], writes=[f"ps{k // 4}"])
        for k in range(8):
            ps = psA[k // 4]
            P.c("act", lambda e, ps=ps, k=k: e.activation(out=hx2T[:, k, 1 + i * 128: 1 + (i + 1) * 128], in_=ps[:, (k % 4) * 128:(k % 4 + 1) * 128],
                                                        func=AF.Identity, scale=g2x[:, k:k + 1], bias=modT[:, 16 + k, 0:1]),
                reads=[f"ps{k // 4}", "g2x", "modT"], writes=["hx2T"])

    wup_src = wup_d.rearrange("(k p) n -> p k n", p=128)
    passes = [(0, 4), (4, 8), (8, 12), (12, 16), (16, 19), (19, 22)]

    def load_up(j, sl):
        P.dma("pool", lambda e: e.dma_start(out=wa[:, sl, :, :], in_=wup_src[:, :, j * 128:(j + 1) * 128]), f"wa{sl}", writes=[f"wa{sl}"])
        P.dma("pool", lambda e: e.dma_start(out=wg[:, sl, :, :], in_=wup_src[:, :, 2816 + j * 128: 2816 + (j + 1) * 128]), f"wg{sl}", writes=[f"wg{sl}"])

    def load_dn(j, sl):
        P.dma("pool", lambda e: e.dma_start(out=wdn[:, sl, :], in_=wdn_d[j * 128:(j + 1) * 128, :]), f"wdn{sl}", writes=[f"wdn{sl}"])

    def scale_dn(sl):
        P.c("dve", lambda e: e.tensor_tensor(out=wdn[:, sl, :], in0=wdn[:, sl, :], in1=bc5[:, :], op=ALU.mult), reads=[f"wdn{sl}", "bc5"], writes=[f"wdn{sl}"])

    def chunk_tail(t_):
        glb, ub, abuf, jj, gln, ubn, abn = t_
        P.c("act", lambda e: e.activation(out=glb[:, :], in_=ub[:, :], func=AF.Gelu_apprx_tanh), reads=[ubn], writes=[gln])
        P.c("dve", lambda e: e.tensor_tensor(out=hid[:, jj, :], in0=glb[:, :], in1=abuf[:, :], op=ALU.mult), reads=[gln, abn], writes=[f"hid{jj}"])

    for sl in range(passes[0][1]):
        load_up(sl, sl)
    for jj_ in range(passes[0][1] - passes[0][0]):
        load_dn(passes[0][0] + jj_, jj_)
    order = [0, NT - 1] + list(range(1, NT - 1))
    norm2_front(order[0], 0)
    for n_, i in enumerate(order):
        if n_ + 1 < len(order):
            norm2_front(order[n_ + 1], (n_ + 1) % 2)
        norm2_back(i, n_ % 2)
        if n_ == 1:
            P.c("dve", lambda e: e.tensor_copy(out=hxe[:, 0:8], in_=hx2T[:, :, 1]), reads=["hx2T"], writes=["hxe"])
            P.c("dve", lambda e: e.tensor_copy(out=hxe[:, 8:16], in_=hx2T[:, :, T]), reads=["hx2T"], writes=["hxe"])
            xc_in = nc.dram_tensor("xc_in", [128, 16], F32)
            xc_out = nc.dram_tensor("xc_out", [512, 16], F32)
            P.dma("sp", lambda e: e.dma_start(out=xc_in[:, :], in_=hxe[:, :]), "xc_st", reads=["hxe"], writes=["xc_in"])
            P.dma("pool", lambda e: e.collective_compute("AllGather", ALU.bypass, replica_groups=[[0, 1, 2, 3], [4, 5, 6, 7]],
                                                          ins=[xc_in.ap().opt()], outs=[xc_out.ap().opt()]),
                  "ccC", reads=["xc_in"], writes=["xc_out"], inc=1)
            P.dma("sp", lambda e: e.dma_start(out=xgC[:, :, :], in_=xc_out.ap().rearrange("(r p) n -> p r n", p=128)), "xc_ld", reads=["xc_out"], writes=["xgC"])
    P.c("pool", lambda e: e.memset(halo[:, :], 0.0), writes=["halo"])
    for r_ in range(4):
        P.c("dve", lambda e, r_=r_: e.scalar_tensor_tensor(out=halo[:, 0:8], in0=xgC[:, r_, 8:16], scalar=oh[:, r_:r_ + 1], in1=halo[:, 0:8],
                                                          op0=ALU.mult, op1=ALU.add), reads=["xgC", "oh", "halo"], writes=["halo"])
        P.c("dve", lambda e, r_=r_: e.scalar_tensor_tensor(out=halo[:, 8:16], in0=xgC[:, r_, 0:8], scalar=oh[:, 4 + r_:5 + r_], in1=halo[:, 8:16],
                                                          op0=ALU.mult, op1=ALU.add), reads=["xgC", "oh", "halo"], writes=["halo"])
    P.c("dve", lambda e: e.tensor_copy(out=hx2T[:, :, 0], in_=halo[:, 0:8]), reads=["halo"], writes=["hx2T"])
    P.c("dve", lambda e: e.tensor_copy(out=hx2T[:, :, T + 1], in_=halo[:, 8:16]), reads=["halo"], writes=["hx2T"])
    stage("G0")

    alias(["glb1"], ["xs2", "junk3"])
    alias(["glb0"], ["xs2"])
    alias(["abuf1"], ["xs2b"])
    cnum = 0
    pend_tail = []
    for pi, (j0, j1) in enumerate(passes):
        nj = j1 - j0
        if pi > 0:
            for jj in range(nj):
                load_dn(j0 + jj, jj)
        for jj in range(nj):
            j = j0 + jj
            pb = cnum % 2
            cnum += 1
            gbuf, abuf, glb, ub = gbuf2[pb], abuf2[pb], glb2[pb], ub2[pb]
            gbn, abn, gln, ubn = f"gbuf{pb}", f"abuf{pb}", f"glb{pb}", f"ub{pb}"
            for tg in range(4):
                psa, psg = (psA[2], psA[3]) if tg % 2 == 0 else (psA[4], psA[5])
                pan, pgn = ("ps2", "ps3") if tg % 2 == 0 else ("ps4", "ps5")
                csl = slice(1 + tg * 512, 1 + (tg + 1) * 512)
                for k in range(8):
                    P.c("pe", lambda e, psa=psa, k=k, jj=jj, csl=csl: e.matmul(psa[:, :], lhsT=wa[:, jj, k, :], rhs=hx2T[:, k, csl], start=(k == 0), stop=(k == 7)),
                        reads=[f"wa{jj}", "hx2T"], writes=[pan])
                for k in range(8):
                    P.c("pe", lambda e, psg=psg, k=k, jj=jj, csl=csl: e.matmul(psg[:, :], lhsT=wg[:, jj, k, :], rhs=hx2T[:, k, csl], start=(k == 0), stop=(k == 7)),
                        reads=[f"wg{jj}", "hx2T"], writes=[pgn])
                P.c("act", lambda e, psa=psa, tg=tg, abuf=abuf: e.activation(out=abuf[:, tg * 512:(tg + 1) * 512], in_=psa[:, :], func=AF.Copy), reads=[pan], writes=[abn])
                P.c("act", lambda e, psg=psg, csl=csl, gbuf=gbuf: e.activation(out=gbuf[:, csl], in_=psg[:, :], func=AF.Copy), reads=[pgn], writes=[gbn])
            for k in range(8):
                P.c("pe", lambda e, k=k, jj=jj: e.matmul(psA[6][:, 0:2], lhsT=wg[:, jj, k, :], rhs=hx2T[:, k, 0:T + 2:T + 1], start=(k == 0), stop=(k == 7)),
                    reads=[f"wg{jj}", "hx2T"], writes=["ps6"])
            P.c("act", lambda e, gbuf=gbuf: e.activation(out=gbuf[:, 0:T + 2:T + 1], in_=psA[6][:, 0:2], func=AF.Copy), reads=["ps6"], writes=[gbn])
            while len(pend_tail) > 0:
                chunk_tail(pend_tail.pop(0))
            if pi + 1 < len(passes) and jj < passes[pi + 1][1] - passes[pi + 1][0]:
                load_up(passes[pi + 1][0] + jj, jj)
            P.c("act", lambda e, j=j, ub=ub, gbuf=gbuf: e.activation(out=ub[:, :], in_=gbuf[:, 1:T + 1], func=AF.Identity, scale=cw[:, j, 1:2], bias=cb[:, j:j + 1]),
                reads=[gbn, "cw", "cb"], writes=[ubn])
            P.c("dve", lambda e, j=j, ub=ub, gbuf=gbuf: e.scalar_tensor_tensor(out=ub[:, :], in0=gbuf[:, 0:T], scalar=cw[:, j, 0:1], in1=ub[:, :], op0=ALU.mult, op1=ALU.add),
                reads=[gbn, "cw", ubn], writes=[ubn])
            P.c("dve", lambda e, j=j, ub=ub, gbuf=gbuf: e.scalar_tensor_tensor(out=ub[:, :], in0=gbuf[:, 2:T + 2], scalar=cw[:, j, 2:3], in1=ub[:, :], op0=ALU.mult, op1=ALU.add),
                reads=[gbn, "cw", ubn], writes=[ubn])
            pend_tail.append((glb, ub, abuf, jj, gln, ubn, abn))
        while len(pend_tail) > 0:
            chunk_tail(pend_tail.pop(0))
        for jj in range(nj):
            scale_dn(jj)
        for i in range(NT):
            for hh in range(2):
                ps = psA[hh]
                for jj in range(nj):
                    P.c("pe", lambda e, ps=ps, jj=jj, i=i, hh=hh, nj=nj: e.matmul(ps[:, :], lhsT=hid[:, jj, i * 128:(i + 1) * 128], rhs=wdn[:, jj, hh * 512:(hh + 1) * 512],
                                                                               start=(jj == 0), stop=(jj == nj - 1)), reads=[f"hid{jj}", f"wdn{jj}"], writes=[f"ps{hh}"])
                P.c("dve", lambda e, ps=ps, i=i, hh=hh: e.tensor_tensor(out=x_mid[:, i, hh * 512:(hh + 1) * 512], in0=ps[:, :],
                                                                      in1=x_mid[:, i, hh * 512:(hh + 1) * 512], op=ALU.add),
                    reads=[f"ps{hh}", f"x_mid{i}"], writes=[f"x_mid{i}"])
    stage("G1")
    alias(["junk3"], ["glb1"])
    def fin_front(i):
        s_, sn_ = (ss2, "ss2") if i % 2 == 0 else (ss2b, "ss2b")
        P.c("act", lambda e: e.activation(out=junk3[:, :], in_=x_mid[:, i, :], func=AF.Square, accum_out=s_[:, 0:1]),
            reads=[f"x_mid{i}"], writes=["junk3", sn_ + "a"])
        P.c("act", lambda e: e.activation(out=s_[:, 1:2], in_=s_[:, 0:1], func=AF.Sqrt, scale=1.0 / D, bias=epst[:, 0:1]), reads=[sn_ + "a", "epst"], writes=[sn_ + "b"])
        P.c("dve", lambda e: e.reciprocal(out=s_[:, 2:3], in_=s_[:, 1:2]), reads=[sn_ + "b"], writes=[sn_ + "c"])

    fin_front(0)
    for i in range(NT):
        if i + 1 < NT:
            fin_front(i + 1)
        s_, sn_ = (ss2, "ss2") if i % 2 == 0 else (ss2b, "ss2b")
        P.c("dve", lambda e, i=i, s_=s_: e.scalar_tensor_tensor(out=x_mid[:, i, :], in0=x_mid[:, i, :], scalar=s_[:, 2:3], in1=fnw[:, :], op0=ALU.mult, op1=ALU.mult),
            reads=[f"x_mid{i}", sn_ + "c", "fnw"], writes=[f"x_mid{i}"])
        P.dma("sp", lambda e, i=i: e.dma_start(out=out_d[i * 128:(i + 1) * 128, :], in_=x_mid[:, i, :]), "xout", reads=[f"x_mid{i}"])
    stage("end")


def _s5(nc, P, debug, stage, dout, dbg, alias, big0, r2_base, r2_end, spare_off, uT, ucT, mixT, psA, psTq, ident_f, ident_b, ones_f, negpi, msk, d_in):
    TWO_PI = 2.0 * np.pi
    AS = Arena(nc)
    AS.off = big0
    S_END = big0 + 80 * 1024
    XR = AS.alloc("XR", [128, 32, 128], F32)
    QN = AS.alloc("QN", [128, 32, 128], F32)
    Macc = nc.alloc_sbuf_tensor_at("Macc", [128, 32, 128], F32, offset=r2_base)
    sm0 = AS.off
    V1 = AS.alloc("V1", [128, 13, 64], F32); V2 = AS.alloc("V2", [128, 13, 64], F32)
    bglu = AS.alloc("bglu", [128, 4], F32)
    dead0 = AS.off

    def f32t(name, w):
        return AS.alloc(name, [128, w], F32)
    lamre = f32t("lamre", 64); lamim = f32t("lamim", 64); lstep = f32t("lstep", 64)
    dtt = f32t("dtt", 64); rho = f32t("rho", 64); tht = f32t("tht", 64); mag = f32t("mag", 64)
    sn = f32t("sn", 64); cs = f32t("cs", 64); ta = f32t("ta", 64); tb = f32t("tb", 64); tc = f32t("tc", 64)
    tiI = AS.alloc("tiI", [128, 64], I32)
    kre = f32t("kre", 64); kim = f32t("kim", 64)
    PWre = AS.alloc("PWre", [128, 9, 64], F32); PWim = AS.alloc("PWim", [128, 9, 64], F32)
    NPre = AS.alloc("NPre", [128, 8, 64], F32); NPim = AS.alloc("NPim", [128, 8, 64], F32)
    HPre = AS.alloc("HPre", [128, 13, 64], F32); HPim = AS.alloc("HPim", [128, 13, 64], F32)
    bre = AS.alloc("bre", [128, 32, 16], F32); bim = AS.alloc("bim", [128, 32, 16], F32)
    cre = AS.alloc("cre", [128, 32, 16], F32); cim = AS.alloc("cim", [128, 32, 16], F32)
    Cr = AS.alloc("Cr", [128, 32, 16], F32)
    d128 = AS.alloc("d128", [128, 32, 16], F32)
    BBre = AS.alloc("BBre", [128, 64, 16], F32); BBim = AS.alloc("BBim", [128, 64, 16], F32)
    e1 = nc.alloc_sbuf_tensor_at("s5e1", [128, 32, 16], F32, offset=spare_off)
    e2 = nc.alloc_sbuf_tensor_at("s5e2", [128, 32, 16], F32, offset=spare_off + 2048)
    maskF = AS.alloc("maskF", [128, 128], F32); maskB = AS.alloc("maskB", [128, 128], F32)
    tmpM = AS.alloc("tmpM", [128, 128], F32)
    WRf = nc.alloc_sbuf_tensor_at("WRf", [128, 32, 128], F32, offset=r2_base + 8 * T * 2 + 56 * 1024 - 16 * 1024)
    assert AS.off <= S_END, (AS.off, S_END)
    AR2 = Arena(nc)
    AR2.off = r2_base + 8 * T * 2
    WR = AR2.alloc("WR", [128, 64, 128], BF16)
    PV = AR2.alloc("PV", [128, 64, 128], BF16)
    Mbf = AR2.alloc("Mbf", [128, 32, 128], BF16)
    Eblk = AR2.alloc("Eblk", [128, 64, 128], BF16)
    assert AR2.off <= r2_end, (AR2.off, r2_end)
    s5names = ["XR", "QN", "Macc", "s5tab", "WRb", "WRf", "PV", "Mbf", "Eblk", "s5tmp", "L1lo", "L1hi", "e1lo", "e2lo", "e1hi", "e2hi"]
    alias(s5names, ["qT", "kT", "k_tm", "v_tm", "gate", "Rb_store", "Rf_store", "scm1", "qf1", "qb1", "yn1", "retx1", "bst1", "bst21", "bst31", "mv1", "rs1", "rs41", "xg", "Dm", "XiF", "XiB", "dm", "dpos", "dneg", "mge", "mlt", "e1", "e2", "tfree",
                    "pidx", "zarg", "Zeta", "G128", "G2048", "nld", "ld128", "Rf", "Rb", "Rcf", "Rcb", "Rf_bf", "hacc_t", "vz0", "vz1", "scm", "qf", "qb", "yn",
                    "retx", "bst", "bst2", "bst3", "junk2", "mv", "rs", "rs4", "kc_tm", "vc_tm"])
    TAB = "s5tab"

    for (t_, nm) in ((lamre, "lamre"), (lamim, "lamim"), (lstep, "lstep"), (bre, "bre"), (bim, "bim"), (cre, "cre"), (cim, "cim"), (d128, "d128"), (bglu, "bglu")):
        src_ap = d_in[nm]
        if len(t_.shape) == 3:
            P.dma("sp", lambda e, t_=t_, src_ap=src_ap: e.dma_start(out=t_[:, :, :], in_=src_ap), "s5ld", writes=[TAB])
        else:
            P.dma("sp", lambda e, t_=t_, src_ap=src_ap: e.dma_start(out=t_[:, :], in_=src_ap), "s5ld", writes=[TAB])

    cnt = [0]

    def ew():
        cnt[0] += 1
        return "dve" if cnt[0] % 2 else "pool"

    def tt(out, a, b, op, eng=None, rd=(TAB,), wr=(TAB,)):
        P.c(eng or ew(), lambda e: e.tensor_tensor(out=out, in0=a, in1=b, op=op), reads=list(rd), writes=list(wr))

    def ts(out, a, s1, s2, op0, op1=None, eng="dve", rd=(TAB,), wr=(TAB,)):
        if op1 is None:
            P.c(eng, lambda e: e.tensor_scalar(out=out, in0=a, scalar1=s1, scalar2=None, op0=op0), reads=list(rd), writes=list(wr))
        else:
            P.c(eng, lambda e: e.tensor_scalar(out=out, in0=a, scalar1=s1, scalar2=s2, op0=op0, op1=op1), reads=list(rd), writes=list(wr))

    def sin_of(dst, src_, off):
        ts(ta[:, :], src_, 1.0 / TWO_PI, off, ALU.mult, ALU.add)
        P.c("dve", lambda e: e.tensor_copy(out=tiI[:, :], in_=ta[:, :]), reads=[TAB], writes=[TAB])
        P.c("dve", lambda e: e.tensor_copy(out=tb[:, :], in_=tiI[:, :]), reads=[TAB], writes=[TAB])
        tt(ta[:, :], ta[:, :], tb[:, :], ALU.subtract, eng="dve")
        ts(tb[:, :], ta[:, :], 0.0, None, ALU.is_lt)
        tt(ta[:, :], ta[:, :], tb[:, :], ALU.add, eng="dve")
        P.c("act", lambda e: e.activation(out=dst, in_=ta[:, :], func=AF.Sin, scale=TWO_PI, bias=negpi[:, 0:1]), reads=[TAB, "negpi"], writes=[TAB])

    def cmul(ore, oim, are, aim, bre_, bim_, w=64):
        t1, t2 = ta[:, 0:w], tb[:, 0:w]
        tt(t1, are, bre_, ALU.mult, eng="dve"); tt(t2, aim, bim_, ALU.mult, eng="dve"); tt(ore, t1, t2, ALU.subtract, eng="dve")
        tt(t1, are, bim_, ALU.mult, eng="dve"); tt(t2, aim, bre_, ALU.mult, eng="dve"); tt(oim, t1, t2, ALU.add, eng="dve")

    P.c("act", lambda e: e.activation(out=dtt[:, :], in_=lstep[:, :], func=AF.Exp), reads=[TAB], writes=[TAB])
    tt(rho[:, :], lamre[:, :], dtt[:, :], ALU.mult, eng="dve")
    tt(tht[:, :], lamim[:, :], dtt[:, :], ALU.mult, eng="dve")
    P.c("act", lambda e: e.activation(out=mag[:, :], in_=rho[:, :], func=AF.Exp), reads=[TAB], writes=[TAB])
    sin_of(sn[:, :], tht[:, :], 0.5)
    sin_of(cs[:, :], tht[:, :], 0.75)
    P.c("pool", lambda e: e.memset(PWre[:, 0, :], 1.0), reads=[TAB], writes=[TAB])
    P.c("pool", lambda e: e.memset(PWim[:, 0, :], 0.0), reads=[TAB], writes=[TAB])
    tt(PWre[:, 1, :], mag[:, :], cs[:, :], ALU.mult, eng="dve")
    tt(PWim[:, 1, :], mag[:, :], sn[:, :], ALU.mult, eng="dve")
    ts(tc[:, :], PWre[:, 1, :], -1.0, None, ALU.add)
    tt(kre[:, :], tc[:, :], lamre[:, :], ALU.mult, eng="dve"); tt(ta[:, :], PWim[:, 1, :], lamim[:, :], ALU.mult, eng="dve")
    tt(kre[:, :], kre[:, :], ta[:, :], ALU.add, eng="dve")
    tt(kim[:, :], PWim[:, 1, :], lamre[:, :], ALU.mult, eng="dve"); tt(ta[:, :], tc[:, :], lamim[:, :], ALU.mult, eng="dve")
    tt(kim[:, :], kim[:, :], ta[:, :], ALU.subtract, eng="dve")
    tt(ta[:, :], lamre[:, :], lamre[:, :], ALU.mult, eng="dve"); tt(tb[:, :], lamim[:, :], lamim[:, :], ALU.mult, eng="dve")
    tt(ta[:, :], ta[:, :], tb[:, :], ALU.add, eng="dve")
    P.c("dve", lambda e: e.reciprocal(out=tb[:, :], in_=ta[:, :]), reads=[TAB], writes=[TAB])
    tt(kre[:, :], kre[:, :], tb[:, :], ALU.mult, eng="dve"); tt(kim[:, :], kim[:, :], tb[:, :], ALU.mult, eng="dve")
    for d in range(2):
        cs_ = slice(d * 32, (d + 1) * 32)
        kr = kre[:, cs_].unsqueeze(2).broadcast_to([128, 32, 16]); ki = kim[:, cs_].unsqueeze(2).broadcast_to([128, 32, 16])
        tt(e1[:, :, :], kr, bre[:, :, :], ALU.mult); tt(e2[:, :, :], ki, bim[:, :, :], ALU.mult)
        tt(BBre[:, cs_, :], e1[:, :, :], e2[:, :, :], ALU.subtract, eng="dve")
        tt(e1[:, :, :], kr, bim[:, :, :], ALU.mult); tt(e2[:, :, :], ki, bre[:, :, :], ALU.mult)
        tt(BBim[:, cs_, :], e1[:, :, :], e2[:, :, :], ALU.add, eng="dve")
    P.c("dve", lambda e: e.tensor_copy(out=Cr[0:64, :, :], in_=cre[0:64, :, :]), reads=[TAB], writes=[TAB])
    ts(Cr[64:128, :, :], cim[64:128, :, :], -1.0, None, ALU.mult)
    for e_ in range(2, 9):
        cmul(PWre[:, e_, :], PWim[:, e_, :], PWre[:, e_ - 1, :], PWim[:, e_ - 1, :], PWre[:, 1, :], PWim[:, 1, :])
    P.c("pool", lambda e: e.memset(NPre[:, 0, :], 1.0), reads=[TAB], writes=[TAB])
    P.c("pool", lambda e: e.memset(NPim[:, 0, :], 0.0), reads=[TAB], writes=[TAB])
    tt(ta[:, :], PWre[:, 1, :], PWre[:, 1, :], ALU.mult, eng="dve"); tt(tb[:, :], PWim[:, 1, :], PWim[:, 1, :], ALU.mult, eng="dve")
    tt(ta[:, :], ta[:, :], tb[:, :], ALU.add, eng="dve")
    P.c("dve", lambda e: e.reciprocal(out=tc[:, :], in_=ta[:, :]), reads=[TAB], writes=[TAB])
    tt(NPre[:, 1, :], PWre[:, 1, :], tc[:, :], ALU.mult, eng="dve")
    tt(NPim[:, 1, :], PWim[:, 1, :], tc[:, :], ALU.mult, eng="dve")
    ts(NPim[:, 1, :], NPim[:, 1, :], -1.0, None, ALU.mult)
    for e_ in range(2, 8):
        cmul(NPre[:, e_, :], NPim[:, e_, :], NPre[:, e_ - 1, :], NPim[:, e_ - 1, :], NPre[:, 1, :], NPim[:, 1, :])
    P.c("dve", lambda e: e.tensor_copy(out=HPre[:, 0, :], in_=PWre[:, 8, :]), reads=[TAB], writes=[TAB])
    P.c("dve", lambda e: e.tensor_copy(out=HPim[:, 0, :], in_=PWim[:, 8, :]), reads=[TAB], writes=[TAB])
    for k in range(4):
        b0 = 3 * k
        cmul(HPre[:, b0 + 1, :], HPim[:, b0 + 1, :], HPre[:, b0, :], HPim[:, b0, :], HPre[:, b0, :], HPim[:, b0, :])
        cmul(HPre[:, b0 + 2, :], HPim[:, b0 + 2, :], HPre[:, b0 + 1, :], HPim[:, b0 + 1, :], HPre[:, b0, :], HPim[:, b0, :])
        cmul(HPre[:, b0 + 3, :], HPim[:, b0 + 3, :], HPre[:, b0 + 1, :], HPim[:, b0 + 1, :], HPre[:, b0 + 1, :], HPim[:, b0 + 1, :])
    P.c("dve", lambda e: e.tensor_copy(out=V1[0:64, :, :], in_=HPre[0:64, :, :]), reads=[TAB], writes=[TAB])
    ts(V1[64:128, :, :], HPim[64:128, :, :], -1.0, None, ALU.mult)
    P.c("dve", lambda e: e.tensor_copy(out=V2[0:64, :, :], in_=HPim[0:64, :, :]), reads=[TAB], writes=[TAB])
    P.c("dve", lambda e: e.tensor_copy(out=V2[64:128, :, :], in_=HPre[64:128, :, :]), reads=[TAB], writes=[TAB])
    P.c("pool", lambda e: e.iota(maskF[:, :], pattern=[[1, 8], [0, 16]], base=0, channel_multiplier=0, allow_small_or_imprecise_dtypes=True),
        reads=[TAB], writes=[TAB])
    P.c("pool", lambda e: e.iota(tiI[:, 0:1], pattern=[[0, 1]], base=0, channel_multiplier=1), reads=[TAB], writes=[TAB])
    P.c("dve", lambda e: e.tensor_single_scalar(out=tiI[:, 1:2], in_=tiI[:, 0:1], scalar=4, op=ALU.arith_shift_right), reads=[TAB], writes=[TAB])
    P.c("dve", lambda e: e.tensor_copy(out=ta[:, 0:1], in_=tiI[:, 1:2]), reads=[TAB], writes=[TAB])
    ts(maskB[:, :], maskF[:, :], ta[:, 0:1], None, ALU.is_le)
    ts(maskF[:, :], maskF[:, :], ta[:, 0:1], None, ALU.is_ge)
    if debug:
        for (nm, t_, w) in (("d_PWre", PWre, 9 * 64), ("d_PWim", PWim, 9 * 64), ("d_HPre", HPre, 13 * 64), ("d_HPim", HPim, 13 * 64),
                            ("d_BBre", BBre, 64 * 16), ("d_BBim", BBim, 64 * 16), ("d_NPre", NPre, 8 * 64)):
            dd = dout(nm, [128, w])
            P.dma("sp", lambda e, dd=dd, t_=t_: e.dma_start(out=dd[:, :], in_=t_[:, :, :].rearrange("p a b -> p (a b)")), "dbg", reads=[TAB])
        dd = dout("d_maskF", [128, 128])
        P.dma("sp", lambda e, dd=dd: e.dma_start(out=dd[:, :], in_=maskF[:, :]), "dbg", reads=[TAB])
    stage("S5tab")

    XR4 = XR[:, :, :].rearrange("p g (s q) -> p g s q", s=8)
    QN4 = QN[:, :, :].rearrange("p g (s q) -> p g s q", s=8)
    WRf4 = WRf[:, :, :].rearrange("p g (s q) -> p g s q", s=8)
    LO, HI = slice(0, 64), slice(64, 128)
    psM = [psA[0], psA[1]]
    psPV = psA[2]

    P.c("dve", lambda e: e.tensor_copy(out=e1[HI, :, :], in_=BBre[HI, 0:32, :]), reads=[TAB], writes=["e1hi"])
    P.c("dve", lambda e: e.tensor_copy(out=e2[HI, :, :], in_=BBre[HI, 32:64, :]), reads=[TAB], writes=["e2hi"])
    P.c("dve", lambda e: e.tensor_copy(out=BBre[HI, :, :], in_=BBim[HI, :, :]), reads=[TAB, "e1hi", "e2hi"], writes=[TAB])
    P.c("dve", lambda e: e.tensor_scalar(out=BBim[HI, 0:32, :], in0=e1[HI, :, :], scalar1=-1.0, scalar2=None, op0=ALU.mult), reads=[TAB, "e1hi"], writes=[TAB])
    P.c("dve", lambda e: e.tensor_scalar(out=BBim[HI, 32:64, :], in0=e2[HI, :, :], scalar1=-1.0, scalar2=None, op0=ALU.mult), reads=[TAB, "e2hi"], writes=[TAB])
    P.c("dve", lambda e: e.tensor_copy(out=cim[HI, :, :], in_=cre[HI, :, :]), reads=[TAB], writes=[TAB])

    def ctab(dst, dname, Are, Aim, e_idx, d, X1, X2, eng, ta_, tan):
        cs_ = slice(d * 32, (d + 1) * 32)
        ar = Are[:, e_idx, cs_].unsqueeze(2).broadcast_to([128, 32, 16])
        ai = Aim[:, e_idx, cs_].unsqueeze(2).broadcast_to([128, 32, 16])
        P.c(eng, lambda e: e.tensor_tensor(out=dst, in0=ar, in1=X1, op=ALU.mult), reads=[TAB], writes=[dname])
        P.c(eng, lambda e: e.tensor_tensor(out=ta_, in0=ai, in1=X2, op=ALU.mult), reads=[TAB], writes=[tan])
        P.c(eng, lambda e: e.tensor_tensor(out=dst, in0=dst, in1=ta_, op=ALU.subtract), reads=[dname, tan], writes=[dname])

    for d in range(2):
        cs_ = slice(d * 32, (d + 1) * 32)
        for s_ in range(8):
            eb = (7 - s_) if d == 0 else s_
            ctab(XR4[:, :, s_, :], "L1lo", PWre, PWim, eb, d, BBre[:, cs_, :], BBim[:, cs_, :], "dve", e1[:, :, :], "e1lo")
            ew_ = (s_ + 1) if d == 0 else (8 - s_)
            ctab(WRf4[:, :, s_, :], "WRf", PWre, PWim, ew_, d, Cr[:, :, :], cim[:, :, :], "dve", e1[:, :, :], "e1lo")
            en = (7 - s_) if d == 0 else s_
            ctab(QN4[:, :, s_, :], "L1hi", NPre, NPim, en, d, Cr[:, :, :], cim[:, :, :], "pool", e2[:, :, :], "e2lo")
        P.c("act", lambda e, cs_=cs_: e.activation(out=WR[:, cs_, :], in_=WRf[:, :, :], func=AF.Copy), reads=["WRf"], writes=["WRb"])
        for g in range(32):
            ps = psM[g % 2]
            P.c("pe", lambda e, ps=ps, g=g: e.matmul(ps[:, 0:128], lhsT=XR[:, g, :], rhs=QN[:, g, :], start=True, stop=True),
                reads=["L1lo", "L1hi"], writes=[f"ps{g % 2}"])
            if d == 0:
                P.c("dve", lambda e, ps=ps, g=g: e.tensor_tensor(out=Macc[:, g, :], in0=ps[:, 0:128], in1=maskF[:, :], op=ALU.mult),
                    reads=[f"ps{g % 2}", TAB], writes=["Macc"])
            else:
                P.c("dve", lambda e, ps=ps, g=g: e.tensor_tensor(out=tmpM[:, :], in0=ps[:, 0:128], in1=maskB[:, :], op=ALU.mult),
                    reads=[f"ps{g % 2}", TAB], writes=["tmpM"])
                P.c("dve", lambda e, g=g: e.tensor_tensor(out=Macc[:, g, :], in0=Macc[:, g, :], in1=tmpM[:, :], op=ALU.add),
                    reads=["tmpM", "Macc"], writes=["Macc"])
        for g4 in range(8):
            for j in range(4):
                g = g4 * 4 + j
                P.c("pe", lambda e, g=g, j=j: e.transpose(psPV[:, j * 128:(j + 1) * 128], XR[:, g, :], ident_f[:, :]),
                    reads=["L1lo", "L1hi", "ident_f"], writes=["ps2"])
            P.c("act", lambda e, g4=g4, d=d: e.activation(out=PV[:, d * 32 + g4 * 4: d * 32 + g4 * 4 + 4, :],
                                                        in_=psPV[:, :].rearrange("p (j m) -> p j m", j=4), func=AF.Copy),
                reads=["ps2"], writes=["PV"])
    idb = ident_f[:, :].rearrange("p (t q) -> p t q", t=8).unsqueeze(1).broadcast_to([128, 32, 8, 16])
    d4 = d128[:, :, :].unsqueeze(2).broadcast_to([128, 32, 8, 16])
    P.c("dve", lambda e: e.tensor_tensor(out=QN4, in0=idb, in1=d4, op=ALU.mult), reads=[TAB, "ident_f", "L1lo", "L1hi"], writes=["L1lo", "L1hi"])
    P.c("dve", lambda e: e.tensor_tensor(out=Mbf[:, :, :], in0=Macc[:, :, :], in1=QN[:, :, :], op=ALU.add), reads=["Macc", "L1lo", "L1hi"], writes=["Mbf"])
    if debug:
        dd = nc.dram_tensor("d_Mbf", [128, 32 * 128], BF16, kind="ExternalOutput").ap(); dbg["d_Mbf"] = dd
        P.dma("sp", lambda e, dd=dd: e.dma_start(out=dd[:, :], in_=Mbf[:, :, :].rearrange("p a b -> p (a b)")), "dbg", reads=["Mbf"])
        dd = nc.dram_tensor("d_PV", [128, 64 * 128], BF16, kind="ExternalOutput").ap(); dbg["d_PV"] = dd
        P.dma("sp", lambda e, dd=dd: e.dma_start(out=dd[:, :], in_=PV[:, :, :].rearrange("p a b -> p (a b)")), "dbg", reads=["PV"])
        dd = nc.dram_tensor("d_WR", [128, 64 * 128], BF16, kind="ExternalOutput").ap(); dbg["d_WR"] = dd
        P.dma("sp", lambda e, dd=dd: e.dma_start(out=dd[:, :], in_=WR[:, :, :].rearrange("p a b -> p (a b)")), "dbg", reads=["WRb"])
    stage("S5l1")

    B1 = Arena(nc); B1.off = big0
    Z = B1.alloc("Z", [128, 32, 256], BF16)
    Zc = B1.alloc("Zc", [128, 32, 32], BF16)
    Ysb = B1.alloc("Ysb", [128, 8, 256], BF16)
    WV = 4
    Xw = [B1.alloc(f"Xw{i}", [128, WV, 288], BF16) for i in range(2)]
    TAw = [B1.alloc(f"TAw{i}", [128, WV, 80], BF16) for i in range(2)]
    TBw = [B1.alloc(f"TBw{i}", [128, WV, 80], BF16) for i in range(2)]
    Fall = B1.alloc("Fall", [128, 64], F32)
    Fctx = B1.alloc("Fctx", [128, 64], F32)
    hacc = B1.alloc("hacc", [128, 64], F32)
    hacc_bf = B1.alloc("hacc_bf", [128, 64], BF16)
    htmp = B1.alloc("htmp", [128, 64], F32)
    xgB = B1.alloc("xgB", [128, 4, 64], F32)
    D2 = B1.alloc("D2", [128, 64], F32)
    assert B1.off <= big0 + 32 * 1024, B1.off - big0
    B2 = Arena(nc); B2.off = dead0
    s5gT = B2.alloc("s5gT", [128, 4, T], BF16)
    wglu = B2.alloc("wglu", [128, 4, 512], BF16)
    NR = 36
    Rring = [B2.alloc(f"Rr{i}", [128, 128], BF16) for i in range(NR)]
    sgt = B2.alloc("sgt", [128, 512], BF16)
    XAw = [B2.alloc(f"XAw{i}", [128, WV, 256], BF16) for i in range(2)]
    XBw = [B2.alloc(f"XBw{i}", [128, WV, 256], BF16) for i in range(2)]
    assert B2.off <= S_END, (B2.off, S_END)
    post = ["Z", "Zc", "Ysb", "Fall", "Fctx", "hacc", "hacc_bf", "htmp", "xgB", "D2", "s5gT", "wglu", "sgt"] + \
           [f"{n}{i}" for n in ("Xw", "TAw", "TBw", "XAw", "XBw") for i in range(2)] + [f"Rr{i}" for i in range(NR)]
    alias(post, ["L1lo", "L1hi", "e1lo", "e1hi", "e2lo", "e2hi", "tmpM", TAB])
    P.dma("pool", lambda e: e.dma_start(out=wglu[:, :, :], in_=d_in["wglu"].rearrange("(k p) n -> p k n", p=128)), "wglu", reads=[TAB], writes=["wglu"])
    P.c("dve", lambda e: e.tensor_tensor(out=D2[:, :], in0=ident_f[:, 0:64], in1=ident_f[:, 64:128], op=ALU.add), reads=["ident_f", TAB], writes=["D2"])
    P.c("pool", lambda e: e.memset(Eblk[:, :, :], 0.0), reads=[], writes=["Eblk", "WRf"])
    for a_ in range(8):
        for b_ in range(8):
            P.c("pool", lambda e, a_=a_, b_=b_: e.affine_select(out=Eblk[:, a_ * 8 + b_, 16 * b_:16 * b_ + 16], in_=ones_f[:, 0:16], pattern=[[1, 16]],
                                                              compare_op=ALU.is_equal, fill=0.0, base=16 * a_, channel_multiplier=-1),
                reads=["ones_f"], writes=["Eblk"])
    psZ = [psA[3], psA[4]]
    ev = [0]

    def evac(out, in_, reads, writes, eng=None):
        ev[0] += 1
        if eng is None:
            eng = "act" if ev[0] % 2 else "dve"
        if eng == "act":
            P.c("act", lambda e: e.activation(out=out, in_=in_, func=AF.Copy), reads=reads, writes=writes)
        else:
            P.c("dve", lambda e: e.tensor_copy(out=out, in_=in_), reads=reads, writes=writes)

    for g in range(32):
        fc, g8 = g // 8, g % 8
        ps = psZ[g % 2]
        for s_ in range(8):
            P.c("pe", lambda e, ps=ps, fc=fc, g8=g8, s_=s_: e.matmul(ps[:, 0:256], lhsT=Eblk[:, g8 * 8 + s_, :], rhs=uT[:, fc, s_::8],
                                                                    start=(s_ == 0), stop=(s_ == 7)), reads=["Eblk", "uT"], writes=[f"ps{3 + g % 2}"])
        for s_ in range(8):
            P.c("pe", lambda e, ps=ps, fc=fc, g8=g8, s_=s_: e.matmul(ps[:, 256:288], lhsT=Eblk[:, g8 * 8 + s_, :], rhs=ucT[:, fc, s_::8],
                                                                    start=(s_ == 0), stop=(s_ == 7)), reads=["Eblk", "ucT"], writes=[f"ps{3 + g % 2}"])
        en_ = "act" if g % 2 else "dve"
        evac(Z[:, g, :], ps[:, 0:256], [f"ps{3 + g % 2}"], ["Z"], eng=en_)
        evac(Zc[:, g, :], ps[:, 256:288], [f"ps{3 + g % 2}"], ["Zc"], eng=en_)

    rr = [0]
    reng = ["dve", "pool"]

    def build_R(dst, dname, hp_idx, q):
        for half, Vt in ((0, V1), (1, V2)):
            rr[0] += 1
            eng = reng[rr[0] % 2]
            o_ = dst[:, half * 64:(half + 1) * 64]
            sc = Vt[:, hp_idx, q:q + 1]
            if eng == "act":
                P.c("act", lambda e, o_=o_, sc=sc: e.activation(out=o_, in_=D2[:, :], func=AF.Identity, scale=sc), reads=["D2", TAB], writes=[dname])
            elif eng == "dve":
                P.c("dve", lambda e, o_=o_, sc=sc: e.tensor_scalar(out=o_, in0=D2[:, :], scalar1=sc, scalar2=None, op0=ALU.mult), reads=["D2", TAB], writes=[dname])
            else:
                P.c("pool", lambda e, o_=o_, sc=sc: e.tensor_scalar(out=o_, in0=D2[:, :], scalar1=sc, scalar2=0.0, op0=ALU.mult, op1=ALU.add),
                    reads=["D2", TAB], writes=[dname])

    ring = [0]

    def get_R(hp_idx, q):
        slot = ring[0] % NR
        ring[0] += 1
        build_R(Rring[slot][:, :], f"Rr{slot}", hp_idx, q)
        return Rring[slot], f"Rr{slot}"

    def tree_wave(w):
        par = w % 2
        qs = [w * WV + c_ for c_ in range(WV)]
        d = qs[0] // 32
        bankL = [psA[1 + 2 * par], psA[2 + 2 * par]]
        bnL = [f"ps{1 + 2 * par}", f"ps{2 + 2 * par}"]
        psC, pcn = psA[5 + par], f"ps{5 + par}"
        for c_, q in enumerate(qs):
            g = q % 32
            bk, bn = bankL[c_ // 2], bnL[c_ // 2]
            P.c("pe", lambda e, bk=bk, q=q, g=g, c_=c_: e.matmul(bk[:, (c_ % 2) * 256:(c_ % 2 + 1) * 256], lhsT=PV[:, q, :], rhs=Z[:, g, :], start=True, stop=True),
                reads=["PV", "Z"], writes=[bn])
            P.c("pe", lambda e, q=q, g=g, c_=c_, psC=psC: e.matmul(psC[:, c_ * 32:(c_ + 1) * 32], lhsT=PV[:, q, :], rhs=Zc[:, g, :], start=True, stop=True),
                reads=["PV", "Zc"], writes=[pcn])
        yield
        xw, xwn = Xw[par], f"Xw{par}"
        for h_ in range(2):
            evac(xw[:, 2 * h_:2 * h_ + 2, 0:256], bankL[h_][:, :].rearrange("p (c n) -> p c n", c=2), [bnL[h_]], [xwn], eng="act")
        evac(xw[:, :, 256:288], psC[:, 0:WV * 32].rearrange("p (c n) -> p c n", c=WV), [pcn], [xwn], eng="act")
        yield
        cur, cname = xw, xwn
        loc_off, loc_n, ctx_off, ctx_n = 0, 256, 256, 32
        bufs = [(TAw[par], f"TAw{par}"), (TBw[par], f"TBw{par}")]
        tb_ = par * 256
        Rw_next = [[None] + [get_R(j - 1, q) for j in (1, 2, 3)] for q in qs]
        for k in range(4):
            nxt, nname = bufs[k % 2]
            lev = []
            for (off, n_, is_ctx) in ((loc_off, loc_n, False), (ctx_off, ctx_n, True)):
                if n_ <= 1:
                    continue
                rad = 4 if n_ >= 4 else n_
                no = n_ // rad
                base = 128 if not is_ctx else 384
                lev.append((off, n_, is_ctx, rad, no, base))
            Rw = Rw_next
            for c_, q in enumerate(qs):
                for (off, n_, is_ctx, rad, no, base) in lev:
                    ocol = base + c_ * no
                    bk_, bkn_ = (psC, pcn)
                    for jj in range(rad):
                        pw = (rad - 1 - jj) if d == 0 else jj
                        lt, ltn = (ident_b, "ident_b") if pw == 0 else Rw[c_][pw]
                        P.c("pe", lambda e, bk_=bk_, lt=lt, cur=cur, c_=c_, off=off, jj=jj, rad=rad, n_=n_, ocol=ocol, no=no: e.matmul(
                            bk_[:, ocol:ocol + no], lhsT=lt[:, :], rhs=cur[:, c_, off + jj:off + n_:rad], start=(jj == 0), stop=(jj == rad - 1)),
                            reads=[ltn, cname], writes=[bkn_])
            if k < 3:
                Rw_next = [[None] + [get_R(3 * (k + 1) + j - 1, q) for j in (1, 2, 3)] for q in qs]
            yield
            for (off, n_, is_ctx, rad, no, base) in lev:
                bk_, bkn_ = (psC, pcn)
                if no == 1:
                    dstF, dn = (Fctx, "Fctx") if is_ctx else (Fall, "Fall")
                    P.c("act", lambda e, bk_=bk_, dstF=dstF, base=base, q0=qs[0]: e.activation(out=dstF[:, q0:q0 + WV], in_=bk_[:, base:base + WV], func=AF.Copy),
                        reads=[bkn_], writes=[dn])
                else:
                    o2 = 64 if is_ctx else 0
                    evac(nxt[:, :, o2:o2 + no], bk_[:, base:base + WV * no].rearrange("p (c n) -> p c n", c=WV), [bkn_], [nname], eng="act")
            cur, cname = nxt, nname
            loc_off, loc_n = 0, (loc_n // 4 if loc_n > 1 else 0)
            ctx_off, ctx_n = 64, (ctx_n // (4 if ctx_n >= 4 else ctx_n) if ctx_n > 1 else 0)

    def lockstep(gens):
        gens = list(gens)
        while gens:
            for g_ in list(gens):
                try:
                    next(g_)
                except StopIteration:
                    gens.remove(g_)

    def rolling(gens, depth=2, stagger=2):
        it = iter(gens)
        active = []
        pending = next(it, None)
        while active or pending is not None:
            if pending is not None and len(active) < depth and (not active or active[-1][1] >= stagger):
                active.append([pending, 0])
                pending = next(it, None)
            for ent in list(active):
                try:
                    next(ent[0])
                    ent[1] += 1
                except StopIteration:
                    active.remove(ent)

    rolling([tree_wave(w) for w in range(64 // WV)])
    if debug:
        for (nm, t_) in (("d_Fall", Fall), ("d_Fctx", Fctx)):
            dd = dout(nm, [128, 64])
            P.dma("sp", lambda e, dd=dd, t_=t_: e.dma_start(out=dd[:, :], in_=t_[:, :]), "dbg", reads=["Fall", "Fctx"])
    stage("S5tree")

    xb_in = nc.dram_tensor("xb_in", [128, 64], F32)
    xb_out = nc.dram_tensor("xb_out", [512, 64], F32)
    P.dma("sp", lambda e: e.dma_start(out=xb_in[:, :], in_=Fall[:, :]), "xb_st", reads=["Fall"], writes=["xb_in"])
    P.dma("pool", lambda e: e.collective_compute("AllGather", ALU.bypass, replica_groups=[[0, 1, 2, 3], [4, 5, 6, 7]],
                                                  ins=[xb_in.ap().opt()], outs=[xb_out.ap().opt()]),
          "ccB", reads=["xb_in"], writes=["xb_out"], inc=1)
    P.dma("sp", lambda e: e.dma_start(out=xgB[:, :, :], in_=xb_out.ap().rearrange("(r p) n -> p r n", p=128)), "xb_ld", reads=["xb_out"], writes=["xgB"])
    P.c("dve", lambda e: e.tensor_copy(out=hacc[:, :], in_=Fctx[:, :]), reads=["Fctx"], writes=["hacc"])
    psH = psA[1]
    for n_ in range(3):
        P.c("act", lambda e: e.activation(out=hacc_bf[:, :], in_=hacc[:, :], func=AF.Copy), reads=["hacc"], writes=["hacc_bf"])
        for q in range(64):
            Rt, Rn = get_R(12, q)
            P.c("pe", lambda e, q=q, Rt=Rt: e.matmul(psH[:, q:q + 1], lhsT=Rt[:, :], rhs=hacc_bf[:, q:q + 1], start=True, stop=True),
                reads=[Rn, "hacc_bf"], writes=["ps1"])
        for (cs_, rank, mcol) in ((slice(0, 32), n_, n_), (slice(32, 64), 3 - n_, 3 + n_)):
            P.c("dve", lambda e, cs_=cs_, rank=rank: e.tensor_tensor(out=htmp[:, cs_], in0=psH[:, cs_], in1=xgB[:, rank, cs_], op=ALU.add),
                reads=["ps1", "xgB"], writes=["htmp"])
            P.c("dve", lambda e, cs_=cs_: e.tensor_tensor(out=htmp[:, cs_], in0=htmp[:, cs_], in1=hacc[:, cs_], op=ALU.subtract),
                reads=["htmp", "hacc"], writes=["htmp"])
            P.c("dve", lambda e, cs_=cs_, mcol=mcol: e.scalar_tensor_tensor(out=hacc[:, cs_], in0=htmp[:, cs_], scalar=msk[:, mcol:mcol + 1],
                                                                           in1=hacc[:, cs_], op0=ALU.mult, op1=ALU.add),
                reads=["htmp", "hacc", "msk"], writes=["hacc"])
    P.c("act", lambda e: e.activation(out=hacc_bf[:, :], in_=hacc[:, :], func=AF.Copy), reads=["hacc"], writes=["hacc_bf"])
    stage("S5xchg")

    psY = psA[0]
    psS2 = [psA[5], psA[6]]
    def ks_wave(fc, gp):
        w = fc * 4 + gp
        par = w % 2
        g0 = fc * 8 + gp * 2
        qs = [g0, g0 + 1, 32 + g0, 33 + g0]
        banks = [psA[1 + 2 * par], psA[2 + 2 * par]]
        bns = [f"ps{1 + 2 * par}", f"ps{2 + 2 * par}"]
        for c_, q in enumerate(qs):
            P.c("pe", lambda e, c_=c_, q=q, bk=banks[c_ // 2]: e.matmul(bk[:, (c_ % 2) * 256:(c_ % 2 + 1) * 256], lhsT=PV[:, q, :], rhs=Z[:, q % 32, :],
                                                   start=True, stop=True), reads=["PV", "Z"], writes=[bns[c_ // 2]])
        yield
        xa, xan, xb_, xbn = XAw[par], f"XAw{par}", XBw[par], f"XBw{par}"
        evac(xa[:, 0:2, 1:256], banks[0][:, :].rearrange("p (c n) -> p c n", c=2)[:, :, 0:255], [bns[0]], [xan], eng="act")
        evac(xa[:, 2:4, 0:255], banks[1][:, :].rearrange("p (c n) -> p c n", c=2)[:, :, 1:256], [bns[1]], [xan], eng="act")
        P.c("dve", lambda e, xa=xa, g0=g0: e.tensor_copy(out=xa[:, 0:2, 0], in_=hacc_bf[:, g0:g0 + 2]), reads=["hacc_bf"], writes=[xan])
        P.c("dve", lambda e, xa=xa, g0=g0: e.tensor_copy(out=xa[:, 2:4, 255], in_=hacc_bf[:, 32 + g0:34 + g0]), reads=["hacc_bf"], writes=[xan])
        yield
        cur, cn, oth, on = xa, xan, xb_, xbn
        Rw_next = [[get_R(j - 1, q) for j in (1, 2, 3)] for q in qs]
        for k in range(4):
            Rw = Rw_next
            for c_, q in enumerate(qs):
                pk, pkn = banks[c_ // 2], bns[c_ // 2]
                cb_ = (c_ % 2) * 256
                P.c("pe", lambda e, pk=pk, cb_=cb_, cur=cur, c_=c_: e.matmul(pk[:, cb_:cb_ + 256], lhsT=ident_b[:, :], rhs=cur[:, c_, :], start=True, stop=False),
                    reads=["ident_b", cn], writes=[pkn])
                for j in (1, 2, 3):
                    sh = j * (4 ** k)
                    Rt, Rn = Rw[c_][j - 1]
                    if c_ < 2:
                        P.c("pe", lambda e, pk=pk, cb_=cb_, Rt=Rt, cur=cur, c_=c_, sh=sh, j=j: e.matmul(
                            pk[:, cb_ + sh:cb_ + 256], lhsT=Rt[:, :], rhs=cur[:, c_, 0:256 - sh], start=False, stop=(j == 3)), reads=[Rn, cn], writes=[pkn])
                    else:
                        P.c("pe", lambda e, pk=pk, cb_=cb_, Rt=Rt, cur=cur, c_=c_, sh=sh, j=j: e.matmul(
                            pk[:, cb_:cb_ + 256 - sh], lhsT=Rt[:, :], rhs=cur[:, c_, sh:256], start=False, stop=(j == 3)), reads=[Rn, cn], writes=[pkn])
            if k < 3:
                Rw_next = [[get_R(3 * (k + 1) + j - 1, q) for j in (1, 2, 3)] for q in qs]
            yield
            evac(oth[:, 0:2, :], banks[0][:, :].rearrange("p (c n) -> p c n", c=2), [bns[0]], [on], eng="act")
            evac(oth[:, 2:4, :], banks[1][:, :].rearrange("p (c n) -> p c n", c=2), [bns[1]], [on], eng="act")
            cur, cn, oth, on = oth, on, cur, cn
            yield
        for c_ in range(2):
            g = g0 + c_
            ysl = slice(c_ * 256, (c_ + 1) * 256)
            P.c("pe", lambda e, g=g, ysl=ysl: e.matmul(psY[:, ysl], lhsT=Mbf[:, g, :], rhs=Z[:, g, :], start=True, stop=False), reads=["Mbf", "Z"], writes=["ps0"])
            P.c("pe", lambda e, g=g, ysl=ysl, cur=cur, c_=c_: e.matmul(psY[:, ysl], lhsT=WR[:, g, :], rhs=cur[:, c_, :], start=False, stop=False),
                reads=["WRb", cn], writes=["ps0"])
            P.c("pe", lambda e, g=g, ysl=ysl, cur=cur, c_=c_: e.matmul(psY[:, ysl], lhsT=WR[:, 32 + g, :], rhs=cur[:, 2 + c_, :], start=False, stop=True),
                reads=["WRb", cn], writes=["ps0"])
        evac(Ysb[:, 2 * gp:2 * gp + 2, :], psY[:, :].rearrange("p (c n) -> p c n", c=2), ["ps0"], ["Ysb"], eng="act")

    for fc in range(4):
        rolling([ks_wave(fc, gp) for gp in range(4)])
        for s_ in range(8):
            ps = psS2[s_ % 2]
            for g8 in range(8):
                P.c("pe", lambda e, ps=ps, s_=s_, g8=g8: e.matmul(ps[:, 0:256], lhsT=Eblk[:, s_ * 8 + g8, :], rhs=Ysb[:, g8, :],
                                                                 start=(g8 == 0), stop=(g8 == 7)), reads=["Eblk", "Ysb"], writes=[f"ps{5 + s_ % 2}"])
            P.c("act", lambda e, ps=ps, s_=s_, fc=fc: e.activation(out=s5gT[:, fc, s_::8], in_=ps[:, 0:256], func=AF.Gelu_apprx_tanh),
                reads=[f"ps{5 + s_ % 2}"], writes=["s5gT"])
    stage("S5y")
    psG = [psA[0], psA[1]]
    for tb_ in range(4):
        tsl = slice(tb_ * 512, (tb_ + 1) * 512)
        for oc in range(4):
            ps = psG[oc % 2]
            for kc in range(4):
                P.c("pe", lambda e, ps=ps, kc=kc, oc=oc, tsl=tsl: e.matmul(ps[:, :], lhsT=wglu[:, kc, oc * 128:(oc + 1) * 128], rhs=s5gT[:, kc, tsl],
                                                                          start=(kc == 0), stop=(kc == 3)), reads=["wglu", "s5gT"], writes=[f"ps{oc % 2}"])
            P.c("act", lambda e, ps=ps, oc=oc: e.activation(out=sgt[:, :], in_=ps[:, :], func=AF.Sigmoid, bias=bglu[:, oc:oc + 1]),
                reads=[f"ps{oc % 2}", TAB], writes=["sgt"])
            P.c("dve", lambda e, oc=oc, tsl=tsl: e.tensor_tensor(out=mixT[:, oc, tsl], in0=s5gT[:, oc, tsl], in1=sgt[:, :], op=ALU.mult),
                reads=["s5gT", "sgt", "Macc"], writes=["mixT", "Macc"])
    if debug:
        dd = nc.dram_tensor("d_s5gT", [128, 4 * T], BF16, kind="ExternalOutput").ap(); dbg["d_s5gT"] = dd
        P.dma("sp", lambda e, dd=dd: e.dma_start(out=dd[:, :], in_=s5gT[:, :, :].rearrange("p a b -> p (a b)")), "dbg", reads=["s5gT"])
        dd = nc.dram_tensor("d_mixT2", [128, 8 * T], BF16, kind="ExternalOutput").ap(); dbg["d_mixT2"] = dd
        P.dma("sp", lambda e, dd=dd: e.dma_start(out=dd[:, :], in_=mixT[:, :, :].rearrange("p a b -> p (a b)")), "dbg", reads=["mixT"])
    stage("S5glu")
    return None


def _s5_host(inputs):
    def dup(a):
        return np.ascontiguousarray(np.concatenate([a, a], 0)).astype(np.float32)
    lre = np.concatenate([inputs["s5_lambda_re_f"][0], inputs["s5_lambda_re_b"][0]], 0)
    lim = np.concatenate([inputs["s5_lambda_im_f"][0], inputs["s5_lambda_im_b"][0]], 0)
    lst = np.concatenate([inputs["s5_log_step_f"][0], inputs["s5_log_step_b"][0]], 0)
    return {
        "s5_lamre": dup(lre.T), "s5_lamim": dup(lim.T),
        "s5_lstep": np.ascontiguousarray(np.broadcast_to(lst[None, :], (128, 64))).astype(np.float32),
        "s5_bre": dup(inputs["s5_b_re"][0].transpose(1, 0, 2)), "s5_bim": dup(inputs["s5_b_im"][0].transpose(1, 0, 2)),
        "s5_cre": dup(inputs["s5_c_re"][0].transpose(2, 0, 1)), "s5_cim": dup(inputs["s5_c_im"][0].transpose(2, 0, 1)),
        "s5_d128": np.ascontiguousarray(np.broadcast_to(inputs["s5_d"][0].reshape(1, 32, 16), (128, 32, 16))).astype(np.float32),
        "s5_bglu": np.ascontiguousarray(inputs["s5_b_glu"][0].reshape(4, 128).T).astype(np.float32),
        "w_glu": np.ascontiguousarray(inputs["s5_w_glu"][0]).astype(np.float32),
    }


def make_inputs(inputs):
    x = np.asarray(inputs["x"], np.float32)
    per = []
    for r in range(NCORES):
        b, seg = r // 4, r % 4
        cv = np.stack([inputs["c"][b].reshape(8, 128).T, inputs["c_ctx"].reshape(8, 128).T], -1)
        m = {
            "x_loc": np.ascontiguousarray(x[b, seg * T:(seg + 1) * T]),
            "cvec": np.ascontiguousarray(cv.reshape(128, 16)).astype(np.float32),
            "w_mod": np.ascontiguousarray(inputs["w_mod"][0]),
            "b_mod2": np.ascontiguousarray(np.broadcast_to(inputs["b_mod"][0][None, :], (2, 6 * D))).astype(np.float32),
            "n1w": np.ascontiguousarray(inputs["norm1_w"][0].reshape(8, 128).T),
            "n2w": np.ascontiguousarray(inputs["norm2_w"][0].reshape(8, 128).T),
            "w_in": np.ascontiguousarray(inputs["w_in"][0]),
            "segf": np.full((128, 1), float(seg), np.float32),
            "ctxb": np.ascontiguousarray(inputs["ctx"][b]).astype(np.float32),
            "ldv": np.ascontiguousarray(np.broadcast_to(np.concatenate([inputs["ret_log_decay_f"][0], inputs["ret_log_decay_b"][0]])[None, :], (128, 8))).astype(np.float32),
            **_s5_host(inputs),
            "w_out": np.ascontiguousarray(inputs["w_out"][0]).astype(np.float32),
            "fnw_b": np.ascontiguousarray(np.broadcast_to(inputs["final_norm_w"][None, :], (128, D))).astype(np.float32),
            "cw": np.ascontiguousarray(inputs["conv_w"][0].reshape(3, 22, 128).transpose(2, 1, 0).reshape(128, 66)).astype(np.float32),
            "cb": np.ascontiguousarray(inputs["conv_b"][0].reshape(22, 128).T).astype(np.float32),
            "oh": np.ascontiguousarray(np.broadcast_to(np.array([float(r_ == seg - 1) for r_ in range(4)] + [float(r_ == seg + 1) for r_ in range(4)],
                                                                 np.float32)[None, :], (128, 8))),
            "w_up": np.ascontiguousarray(inputs["w_up"][0]).astype(np.float32),
            "w_down": np.ascontiguousarray(inputs["w_down"][0]).astype(np.float32),
            "msk": np.ascontiguousarray(np.broadcast_to(np.array([0 < seg, 1 < seg, 2 < seg, 3 > seg, 2 > seg, 1 > seg, 0, 0], np.float32)[None, :], (128, 8))),
        }
        per.append(m)
    return per


def kernel(**inputs):
    nc, _ = build(debug=False)
    per = make_inputs(inputs)
    res = run_bass_kernel_spmd(nc, per, core_ids=list(range(NCORES)))
    out = np.zeros((2, 4 * T, D), np.float32)
    for r in range(NCORES):
        b, seg = r // 4, r % 4
        out[b, seg * T:(seg + 1) * T] = res.results[r]["out"]
    return out
```

```python
import contextlib
import numpy as np
import concourse.bass as bass
import concourse.mybir as mybir
from concourse.bass_utils import run_bass_kernel_spmd

F32 = mybir.dt.float32
BF16 = mybir.dt.bfloat16
I32 = mybir.dt.int32
AF = mybir.ActivationFunctionType
ALU = mybir.AluOpType

D = 1024
T = 2048
NT = 16
NCORES = 8
SB_BASE = 16512
SB_TOP = 229344


class Res:
    __slots__ = ("name", "last_w", "readers")

    def __init__(self, name):
        self.name = name
        self.last_w = None
        self.readers = []


class Op:
    __slots__ = ("eng", "fn", "deps", "needs_inc", "sig", "kind", "group", "inc")

    def __init__(self, eng, fn, kind, group, inc):
        self.eng = eng
        self.fn = fn
        self.deps = []
        self.needs_inc = False
        self.sig = None
        self.kind = kind
        self.group = group
        self.inc = inc


class Prog:
    ENGS = ("pe", "act", "dve", "pool", "sp")

    def __init__(self, nc):
        self.nc = nc
        self.ops = []
        self.res = {}

    def _r(self, x):
        if isinstance(x, Res):
            return x
        if x not in self.res:
            self.res[x] = Res(x)
        return self.res[x]

    def _add(self, op, reads, writes):
        reads = [self._r(x) for x in reads]
        writes = [self._r(x) for x in writes]
        deps = set()
        for x in reads:
            if x.last_w is not None:
                deps.add(x.last_w)
            if x.name.startswith("ps"):
                for rd in x.readers:
                    if rd.eng != op.eng:
                        deps.add(rd)
        for x in writes:
            if x.last_w is not None:
                deps.add(x.last_w)
            deps.update(x.readers)
        deps.discard(op)
        for d in deps:
            if d.kind == "c" and op.kind == "c" and d.eng == "pe" and op.eng == "pe":
                continue
            op.deps.append(d)
            d.needs_inc = True
        for x in reads:
            x.readers.append(op)
        for x in writes:
            x.last_w = op
            x.readers = []
        self.ops.append(op)
        return op

    def c(self, eng, fn, reads=(), writes=()):
        return self._add(Op(eng, fn, "c", None, 1), reads, writes)

    def dma(self, eng, fn, group, reads=(), writes=(), inc=16):
        op = Op(eng, fn, "d", group, inc)
        op.needs_inc = True
        return self._add(op, reads, writes)

    def emit(self, final_wait_groups=()):
        nc = self.nc
        sems = {}
        with contextlib.ExitStack() as st:
            cnt = {}
            for op in self.ops:
                if op.kind == "c":
                    if op.needs_inc:
                        k = "E_" + op.eng
                        cnt[k] = cnt.get(k, 0) + 1
                        op.sig = (k, cnt[k])
                else:
                    k = "D_" + op.group
                    cnt[k] = cnt.get(k, 0) + op.inc
                    op.sig = (k, cnt[k])
            for k in cnt:
                sems[k] = st.enter_context(nc.semaphore(k))
            engobj = {"pe": "tensor", "act": "scalar", "dve": "vector", "pool": "gpsimd", "sp": "sync"}
            finals = [("D_" + g, cnt["D_" + g]) for g in final_wait_groups]
            with nc.Block() as block:
                for e in self.ENGS:
                    ops = [o for o in self.ops if o.eng == e]

                    def body(eng, ops=ops, e=e):
                        seen = {}
                        for op in ops:
                            need = {}
                            for d in op.deps:
                                s, v = d.sig
                                if v > need.get(s, 0):
                                    need[s] = v
                            for s, v in need.items():
                                if seen.get(s, 0) >= v:
                                    continue
                                eng.wait_ge(sems[s], v)
                                seen[s] = v
                            ins = op.fn(eng)
                            if op.needs_inc:
                                ins.then_inc(sems[op.sig[0]], op.inc)
                        if e == "sp":
                            for s, v in finals:
                                eng.wait_ge(sems[s], v)
                    if ops or e == "sp":
                        getattr(block, engobj[e])(body)
        return cnt


class Arena:
    def __init__(self, nc):
        self.nc = nc
        self.off = SB_BASE
        self.n = 0

    def alloc(self, name, shape, dtype, at=None):
        esz = 4 if dtype in (F32, I32) else 2
        per = int(np.prod(shape[1:])) * esz
        per = (per + 63) // 64 * 64
        if at is None:
            at = self.off
            self.off += per
            assert self.off <= SB_TOP, (name, self.off)
        self.n += 1
        return self.nc.alloc_sbuf_tensor_at(f"{name}_{self.n}", list(shape), dtype, offset=at)


class _Stop(Exception):
    pass


def build(debug=False, stop=None):
    nc = bass.Bass("TRN2", target_bir_lowering=False)
    P = Prog(nc)
    A = Arena(nc)

    def din(name, shape, dt=F32):
        return nc.dram_tensor(name, list(shape), dt, kind="ExternalInput").ap()

    x_d = din("x_loc", [T, D])
    cvec_d = din("cvec", [128, 16])
    wmod_d = din("w_mod", [D, 6 * D])
    bmod_d = din("b_mod2", [2, 6 * D])
    n1w_d = din("n1w", [128, 8])
    n2w_d = din("n2w", [128, 8])
    win_d = din("w_in", [D, 2560])
    segf_d = din("segf", [128, 1])
    ctx_d = din("ctxb", [256, D])
    ldv_d = din("ldv", [128, 8])
    msk_d = din("msk", [128, 8])
    out_d = nc.dram_tensor("out", [T, D], F32, kind="ExternalOutput").ap()
    dbg = {}

    def dout(name, shape):
        dbg[name] = nc.dram_tensor(name, list(shape), F32, kind="ExternalOutput").ap()
        return dbg[name]

    def stage(name):
        if stop == name:
            raise _Stop()

    stopped = False
    try:
        _body(nc, P, A, debug, stage, dout, dbg, din, x_d, cvec_d, wmod_d, bmod_d, n1w_d, n2w_d, win_d, segf_d, ctx_d, ldv_d, msk_d, out_d)
    except _Stop:
        stopped = True
    if stopped:
        xfin = nc.alloc_sbuf_tensor_at("xfin", [128, D], F32, offset=SB_TOP - 4096)
        rfin = ["xfin"] + [n for n in P.res]
        for i in range(NT):
            P.dma("sp", lambda e, i=i: e.dma_start(out=xfin[:, :], in_=x_d[i * 128:(i + 1) * 128, :]), "xinF", reads=[], writes=rfin if i == 0 else ["xfin"])
            P.dma("sp", lambda e, i=i: e.dma_start(out=out_d[i * 128:(i + 1) * 128, :], in_=xfin[:, :]), "xout", reads=["xfin"])
    has_dbg = any(o.kind == "d" and o.group == "dbg" for o in P.ops)
    P.emit(final_wait_groups=["xout"] + (["dbg"] if has_dbg else []))
    return nc, dbg


def _body(nc, P, A, debug, stage, dout, dbg, din, x_d, cvec_d, wmod_d, bmod_d, n1w_d, n2w_d, win_d, segf_d, ctx_d, ldv_d, msk_d, out_d):

    ident_f = A.alloc("ident_f", [128, 128], F32)
    ident_b = A.alloc("ident_b", [128, 128], BF16)
    ones_f = A.alloc("ones_f", [128, 128], F32)
    P.c("pool", lambda e: e.memset(ones_f[:, :], 1.0), writes=["ones_f"])
    negpi = A.alloc("negpi", [128, 1], F32)
    epst = A.alloc("epst", [128, 1], F32)
    P.c("pool", lambda e: e.memset(negpi[:, :], -float(np.pi)), writes=["negpi"])
    P.c("pool", lambda e: e.memset(epst[:, :], 1e-6), writes=["epst"])
    P.c("pool", lambda e: e.memset(ident_f[:, :], 0.0), writes=["ident_f"])
    P.c("pool", lambda e: e.affine_select(out=ident_f[:, :], in_=ones_f[:, :], pattern=[[1, 128]],
                                          compare_op=ALU.is_equal, fill=0.0, base=0, channel_multiplier=-1),
        reads=["ones_f"], writes=["ident_f"])
    P.c("dve", lambda e: e.tensor_copy(out=ident_b[:, :], in_=ident_f[:, :]), reads=["ident_f"], writes=["ident_b"])

    cvec = A.alloc("cvec", [128, 16], F32)
    s_bf = A.alloc("s_bf", [128, 16], BF16)
    n1w = A.alloc("n1w", [128, 8], F32)
    n2w = A.alloc("n2w", [128, 8], F32)
    modT = A.alloc("modT", [128, 32, 2], F32)
    g1x = A.alloc("g1x", [128, 8], F32)
    g1c = A.alloc("g1c", [128, 8], F32)
    g2x = A.alloc("g2x", [128, 8], F32)
    bc2 = A.alloc("bc2", [128, D], F32)
    bc5 = A.alloc("bc5", [128, D], F32)
    segf = A.alloc("segf", [128, 1], F32)
    colf = A.alloc("colf", [128, 1], F32)
    rowb = A.alloc("rowb", [128, 1], F32)
    rowv = A.alloc("rowv", [128, 16], F32)
    invf = A.alloc("invf", [128, 32], F32)
    big0 = A.off
    wm = A.alloc("wm", [128, 8, 2048], BF16)
    bmod = A.alloc("bmod", [2, 6 * D], F32)
    modrow = A.alloc("modrow", [2, 6 * D], F32)
    ang = A.alloc("ang", [128, 16, 64], F32)
    tq = A.alloc("tq", [128, 1024], F32)
    ti = A.alloc("ti", [128, 1024], I32)
    tf = A.alloc("tf", [128, 1024], F32)
    assert A.off - big0 == 96 * 1024, A.off - big0
    r2_base = A.off
    scratchB_off = A.off
    A.off += 37 * 1024 + 512
    P.dma("sp", lambda e: e.dma_start(out=cvec[:, :], in_=cvec_d[:, :]), "small", writes=["cvec"])
    P.dma("sp", lambda e: e.dma_start(out=bmod[:, :], in_=bmod_d[:, :]), "small2", writes=["bmod"])
    P.dma("sp", lambda e: e.dma_start(out=n1w[:, :], in_=n1w_d[:, :]), "small3", writes=["n1w"])
    P.dma("sp", lambda e: e.dma_start(out=n2w[:, :], in_=n2w_d[:, :]), "small4", writes=["n2w"])
    P.c("act", lambda e: e.activation(out=s_bf[:, :], in_=cvec[:, :], func=AF.Silu), reads=["cvec"], writes=["s_bf"])

    wm_src = wmod_d.rearrange("(k p) n -> p k n", p=128)
    psA = [nc.alloc_psum_tensor(f"ps{i}", [128, 512], F32) for i in range(7)]
    s3 = s_bf[:, :].rearrange("p (k w) -> p k w", w=2)
    wmB = nc.alloc_sbuf_tensor_at("wmB", [128, 8, 2048], BF16, offset=scratchB_off)
    wms = [(wm, "wm"), (wmB, "wmB"), (wm, "wm")]
    def wm_dma(cb):
        wt_, wn_ = wms[cb]
        P.dma("pool", lambda e: e.dma_start(out=wt_[:, :, :], in_=wm_src[:, :, cb * 2048:(cb + 1) * 2048]), f"wmd{cb}", writes=[wn_])

    wm_dma(0)
    wm_dma(1)
    for nb in range(12):
        ps = psA[nb % 2]
        wt_, wn_ = wms[nb // 4]
        if nb == 4:
            wm_dma(2)
        for k in range(8):
            P.c("pe", lambda e, ps=ps, k=k, nb=nb, wt_=wt_: e.matmul(ps[0:2, :], lhsT=s3[:, k, :], rhs=wt_[:, k, (nb % 4) * 512:(nb % 4 + 1) * 512],
                                                                  start=(k == 0), stop=(k == 7)),
                reads=["s_bf", wn_], writes=[f"ps{nb % 2}"])
        P.c("dve", lambda e, ps=ps, nb=nb: e.tensor_tensor(out=modrow[:, nb * 512:(nb + 1) * 512], in0=ps[0:2, :],
                                                         in1=bmod[:, nb * 512:(nb + 1) * 512], op=ALU.add),
            reads=[f"ps{nb % 2}", "bmod"], writes=["modrow"])
    psT = psA[2]
    chunks = list(range(0, 16)) + list(range(24, 40))
    for j, ch in enumerate(chunks):
        P.c("pe", lambda e, j=j, ch=ch: e.transpose(psT[:, 2 * j:2 * j + 2], modrow[:, ch * 128:(ch + 1) * 128], ident_f[0:2, 0:2]),
            reads=["modrow", "ident_f"], writes=["ps2"])
    P.c("dve", lambda e: e.tensor_copy(out=modT[:, :, :].rearrange("p a b -> p (a b)"), in_=psT[:, 0:64]),
        reads=["ps2"], writes=["modT"])
    P.c("dve", lambda e: e.scalar_tensor_tensor(out=g1x[:, :], in0=modT[:, 8:16, 0], scalar=1.0, in1=n1w[:, :],
                                                op0=ALU.add, op1=ALU.mult), reads=["modT", "n1w"], writes=["g1x"])
    P.c("dve", lambda e: e.scalar_tensor_tensor(out=g1c[:, :], in0=modT[:, 8:16, 1], scalar=1.0, in1=n1w[:, :],
                                                op0=ALU.add, op1=ALU.mult), reads=["modT", "n1w"], writes=["g1c"])
    P.c("dve", lambda e: e.scalar_tensor_tensor(out=g2x[:, :], in0=modT[:, 24:32, 0], scalar=1.0, in1=n2w[:, :],
                                                op0=ALU.add, op1=ALU.mult), reads=["modT", "n2w"], writes=["g2x"])

    if debug:
        d_mod = dout("d_mod", [2, 6 * D])
        P.dma("sp", lambda e: e.dma_start(out=d_mod[:, :], in_=modrow[:, :]), "dbg", reads=["modrow"])
        d_g1x = dout("d_g1x", [128, 8])
        P.dma("sp", lambda e: e.dma_start(out=d_g1x[:, :], in_=g1x[:, :]), "dbg", reads=["g1x"])

    stage("A")
    for (bc, base, nm) in ((bc2, 2048, "bc2"), (bc5, 5120, "bc5")):
        for hh in range(2):
            ps = psA[3 + hh]
            P.c("pe", lambda e, ps=ps, base=base, hh=hh: e.matmul(ps[:, :], lhsT=ones_f[0:1, :], rhs=modrow[0:1, base + hh * 512: base + (hh + 1) * 512],
                                                                start=True, stop=True), reads=["ones_f", "modrow"], writes=[f"ps{3 + hh}"])
            P.c("act", lambda e, ps=ps, bc=bc, hh=hh: e.activation(out=bc[:, hh * 512:(hh + 1) * 512], in_=ps[:, :], func=AF.Copy),
                reads=[f"ps{3 + hh}"], writes=[nm])

    P.dma("sp", lambda e: e.dma_start(out=segf[:, :], in_=segf_d[:, :]), "small5", writes=["segf"])
    for hb in range(2):
        P.c("pool", lambda e, hb=hb: e.iota(colf[hb * 64:(hb + 1) * 64, :], pattern=[[0, 1]], base=0, channel_multiplier=1,
                                           allow_small_or_imprecise_dtypes=True), writes=["colf"])
        P.c("pool", lambda e, hb=hb: e.memset(rowb[hb * 64:(hb + 1) * 64, :], float(hb)), writes=["rowb"])
    P.c("pool", lambda e: e.iota(rowv[:, :], pattern=[[2, 16]], base=0, channel_multiplier=0, allow_small_or_imprecise_dtypes=True),
        writes=["rowv"])
    for j in range(32):
        P.c("pool", lambda e, j=j: e.memset(invf[:, j:j + 1], float(np.float32(10000.0) ** (-np.float32(j) / np.float32(32)))),
            writes=["invf"])
    P.c("dve", lambda e: e.scalar_tensor_tensor(out=rowb[:, :], in0=segf[:, :], scalar=32.0, in1=rowb[:, :], op0=ALU.mult, op1=ALU.add),
        reads=["segf", "rowb"], writes=["rowb"])
    P.c("dve", lambda e: e.tensor_scalar(out=rowv[:, :], in0=rowv[:, :], scalar1=rowb[:, 0:1], scalar2=None, op0=ALU.add),
        reads=["rowv", "rowb"], writes=["rowv"])
    ropeC = A.alloc("ropeC", [128, 16, 64], F32)
    ropeS = A.alloc("ropeS", [128, 16, 64], F32)
    ropeCk = A.alloc("ropeCk", [128, 16, 64], F32)
    ropeSk = A.alloc("ropeSk", [128, 16, 64], F32)
    for i in range(16):
        P.c("dve", lambda e, i=i: e.tensor_scalar(out=ang[:, i, 0:32], in0=invf[:, :], scalar1=rowv[:, i:i + 1], scalar2=None, op0=ALU.mult),
            reads=["invf", "rowv"], writes=["ang"])
        P.c("dve", lambda e, i=i: e.tensor_scalar(out=ang[:, i, 32:64], in0=invf[:, :], scalar1=colf[:, 0:1], scalar2=None, op0=ALU.mult),
            reads=["invf", "colf"], writes=["ang"])
    angf = ang[:, :, :].rearrange("p a b -> p (a b)")
    TWO_PI = 2.0 * np.pi
    for (dst, off, nm) in ((ropeS, 0.5, "ropeS"), (ropeC, 0.75, "ropeC")):
        dflat = dst[:, :, :].rearrange("p a b -> p (a b)")
        P.c("dve", lambda e, off=off: e.tensor_scalar(out=tq[:, :], in0=angf, scalar1=1.0 / TWO_PI, scalar2=off, op0=ALU.mult, op1=ALU.add),
            reads=["ang"], writes=["tq"])
        P.c("dve", lambda e: e.tensor_copy(out=ti[:, :], in_=tq[:, :]), reads=["tq"], writes=["ti"])
        P.c("dve", lambda e: e.tensor_copy(out=tf[:, :], in_=ti[:, :]), reads=["ti"], writes=["tf"])
        P.c("dve", lambda e: e.tensor_tensor(out=tq[:, :], in0=tq[:, :], in1=tf[:, :], op=ALU.subtract), reads=["tq", "tf"], writes=["tq"])
        P.c("dve", lambda e: e.tensor_scalar(out=tf[:, :], in0=tq[:, :], scalar1=0.0, scalar2=None, op0=ALU.is_lt), reads=["tq"], writes=["tf"])
        P.c("dve", lambda e: e.tensor_tensor(out=tq[:, :], in0=tq[:, :], in1=tf[:, :], op=ALU.add), reads=["tq", "tf"], writes=["tq"])
        P.c("act", lambda e, dflat=dflat: e.activation(out=dflat, in_=tq[:, :], func=AF.Sin, scale=TWO_PI, bias=negpi[:, 0:1]),
            reads=["tq", "negpi"], writes=[nm])
    KS = float(128.0 ** -0.5)
    P.c("dve", lambda e: e.tensor_scalar(out=ropeCk[:, :, :], in0=ropeC[:, :, :], scalar1=KS, scalar2=None, op0=ALU.mult), reads=["ropeC"], writes=["ropeCk"])
    P.c("dve", lambda e: e.tensor_scalar(out=ropeSk[:, :, :], in0=ropeS[:, :, :], scalar1=KS, scalar2=None, op0=ALU.mult), reads=["ropeS"], writes=["ropeSk"])

    win = A.alloc("win", [128, 8, 2560], BF16)
    win_src = win_d.rearrange("(k p) n -> p k n", p=128)
    for cb in range(2):
        P.dma("pool", lambda e, cb=cb: e.dma_start(out=win[:, :, cb * 1280:(cb + 1) * 1280], in_=win_src[:, :, cb * 1280:(cb + 1) * 1280]),
              f"win{cb}", writes=["win"])

    A2 = Arena(nc)
    A2.off = big0
    qT = A2.alloc("qT", [128, 4, T], BF16)
    kT = A2.alloc("kT", [128, 4, T], BF16)
    k_tm = A2.alloc("k_tm", [128, NT, 512], BF16)
    v_tm = A2.alloc("v_tm", [128, NT, 512], BF16)
    gate = A2.alloc("gate", [128, NT, 512], BF16)
    uT = A2.alloc("uT", [128, 4, T], BF16)
    assert A2.off <= big0 + 96 * 1024
    stage_a_res = [P._r(n) for n in ("wm", "modrow", "bmod", "ang", "tq", "ti", "tf")]
    for nm in ("qT", "kT", "k_tm", "v_tm", "gate", "uT"):
        r_ = P._r(nm)
        for o in stage_a_res:
            r_.readers.extend(o.readers)
            if o.last_w is not None:
                r_.readers.append(o.last_w)

    r2_end = A.off
    SBA = Arena(nc); SBA.off = scratchB_off
    xts = [SBA.alloc(f"xt{i}", [128, D], F32) for i in range(2)]
    xsl = [SBA.alloc(f"xs{i}", [128, D], F32) for i in range(2)]
    ssl = [SBA.alloc(f"ss{i}", [128, 4], F32) for i in range(2)]
    hxT = [SBA.alloc(f"hxT{i}", [128, 8, 512], BF16) for i in range(2)]
    qrot = SBA.alloc("qrot", [128, 512], BF16)
    t1 = SBA.alloc("t1", [128, 256], F32)
    t2 = SBA.alloc("t2", [128, 256], F32)
    assert SBA.off <= scratchB_off + 37 * 1024 + 512, SBA.off - scratchB_off
    for nm_ in ("xt0", "xt1", "xs0", "xs1", "ss0a", "ss0b", "ss0c", "ss1a", "ss1b", "ss1c", "hxT0", "hxT1", "qrot", "t1", "t2"):
        r_ = P._r(nm_)
        o = P._r("wmB")
        r_.readers.extend(o.readers)
        if o.last_w is not None:
            r_.readers.append(o.last_w)
    psX = [psA[0], psA[1]]
    psP = [psA[2], psA[3], psA[4], psA[5]]
    psU = psA[6]
    psTq = nc.alloc_psum_tensor("psTq", [128, 1024], BF16)

    def norm_front(src_ap, pp):
        xt, xs_, s_ = xts[pp], xsl[pp], ssl[pp]
        P.dma("sp", lambda e: e.dma_start(out=xt[:, :], in_=src_ap), f"xin{pp}", writes=[f"xt{pp}"])
        P.c("act", lambda e: e.activation(out=xs_[:, :], in_=xt[:, :], func=AF.Square, accum_out=s_[:, 0:1]),
            reads=[f"xt{pp}"], writes=[f"xs{pp}", f"ss{pp}a"])
        P.c("act", lambda e: e.activation(out=s_[:, 1:2], in_=s_[:, 0:1], func=AF.Sqrt, scale=1.0 / D, bias=epst[:, 0:1]),
            reads=[f"ss{pp}a", "epst"], writes=[f"ss{pp}b"])
        P.c("dve", lambda e: e.reciprocal(out=s_[:, 2:3], in_=s_[:, 1:2]), reads=[f"ss{pp}b"], writes=[f"ss{pp}c"])
        P.c("dve", lambda e: e.tensor_scalar(out=xs_[:, :], in0=xt[:, :], scalar1=s_[:, 2:3], scalar2=None, op0=ALU.mult),
            reads=[f"xt{pp}", f"ss{pp}c"], writes=[f"xs{pp}"])

    def norm_back(pp, gvec, shvec_fn, dstT, col0, tag):
        xs_ = xsl[pp]
        for k in range(8):
            ps = psX[k // 4]
            P.c("pe", lambda e, ps=ps, k=k: e.transpose(ps[:, (k % 4) * 128:(k % 4 + 1) * 128], xs_[:, k * 128:(k + 1) * 128], ident_f[:, :]),
                reads=[f"xs{pp}", "ident_f"], writes=[f"ps{k // 4}"])
        for k in range(8):
            ps = psX[k // 4]
            P.c("act", lambda e, ps=ps, k=k: e.activation(out=dstT[:, k, col0:col0 + 128], in_=ps[:, (k % 4) * 128:(k % 4 + 1) * 128],
                                                        func=AF.Identity, scale=gvec[:, k:k + 1], bias=shvec_fn(k)),
                reads=[f"ps{k // 4}", "g1x", "g1c", "g2x", "modT"], writes=[tag])

    def rope(ps, Ct, St, i, dst_ap):
        pv = ps[:, :].rearrange("p (h j w) -> p h j w", h=4, w=2)
        dv = dst_ap.rearrange("p (h j w) -> p h j w", h=4, w=2)
        Cb = Ct[:, i, :].unsqueeze(1).broadcast_to([128, 4, 64])
        Sb = St[:, i, :].unsqueeze(1).broadcast_to([128, 4, 64])
        a = t1[:, :].rearrange("p (h j) -> p h j", h=4)
        b = t2[:, :].rearrange("p (h j) -> p h j", h=4)
        return pv, dv, Cb, Sb, a, b

    norm_front(x_d[0:128, :], 0)
    for grp in range(4):
        hT = hxT[grp % 2]
        for ti_ in range(4):
            i = grp * 4 + ti_
            if i + 1 < NT:
                norm_front(x_d[(i + 1) * 128:(i + 2) * 128, :], (i + 1) % 2)
            norm_back(i % 2, g1x, lambda k: modT[:, k, 0:1], hT, ti_ * 128, f"hxT{grp % 2}")
            for nb in range(4):
                ps = psP[nb]
                for k in range(8):
                    P.c("pe", lambda e, ps=ps, k=k, nb=nb, hT=hT, ti_=ti_: e.matmul(
                        ps[:, :], lhsT=hT[:, k, ti_ * 128:(ti_ + 1) * 128], rhs=win[:, k, 512 + nb * 512: 1024 + nb * 512],
                        start=(k == 0), stop=(k == 7)), reads=[f"hxT{grp % 2}", "win"], writes=[f"ps{2 + nb}"])
            for (nb, Ct, St, dstT, nm) in ((0, ropeC, ropeS, qT, "q"), (1, ropeCk, ropeSk, kT, "k")):
                ps = psP[nb]
                dst_ap = qrot[:, :] if nb == 0 else k_tm[:, i, :]
                dres = "qrot" if nb == 0 else "k_tm"
                pv, dv, Cb, Sb, a, b = rope(ps, Ct, St, i, dst_ap)
                rd = [f"ps{2 + nb}", "ropeC", "ropeS", "ropeCk", "ropeSk"]
                P.c("dve", lambda e, pv=pv, Cb=Cb, a=a: e.tensor_tensor(out=a, in0=pv[:, :, :, 0], in1=Cb, op=ALU.mult), reads=rd, writes=["t1"])
                P.c("dve", lambda e, pv=pv, Sb=Sb, b=b: e.tensor_tensor(out=b, in0=pv[:, :, :, 1], in1=Sb, op=ALU.mult), reads=rd, writes=["t2"])
                P.c("dve", lambda e, dv=dv, a=a, b=b: e.tensor_tensor(out=dv[:, :, :, 0], in0=a, in1=b, op=ALU.subtract),
                    reads=["t1", "t2"], writes=[dres])
                P.c("dve", lambda e, pv=pv, Sb=Sb, a=a: e.tensor_tensor(out=a, in0=pv[:, :, :, 0], in1=Sb, op=ALU.mult), reads=rd + [dres], writes=["t1"])
                P.c("dve", lambda e, pv=pv, Cb=Cb, b=b: e.tensor_tensor(out=b, in0=pv[:, :, :, 1], in1=Cb, op=ALU.mult), reads=rd + [dres], writes=["t2"])
                P.c("dve", lambda e, dv=dv, a=a, b=b: e.tensor_tensor(out=dv[:, :, :, 1], in0=a, in1=b, op=ALU.add),
                    reads=["t1", "t2"], writes=[dres])
                for h in range(4):
                    P.c("pe", lambda e, h=h, dst_ap=dst_ap: e.transpose(psTq[:, h * 128:(h + 1) * 128], dst_ap[:, h * 128:(h + 1) * 128], ident_b[:, :]),
                        reads=[dres, "ident_b"], writes=["psTq"])
                P.c("act", lambda e, dstT=dstT, i=i: e.activation(out=dstT[:, :, i * 128:(i + 1) * 128],
                                                                 in_=psTq[:, 0:512].rearrange("p (h t) -> p h t", h=4), func=AF.Copy),
                    reads=["psTq"], writes=[nm + "T"])
            P.c("act", lambda e, i=i: e.activation(out=v_tm[:, i, :], in_=psP[2][:, :], func=AF.Copy), reads=["ps4"], writes=["v_tm"])
            P.c("act", lambda e, i=i: e.activation(out=gate[:, i, :], in_=psP[3][:, :], func=AF.Silu), reads=["ps5"], writes=["gate"])
        for fc in range(4):
            for k in range(8):
                P.c("pe", lambda e, fc=fc, k=k, hT=hT: e.matmul(psU[:, :], lhsT=win[:, k, fc * 128:(fc + 1) * 128], rhs=hT[:, k, :],
                                                              start=(k == 0), stop=(k == 7)), reads=[f"hxT{grp % 2}", "win"], writes=["ps6"])
            P.c("dve", lambda e, fc=fc, grp=grp: e.tensor_copy(out=uT[:, fc, grp * 512:(grp + 1) * 512], in_=psU[:, :]),
                reads=["ps6"], writes=["uT"])

    stage("B")

    def alias(new_names, old_names):
        olds = [P._r(n) for n in old_names]
        for nm in new_names:
            r_ = P._r(nm)
            for o in olds:
                r_.readers.extend(o.readers)
                if o.last_w is not None:
                    r_.readers.append(o.last_w)

    spare_off = A.off
    kc_tm = A.alloc("kc_tm", [128, 2, 512], BF16)
    vc_tm = A.alloc("vc_tm", [128, 2, 512], BF16)
    ucT = A.alloc("ucT", [128, 4, 256], BF16)
    KSC = float(128.0 ** -0.5)
    hT = hxT[0]
    for j in range(2):
        norm_front(ctx_d[j * 128:(j + 1) * 128, :], j)
        norm_back(j, g1c, lambda k: modT[:, k, 1:2], hT, j * 128, "hxT0")
        for (nb, dst, nm) in ((1, kc_tm, "kc_tm"), (2, vc_tm, "vc_tm")):
            ps = psP[nb]
            for k in range(8):
                P.c("pe", lambda e, ps=ps, k=k, nb=nb, j=j: e.matmul(ps[:, :], lhsT=hT[:, k, j * 128:(j + 1) * 128],
                                                                  rhs=win[:, k, 512 + nb * 512: 1024 + nb * 512], start=(k == 0), stop=(k == 7)),
                    reads=["hxT0", "win"], writes=[f"ps{2 + nb}"])
            P.c("act", lambda e, ps=ps, dst=dst, j=j, nb=nb: e.activation(out=dst[:, j, :], in_=ps[:, :], func=AF.Copy, scale=(KSC if nb == 1 else 1.0)),
                reads=[f"ps{2 + nb}"], writes=[nm])
    for fc in range(4):
        for k in range(8):
            P.c("pe", lambda e, fc=fc, k=k: e.matmul(psU[:, 0:256], lhsT=win[:, k, fc * 128:(fc + 1) * 128], rhs=hT[:, k, 0:256],
                                                   start=(k == 0), stop=(k == 7)), reads=["hxT0", "win"], writes=["ps6"])
        P.c("dve", lambda e, fc=fc: e.tensor_copy(out=ucT[:, fc, :], in_=psU[:, 0:256]), reads=["ps6"], writes=["ucT"])

    stage("C")
    ldv = A.alloc("ldv", [128, 8], F32)
    msk = A.alloc("msk", [128, 8], F32)
    P.dma("sp", lambda e: e.dma_start(out=ldv[:, :], in_=ldv_d[:, :]), "small6", writes=["ldv"])
    P.dma("sp", lambda e: e.dma_start(out=msk[:, :], in_=msk_d[:, :]), "small7", writes=["msk"])
    AR = Arena(nc)
    AR.off = r2_base
    old_r2 = ["ropeC", "ropeS", "ropeCk", "ropeSk", "win", "xt0", "xt1", "xs0", "xs1", "ss0a", "ss0b", "ss0c", "ss1a", "ss1b", "ss1c", "hxT0", "hxT1", "qrot", "t1", "t2",
              "invf", "rowv", "colf", "rowb"]
    mixT = AR.alloc("mixT", [128, 8, T], BF16)
    Rb_store = AR.alloc("Rb_store", [128, NT, 512], BF16)
    xg_off = AR.off
    xg = AR.alloc("xg", [128, 4, 1024], F32)
    Dm = AR.alloc("Dm", [128, 4, 128], F32)
    XiF = AR.alloc("XiF", [128, 4, 128], F32)
    XiB = AR.alloc("XiB", [128, 4, 128], F32)
    dm = AR.alloc("dm", [128, 128], F32)
    tabs_off = AR.off
    dpos = AR.alloc("dpos", [128, 128], F32)
    dneg = AR.alloc("dneg", [128, 128], F32)
    mge = AR.alloc("mge", [128, 128], F32)
    mlt = AR.alloc("mlt", [128, 128], F32)
    e1 = AR.alloc("e1", [128, 128], F32)
    e2 = AR.alloc("e2", [128, 128], F32)
    tfree = AR.alloc("tfree", [128, 128], F32)
    pidx = AR.alloc("pidx", [128, 2], F32)
    zarg = AR.alloc("zarg", [128, 8], F32)
    Zeta = AR.alloc("Zeta", [128, 8], F32)
    G128 = AR.alloc("G128", [128, 8], F32)
    G2048 = AR.alloc("G2048", [128, 8], F32)
    nld = AR.alloc("nld", [128, 8], F32)
    ld128 = AR.alloc("ld128", [128, 8], F32)
    Rf = AR.alloc("Rf", [128, 4, 128], F32)
    Rb = AR.alloc("Rb", [128, 4, 128], F32)
    Rcf_off = AR.off
    Rcf = AR.alloc("Rcf", [128, 4, 128], F32)
    Rcb_off = AR.off
    Rcb = AR.alloc("Rcb", [128, 4, 128], F32)
    Rfbf_off = AR.off
    Rf_bf = AR.alloc("Rf_bf", [128, 4, 128], BF16)
    vzs = [AR.alloc(f"vz{i}", [128, 4, 128], BF16) for i in range(2)]
    scm = AR.alloc("scm", [128, 4, 128], BF16)
    qf = AR.alloc("qf", [128, 4, 128], BF16)
    qb = AR.alloc("qb", [128, 4, 128], BF16)
    yn = AR.alloc("yn", [128, 4, 128], F32)
    hacc_t = yn
    retx = AR.alloc("retx", [128, 512], BF16)
    bst = AR.alloc("bst", [128, 4, 6], F32)
    junk2 = AR.alloc("junk2", [128, 128], BF16)
    mv = AR.alloc("mv", [128, 4, 2], F32)
    rs = AR.alloc("rs", [128, 8], F32)
    assert AR.off <= r2_end, (AR.off, r2_end)
    new_r2 = ["mixT", "Rb_store", "xg", "Dm", "XiF", "XiB", "dm", "dpos", "dneg", "mge", "mlt", "e1", "e2", "tfree", "pidx", "zarg", "Zeta",
              "G128", "G2048", "nld", "ld128", "Rf", "Rb", "Rcf", "Rcb", "Rf_bf", "hacc_t", "vz0", "vz1", "scm", "qf", "qb", "yn", "retx", "bst", "bst2", "bst3", "junk2", "mv", "rs"]
    alias(new_r2, old_r2)

    P.c("pool", lambda e: e.iota(pidx[:, 0:1], pattern=[[0, 1]], base=0, channel_multiplier=1, allow_small_or_imprecise_dtypes=True), writes=["pidx"])
    P.c("pool", lambda e: e.iota(tfree[:, :], pattern=[[1, 128]], base=0, channel_multiplier=0, allow_small_or_imprecise_dtypes=True), writes=["tfree"])
    P.c("pool", lambda e: e.iota(dm[:, :], pattern=[[1, 128]], base=0, channel_multiplier=-1, allow_small_or_imprecise_dtypes=True), writes=["dm"])
    P.c("dve", lambda e: e.tensor_scalar(out=pidx[:, 1:2], in0=pidx[:, 0:1], scalar1=-1.0, scalar2=127.0, op0=ALU.mult, op1=ALU.add),
        reads=["pidx"], writes=["pidx"])
    P.c("dve", lambda e: e.tensor_scalar(out=zarg[:, 0:4], in0=ldv[:, 0:4], scalar1=pidx[:, 1:2], scalar2=None, op0=ALU.mult), reads=["ldv", "pidx"], writes=["zarg"])
    P.c("dve", lambda e: e.tensor_scalar(out=zarg[:, 4:8], in0=ldv[:, 4:8], scalar1=pidx[:, 0:1], scalar2=None, op0=ALU.mult), reads=["ldv", "pidx"], writes=["zarg"])
    P.c("act", lambda e: e.activation(out=Zeta[:, :], in_=zarg[:, :], func=AF.Exp), reads=["zarg"], writes=["Zeta"])
    P.c("act", lambda e: e.activation(out=G128[:, :], in_=ldv[:, :], func=AF.Exp, scale=128.0), reads=["ldv"], writes=["G128"])
    P.c("act", lambda e: e.activation(out=G2048[:, :], in_=ldv[:, :], func=AF.Exp, scale=2048.0), reads=["ldv"], writes=["G2048"])
    P.c("dve", lambda e: e.tensor_scalar(out=nld[:, :], in0=ldv[:, :], scalar1=-1.0, scalar2=None, op0=ALU.mult), reads=["ldv"], writes=["nld"])
    P.c("dve", lambda e: e.tensor_scalar(out=ld128[:, :], in0=ldv[:, :], scalar1=128.0, scalar2=None, op0=ALU.mult), reads=["ldv"], writes=["ld128"])
    P.c("dve", lambda e: e.tensor_scalar(out=dpos[:, :], in0=dm[:, :], scalar1=0.0, scalar2=None, op0=ALU.max), reads=["dm"], writes=["dpos"])
    P.c("dve", lambda e: e.tensor_scalar(out=dneg[:, :], in0=dm[:, :], scalar1=-1.0, scalar2=0.0, op0=ALU.mult, op1=ALU.max), reads=["dm"], writes=["dneg"])
    P.c("dve", lambda e: e.tensor_scalar(out=mge[:, :], in0=dm[:, :], scalar1=0.0, scalar2=None, op0=ALU.is_ge), reads=["dm"], writes=["mge"])
    P.c("dve", lambda e: e.tensor_scalar(out=mlt[:, :], in0=dm[:, :], scalar1=0.0, scalar2=None, op0=ALU.is_lt), reads=["dm"], writes=["mlt"])
    for h in range(4):
        P.c("act", lambda e, h=h: e.activation(out=XiF[:, h, :], in_=tfree[:, :], func=AF.Exp, scale=ldv[:, h:h + 1], bias=ldv[:, h:h + 1]),
            reads=["tfree", "ldv"], writes=["XiF"])
        P.c("act", lambda e, h=h: e.activation(out=XiB[:, h, :], in_=tfree[:, :], func=AF.Exp, scale=nld[:, 4 + h:5 + h], bias=ld128[:, 4 + h:5 + h]),
            reads=["tfree", "nld", "ld128"], writes=["XiB"])
        P.c("act", lambda e, h=h: e.activation(out=e1[:, :], in_=dpos[:, :], func=AF.Exp, scale=ldv[:, h:h + 1]), reads=["dpos", "ldv"], writes=["e1"])
        P.c("act", lambda e, h=h: e.activation(out=e2[:, :], in_=dneg[:, :], func=AF.Exp, scale=ldv[:, 4 + h:5 + h]), reads=["dneg", "ldv"], writes=["e2"])
        P.c("dve", lambda e: e.tensor_tensor(out=e1[:, :], in0=e1[:, :], in1=mge[:, :], op=ALU.mult), reads=["e1", "mge"], writes=["e1"])
        P.c("dve", lambda e: e.tensor_tensor(out=e2[:, :], in0=e2[:, :], in1=mlt[:, :], op=ALU.mult), reads=["e2", "mlt"], writes=["e2"])
        P.c("dve", lambda e, h=h: e.tensor_tensor(out=Dm[:, h, :], in0=e1[:, :], in1=e2[:, :], op=ALU.add), reads=["e1", "e2"], writes=["Dm"])

    stage("Dtab")
    psS, psO, psKV = psA[0], psA[1], psA[2]

    vz_extra = [nc.alloc_sbuf_tensor_at("vz2", [128, 4, 128], BF16, offset=tabs_off), nc.alloc_sbuf_tensor_at("vz3", [128, 4, 128], BF16, offset=tabs_off + 1024)]
    vz_tab = [[vzs[0], vz_extra[0]], [vzs[1], vz_extra[1]]]
    vz_nm = [["vz0", "vz2"], ["vz1", "vz3"]]
    kv_bank = [[(psA[2], "ps2"), (psA[4], "ps4")], [(psA[3], "ps3"), (psA[5], "ps5")]]

    def kv_front(ksrc, vsrc, d, kres, vres, lane, par):
        zb = Zeta[:, d * 4:(d + 1) * 4].unsqueeze(2).broadcast_to([128, 4, 128])
        vz_, vzn = vz_tab[lane % 2][par], vz_nm[lane % 2][par]
        pk_, pkn = kv_bank[lane % 2][par]
        if lane % 2 == 0:
            for h in range(4):
                P.c("act", lambda e, h=h: e.activation(out=vz_[:, h, :], in_=vsrc[:, h * 128:(h + 1) * 128], func=AF.Identity, scale=Zeta[:, d * 4 + h:d * 4 + h + 1]),
                    reads=[vres, "Zeta"], writes=[vzn])
        else:
            P.c("pool", lambda e: e.tensor_tensor(out=vz_[:, :, :], in0=vsrc.rearrange("p (h e) -> p h e", h=4), in1=zb, op=ALU.mult),
                reads=[vres, "Zeta"], writes=[vzn])
        for h in range(4):
            P.c("pe", lambda e, h=h: e.matmul(pk_[:, h * 128:(h + 1) * 128], lhsT=ksrc[:, h * 128:(h + 1) * 128], rhs=vz_[:, h, :], start=True, stop=True),
                reads=[kres, vzn], writes=[pkn])

    def kv_back(Rst, rname, d, lane, par):
        pk_, pkn = kv_bank[lane % 2][par]
        for h in range(4):
            P.c("dve", lambda e, h=h: e.scalar_tensor_tensor(out=Rst[:, h, :], in0=Rst[:, h, :], scalar=G128[:, d * 4 + h:d * 4 + h + 1],
                                                            in1=pk_[:, h * 128:(h + 1) * 128], op0=ALU.mult, op1=ALU.add),
                reads=[rname, "G128", pkn], writes=[rname])

    def kv_step(Rst, rname, ksrc, vsrc, d, kres, vres, lane=0):
        kv_front(ksrc, vsrc, d, kres, vres, lane, 0)
        kv_back(Rst, rname, d, lane, 0)

    def zero(t_, nm):
        P.c("pool", lambda e: e.memset(t_[:, :, :], 0.0), writes=[nm])

    alias(["vz2", "vz3"], ["dpos", "dneg", "mge", "mlt", "e1", "e2"])
    zero(Rf, "Rf"); zero(Rb, "Rb"); zero(Rcf, "Rcf"); zero(Rcb, "Rcb")
    def fronts_A(n_):
        kv_front(k_tm[:, n_, :], v_tm[:, n_, :], 0, "k_tm", "v_tm", 0, n_ % 2)
        kv_front(k_tm[:, NT - 1 - n_, :], v_tm[:, NT - 1 - n_, :], 1, "k_tm", "v_tm", 1, n_ % 2)

    fronts_A(0)
    for n_ in range(NT):
        if n_ + 1 < NT:
            fronts_A(n_ + 1)
        kv_back(Rf, "Rf", 0, 0, n_ % 2)
        kv_back(Rb, "Rb", 1, 1, n_ % 2)
    for n_ in range(2):
        kv_step(Rcf, "Rcf", kc_tm[:, n_, :], vc_tm[:, n_, :], 0, "kc_tm", "vc_tm", 0)
        kv_step(Rcb, "Rcb", kc_tm[:, 1 - n_, :], vc_tm[:, 1 - n_, :], 1, "kc_tm", "vc_tm", 1)

    stage("DpassA")
    xa_in = nc.dram_tensor("xa_in", [128, 1024], F32)
    xa_out = nc.dram_tensor("xa_out", [512, 1024], F32)
    P.dma("sp", lambda e: e.dma_start(out=xa_in[:, 0:512], in_=Rf[:, :, :].rearrange("p h e -> p (h e)")), "xa_st", reads=["Rf"], writes=["xa_in"])
    P.dma("sp", lambda e: e.dma_start(out=xa_in[:, 512:1024], in_=Rb[:, :, :].rearrange("p h e -> p (h e)")), "xa_st", reads=["Rb"], writes=["xa_in"])
    P.dma("pool", lambda e: e.collective_compute("AllGather", ALU.bypass, replica_groups=[[0, 1, 2, 3], [4, 5, 6, 7]],
                                                  ins=[xa_in.ap().opt()], outs=[xa_out.ap().opt()]),
          "ccA", reads=["xa_in"], writes=["xa_out"], inc=1)
    P.dma("sp", lambda e: e.dma_start(out=xg[:, :, :], in_=xa_out.ap().rearrange("(r p) n -> p r n", p=128)), "xa_ld", reads=["xa_out"], writes=["xg"])

    stage("Dxchg")

    def horner(acc, aname, d, order, mcol0):
        gb = G2048[:, d * 4:(d + 1) * 4].unsqueeze(2).broadcast_to([128, 4, 128])
        for n_, i in enumerate(order):
            P.c("dve", lambda e: e.tensor_tensor(out=hacc_t[:, :, :], in0=acc[:, :, :], in1=gb, op=ALU.mult), reads=[aname, "G2048"], writes=["hacc_t"])
            P.c("dve", lambda e, i=i: e.tensor_tensor(out=hacc_t[:, :, :], in0=hacc_t[:, :, :],
                                                     in1=xg[:, i, d * 512:(d + 1) * 512].rearrange("p (h e) -> p h e", h=4), op=ALU.add),
                reads=["hacc_t", "xg"], writes=["hacc_t"])
            P.c("dve", lambda e: e.tensor_tensor(out=hacc_t[:, :, :], in0=hacc_t[:, :, :], in1=acc[:, :, :], op=ALU.subtract),
                reads=["hacc_t", aname], writes=["hacc_t"])
            P.c("dve", lambda e, n_=n_: e.scalar_tensor_tensor(out=acc[:, :, :], in0=hacc_t[:, :, :], scalar=msk[:, mcol0 + n_:mcol0 + n_ + 1],
                                                              in1=acc[:, :, :], op0=ALU.mult, op1=ALU.add),
                reads=["hacc_t", aname, "msk"], writes=[aname])

    horner(Rcf, "Rcf", 0, [0, 1, 2], 0)
    horner(Rcb, "Rcb", 1, [3, 2, 1], 3)

    stage("Dhorner")
    alias(["yn"], ["hacc_t"])
    P.c("dve", lambda e: e.tensor_copy(out=Rb[:, :, :], in_=Rcb[:, :, :]), reads=["Rcb"], writes=["Rb"])
    P.c("dve", lambda e: e.tensor_copy(out=Rf[:, :, :], in_=Rcf[:, :, :]), reads=["Rcf"], writes=["Rf"])
    Rf_store = nc.alloc_sbuf_tensor_at("Rf_store", [128, NT, 512], BF16, offset=xg_off)
    alias(["Rf_store"], ["xg"])
    def fronts_B(n_):
        kv_front(k_tm[:, NT - 1 - n_, :], v_tm[:, NT - 1 - n_, :], 1, "k_tm", "v_tm", 0, n_ % 2)
        kv_front(k_tm[:, n_, :], v_tm[:, n_, :], 0, "k_tm", "v_tm", 1, n_ % 2)

    fronts_B(0)
    for n_ in range(NT):
        ib = NT - 1 - n_
        P.c("act", lambda e, ib=ib: e.activation(out=Rb_store[:, ib, :], in_=Rb[:, :, :].rearrange("p h e -> p (h e)"), func=AF.Copy),
            reads=["Rb"], writes=["Rb_store"])
        P.c("act", lambda e, n_=n_: e.activation(out=Rf_store[:, n_, :], in_=Rf[:, :, :].rearrange("p h e -> p (h e)"), func=AF.Copy),
            reads=["Rf"], writes=["Rf_store"])
        if n_ < NT - 1:
            if n_ + 1 < NT - 1:
                fronts_B(n_ + 1)
            kv_back(Rb, "Rb", 1, 0, n_ % 2)
            kv_back(Rf, "Rf", 0, 1, n_ % 2)
    stage("DBb")
    scm_l = [scm, nc.alloc_sbuf_tensor_at("scm1", [128, 4, 128], BF16, offset=tabs_off)]
    qf_l = [qf, nc.alloc_sbuf_tensor_at("qf1", [128, 4, 128], BF16, offset=tabs_off + 1024)]
    qb_l = [qb, nc.alloc_sbuf_tensor_at("qb1", [128, 4, 128], BF16, offset=tabs_off + 2048)]
    yn_l = [yn, nc.alloc_sbuf_tensor_at("yn1", [128, 4, 128], F32, offset=Rcf_off)]
    retx_l = [retx, nc.alloc_sbuf_tensor_at("retx1", [128, 512], BF16, offset=Rfbf_off)]
    bst_l = [bst, nc.alloc_sbuf_tensor_at("bst1", [128, 4, 6], F32, offset=Rcb_off)]
    mv_l = [mv, nc.alloc_sbuf_tensor_at("mv1", [128, 4, 2], F32, offset=Rcb_off + 128)]
    rs_l = [rs, nc.alloc_sbuf_tensor_at("rs1", [128, 8], F32, offset=Rcb_off + 192)]
    alias(["scm1", "qf1", "qb1"], ["dpos", "dneg", "mge", "mlt", "e1", "e2", "vz2", "vz3"])
    alias(["yn1"], ["Rcf"])
    alias(["bst1", "bst21", "bst31", "mv1", "rs1", "rs41"], ["Rcb"])
    psS_l = [(psA[0], "ps0"), (psA[6], "ps6")]
    psO_l = [(psA[1], "ps1"), (psA[4], "ps4")]

    def ret_front(i):
        pp = i % 2
        sfx = "" if pp == 0 else "1"
        tsl = slice(i * 128, (i + 1) * 128)
        pS, pSn = psS_l[pp]
        pO, pOn = psO_l[pp]
        scm_, qf_, qb_, yn_ = scm_l[pp], qf_l[pp], qb_l[pp], yn_l[pp]
        for h in range(4):
            P.c("pe", lambda e, h=h: e.matmul(pS[:, h * 128:(h + 1) * 128], lhsT=kT[:, h, tsl], rhs=qT[:, h, tsl], start=True, stop=True),
                reads=["kT", "qT"], writes=[pSn])
        P.c("dve", lambda e: e.tensor_tensor(out=scm_[:, :, :], in0=pS[:, :].rearrange("p (h t) -> p h t", h=4), in1=Dm[:, :, :], op=ALU.mult),
            reads=[pSn, "Dm"], writes=["scm" + sfx])
        P.c("pool", lambda e: e.tensor_tensor(out=qf_[:, :, :], in0=qT[:, :, tsl], in1=XiF[:, :, :], op=ALU.mult), reads=["qT", "XiF"], writes=["qf" + sfx])
        P.c("pool", lambda e: e.tensor_tensor(out=qb_[:, :, :], in0=qT[:, :, tsl], in1=XiB[:, :, :], op=ALU.mult), reads=["qT", "XiB"], writes=["qb" + sfx])
        for h in range(4):
            osl = slice(h * 128, (h + 1) * 128)
            P.c("pe", lambda e, h=h, osl=osl: e.matmul(pO[:, osl], lhsT=scm_[:, h, :], rhs=v_tm[:, i, osl], start=True, stop=False),
                reads=["scm" + sfx, "v_tm"], writes=[pOn])
            P.c("pe", lambda e, h=h, osl=osl: e.matmul(pO[:, osl], lhsT=qf_[:, h, :], rhs=Rf_store[:, i, osl], start=False, stop=False),
                reads=["qf" + sfx, "Rf_store"], writes=[pOn])
            P.c("pe", lambda e, h=h, osl=osl: e.matmul(pO[:, osl], lhsT=qb_[:, h, :], rhs=Rb_store[:, i, osl], start=False, stop=True),
                reads=["qb" + sfx, "Rb_store"], writes=[pOn])
        P.c("act", lambda e: e.activation(out=yn_[:, :, :], in_=pO[:, :].rearrange("p (h e) -> p h e", h=4), func=AF.Copy), reads=[pOn], writes=["yn" + sfx])
        if debug and i in (0, 15):
            dd = dout(f"d_rety{i}", [128, 512])
            P.dma("sp", lambda e, dd=dd: e.dma_start(out=dd[:, :], in_=yn_[:, :, :].rearrange("p h e -> p (h e)")), "dbg", reads=["yn" + sfx])

    def ret_back(i):
        pp = i % 2
        sfx = "" if pp == 0 else "1"
        tsl = slice(i * 128, (i + 1) * 128)
        yn_, retx_, bst_, mv_, rs_ = yn_l[pp], retx_l[pp], bst_l[pp], mv_l[pp], rs_l[pp]
        ynn, rxn = "yn" + sfx, "retx" + sfx
        P.c("dve", lambda e: e.tensor_reduce(out=bst_[:, 0, 0:4], in_=yn_[:, :, :], axis=mybir.AxisListType.X, op=ALU.add), reads=[ynn], writes=["bst" + sfx])
        for h in range(4):
            P.c("act", lambda e, h=h: e.activation(out=junk2[:, :], in_=yn_[:, h, :], func=AF.Square, accum_out=bst_[:, 1, h:h + 1]),
                reads=[ynn], writes=["junk2", "bst2" + sfx])
        P.c("dve", lambda e: e.tensor_scalar(out=mv_[:, :, 0], in0=bst_[:, 0, 0:4], scalar1=1.0 / 128.0, scalar2=None, op0=ALU.mult), reads=["bst" + sfx], writes=["mv" + sfx])
        P.c("dve", lambda e: e.tensor_tensor(out=bst_[:, 2, 0:4], in0=mv_[:, :, 0], in1=mv_[:, :, 0], op=ALU.mult), reads=["mv" + sfx], writes=["bst3" + sfx])
        P.c("dve", lambda e: e.scalar_tensor_tensor(out=mv_[:, :, 1], in0=bst_[:, 1, 0:4], scalar=1.0 / 128.0, in1=bst_[:, 2, 0:4], op0=ALU.mult, op1=ALU.subtract),
            reads=["bst2" + sfx, "bst3" + sfx, "mv" + sfx], writes=["mv" + sfx])
        P.c("act", lambda e: e.activation(out=rs_[:, 0:4], in_=mv_[:, :, 1], func=AF.Sqrt, bias=epst[:, 0:1]), reads=["mv" + sfx, "epst"], writes=["rs" + sfx])
        P.c("dve", lambda e: e.reciprocal(out=rs_[:, 4:8], in_=rs_[:, 0:4]), reads=["rs" + sfx], writes=["rs4" + sfx])
        for h in range(4):
            P.c("dve", lambda e, h=h: e.tensor_scalar(out=yn_[:, h, :], in0=yn_[:, h, :], scalar1=mv_[:, h, 0:1], scalar2=rs_[:, 4 + h:5 + h],
                                                     op0=ALU.subtract, op1=ALU.mult), reads=[ynn, "mv" + sfx, "rs4" + sfx], writes=[ynn])
        P.c("dve", lambda e: e.tensor_tensor(out=retx_[:, :], in0=yn_[:, :, :].rearrange("p h e -> p (h e)"), in1=gate[:, i, :], op=ALU.mult),
            reads=[ynn, "gate"], writes=[rxn])
        for h in range(4):
            P.c("pe", lambda e, h=h: e.transpose(psTq[:, h * 128:(h + 1) * 128], retx_[:, h * 128:(h + 1) * 128], ident_b[:, :]),
                reads=[rxn, "ident_b"], writes=["psTq"])
        P.c("act", lambda e: e.activation(out=mixT[:, 4:8, tsl], in_=psTq[:, 0:512].rearrange("p (h t) -> p h t", h=4), func=AF.Copy),
            reads=["psTq"], writes=["mixT"])

    ret_front(0)
    for i in range(NT):
        if i + 1 < NT:
            ret_front(i + 1)
        ret_back(i)

    if debug:
        for (nm, tns) in (("d_rcf", Rcf), ("d_rcb", Rcb)):
            dd = dout(nm, [128, 512])
            P.dma("sp", lambda e, dd=dd, tns=tns: e.dma_start(out=dd[:, :], in_=tns[:, :, :].rearrange("p h e -> p (h e)")), "dbg", reads=["Rcf", "Rcb"])
        dd = nc.dram_tensor("d_mixT", [128, 8 * T], BF16, kind="ExternalOutput").ap()
        dbg["d_mixT"] = dd
        P.dma("sp", lambda e, dd=dd: e.dma_start(out=dd[:, :], in_=mixT[:, :, :].rearrange("p a b -> p (a b)")), "dbg", reads=["mixT"])

    stage("ret")
    d_in = {}
    for (nm, shp) in (("lamre", [128, 64]), ("lamim", [128, 64]), ("lstep", [128, 64]), ("bre", [128, 32, 16]), ("bim", [128, 32, 16]),
                      ("cre", [128, 32, 16]), ("cim", [128, 32, 16]), ("d128", [128, 32, 16]), ("bglu", [128, 4])):
        d_in[nm] = din("s5_" + nm, shp)
    d_in["wglu"] = din("w_glu", [512, 512])
    _s5(nc, P, debug, stage, dout, dbg, alias, big0, r2_base, r2_end, spare_off, uT, ucT, mixT, psA, psTq, ident_f, ident_b, ones_f, negpi, msk, d_in)
    stage("s5")
    wout_d = din("w_out", [D, D])
    fnw_d = din("fnw_b", [128, D])
    cw_d = din("cw", [128, 66])
    cb_d = din("cb", [128, 22])
    oh_d = din("oh", [128, 8])
    wup_d = din("w_up", [D, 5632])
    wdn_d = din("w_down", [2816, D])
    SP = Arena(nc); SP.off = spare_off
    fnw = SP.alloc("fnw", [128, D], F32)
    cw = SP.alloc("cw", [128, 22, 3], F32)
    cb = SP.alloc("cb", [128, 22], F32)
    oh = SP.alloc("oh", [128, 8], F32)
    hxe = SP.alloc("hxe", [128, 16], F32)
    xgC = SP.alloc("xgC", [128, 4, 16], F32)
    halo = SP.alloc("halo", [128, 16], F32)
    ss2 = SP.alloc("ss2", [128, 4], F32)
    assert SP.off <= spare_off + 6 * 1024, SP.off - spare_off
    alias(["fnw", "cw", "cb", "oh", "hxe", "xgC", "halo", "ss2a", "ss2b", "ss2c"], ["kc_tm", "vc_tm", "ucT", "e1lo", "e1hi", "e2lo", "e2hi", "s5tab"])
    P.dma("sp", lambda e: e.dma_start(out=fnw[:, :], in_=fnw_d[:, :]), "small8", writes=["fnw"])
    P.dma("sp", lambda e: e.dma_start(out=cw[:, :, :], in_=cw_d.rearrange("p (j w) -> p j w", w=3)), "small9", writes=["cw"])
    P.dma("sp", lambda e: e.dma_start(out=cb[:, :], in_=cb_d[:, :]), "small10", writes=["cb"])
    P.dma("sp", lambda e: e.dma_start(out=oh[:, :], in_=oh_d[:, :]), "small11", writes=["oh"])
    FS = Arena(nc); FS.off = big0
    x_mid = FS.alloc("x_mid", [128, NT, D], F32)
    wout = FS.alloc("wout", [128, 8, D], BF16)
    xf = [FS.alloc(f"xf{i}", [128, D], F32) for i in range(2)]
    assert FS.off <= big0 + 96 * 1024
    s_dead = ["qT", "kT", "k_tm", "v_tm", "gate", "uT", "XR", "QN", "s5tab", "L1lo", "L1hi", "e1lo", "e1hi", "e2lo", "e2hi", "tmpM", "Z", "Zc", "Ysb",
              "Fall", "Fctx", "hacc", "hacc_bf", "htmp", "xgB", "D2", "s5gT", "wglu", "sgt", "Xw0", "Xw1", "TAw0", "TAw1", "TBw0", "TBw1",
              "XAw0", "XAw1", "XBw0", "XBw1"] + [f"Rr{i}" for i in range(36)]
    alias(["x_mid", "wout", "xf0", "xf1"], s_dead)
    P.dma("pool", lambda e: e.dma_start(out=wout[:, :, :], in_=wout_d.rearrange("(k p) n -> p k n", p=128)), "wout", writes=["wout"])
    for k in range(8):
        P.c("dve", lambda e, k=k: e.tensor_tensor(out=wout[:, k, :], in0=wout[:, k, :], in1=bc2[:, :], op=ALU.mult), reads=["wout", "bc2"], writes=["wout"])
    for i in range(NT):
        xt_ = xf[i % 2]
        P.dma("sp", lambda e, xt_=xt_, i=i: e.dma_start(out=xt_[:, :], in_=x_d[i * 128:(i + 1) * 128, :]), f"xf{i % 2}", writes=[f"xf{i % 2}"])
        for hh in range(2):
            ps = psA[hh]
            for k in range(8):
                P.c("pe", lambda e, ps=ps, k=k, hh=hh, i=i: e.matmul(ps[:, :], lhsT=mixT[:, k, i * 128:(i + 1) * 128], rhs=wout[:, k, hh * 512:(hh + 1) * 512],
                                                                  start=(k == 0), stop=(k == 7)), reads=["mixT", "wout"], writes=[f"ps{hh}"])
            P.c("dve", lambda e, ps=ps, hh=hh, i=i, xt_=xt_: e.tensor_tensor(out=x_mid[:, i, hh * 512:(hh + 1) * 512], in0=ps[:, :],
                                                                           in1=xt_[:, hh * 512:(hh + 1) * 512], op=ALU.add),
                reads=[f"ps{hh}", f"xf{i % 2}"], writes=[f"x_mid{i}"])
    if debug:
        dd = dout("d_xmid", [128, NT * D])
        P.dma("sp", lambda e, dd=dd: e.dma_start(out=dd[:, :], in_=x_mid[:, :, :].rearrange("p a b -> p (a b)")), "dbg", reads=[f"x_mid{i}" for i in range(NT)])
    stage("F")

    GR = Arena(nc); GR.off = r2_base
    hx2T = GR.alloc("hx2T", [128, 8, T + 2], BF16)
    NJ = 4
    wa = GR.alloc("wa", [128, NJ, 8, 128], BF16)
    wg = GR.alloc("wg", [128, NJ, 8, 128], BF16)
    wdn = GR.alloc("wdn", [128, NJ, D], BF16)
    gbuf2 = [GR.alloc(f"gbuf{i}", [128, T + 2], F32) for i in range(2)]
    abuf2 = [GR.alloc(f"abuf{i}", [128, T], BF16) for i in range(2)]
    glb2 = [GR.alloc(f"glb{i}", [128, T], BF16) for i in range(2)]
    assert GR.off <= r2_end, (GR.off, r2_end)
    xs2 = nc.alloc_sbuf_tensor_at("xs2", [128, D], F32, offset=GR.off - 8192)
    junk3 = nc.alloc_sbuf_tensor_at("junk3", [128, D], BF16, offset=GR.off - 4096)
    gnames = ["hx2T"] + [f"{n}{i}" for n in ("wa", "wg", "wdn") for i in range(NJ)] + ["gbuf0", "gbuf1", "abuf0", "abuf1", "glb0", "glb1", "xs2", "junk3"]
    alias(gnames, ["mixT", "WRb", "WRf", "PV", "Mbf", "Eblk", "Macc", "L1lo", "L1hi"])
    HS = Arena(nc); HS.off = big0 + NT * D * 4
    hid = HS.alloc("hid", [128, NJ, T], BF16)
    ub2 = [HS.alloc(f"ub{i}", [128, T], F32) for i in range(2)]
    assert HS.off <= big0 + 96 * 1024
    alias([f"hid{i}" for i in range(NJ)] + ["ub0", "ub1"], ["wout", "xf0", "xf1"])

    xs2b = nc.alloc_sbuf_tensor_at("xs2b", [128, D], F32, offset=GR.off - 12288)
    ss2b = SP.alloc("ss2b", [128, 4], F32)
    xs2s = [(xs2, "xs2", ss2, "ss2"), (xs2b, "xs2b", ss2b, "ss2b")]

    def norm2_front(i, pp):
        xs_, xn_, s_, sn_ = xs2s[pp]
        P.c("act", lambda e: e.activation(out=junk3[:, :], in_=x_mid[:, i, :], func=AF.Square, accum_out=s_[:, 0:1]),
            reads=[f"x_mid{i}"], writes=["junk3", sn_ + "a"])
        P.c("act", lambda e: e.activation(out=s_[:, 1:2], in_=s_[:, 0:1], func=AF.Sqrt, scale=1.0 / D, bias=epst[:, 0:1]), reads=[sn_ + "a", "epst"], writes=[sn_ + "b"])
        P.c("dve", lambda e: e.reciprocal(out=s_[:, 2:3], in_=s_[:, 1:2]), reads=[sn_ + "b"], writes=[sn_ + "c"])
        P.c("dve", lambda e: e.tensor_scalar(out=xs_[:, :], in0=x_mid[:, i, :], scalar1=s_[:, 2:3], scalar2=None, op0=ALU.mult),
            reads=[f"x_mid{i}", sn_ + "c"], writes=[xn_])

    def norm2_back(i, pp):
        xs_, xn_, s_, sn_ = xs2s[pp]
        for k in range(8):
            ps = psA[k // 4]
            P.c("pe", lambda e, ps=ps, k=k: e.transpose(ps[:, (k % 4) * 128:(k % 4 + 1) * 128], xs_[:, k * 128:(k + 1) * 128], ident_f[:, :]),
                reads=[xn_, "ident_f"], writes=[f"ps{k // 4}"])
        for k in range(8):
            ps = psA[k // 4]
            P.c("act", lambda e, ps=ps, k=k: e.activation(out=hx2T[:, k, 1 + i * 128: 1 + (i + 1) * 128], in_=ps[:, (k % 4) * 128:(k % 4 + 1) * 128],
                                                        func=AF.Identity, scale=g2x[:, k:k + 1], bias=modT[:, 16 + k, 0:1]),
                reads=[f"ps{k // 4}", "g2x", "modT"], writes=["hx2T"])

    wup_src = wup_d.rearrange("(k p) n -> p k n", p=128)
    passes = [(0, 4), (4, 8), (8, 12), (12, 16), (16, 19), (19, 22)]

    def load_up(j, sl):
        P.dma("pool", lambda e: e.dma_start(out=wa[:, sl, :, :], in_=wup_src[:, :, j * 128:(j + 1) * 128]), f"wa{sl}", writes=[f"wa{sl}"])
        P.dma("pool", lambda e: e.dma_start(out=wg[:, sl, :, :], in_=wup_src[:, :, 2816 + j * 128: 2816 + (j + 1) * 128]), f"wg{sl}", writes=[f"wg{sl}"])

    def load_dn(j, sl):
        P.dma("pool", lambda e: e.dma_start(out=wdn[:, sl, :], in_=wdn_d[j * 128:(j + 1) * 128, :]), f"wdn{sl}", writes=[f"wdn{sl}"])

    def scale_dn(sl):
        P.c("pool", lambda e: e.tensor_tensor(out=wdn[:, sl, :], in0=wdn[:, sl, :], in1=bc5[:, :], op=ALU.mult), reads=[f"wdn{sl}", "bc5"], writes=[f"wdn{sl}"])

    def chunk_tail(t_):
        glb, ub, abuf, jj, gln, ubn, abn = t_
        P.c("act", lambda e: e.activation(out=glb[:, :], in_=ub[:, :], func=AF.Gelu_apprx_tanh), reads=[ubn], writes=[gln])
        P.c("dve", lambda e: e.tensor_tensor(out=hid[:, jj, :], in0=glb[:, :], in1=abuf[:, :], op=ALU.mult), reads=[gln, abn], writes=[f"hid{jj}"])

    for sl in range(passes[0][1]):
        load_up(sl, sl)
    for jj_ in range(passes[0][1] - passes[0][0]):
        load_dn(passes[0][0] + jj_, jj_)
    order = [0, NT - 1] + list(range(1, NT - 1))
    norm2_front(order[0], 0)
    for n_, i in enumerate(order):
        if n_ + 1 < len(order):
            norm2_front(order[n_ + 1], (n_ + 1) % 2)
        norm2_back(i, n_ % 2)
        if n_ == 1:
            P.c("dve", lambda e: e.tensor_copy(out=hxe[:, 0:8], in_=hx2T[:, :, 1]), reads=["hx2T"], writes=["hxe"])
            P.c("dve", lambda e: e.tensor_copy(out=hxe[:, 8:16], in_=hx2T[:, :, T]), reads=["hx2T"], writes=["hxe"])
            xc_in = nc.dram_tensor("xc_in", [128, 16], F32)
            xc_out = nc.dram_tensor("xc_out", [512, 16], F32)
            P.dma("sp", lambda e: e.dma_start(out=xc_in[:, :], in_=hxe[:, :]), "xc_st", reads=["hxe"], writes=["xc_in"])
            P.dma("pool", lambda e: e.collective_compute("AllGather", ALU.bypass, replica_groups=[[0, 1, 2, 3], [4, 5, 6, 7]],
                                                          ins=[xc_in.ap().opt()], outs=[xc_out.ap().opt()]),
                  "ccC", reads=["xc_in"], writes=["xc_out"], inc=1)
            P.dma("sp", lambda e: e.dma_start(out=xgC[:, :, :], in_=xc_out.ap().rearrange("(r p) n -> p r n", p=128)), "xc_ld", reads=["xc_out"], writes=["xgC"])
    P.c("pool", lambda e: e.memset(halo[:, :], 0.0), writes=["halo"])
    for r_ in range(4):
        P.c("dve", lambda e, r_=r_: e.scalar_tensor_tensor(out=halo[:, 0:8], in0=xgC[:, r_, 8:16], scalar=oh[:, r_:r_ + 1], in1=halo[:, 0:8],
                                                          op0=ALU.mult, op1=ALU.add), reads=["xgC", "oh", "halo"], writes=["halo"])
        P.c("dve", lambda e, r_=r_: e.scalar_tensor_tensor(out=halo[:, 8:16], in0=xgC[:, r_, 0:8], scalar=oh[:, 4 + r_:5 + r_], in1=halo[:, 8:16],
                                                          op0=ALU.mult, op1=ALU.add), reads=["xgC", "oh", "halo"], writes=["halo"])
    P.c("dve", lambda e: e.tensor_copy(out=hx2T[:, :, 0], in_=halo[:, 0:8]), reads=["halo"], writes=["hx2T"])
    P.c("dve", lambda e: e.tensor_copy(out=hx2T[:, :, T + 1], in_=halo[:, 8:16]), reads=["halo"], writes=["hx2T"])
    stage("G0")

    alias(["glb1"], ["xs2", "junk3"])
    alias(["glb0"], ["xs2"])
    alias(["abuf1"], ["xs2b"])
    cnum = 0
    pend_tail = []
    for pi, (j0, j1) in enumerate(passes):
        nj = j1 - j0
        if pi > 0:
            for jj in range(nj):
                load_dn(j0 + jj, jj)
        for jj in range(nj):
            j = j0 + jj
            pb = cnum % 2
            cnum += 1
            gbuf, abuf, glb, ub = gbuf2[pb], abuf2[pb], glb2[pb], ub2[pb]
            gbn, abn, gln, ubn = f"gbuf{pb}", f"abuf{pb}", f"glb{pb}", f"ub{pb}"
            for tg in range(4):
                psa, psg = (psA[2], psA[3]) if tg % 2 == 0 else (psA[4], psA[5])
                pan, pgn = ("ps2", "ps3") if tg % 2 == 0 else ("ps4", "ps5")
                csl = slice(1 + tg * 512, 1 + (tg + 1) * 512)
                for k in range(8):
                    P.c("pe", lambda e, psa=psa, k=k, jj=jj, csl=csl: e.matmul(psa[:, :], lhsT=wa[:, jj, k, :], rhs=hx2T[:, k, csl], start=(k == 0), stop=(k == 7)),
                        reads=[f"wa{jj}", "hx2T"], writes=[pan])
                for k in range(8):
                    P.c("pe", lambda e, psg=psg, k=k, jj=jj, csl=csl: e.matmul(psg[:, :], lhsT=wg[:, jj, k, :], rhs=hx2T[:, k, csl], start=(k == 0), stop=(k == 7)),
                        reads=[f"wg{jj}", "hx2T"], writes=[pgn])
                P.c("act", lambda e, psa=psa, tg=tg, abuf=abuf: e.activation(out=abuf[:, tg * 512:(tg + 1) * 512], in_=psa[:, :], func=AF.Copy), reads=[pan], writes=[abn])
                P.c("act", lambda e, psg=psg, csl=csl, gbuf=gbuf: e.activation(out=gbuf[:, csl], in_=psg[:, :], func=AF.Copy), reads=[pgn], writes=[gbn])
            for k in range(8):
                P.c("pe", lambda e, k=k, jj=jj: e.matmul(psA[6][:, 0:2], lhsT=wg[:, jj, k, :], rhs=hx2T[:, k, 0:T + 2:T + 1], start=(k == 0), stop=(k == 7)),
                    reads=[f"wg{jj}", "hx2T"], writes=["ps6"])
            P.c("act", lambda e, gbuf=gbuf: e.activation(out=gbuf[:, 0:T + 2:T + 1], in_=psA[6][:, 0:2], func=AF.Copy), reads=["ps6"], writes=[gbn])
            while len(pend_tail) > 0:
                chunk_tail(pend_tail.pop(0))
            if pi + 1 < len(passes) and jj < passes[pi + 1][1] - passes[pi + 1][0]:
                load_up(passes[pi + 1][0] + jj, jj)
            scale_dn(jj)
            P.c("dve", lambda e, j=j, ub=ub, gbuf=gbuf: e.tensor_scalar(out=ub[:, :], in0=gbuf[:, 1:T + 1], scalar1=cw[:, j, 1:2], scalar2=cb[:, j:j + 1], op0=ALU.mult, op1=ALU.add),
                reads=[gbn, "cw", "cb"], writes=[ubn])
            P.c("dve", lambda e, j=j, ub=ub, gbuf=gbuf: e.scalar_tensor_tensor(out=ub[:, :], in0=gbuf[:, 0:T], scalar=cw[:, j, 0:1], in1=ub[:, :], op0=ALU.mult, op1=ALU.add),
                reads=[gbn, "cw", ubn], writes=[ubn])
            P.c("dve", lambda e, j=j, ub=ub, gbuf=gbuf: e.scalar_tensor_tensor(out=ub[:, :], in0=gbuf[:, 2:T + 2], scalar=cw[:, j, 2:3], in1=ub[:, :], op0=ALU.mult, op1=ALU.add),
                reads=[gbn, "cw", ubn], writes=[ubn])
            pend_tail.append((glb, ub, abuf, jj, gln, ubn, abn))
        while len(pend_tail) > 0:
            chunk_tail(pend_tail.pop(0))
        for i in range(NT):
            for hh in range(2):
                ps = psA[hh]
                for jj in range(nj):
                    P.c("pe", lambda e, ps=ps, jj=jj, i=i, hh=hh, nj=nj: e.matmul(ps[:, :], lhsT=hid[:, jj, i * 128:(i + 1) * 128], rhs=wdn[:, jj, hh * 512:(hh + 1) * 512],
                                                                               start=(jj == 0), stop=(jj == nj - 1)), reads=[f"hid{jj}", f"wdn{jj}"], writes=[f"ps{hh}"])
                P.c("dve", lambda e, ps=ps, i=i, hh=hh: e.tensor_tensor(out=x_mid[:, i, hh * 512:(hh + 1) * 512], in0=ps[:, :],
                                                                      in1=x_mid[:, i, hh * 512:(hh + 1) * 512], op=ALU.add),
                    reads=[f"ps{hh}", f"x_mid{i}"], writes=[f"x_mid{i}"])
    stage("G1")
    alias(["junk3"], ["glb1"])
    def fin_front(i):
        s_, sn_ = (ss2, "ss2") if i % 2 == 0 else (ss2b, "ss2b")
        P.c("act", lambda e: e.activation(out=junk3[:, :], in_=x_mid[:, i, :], func=AF.Square, accum_out=s_[:, 0:1]),
            reads=[f"x_mid{i}"], writes=["junk3", sn_ + "a"])
        P.c("act", lambda e: e.activation(out=s_[:, 1:2], in_=s_[:, 0:1], func=AF.Sqrt, scale=1.0 / D, bias=epst[:, 0:1]), reads=[sn_ + "a", "epst"], writes=[sn_ + "b"])
        P.c("dve", lambda e: e.reciprocal(out=s_[:, 2:3], in_=s_[:, 1:2]), reads=[sn_ + "b"], writes=[sn_ + "c"])

    fin_front(0)
    for i in range(NT):
        if i + 1 < NT:
            fin_front(i + 1)
        s_, sn_ = (ss2, "ss2") if i % 2 == 0 else (ss2b, "ss2b")
        P.c("dve", lambda e, i=i, s_=s_: e.scalar_tensor_tensor(out=x_mid[:, i, :], in0=x_mid[:, i, :], scalar=s_[:, 2:3], in1=fnw[:, :], op0=ALU.mult, op1=ALU.mult),
            reads=[f"x_mid{i}", sn_ + "c", "fnw"], writes=[f"x_mid{i}"])
        P.dma("sp", lambda e, i=i: e.dma_start(out=out_d[i * 128:(i + 1) * 128, :], in_=x_mid[:, i, :]), "xout", reads=[f"x_mid{i}"])
    stage("end")


def _s5(nc, P, debug, stage, dout, dbg, alias, big0, r2_base, r2_end, spare_off, uT, ucT, mixT, psA, psTq, ident_f, ident_b, ones_f, negpi, msk, d_in):
    TWO_PI = 2.0 * np.pi
    AS = Arena(nc)
    AS.off = big0
    S_END = big0 + 80 * 1024
    XR = AS.alloc("XR", [128, 32, 128], F32)
    QN = AS.alloc("QN", [128, 32, 128], F32)
    Macc = nc.alloc_sbuf_tensor_at("Macc", [128, 32, 128], F32, offset=r2_base)
    sm0 = AS.off
    V1 = AS.alloc("V1", [128, 13, 64], F32); V2 = AS.alloc("V2", [128, 13, 64], F32)
    bglu = AS.alloc("bglu", [128, 4], F32)
    dead0 = AS.off

    def f32t(name, w):
        return AS.alloc(name, [128, w], F32)
    lamre = f32t("lamre", 64); lamim = f32t("lamim", 64); lstep = f32t("lstep", 64)
    dtt = f32t("dtt", 64); rho = f32t("rho", 64); tht = f32t("tht", 64); mag = f32t("mag", 64)
    sn = f32t("sn", 64); cs = f32t("cs", 64); ta = f32t("ta", 64); tb = f32t("tb", 64); tc = f32t("tc", 64)
    tiI = AS.alloc("tiI", [128, 64], I32)
    kre = f32t("kre", 64); kim = f32t("kim", 64)
    PWre = AS.alloc("PWre", [128, 9, 64], F32); PWim = AS.alloc("PWim", [128, 9, 64], F32)
    NPre = AS.alloc("NPre", [128, 8, 64], F32); NPim = AS.alloc("NPim", [128, 8, 64], F32)
    HPre = AS.alloc("HPre", [128, 13, 64], F32); HPim = AS.alloc("HPim", [128, 13, 64], F32)
    bre = AS.alloc("bre", [128, 32, 16], F32); bim = AS.alloc("bim", [128, 32, 16], F32)
    cre = AS.alloc("cre", [128, 32, 16], F32); cim = AS.alloc("cim", [128, 32, 16], F32)
    Cr = AS.alloc("Cr", [128, 32, 16], F32)
    d128 = AS.alloc("d128", [128, 32, 16], F32)
    BBre = AS.alloc("BBre", [128, 64, 16], F32); BBim = AS.alloc("BBim", [128, 64, 16], F32)
    e1 = nc.alloc_sbuf_tensor_at("s5e1", [128, 32, 16], F32, offset=spare_off)
    e2 = nc.alloc_sbuf_tensor_at("s5e2", [128, 32, 16], F32, offset=spare_off + 2048)
    maskF = AS.alloc("maskF", [128, 128], F32); maskB = AS.alloc("maskB", [128, 128], F32)
    tmpM = AS.alloc("tmpM", [128, 128], F32)
    WRf = nc.alloc_sbuf_tensor_at("WRf", [128, 32, 128], F32, offset=r2_base + 8 * T * 2 + 56 * 1024 - 16 * 1024)
    assert AS.off <= S_END, (AS.off, S_END)
    AR2 = Arena(nc)
    AR2.off = r2_base + 8 * T * 2
    WR = AR2.alloc("WR", [128, 64, 128], BF16)
    PV = AR2.alloc("PV", [128, 64, 128], BF16)
    Mbf = AR2.alloc("Mbf", [128, 32, 128], BF16)
    Eblk = AR2.alloc("Eblk", [128, 64, 128], BF16)
    assert AR2.off <= r2_end, (AR2.off, r2_end)
    s5names = ["XR", "QN", "Macc", "s5tab", "WRb", "WRf", "PV", "Mbf", "Eblk", "s5tmp", "L1lo", "L1hi", "e1lo", "e2lo", "e1hi", "e2hi"]
    alias(s5names, ["qT", "kT", "k_tm", "v_tm", "gate", "Rb_store", "Rf_store", "scm1", "qf1", "qb1", "yn1", "retx1", "bst1", "bst21", "bst31", "mv1", "rs1", "rs41", "xg", "Dm", "XiF", "XiB", "dm", "dpos", "dneg", "mge", "mlt", "e1", "e2", "tfree",
                    "pidx", "zarg", "Zeta", "G128", "G2048", "nld", "ld128", "Rf", "Rb", "Rcf", "Rcb", "Rf_bf", "hacc_t", "vz0", "vz1", "scm", "qf", "qb", "yn",
                    "retx", "bst", "bst2", "bst3", "junk2", "mv", "rs", "rs4", "kc_tm", "vc_tm"])
    TAB = "s5tab"

    for (t_, nm) in ((lamre, "lamre"), (lamim, "lamim"), (lstep, "lstep"), (bre, "bre"), (bim, "bim"), (cre, "cre"), (cim, "cim"), (d128, "d128"), (bglu, "bglu")):
        src_ap = d_in[nm]
        if len(t_.shape) == 3:
            P.dma("sp", lambda e, t_=t_, src_ap=src_ap: e.dma_start(out=t_[:, :, :], in_=src_ap), "s5ld", writes=[TAB])
        else:
            P.dma("sp", lambda e, t_=t_, src_ap=src_ap: e.dma_start(out=t_[:, :], in_=src_ap), "s5ld", writes=[TAB])

    cnt = [0]

    def ew():
        cnt[0] += 1
        return "dve" if cnt[0] % 2 else "pool"

    def tt(out, a, b, op, eng=None, rd=(TAB,), wr=(TAB,)):
        P.c(eng or ew(), lambda e: e.tensor_tensor(out=out, in0=a, in1=b, op=op), reads=list(rd), writes=list(wr))

    def ts(out, a, s1, s2, op0, op1=None, eng="dve", rd=(TAB,), wr=(TAB,)):
        if op1 is None:
            P.c(eng, lambda e: e.tensor_scalar(out=out, in0=a, scalar1=s1, scalar2=None, op0=op0), reads=list(rd), writes=list(wr))
        else:
            P.c(eng, lambda e: e.tensor_scalar(out=out, in0=a, scalar1=s1, scalar2=s2, op0=op0, op1=op1), reads=list(rd), writes=list(wr))

    def sin_of(dst, src_, off):
        ts(ta[:, :], src_, 1.0 / TWO_PI, off, ALU.mult, ALU.add)
        P.c("dve", lambda e: e.tensor_copy(out=tiI[:, :], in_=ta[:, :]), reads=[TAB], writes=[TAB])
        P.c("dve", lambda e: e.tensor_copy(out=tb[:, :], in_=tiI[:, :]), reads=[TAB], writes=[TAB])
        tt(ta[:, :], ta[:, :], tb[:, :], ALU.subtract, eng="dve")
        ts(tb[:, :], ta[:, :], 0.0, None, ALU.is_lt)
        tt(ta[:, :], ta[:, :], tb[:, :], ALU.add, eng="dve")
        P.c("act", lambda e: e.activation(out=dst, in_=ta[:, :], func=AF.Sin, scale=TWO_PI, bias=negpi[:, 0:1]), reads=[TAB, "negpi"], writes=[TAB])

    def cmul(ore, oim, are, aim, bre_, bim_, w=64):
        t1, t2 = ta[:, 0:w], tb[:, 0:w]
        tt(t1, are, bre_, ALU.mult, eng="dve"); tt(t2, aim, bim_, ALU.mult, eng="dve"); tt(ore, t1, t2, ALU.subtract, eng="dve")
        tt(t1, are, bim_, ALU.mult, eng="dve"); tt(t2, aim, bre_, ALU.mult, eng="dve"); tt(oim, t1, t2, ALU.add, eng="dve")

    P.c("act", lambda e: e.activation(out=dtt[:, :], in_=lstep[:, :], func=AF.Exp), reads=[TAB], writes=[TAB])
    tt(rho[:, :], lamre[:, :], dtt[:, :], ALU.mult, eng="dve")
    tt(tht[:, :], lamim[:, :], dtt[:, :], ALU.mult, eng="dve")
    P.c("act", lambda e: e.activation(out=mag[:, :], in_=rho[:, :], func=AF.Exp), reads=[TAB], writes=[TAB])
    sin_of(sn[:, :], tht[:, :], 0.5)
    sin_of(cs[:, :], tht[:, :], 0.75)
    P.c("pool", lambda e: e.memset(PWre[:, 0, :], 1.0), reads=[TAB], writes=[TAB])
    P.c("pool", lambda e: e.memset(PWim[:, 0, :], 0.0), reads=[TAB], writes=[TAB])
    tt(PWre[:, 1, :], mag[:, :], cs[:, :], ALU.mult, eng="dve")
    tt(PWim[:, 1, :], mag[:, :], sn[:, :], ALU.mult, eng="dve")
    ts(tc[:, :], PWre[:, 1, :], -1.0, None, ALU.add)
    tt(kre[:, :], tc[:, :], lamre[:, :], ALU.mult, eng="dve"); tt(ta[:, :], PWim[:, 1, :], lamim[:, :], ALU.mult, eng="dve")
    tt(kre[:, :], kre[:, :], ta[:, :], ALU.add, eng="dve")
    tt(kim[:, :], PWim[:, 1, :], lamre[:, :], ALU.mult, eng="dve"); tt(ta[:, :], tc[:, :], lamim[:, :], ALU.mult, eng="dve")
    tt(kim[:, :], kim[:, :], ta[:, :], ALU.subtract, eng="dve")
    tt(ta[:, :], lamre[:, :], lamre[:, :], ALU.mult, eng="dve"); tt(tb[:, :], lamim[:, :], lamim[:, :], ALU.mult, eng="dve")
    tt(ta[:, :], ta[:, :], tb[:, :], ALU.add, eng="dve")
    P.c("dve", lambda e: e.reciprocal(out=tb[:, :], in_=ta[:, :]), reads=[TAB], writes=[TAB])
    tt(kre[:, :], kre[:, :], tb[:, :], ALU.mult, eng="dve"); tt(kim[:, :], kim[:, :], tb[:, :], ALU.mult, eng="dve")
    for d in range(2):
        cs_ = slice(d * 32, (d + 1) * 32)
        kr = kre[:, cs_].unsqueeze(2).broadcast_to([128, 32, 16]); ki = kim[:, cs_].unsqueeze(2).broadcast_to([128, 32, 16])
        tt(e1[:, :, :], kr, bre[:, :, :], ALU.mult); tt(e2[:, :, :], ki, bim[:, :, :], ALU.mult)
        tt(BBre[:, cs_, :], e1[:, :, :], e2[:, :, :], ALU.subtract, eng="dve")
        tt(e1[:, :, :], kr, bim[:, :, :], ALU.mult); tt(e2[:, :, :], ki, bre[:, :, :], ALU.mult)
        tt(BBim[:, cs_, :], e1[:, :, :], e2[:, :, :], ALU.add, eng="dve")
    P.c("dve", lambda e: e.tensor_copy(out=Cr[0:64, :, :], in_=cre[0:64, :, :]), reads=[TAB], writes=[TAB])
    ts(Cr[64:128, :, :], cim[64:128, :, :], -1.0, None, ALU.mult)
    for e_ in range(2, 9):
        cmul(PWre[:, e_, :], PWim[:, e_, :], PWre[:, e_ - 1, :], PWim[:, e_ - 1, :], PWre[:, 1, :], PWim[:, 1, :])
    P.c("pool", lambda e: e.memset(NPre[:, 0, :], 1.0), reads=[TAB], writes=[TAB])
    P.c("pool", lambda e: e.memset(NPim[:, 0, :], 0.0), reads=[TAB], writes=[TAB])
    tt(ta[:, :], PWre[:, 1, :], PWre[:, 1, :], ALU.mult, eng="dve"); tt(tb[:, :], PWim[:, 1, :], PWim[:, 1, :], ALU.mult, eng="dve")
    tt(ta[:, :], ta[:, :], tb[:, :], ALU.add, eng="dve")
    P.c("dve", lambda e: e.reciprocal(out=tc[:, :], in_=ta[:, :]), reads=[TAB], writes=[TAB])
    tt(NPre[:, 1, :], PWre[:, 1, :], tc[:, :], ALU.mult, eng="dve")
    tt(NPim[:, 1, :], PWim[:, 1, :], tc[:, :], ALU.mult, eng="dve")
    ts(NPim[:, 1, :], NPim[:, 1, :], -1.0, None, ALU.mult)
    for e_ in range(2, 8):
        cmul(NPre[:, e_, :], NPim[:, e_, :], NPre[:, e_ - 1, :], NPim[:, e_ - 1, :], NPre[:, 1, :], NPim[:, 1, :])
    P.c("dve", lambda e: e.tensor_copy(out=HPre[:, 0, :], in_=PWre[:, 8, :]), reads=[TAB], writes=[TAB])
    P.c("dve", lambda e: e.tensor_copy(out=HPim[:, 0, :], in_=PWim[:, 8, :]), reads=[TAB], writes=[TAB])
    for k in range(4):
        b0 = 3 * k
        cmul(HPre[:, b0 + 1, :], HPim[:, b0 + 1, :], HPre[:, b0, :], HPim[:, b0, :], HPre[:, b0, :], HPim[:, b0, :])
        cmul(HPre[:, b0 + 2, :], HPim[:, b0 + 2, :], HPre[:, b0 + 1, :], HPim[:, b0 + 1, :], HPre[:, b0, :], HPim[:, b0, :])
        cmul(HPre[:, b0 + 3, :], HPim[:, b0 + 3, :], HPre[:, b0 + 1, :], HPim[:, b0 + 1, :], HPre[:, b0 + 1, :], HPim[:, b0 + 1, :])
    P.c("dve", lambda e: e.tensor_copy(out=V1[0:64, :, :], in_=HPre[0:64, :, :]), reads=[TAB], writes=[TAB])
    ts(V1[64:128, :, :], HPim[64:128, :, :], -1.0, None, ALU.mult)
    P.c("dve", lambda e: e.tensor_copy(out=V2[0:64, :, :], in_=HPim[0:64, :, :]), reads=[TAB], writes=[TAB])
    P.c("dve", lambda e: e.tensor_copy(out=V2[64:128, :, :], in_=HPre[64:128, :, :]), reads=[TAB], writes=[TAB])
    P.c("pool", lambda e: e.iota(maskF[:, :], pattern=[[1, 8], [0, 16]], base=0, channel_multiplier=0, allow_small_or_imprecise_dtypes=True),
        reads=[TAB], writes=[TAB])
    P.c("pool", lambda e: e.iota(tiI[:, 0:1], pattern=[[0, 1]], base=0, channel_multiplier=1), reads=[TAB], writes=[TAB])
    P.c("dve", lambda e: e.tensor_single_scalar(out=tiI[:, 1:2], in_=tiI[:, 0:1], scalar=4, op=ALU.arith_shift_right), reads=[TAB], writes=[TAB])
    P.c("dve", lambda e: e.tensor_copy(out=ta[:, 0:1], in_=tiI[:, 1:2]), reads=[TAB], writes=[TAB])
    ts(maskB[:, :], maskF[:, :], ta[:, 0:1], None, ALU.is_le)
    ts(maskF[:, :], maskF[:, :], ta[:, 0:1], None, ALU.is_ge)
    if debug:
        for (nm, t_, w) in (("d_PWre", PWre, 9 * 64), ("d_PWim", PWim, 9 * 64), ("d_HPre", HPre, 13 * 64), ("d_HPim", HPim, 13 * 64),
                            ("d_BBre", BBre, 64 * 16), ("d_BBim", BBim, 64 * 16), ("d_NPre", NPre, 8 * 64)):
            dd = dout(nm, [128, w])
            P.dma("sp", lambda e, dd=dd, t_=t_: e.dma_start(out=dd[:, :], in_=t_[:, :, :].rearrange("p a b -> p (a b)")), "dbg", reads=[TAB])
        dd = dout("d_maskF", [128, 128])
        P.dma("sp", lambda e, dd=dd: e.dma_start(out=dd[:, :], in_=maskF[:, :]), "dbg", reads=[TAB])
    stage("S5tab")

    XR4 = XR[:, :, :].rearrange("p g (s q) -> p g s q", s=8)
    QN4 = QN[:, :, :].rearrange("p g (s q) -> p g s q", s=8)
    WRf4 = WRf[:, :, :].rearrange("p g (s q) -> p g s q", s=8)
    LO, HI = slice(0, 64), slice(64, 128)
    psM = [psA[0], psA[1]]
    psPV = psA[2]

    P.c("dve", lambda e: e.tensor_copy(out=e1[HI, :, :], in_=BBre[HI, 0:32, :]), reads=[TAB], writes=["e1hi"])
    P.c("dve", lambda e: e.tensor_copy(out=e2[HI, :, :], in_=BBre[HI, 32:64, :]), reads=[TAB], writes=["e2hi"])
    P.c("dve", lambda e: e.tensor_copy(out=BBre[HI, :, :], in_=BBim[HI, :, :]), reads=[TAB, "e1hi", "e2hi"], writes=[TAB])
    P.c("dve", lambda e: e.tensor_scalar(out=BBim[HI, 0:32, :], in0=e1[HI, :, :], scalar1=-1.0, scalar2=None, op0=ALU.mult), reads=[TAB, "e1hi"], writes=[TAB])
    P.c("dve", lambda e: e.tensor_scalar(out=BBim[HI, 32:64, :], in0=e2[HI, :, :], scalar1=-1.0, scalar2=None, op0=ALU.mult), reads=[TAB, "e2hi"], writes=[TAB])
    P.c("dve", lambda e: e.tensor_copy(out=cim[HI, :, :], in_=cre[HI, :, :]), reads=[TAB], writes=[TAB])

    def ctab(dst, dname, Are, Aim, e_idx, d, X1, X2, eng, ta_, tan):
        cs_ = slice(d * 32, (d + 1) * 32)
        ar = Are[:, e_idx, cs_].unsqueeze(2).broadcast_to([128, 32, 16])
        ai = Aim[:, e_idx, cs_].unsqueeze(2).broadcast_to([128, 32, 16])
        P.c(eng, lambda e: e.tensor_tensor(out=dst, in0=ar, in1=X1, op=ALU.mult), reads=[TAB], writes=[dname])
        P.c(eng, lambda e: e.tensor_tensor(out=ta_, in0=ai, in1=X2, op=ALU.mult), reads=[TAB], writes=[tan])
        P.c(eng, lambda e: e.tensor_tensor(out=dst, in0=dst, in1=ta_, op=ALU.subtract), reads=[dname, tan], writes=[dname])

    for d in range(2):
        cs_ = slice(d * 32, (d + 1) * 32)
        for s_ in range(8):
            eb = (7 - s_) if d == 0 else s_
            ctab(XR4[:, :, s_, :], "L1lo", PWre, PWim, eb, d, BBre[:, cs_, :], BBim[:, cs_, :], "dve", e1[:, :, :], "e1lo")
            ew_ = (s_ + 1) if d == 0 else (8 - s_)
            ctab(WRf4[:, :, s_, :], "WRf", PWre, PWim, ew_, d, Cr[:, :, :], cim[:, :, :], "dve", e1[:, :, :], "e1lo")
            en = (7 - s_) if d == 0 else s_
            ctab(QN4[:, :, s_, :], "L1hi", NPre, NPim, en, d, Cr[:, :, :], cim[:, :, :], "pool", e2[:, :, :], "e2lo")
        P.c("act", lambda e, cs_=cs_: e.activation(out=WR[:, cs_, :], in_=WRf[:, :, :], func=AF.Copy), reads=["WRf"], writes=["WRb"])
        for g in range(32):
            ps = psM[g % 2]
            P.c("pe", lambda e, ps=ps, g=g: e.matmul(ps[:, 0:128], lhsT=XR[:, g, :], rhs=QN[:, g, :], start=True, stop=True),
                reads=["L1lo", "L1hi"], writes=[f"ps{g % 2}"])
            if d == 0:
                P.c("dve", lambda e, ps=ps, g=g: e.tensor_tensor(out=Macc[:, g, :], in0=ps[:, 0:128], in1=maskF[:, :], op=ALU.mult),
                    reads=[f"ps{g % 2}", TAB], writes=["Macc"])
            else:
                P.c("dve", lambda e, ps=ps, g=g: e.tensor_tensor(out=tmpM[:, :], in0=ps[:, 0:128], in1=maskB[:, :], op=ALU.mult),
                    reads=[f"ps{g % 2}", TAB], writes=["tmpM"])
                P.c("dve", lambda e, g=g: e.tensor_tensor(out=Macc[:, g, :], in0=Macc[:, g, :], in1=tmpM[:, :], op=ALU.add),
                    reads=["tmpM", "Macc"], writes=["Macc"])
        for g4 in range(8):
            for j in range(4):
                g = g4 * 4 + j
                P.c("pe", lambda e, g=g, j=j: e.transpose(psPV[:, j * 128:(j + 1) * 128], XR[:, g, :], ident_f[:, :]),
                    reads=["L1lo", "L1hi", "ident_f"], writes=["ps2"])
            P.c("act", lambda e, g4=g4, d=d: e.activation(out=PV[:, d * 32 + g4 * 4: d * 32 + g4 * 4 + 4, :],
                                                        in_=psPV[:, :].rearrange("p (j m) -> p j m", j=4), func=AF.Copy),
                reads=["ps2"], writes=["PV"])
    idb = ident_f[:, :].rearrange("p (t q) -> p t q", t=8).unsqueeze(1).broadcast_to([128, 32, 8, 16])
    d4 = d128[:, :, :].unsqueeze(2).broadcast_to([128, 32, 8, 16])
    P.c("dve", lambda e: e.tensor_tensor(out=QN4, in0=idb, in1=d4, op=ALU.mult), reads=[TAB, "ident_f", "L1lo", "L1hi"], writes=["L1lo", "L1hi"])
    P.c("dve", lambda e: e.tensor_tensor(out=Mbf[:, :, :], in0=Macc[:, :, :], in1=QN[:, :, :], op=ALU.add), reads=["Macc", "L1lo", "L1hi"], writes=["Mbf"])
    if debug:
        dd = nc.dram_tensor("d_Mbf", [128, 32 * 128], BF16, kind="ExternalOutput").ap(); dbg["d_Mbf"] = dd
        P.dma("sp", lambda e, dd=dd: e.dma_start(out=dd[:, :], in_=Mbf[:, :, :].rearrange("p a b -> p (a b)")), "dbg", reads=["Mbf"])
        dd = nc.dram_tensor("d_PV", [128, 64 * 128], BF16, kind="ExternalOutput").ap(); dbg["d_PV"] = dd
        P.dma("sp", lambda e, dd=dd: e.dma_start(out=dd[:, :], in_=PV[:, :, :].rearrange("p a b -> p (a b)")), "dbg", reads=["PV"])
        dd = nc.dram_tensor("d_WR", [128, 64 * 128], BF16, kind="ExternalOutput").ap(); dbg["d_WR"] = dd
        P.dma("sp", lambda e, dd=dd: e.dma_start(out=dd[:, :], in_=WR[:, :, :].rearrange("p a b -> p (a b)")), "dbg", reads=["WRb"])
    stage("S5l1")

    B1 = Arena(nc); B1.off = big0
    Z = B1.alloc("Z", [128, 32, 256], BF16)
    Zc = B1.alloc("Zc", [128, 32, 32], BF16)
    Ysb = B1.alloc("Ysb", [128, 8, 256], BF16)
    WV = 4
    Xw = [B1.alloc(f"Xw{i}", [128, WV, 288], BF16) for i in range(2)]
    TAw = [B1.alloc(f"TAw{i}", [128, WV, 80], BF16) for i in range(2)]
    TBw = [B1.alloc(f"TBw{i}", [128, WV, 80], BF16) for i in range(2)]
    Fall = B1.alloc("Fall", [128, 64], F32)
    Fctx = B1.alloc("Fctx", [128, 64], F32)
    hacc = B1.alloc("hacc", [128, 64], F32)
    hacc_bf = B1.alloc("hacc_bf", [128, 64], BF16)
    htmp = B1.alloc("htmp", [128, 64], F32)
    xgB = B1.alloc("xgB", [128, 4, 64], F32)
    D2 = B1.alloc("D2", [128, 64], F32)
    assert B1.off <= big0 + 32 * 1024, B1.off - big0
    B2 = Arena(nc); B2.off = dead0
    s5gT = B2.alloc("s5gT", [128, 4, T], BF16)
    wglu = B2.alloc("wglu", [128, 4, 512], BF16)
    NR = 36
    Rring = [B2.alloc(f"Rr{i}", [128, 128], BF16) for i in range(NR)]
    sgt = B2.alloc("sgt", [128, 512], BF16)
    XAw = [B2.alloc(f"XAw{i}", [128, WV, 256], BF16) for i in range(2)]
    XBw = [B2.alloc(f"XBw{i}", [128, WV, 256], BF16) for i in range(2)]
    assert B2.off <= S_END, (B2.off, S_END)
    post = ["Z", "Zc", "Ysb", "Fall", "Fctx", "hacc", "hacc_bf", "htmp", "xgB", "D2", "s5gT", "wglu", "sgt"] + \
           [f"{n}{i}" for n in ("Xw", "TAw", "TBw", "XAw", "XBw") for i in range(2)] + [f"Rr{i}" for i in range(NR)]
    alias(post, ["L1lo", "L1hi", "e1lo", "e1hi", "e2lo", "e2hi", "tmpM", TAB])
    P.dma("pool", lambda e: e.dma_start(out=wglu[:, :, :], in_=d_in["wglu"].rearrange("(k p) n -> p k n", p=128)), "wglu", reads=[TAB], writes=["wglu"])
    P.c("dve", lambda e: e.tensor_tensor(out=D2[:, :], in0=ident_f[:, 0:64], in1=ident_f[:, 64:128], op=ALU.add), reads=["ident_f", TAB], writes=["D2"])
    P.c("pool", lambda e: e.memset(Eblk[:, :, :], 0.0), reads=[], writes=["Eblk", "WRf"])
    for a_ in range(8):
        for b_ in range(8):
            P.c("pool", lambda e, a_=a_, b_=b_: e.affine_select(out=Eblk[:, a_ * 8 + b_, 16 * b_:16 * b_ + 16], in_=ones_f[:, 0:16], pattern=[[1, 16]],
                                                              compare_op=ALU.is_equal, fill=0.0, base=16 * a_, channel_multiplier=-1),
                reads=["ones_f"], writes=["Eblk"])
    psZ = [psA[3], psA[4]]
    ev = [0]

    def evac(out, in_, reads, writes, eng=None):
        ev[0] += 1
        if eng is None:
            eng = "act" if ev[0] % 2 else "dve"
        if eng == "act":
            P.c("act", lambda e: e.activation(out=out, in_=in_, func=AF.Copy), reads=reads, writes=writes)
        else:
            P.c("dve", lambda e: e.tensor_copy(out=out, in_=in_), reads=reads, writes=writes)

    for g in range(32):
        fc, g8 = g // 8, g % 8
        ps = psZ[g % 2]
        for s_ in range(8):
            P.c("pe", lambda e, ps=ps, fc=fc, g8=g8, s_=s_: e.matmul(ps[:, 0:256], lhsT=Eblk[:, g8 * 8 + s_, :], rhs=uT[:, fc, s_::8],
                                                                    start=(s_ == 0), stop=(s_ == 7)), reads=["Eblk", "uT"], writes=[f"ps{3 + g % 2}"])
        for s_ in range(8):
            P.c("pe", lambda e, ps=ps, fc=fc, g8=g8, s_=s_: e.matmul(ps[:, 256:288], lhsT=Eblk[:, g8 * 8 + s_, :], rhs=ucT[:, fc, s_::8],
                                                                    start=(s_ == 0), stop=(s_ == 7)), reads=["Eblk", "ucT"], writes=[f"ps{3 + g % 2}"])
        en_ = "act" if g % 2 else "dve"
        evac(Z[:, g, :], ps[:, 0:256], [f"ps{3 + g % 2}"], ["Z"], eng=en_)
        evac(Zc[:, g, :], ps[:, 256:288], [f"ps{3 + g % 2}"], ["Zc"], eng=en_)

    rr = [0]
    reng = ["dve", "pool"]

    def build_R(dst, dname, hp_idx, q):
        for half, Vt in ((0, V1), (1, V2)):
            rr[0] += 1
            eng = reng[rr[0] % 2]
            o_ = dst[:, half * 64:(half + 1) * 64]
            sc = Vt[:, hp_idx, q:q + 1]
            if eng == "act":
                P.c("act", lambda e, o_=o_, sc=sc: e.activation(out=o_, in_=D2[:, :], func=AF.Identity, scale=sc), reads=["D2", TAB], writes=[dname])
            elif eng == "dve":
                P.c("dve", lambda e, o_=o_, sc=sc: e.tensor_scalar(out=o_, in0=D2[:, :], scalar1=sc, scalar2=None, op0=ALU.mult), reads=["D2", TAB], writes=[dname])
            else:
                P.c("pool", lambda e, o_=o_, sc=sc: e.tensor_scalar(out=o_, in0=D2[:, :], scalar1=sc, scalar2=0.0, op0=ALU.mult, op1=ALU.add),
                    reads=["D2", TAB], writes=[dname])

    ring = [0]

    def get_R(hp_idx, q):
        slot = ring[0] % NR
        ring[0] += 1
        build_R(Rring[slot][:, :], f"Rr{slot}", hp_idx, q)
        return Rring[slot], f"Rr{slot}"

    def tree_wave(w):
        par = w % 2
        qs = [w * WV + c_ for c_ in range(WV)]
        d = qs[0] // 32
        bankL = [psA[1 + 2 * par], psA[2 + 2 * par]]
        bnL = [f"ps{1 + 2 * par}", f"ps{2 + 2 * par}"]
        psC, pcn = psA[5 + par], f"ps{5 + par}"
        for c_, q in enumerate(qs):
            g = q % 32
            bk, bn = bankL[c_ // 2], bnL[c_ // 2]
            P.c("pe", lambda e, bk=bk, q=q, g=g, c_=c_: e.matmul(bk[:, (c_ % 2) * 256:(c_ % 2 + 1) * 256], lhsT=PV[:, q, :], rhs=Z[:, g, :], start=True, stop=True),
                reads=["PV", "Z"], writes=[bn])
            P.c("pe", lambda e, q=q, g=g, c_=c_, psC=psC: e.matmul(psC[:, c_ * 32:(c_ + 1) * 32], lhsT=PV[:, q, :], rhs=Zc[:, g, :], start=True, stop=True),
                reads=["PV", "Zc"], writes=[pcn])
        yield
        xw, xwn = Xw[par], f"Xw{par}"
        for h_ in range(2):
            evac(xw[:, 2 * h_:2 * h_ + 2, 0:256], bankL[h_][:, :].rearrange("p (c n) -> p c n", c=2), [bnL[h_]], [xwn], eng="act")
        evac(xw[:, :, 256:288], psC[:, 0:WV * 32].rearrange("p (c n) -> p c n", c=WV), [pcn], [xwn], eng="act")
        yield
        cur, cname = xw, xwn
        loc_off, loc_n, ctx_off, ctx_n = 0, 256, 256, 32
        bufs = [(TAw[par], f"TAw{par}"), (TBw[par], f"TBw{par}")]
        tb_ = par * 256
        Rw_next = [[None] + [get_R(j - 1, q) for j in (1, 2, 3)] for q in qs]
        for k in range(4):
            nxt, nname = bufs[k % 2]
            lev = []
            for (off, n_, is_ctx) in ((loc_off, loc_n, False), (ctx_off, ctx_n, True)):
                if n_ <= 1:
                    continue
                rad = 4 if n_ >= 4 else n_
                no = n_ // rad
                base = 128 if not is_ctx else 384
                lev.append((off, n_, is_ctx, rad, no, base))
            Rw = Rw_next
            for c_, q in enumerate(qs):
                for (off, n_, is_ctx, rad, no, base) in lev:
                    ocol = base + c_ * no
                    bk_, bkn_ = (psC, pcn)
                    for jj in range(rad):
                        pw = (rad - 1 - jj) if d == 0 else jj
                        lt, ltn = (ident_b, "ident_b") if pw == 0 else Rw[c_][pw]
                        P.c("pe", lambda e, bk_=bk_, lt=lt, cur=cur, c_=c_, off=off, jj=jj, rad=rad, n_=n_, ocol=ocol, no=no: e.matmul(
                            bk_[:, ocol:ocol + no], lhsT=lt[:, :], rhs=cur[:, c_, off + jj:off + n_:rad], start=(jj == 0), stop=(jj == rad - 1)),
                            reads=[ltn, cname], writes=[bkn_])
            if k < 3:
                Rw_next = [[None] + [get_R(3 * (k + 1) + j - 1, q) for j in (1, 2, 3)] for q in qs]
            yield
            for (off, n_, is_ctx, rad, no, base) in lev:
                bk_, bkn_ = (psC, pcn)
                if no == 1:
                    dstF, dn = (Fctx, "Fctx") if is_ctx else (Fall, "Fall")
                    P.c("act", lambda e, bk_=bk_, dstF=dstF, base=base, q0=qs[0]: e.activation(out=dstF[:, q0:q0 + WV], in_=bk_[:, base:base + WV], func=AF.Copy),
                        reads=[bkn_], writes=[dn])
                else:
                    o2 = 64 if is_ctx else 0
                    evac(nxt[:, :, o2:o2 + no], bk_[:, base:base + WV * no].rearrange("p (c n) -> p c n", c=WV), [bkn_], [nname], eng="act")
            cur, cname = nxt, nname
            loc_off, loc_n = 0, (loc_n // 4 if loc_n > 1 else 0)
            ctx_off, ctx_n = 64, (ctx_n // (4 if ctx_n >= 4 else ctx_n) if ctx_n > 1 else 0)

    def lockstep(gens):
        gens = list(gens)
        while gens:
            for g_ in list(gens):
                try:
                    next(g_)
                except StopIteration:
                    gens.remove(g_)

    def rolling(gens, depth=2, stagger=2):
        it = iter(gens)
        active = []
        pending = next(it, None)
        while active or pending is not None:
            if pending is not None and len(active) < depth and (not active or active[-1][1] >= stagger):
                active.append([pending, 0])
                pending = next(it, None)
            for ent in list(active):
                try:
                    next(ent[0])
                    ent[1] += 1
                except StopIteration:
                    active.remove(ent)

    rolling([tree_wave(w) for w in range(64 // WV)])
    if debug:
        for (nm, t_) in (("d_Fall", Fall), ("d_Fctx", Fctx)):
            dd = dout(nm, [128, 64])
            P.dma("sp", lambda e, dd=dd, t_=t_: e.dma_start(out=dd[:, :], in_=t_[:, :]), "dbg", reads=["Fall", "Fctx"])
    stage("S5tree")

    xb_in = nc.dram_tensor("xb_in", [128, 64], F32)
    xb_out = nc.dram_tensor("xb_out", [512, 64], F32)
    P.dma("sp", lambda e: e.dma_start(out=xb_in[:, :], in_=Fall[:, :]), "xb_st", reads=["Fall"], writes=["xb_in"])
    P.dma("pool", lambda e: e.collective_compute("AllGather", ALU.bypass, replica_groups=[[0, 1, 2, 3], [4, 5, 6, 7]],
                                                  ins=[xb_in.ap().opt()], outs=[xb_out.ap().opt()]),
          "ccB", reads=["xb_in"], writes=["xb_out"], inc=1)
    P.dma("sp", lambda e: e.dma_start(out=xgB[:, :, :], in_=xb_out.ap().rearrange("(r p) n -> p r n", p=128)), "xb_ld", reads=["xb_out"], writes=["xgB"])
    P.c("dve", lambda e: e.tensor_copy(out=hacc[:, :], in_=Fctx[:, :]), reads=["Fctx"], writes=["hacc"])
    psH = psA[1]
    for n_ in range(3):
        P.c("act", lambda e: e.activation(out=hacc_bf[:, :], in_=hacc[:, :], func=AF.Copy), reads=["hacc"], writes=["hacc_bf"])
        for q in range(64):
            Rt, Rn = get_R(12, q)
            P.c("pe", lambda e, q=q, Rt=Rt: e.matmul(psH[:, q:q + 1], lhsT=Rt[:, :], rhs=hacc_bf[:, q:q + 1], start=True, stop=True),
                reads=[Rn, "hacc_bf"], writes=["ps1"])
        for (cs_, rank, mcol) in ((slice(0, 32), n_, n_), (slice(32, 64), 3 - n_, 3 + n_)):
            P.c("dve", lambda e, cs_=cs_, rank=rank: e.tensor_tensor(out=htmp[:, cs_], in0=psH[:, cs_], in1=xgB[:, rank, cs_], op=ALU.add),
                reads=["ps1", "xgB"], writes=["htmp"])
            P.c("dve", lambda e, cs_=cs_: e.tensor_tensor(out=htmp[:, cs_], in0=htmp[:, cs_], in1=hacc[:, cs_], op=ALU.subtract),
                reads=["htmp", "hacc"], writes=["htmp"])
            P.c("dve", lambda e, cs_=cs_, mcol=mcol: e.scalar_tensor_tensor(out=hacc[:, cs_], in0=htmp[:, cs_], scalar=msk[:, mcol:mcol + 1],
                                                                           in1=hacc[:, cs_], op0=ALU.mult, op1=ALU.add),
                reads=["htmp", "hacc", "msk"], writes=["hacc"])
    P.c("act", lambda e: e.activation(out=hacc_bf[:, :], in_=hacc[:, :], func=AF.Copy), reads=["hacc"], writes=["hacc_bf"])
    stage("S5xchg")

    psY = psA[0]
    psS2 = [psA[5], psA[6]]
    def ks_wave(fc, gp):
        w = fc * 4 + gp
        par = w % 2
        g0 = fc * 8 + gp * 2
        qs = [g0, g0 + 1, 32 + g0, 33 + g0]
        banks = [psA[1 + 2 * par], psA[2 + 2 * par]]
        bns = [f"ps{1 + 2 * par}", f"ps{2 + 2 * par}"]
        for c_, q in enumerate(qs):
            P.c("pe", lambda e, c_=c_, q=q, bk=banks[c_ // 2]: e.matmul(bk[:, (c_ % 2) * 256:(c_ % 2 + 1) * 256], lhsT=PV[:, q, :], rhs=Z[:, q % 32, :],
                                                   start=True, stop=True), reads=["PV", "Z"], writes=[bns[c_ // 2]])
        yield
        xa, xan, xb_, xbn = XAw[par], f"XAw{par}", XBw[par], f"XBw{par}"
        evac(xa[:, 0:2, 1:256], banks[0][:, :].rearrange("p (c n) -> p c n", c=2)[:, :, 0:255], [bns[0]], [xan], eng="act")
        evac(xa[:, 2:4, 0:255], banks[1][:, :].rearrange("p (c n) -> p c n", c=2)[:, :, 1:256], [bns[1]], [xan], eng="act")
        P.c("dve", lambda e, xa=xa, g0=g0: e.tensor_copy(out=xa[:, 0:2, 0], in_=hacc_bf[:, g0:g0 + 2]), reads=["hacc_bf"], writes=[xan])
        P.c("dve", lambda e, xa=xa, g0=g0: e.tensor_copy(out=xa[:, 2:4, 255], in_=hacc_bf[:, 32 + g0:34 + g0]), reads=["hacc_bf"], writes=[xan])
        yield
        cur, cn, oth, on = xa, xan, xb_, xbn
        Rw_next = [[get_R(j - 1, q) for j in (1, 2, 3)] for q in qs]
        for k in range(4):
            Rw = Rw_next
            for c_, q in enumerate(qs):
                pk, pkn = banks[c_ // 2], bns[c_ // 2]
                cb_ = (c_ % 2) * 256
                P.c("pe", lambda e, pk=pk, cb_=cb_, cur=cur, c_=c_: e.matmul(pk[:, cb_:cb_ + 256], lhsT=ident_b[:, :], rhs=cur[:, c_, :], start=True, stop=False),
                    reads=["ident_b", cn], writes=[pkn])
                for j in (1, 2, 3):
                    sh = j * (4 ** k)
                    Rt, Rn = Rw[c_][j - 1]
                    if c_ < 2:
                        P.c("pe", lambda e, pk=pk, cb_=cb_, Rt=Rt, cur=cur, c_=c_, sh=sh, j=j: e.matmul(
                            pk[:, cb_ + sh:cb_ + 256], lhsT=Rt[:, :], rhs=cur[:, c_, 0:256 - sh], start=False, stop=(j == 3)), reads=[Rn, cn], writes=[pkn])
                    else:
                        P.c("pe", lambda e, pk=pk, cb_=cb_, Rt=Rt, cur=cur, c_=c_, sh=sh, j=j: e.matmul(
                            pk[:, cb_:cb_ + 256 - sh], lhsT=Rt[:, :], rhs=cur[:, c_, sh:256], start=False, stop=(j == 3)), reads=[Rn, cn], writes=[pkn])
            if k < 3:
                Rw_next = [[get_R(3 * (k + 1) + j - 1, q) for j in (1, 2, 3)] for q in qs]
            yield
            evac(oth[:, 0:2, :], banks[0][:, :].rearrange("p (c n) -> p c n", c=2), [bns[0]], [on], eng="act")
            evac(oth[:, 2:4, :], banks[1][:, :].rearrange("p (c n) -> p c n", c=2), [bns[1]], [on], eng="act")
            cur, cn, oth, on = oth, on, cur, cn
            yield
        for c_ in range(2):
            g = g0 + c_
            ysl = slice(c_ * 256, (c_ + 1) * 256)
            P.c("pe", lambda e, g=g, ysl=ysl: e.matmul(psY[:, ysl], lhsT=Mbf[:, g, :], rhs=Z[:, g, :], start=True, stop=False), reads=["Mbf", "Z"], writes=["ps0"])
            P.c("pe", lambda e, g=g, ysl=ysl, cur=cur, c_=c_: e.matmul(psY[:, ysl], lhsT=WR[:, g, :], rhs=cur[:, c_, :], start=False, stop=False),
                reads=["WRb", cn], writes=["ps0"])
            P.c("pe", lambda e, g=g, ysl=ysl, cur=cur, c_=c_: e.matmul(psY[:, ysl], lhsT=WR[:, 32 + g, :], rhs=cur[:, 2 + c_, :], start=False, stop=True),
                reads=["WRb", cn], writes=["ps0"])
        evac(Ysb[:, 2 * gp:2 * gp + 2, :], psY[:, :].rearrange("p (c n) -> p c n", c=2), ["ps0"], ["Ysb"], eng="act")

    for fc in range(4):
        rolling([ks_wave(fc, gp) for gp in range(4)])
        for s_ in range(8):
            ps = psS2[s_ % 2]
            for g8 in range(8):
                P.c("pe", lambda e, ps=ps, s_=s_, g8=g8: e.matmul(ps[:, 0:256], lhsT=Eblk[:, s_ * 8 + g8, :], rhs=Ysb[:, g8, :],
                                                                 start=(g8 == 0), stop=(g8 == 7)), reads=["Eblk", "Ysb"], writes=[f"ps{5 + s_ % 2}"])
            P.c("act", lambda e, ps=ps, s_=s_, fc=fc: e.activation(out=s5gT[:, fc, s_::8], in_=ps[:, 0:256], func=AF.Gelu_apprx_tanh),
                reads=[f"ps{5 + s_ % 2}"], writes=["s5gT"])
    stage("S5y")
    psG = [psA[0], psA[1]]
    for tb_ in range(4):
        tsl = slice(tb_ * 512, (tb_ + 1) * 512)
        for oc in range(4):
            ps = psG[oc % 2]
            for kc in range(4):
                P.c("pe", lambda e, ps=ps, kc=kc, oc=oc, tsl=tsl: e.matmul(ps[:, :], lhsT=wglu[:, kc, oc * 128:(oc + 1) * 128], rhs=s5gT[:, kc, tsl],
                                                                          start=(kc == 0), stop=(kc == 3)), reads=["wglu", "s5gT"], writes=[f"ps{oc % 2}"])
            P.c("act", lambda e, ps=ps, oc=oc: e.activation(out=sgt[:, :], in_=ps[:, :], func=AF.Sigmoid, bias=bglu[:, oc:oc + 1]),
                reads=[f"ps{oc % 2}", TAB], writes=["sgt"])
            P.c("dve", lambda e, oc=oc, tsl=tsl: e.tensor_tensor(out=mixT[:, oc, tsl], in0=s5gT[:, oc, tsl], in1=sgt[:, :], op=ALU.mult),
                reads=["s5gT", "sgt", "Macc"], writes=["mixT", "Macc"])
    if debug:
        dd = nc.dram_tensor("d_s5gT", [128, 4 * T], BF16, kind="ExternalOutput").ap(); dbg["d_s5gT"] = dd
        P.dma("sp", lambda e, dd=dd: e.dma_start(out=dd[:, :], in_=s5gT[:, :, :].rearrange("p a b -> p (a b)")), "dbg", reads=["s5gT"])
        dd = nc.dram_tensor("d_mixT2", [128, 8 * T], BF16, kind="ExternalOutput").ap(); dbg["d_mixT2"] = dd
        P.dma("sp", lambda e, dd=dd: e.dma_start(out=dd[:, :], in_=mixT[:, :, :].rearrange("p a b -> p (a b)")), "dbg", reads=["mixT"])
    stage("S5glu")
    return None


def _s5_host(inputs):
    def dup(a):
        return np.ascontiguousarray(np.concatenate([a, a], 0)).astype(np.float32)
    lre = np.concatenate([inputs["s5_lambda_re_f"][0], inputs["s5_lambda_re_b"][0]], 0)
    lim = np.concatenate([inputs["s5_lambda_im_f"][0], inputs["s5_lambda_im_b"][0]], 0)
    lst = np.concatenate([inputs["s5_log_step_f"][0], inputs["s5_log_step_b"][0]], 0)
    return {
        "s5_lamre": dup(lre.T), "s5_lamim": dup(lim.T),
        "s5_lstep": np.ascontiguousarray(np.broadcast_to(lst[None, :], (128, 64))).astype(np.float32),
        "s5_bre": dup(inputs["s5_b_re"][0].transpose(1, 0, 2)), "s5_bim": dup(inputs["s5_b_im"][0].transpose(1, 0, 2)),
        "s5_cre": dup(inputs["s5_c_re"][0].transpose(2, 0, 1)), "s5_cim": dup(inputs["s5_c_im"][0].transpose(2, 0, 1)),
        "s5_d128": np.ascontiguousarray(np.broadcast_to(inputs["s5_d"][0].reshape(1, 32, 16), (128, 32, 16))).astype(np.float32),
        "s5_bglu": np.ascontiguousarray(inputs["s5_b_glu"][0].reshape(4, 128).T).astype(np.float32),
        "w_glu": np.ascontiguousarray(inputs["s5_w_glu"][0]).astype(np.float32),
    }


def make_inputs(inputs):
    x = np.asarray(inputs["x"], np.float32)
    per = []
    for r in range(NCORES):
        b, seg = r // 4, r % 4
        cv = np.stack([inputs["c"][b].reshape(8, 128).T, inputs["c_ctx"].reshape(8, 128).T], -1)
        m = {
            "x_loc": np.ascontiguousarray(x[b, seg * T:(seg + 1) * T]),
            "cvec": np.ascontiguousarray(cv.reshape(128, 16)).astype(np.float32),
            "w_mod": np.ascontiguousarray(inputs["w_mod"][0]),
            "b_mod2": np.ascontiguousarray(np.broadcast_to(inputs["b_mod"][0][None, :], (2, 6 * D))).astype(np.float32),
            "n1w": np.ascontiguousarray(inputs["norm1_w"][0].reshape(8, 128).T),
            "n2w": np.ascontiguousarray(inputs["norm2_w"][0].reshape(8, 128).T),
            "w_in": np.ascontiguousarray(inputs["w_in"][0]),
            "segf": np.full((128, 1), float(seg), np.float32),
            "ctxb": np.ascontiguousarray(inputs["ctx"][b]).astype(np.float32),
            "ldv": np.ascontiguousarray(np.broadcast_to(np.concatenate([inputs["ret_log_decay_f"][0], inputs["ret_log_decay_b"][0]])[None, :], (128, 8))).astype(np.float32),
            **_s5_host(inputs),
            "w_out": np.ascontiguousarray(inputs["w_out"][0]).astype(np.float32),
            "fnw_b": np.ascontiguousarray(np.broadcast_to(inputs["final_norm_w"][None, :], (128, D))).astype(np.float32),
            "cw": np.ascontiguousarray(inputs["conv_w"][0].reshape(3, 22, 128).transpose(2, 1, 0).reshape(128, 66)).astype(np.float32),
            "cb": np.ascontiguousarray(inputs["conv_b"][0].reshape(22, 128).T).astype(np.float32),
            "oh": np.ascontiguousarray(np.broadcast_to(np.array([float(r_ == seg - 1) for r_ in range(4)] + [float(r_ == seg + 1) for r_ in range(4)],
                                                                 np.float32)[None, :], (128, 8))),
            "w_up": np.ascontiguousarray(inputs["w_up"][0]).astype(np.float32),
            "w_down": np.ascontiguousarray(inputs["w_down"][0]).astype(np.float32),
            "msk": np.ascontiguousarray(np.broadcast_to(np.array([0 < seg, 1 < seg, 2 < seg, 3 > seg, 2 > seg, 1 > seg, 0, 0], np.float32)[None, :], (128, 8))),
        }
        per.append(m)
    return per


def kernel(**inputs):
    nc, _ = build(debug=False)
    per = make_inputs(inputs)
    res = run_bass_kernel_spmd(nc, per, core_ids=list(range(NCORES)))
    out = np.zeros((2, 4 * T, D), np.float32)
    for r in range(NCORES):
        b, seg = r // 4, r % 4
        out[b, seg * T:(seg + 1) * T] = res.results[r]["out"]
    return out
```

```python
import contextlib
import numpy as np
import concourse.bass as bass
import concourse.mybir as mybir
from concourse.bass_utils import run_bass_kernel_spmd

F32 = mybir.dt.float32
BF16 = mybir.dt.bfloat16
I32 = mybir.dt.int32
AF = mybir.ActivationFunctionType
ALU = mybir.AluOpType

D = 1024
T = 2048
NT = 16
NCORES = 8
SB_BASE = 16512
SB_TOP = 229344


class Res:
    __slots__ = ("name", "last_w", "readers")

    def __init__(self, name):
        self.name = name
        self.last_w = None
        self.readers = []


class Op:
    __slots__ = ("eng", "fn", "deps", "needs_inc", "sig", "kind", "group", "inc")

    def __init__(self, eng, fn, kind, group, inc):
        self.eng = eng
        self.fn = fn
        self.deps = []
        self.needs_inc = False
        self.sig = None
        self.kind = kind
        self.group = group
        self.inc = inc


class Prog:
    ENGS = ("pe", "act", "dve", "pool", "sp")

    def __init__(self, nc):
        self.nc = nc
        self.ops = []
        self.res = {}

    def _r(self, x):
        if isinstance(x, Res):
            return x
        if x not in self.res:
            self.res[x] = Res(x)
        return self.res[x]

    def _add(self, op, reads, writes):
        reads = [self._r(x) for x in reads]
        writes = [self._r(x) for x in writes]
        deps = set()
        for x in reads:
            if x.last_w is not None:
                deps.add(x.last_w)
            if x.name.startswith("ps"):
                for rd in x.readers:
                    if rd.eng != op.eng:
                        deps.add(rd)
        for x in writes:
            if x.last_w is not None:
                deps.add(x.last_w)
            deps.update(x.readers)
        deps.discard(op)
        for d in deps:
            if d.kind == "c" and op.kind == "c" and d.eng == "pe" and op.eng == "pe":
                continue
            op.deps.append(d)
            d.needs_inc = True
        for x in reads:
            x.readers.append(op)
        for x in writes:
            x.last_w = op
            x.readers = []
        self.ops.append(op)
        return op

    def c(self, eng, fn, reads=(), writes=()):
        return self._add(Op(eng, fn, "c", None, 1), reads, writes)

    def dma(self, eng, fn, group, reads=(), writes=(), inc=16):
        op = Op(eng, fn, "d", group, inc)
        op.needs_inc = True
        return self._add(op, reads, writes)

    def emit(self, final_wait_groups=()):
        nc = self.nc
        sems = {}
        with contextlib.ExitStack() as st:
            cnt = {}
            for op in self.ops:
                if op.kind == "c":
                    if op.needs_inc:
                        k = "E_" + op.eng
                        cnt[k] = cnt.get(k, 0) + 1
                        op.sig = (k, cnt[k])
                else:
                    k = "D_" + op.group
                    cnt[k] = cnt.get(k, 0) + op.inc
                    op.sig = (k, cnt[k])
            for k in cnt:
                sems[k] = st.enter_context(nc.semaphore(k))
            engobj = {"pe": "tensor", "act": "scalar", "dve": "vector", "pool": "gpsimd", "sp": "sync"}
            finals = [("D_" + g, cnt["D_" + g]) for g in final_wait_groups]
            with nc.Block() as block:
                for e in self.ENGS:
                    ops = [o for o in self.ops if o.eng == e]

                    def body(eng, ops=ops, e=e):
                        seen = {}
                        for op in ops:
                            need = {}
                            for d in op.deps:
                                s, v = d.sig
                                if v > need.get(s, 0):
                                    need[s] = v
                            for s, v in need.items():
                                if seen.get(s, 0) >= v:
                                    continue
                                eng.wait_ge(sems[s], v)
                                seen[s] = v
                            ins = op.fn(eng)
                            if op.needs_inc:
                                ins.then_inc(sems[op.sig[0]], op.inc)
                        if e == "sp":
                            for s, v in finals:
                                eng.wait_ge(sems[s], v)
                    if ops or e == "sp":
                        getattr(block, engobj[e])(body)
        return cnt


class Arena:
    def __init__(self, nc):
        self.nc = nc
        self.off = SB_BASE
        self.n = 0

    def alloc(self, name, shape, dtype, at=None):
        esz = 4 if dtype in (F32, I32) else 2
        per = int(np.prod(shape[1:])) * esz
        per = (per + 63) // 64 * 64
        if at is None:
            at = self.off
            self.off += per
            assert self.off <= SB_TOP, (name, self.off)
        self.n += 1
        return self.nc.alloc_sbuf_tensor_at(f"{name}_{self.n}", list(shape), dtype, offset=at)


class _Stop(Exception):
    pass


def build(debug=False, stop=None):
    nc = bass.Bass("TRN2", target_bir_lowering=False)
    P = Prog(nc)
    A = Arena(nc)

    def din(name, shape, dt=F32):
        return nc.dram_tensor(name, list(shape), dt, kind="ExternalInput").ap()

    x_d = din("x_loc", [T, D])
    cvec_d = din("cvec", [128, 16])
    wmod_d = din("w_mod", [D, 6 * D])
    bmod_d = din("b_mod2", [2, 6 * D])
    n1w_d = din("n1w", [128, 8])
    n2w_d = din("n2w", [128, 8])
    win_d = din("w_in", [D, 2560])
    segf_d = din("segf", [128, 1])
    ctx_d = din("ctxb", [256, D])
    ldv_d = din("ldv", [128, 8])
    msk_d = din("msk", [128, 8])
    out_d = nc.dram_tensor("out", [T, D], F32, kind="ExternalOutput").ap()
    dbg = {}

    def dout(name, shape):
        dbg[name] = nc.dram_tensor(name, list(shape), F32, kind="ExternalOutput").ap()
        return dbg[name]

    def stage(name):
        if stop == name:
            raise _Stop()

    stopped = False
    try:
        _body(nc, P, A, debug, stage, dout, dbg, din, x_d, cvec_d, wmod_d, bmod_d, n1w_d, n2w_d, win_d, segf_d, ctx_d, ldv_d, msk_d, out_d)
    except _Stop:
        stopped = True
    if stopped:
        xfin = nc.alloc_sbuf_tensor_at("xfin", [128, D], F32, offset=SB_TOP - 4096)
        rfin = ["xfin"] + [n for n in P.res]
        for i in range(NT):
            P.dma("sp", lambda e, i=i: e.dma_start(out=xfin[:, :], in_=x_d[i * 128:(i + 1) * 128, :]), "xinF", reads=[], writes=rfin if i == 0 else ["xfin"])
            P.dma("sp", lambda e, i=i: e.dma_start(out=out_d[i * 128:(i + 1) * 128, :], in_=xfin[:, :]), "xout", reads=["xfin"])
    has_dbg = any(o.kind == "d" and o.group == "dbg" for o in P.ops)
    P.emit(final_wait_groups=["xout"] + (["dbg"] if has_dbg else []))
    return nc, dbg


def _body(nc, P, A, debug, stage, dout, dbg, din, x_d, cvec_d, wmod_d, bmod_d, n1w_d, n2w_d, win_d, segf_d, ctx_d, ldv_d, msk_d, out_d):

    ident_f = A.alloc("ident_f", [128, 128], F32)
    ident_b = A.alloc("ident_b", [128, 128], BF16)
    ones_f = A.alloc("ones_f", [128, 128], F32)
    P.c("pool", lambda e: e.memset(ones_f[:, :], 1.0), writes=["ones_f"])
    negpi = A.alloc("negpi", [128, 1], F32)
    epst = A.alloc("epst", [128, 1], F32)
    P.c("pool", lambda e: e.memset(negpi[:, :], -float(np.pi)), writes=["negpi"])
    P.c("pool", lambda e: e.memset(epst[:, :], 1e-6), writes=["epst"])
    P.c("pool", lambda e: e.memset(ident_f[:, :], 0.0), writes=["ident_f"])
    P.c("pool", lambda e: e.affine_select(out=ident_f[:, :], in_=ones_f[:, :], pattern=[[1, 128]],
                                          compare_op=ALU.is_equal, fill=0.0, base=0, channel_multiplier=-1),
        reads=["ones_f"], writes=["ident_f"])
    P.c("dve", lambda e: e.tensor_copy(out=ident_b[:, :], in_=ident_f[:, :]), reads=["ident_f"], writes=["ident_b"])

    cvec = A.alloc("cvec", [128, 16], F32)
    s_bf = A.alloc("s_bf", [128, 16], BF16)
    n1w = A.alloc("n1w", [128, 8], F32)
    n2w = A.alloc("n2w", [128, 8], F32)
    modT = A.alloc("modT", [128, 32, 2], F32)
    g1x = A.alloc("g1x", [128, 8], F32)
    g1c = A.alloc("g1c", [128, 8], F32)
    g2x = A.alloc("g2x", [128, 8], F32)
    bc2 = A.alloc("bc2", [128, D], F32)
    bc5 = A.alloc("bc5", [128, D], F32)
    segf = A.alloc("segf", [128, 1], F32)
    colf = A.alloc("colf", [128, 1], F32)
    rowb = A.alloc("rowb", [128, 1], F32)
    rowv = A.alloc("rowv", [128, 16], F32)
    invf = A.alloc("invf", [128, 32], F32)
    big0 = A.off
    wm = A.alloc("wm", [128, 8, 2048], BF16)
    bmod = A.alloc("bmod", [2, 6 * D], F32)
    modrow = A.alloc("modrow", [2, 6 * D], F32)
    ang = A.alloc("ang", [128, 16, 64], F32)
    tq = A.alloc("tq", [128, 1024], F32)
    ti = A.alloc("ti", [128, 1024], I32)
    tf = A.alloc("tf", [128, 1024], F32)
    assert A.off - big0 == 96 * 1024, A.off - big0
    r2_base = A.off
    scratchB_off = A.off
    A.off += 37 * 1024 + 512
    P.dma("sp", lambda e: e.dma_start(out=cvec[:, :], in_=cvec_d[:, :]), "small", writes=["cvec"])
    P.dma("sp", lambda e: e.dma_start(out=bmod[:, :], in_=bmod_d[:, :]), "small2", writes=["bmod"])
    P.dma("sp", lambda e: e.dma_start(out=n1w[:, :], in_=n1w_d[:, :]), "small3", writes=["n1w"])
    P.dma("sp", lambda e: e.dma_start(out=n2w[:, :], in_=n2w_d[:, :]), "small4", writes=["n2w"])
    P.c("act", lambda e: e.activation(out=s_bf[:, :], in_=cvec[:, :], func=AF.Silu), reads=["cvec"], writes=["s_bf"])

    wm_src = wmod_d.rearrange("(k p) n -> p k n", p=128)
    psA = [nc.alloc_psum_tensor(f"ps{i}", [128, 512], F32) for i in range(7)]
    s3 = s_bf[:, :].rearrange("p (k w) -> p k w", w=2)
    wmB = nc.alloc_sbuf_tensor_at("wmB", [128, 8, 2048], BF16, offset=scratchB_off)
    wms = [(wm, "wm"), (wmB, "wmB"), (wm, "wm")]
    def wm_dma(cb):
        wt_, wn_ = wms[cb]
        P.dma("pool", lambda e: e.dma_start(out=wt_[:, :, :], in_=wm_src[:, :, cb * 2048:(cb + 1) * 2048]), f"wmd{cb}", writes=[wn_])

    wm_dma(0)
    wm_dma(1)
    for nb in range(12):
        ps = psA[nb % 2]
        wt_, wn_ = wms[nb // 4]
        if nb == 4:
            wm_dma(2)
        for k in range(8):
            P.c("pe", lambda e, ps=ps, k=k, nb=nb, wt_=wt_: e.matmul(ps[0:2, :], lhsT=s3[:, k, :], rhs=wt_[:, k, (nb % 4) * 512:(nb % 4 + 1) * 512],
                                                                  start=(k == 0), stop=(k == 7)),
                reads=["s_bf", wn_], writes=[f"ps{nb % 2}"])
        P.c("dve", lambda e, ps=ps, nb=nb: e.tensor_tensor(out=modrow[:, nb * 512:(nb + 1) * 512], in0=ps[0:2, :],
                                                         in1=bmod[:, nb * 512:(nb + 1) * 512], op=ALU.add),
            reads=[f"ps{nb % 2}", "bmod"], writes=["modrow"])
    psT = psA[2]
    chunks = list(range(0, 16)) + list(range(24, 40))
    for j, ch in enumerate(chunks):
        P.c("pe", lambda e, j=j, ch=ch: e.transpose(psT[:, 2 * j:2 * j + 2], modrow[:, ch * 128:(ch + 1) * 128], ident_f[0:2, 0:2]),
            reads=["modrow", "ident_f"], writes=["ps2"])
    P.c("dve", lambda e: e.tensor_copy(out=modT[:, :, :].rearrange("p a b -> p (a b)"), in_=psT[:, 0:64]),
        reads=["ps2"], writes=["modT"])
    P.c("dve", lambda e: e.scalar_tensor_tensor(out=g1x[:, :], in0=modT[:, 8:16, 0], scalar=1.0, in1=n1w[:, :],
                                                op0=ALU.add, op1=ALU.mult), reads=["modT", "n1w"], writes=["g1x"])
    P.c("dve", lambda e: e.scalar_tensor_tensor(out=g1c[:, :], in0=modT[:, 8:16, 1], scalar=1.0, in1=n1w[:, :],
                                                op0=ALU.add, op1=ALU.mult), reads=["modT", "n1w"], writes=["g1c"])
    P.c("dve", lambda e: e.scalar_tensor_tensor(out=g2x[:, :], in0=modT[:, 24:32, 0], scalar=1.0, in1=n2w[:, :],
                                                op0=ALU.add, op1=ALU.mult), reads=["modT", "n2w"], writes=["g2x"])

    if debug:
        d_mod = dout("d_mod", [2, 6 * D])
        P.dma("sp", lambda e: e.dma_start(out=d_mod[:, :], in_=modrow[:, :]), "dbg", reads=["modrow"])
        d_g1x = dout("d_g1x", [128, 8])
        P.dma("sp", lambda e: e.dma_start(out=d_g1x[:, :], in_=g1x[:, :]), "dbg", reads=["g1x"])

    stage("A")
    for (bc, base, nm) in ((bc2, 2048, "bc2"), (bc5, 5120, "bc5")):
        for hh in range(2):
            ps = psA[3 + hh]
            P.c("pe", lambda e, ps=ps, base=base, hh=hh: e.matmul(ps[:, :], lhsT=ones_f[0:1, :], rhs=modrow[0:1, base + hh * 512: base + (hh + 1) * 512],
                                                                start=True, stop=True), reads=["ones_f", "modrow"], writes=[f"ps{3 + hh}"])
            P.c("act", lambda e, ps=ps, bc=bc, hh=hh: e.activation(out=bc[:, hh * 512:(hh + 1) * 512], in_=ps[:, :], func=AF.Copy),
                reads=[f"ps{3 + hh}"], writes=[nm])

    P.dma("sp", lambda e: e.dma_start(out=segf[:, :], in_=segf_d[:, :]), "small5", writes=["segf"])
    for hb in range(2):
        P.c("pool", lambda e, hb=hb: e.iota(colf[hb * 64:(hb + 1) * 64, :], pattern=[[0, 1]], base=0, channel_multiplier=1,
                                           allow_small_or_imprecise_dtypes=True), writes=["colf"])
        P.c("pool", lambda e, hb=hb: e.memset(rowb[hb * 64:(hb + 1) * 64, :], float(hb)), writes=["rowb"])
    P.c("pool", lambda e: e.iota(rowv[:, :], pattern=[[2, 16]], base=0, channel_multiplier=0, allow_small_or_imprecise_dtypes=True),
        writes=["rowv"])
    for j in range(32):
        P.c("pool", lambda e, j=j: e.memset(invf[:, j:j + 1], float(np.float32(10000.0) ** (-np.float32(j) / np.float32(32)))),
            writes=["invf"])
    P.c("dve", lambda e: e.scalar_tensor_tensor(out=rowb[:, :], in0=segf[:, :], scalar=32.0, in1=rowb[:, :], op0=ALU.mult, op1=ALU.add),
        reads=["segf", "rowb"], writes=["rowb"])
    P.c("dve", lambda e: e.tensor_scalar(out=rowv[:, :], in0=rowv[:, :], scalar1=rowb[:, 0:1], scalar2=None, op0=ALU.add),
        reads=["rowv", "rowb"], writes=["rowv"])
    ropeC = A.alloc("ropeC", [128, 16, 64], F32)
    ropeS = A.alloc("ropeS", [128, 16, 64], F32)
    ropeCk = A.alloc("ropeCk", [128, 16, 64], F32)
    ropeSk = A.alloc("ropeSk", [128, 16, 64], F32)
    for i in range(16):
        P.c("dve", lambda e, i=i: e.tensor_scalar(out=ang[:, i, 0:32], in0=invf[:, :], scalar1=rowv[:, i:i + 1], scalar2=None, op0=ALU.mult),
            reads=["invf", "rowv"], writes=["ang"])
        P.c("dve", lambda e, i=i: e.tensor_scalar(out=ang[:, i, 32:64], in0=invf[:, :], scalar1=colf[:, 0:1], scalar2=None, op0=ALU.mult),
            reads=["invf", "colf"], writes=["ang"])
    angf = ang[:, :, :].rearrange("p a b -> p (a b)")
    TWO_PI = 2.0 * np.pi
    for (dst, off, nm) in ((ropeS, 0.5, "ropeS"), (ropeC, 0.75, "ropeC")):
        dflat = dst[:, :, :].rearrange("p a b -> p (a b)")
        P.c("dve", lambda e, off=off: e.tensor_scalar(out=tq[:, :], in0=angf, scalar1=1.0 / TWO_PI, scalar2=off, op0=ALU.mult, op1=ALU.add),
            reads=["ang"], writes=["tq"])
        P.c("dve", lambda e: e.tensor_copy(out=ti[:, :], in_=tq[:, :]), reads=["tq"], writes=["ti"])
        P.c("dve", lambda e: e.tensor_copy(out=tf[:, :], in_=ti[:, :]), reads=["ti"], writes=["tf"])
        P.c("dve", lambda e: e.tensor_tensor(out=tq[:, :], in0=tq[:, :], in1=tf[:, :], op=ALU.subtract), reads=["tq", "tf"], writes=["tq"])
        P.c("dve", lambda e: e.tensor_scalar(out=tf[:, :], in0=tq[:, :], scalar1=0.0, scalar2=None, op0=ALU.is_lt), reads=["tq"], writes=["tf"])
        P.c("dve", lambda e: e.tensor_tensor(out=tq[:, :], in0=tq[:, :], in1=tf[:, :], op=ALU.add), reads=["tq", "tf"], writes=["tq"])
        P.c("act", lambda e, dflat=dflat: e.activation(out=dflat, in_=tq[:, :], func=AF.Sin, scale=TWO_PI, bias=negpi[:, 0:1]),
            reads=["tq", "negpi"], writes=[nm])
    KS = float(128.0 ** -0.5)
    P.c("dve", lambda e: e.tensor_scalar(out=ropeCk[:, :, :], in0=ropeC[:, :, :], scalar1=KS, scalar2=None, op0=ALU.mult), reads=["ropeC"], writes=["ropeCk"])
    P.c("dve", lambda e: e.tensor_scalar(out=ropeSk[:, :, :], in0=ropeS[:, :, :], scalar1=KS, scalar2=None, op0=ALU.mult), reads=["ropeS"], writes=["ropeSk"])

    win = A.alloc("win", [128, 8, 2560], BF16)
    win_src = win_d.rearrange("(k p) n -> p k n", p=128)
    for cb in range(2):
        P.dma("pool", lambda e, cb=cb: e.dma_start(out=win[:, :, cb * 1280:(cb + 1) * 1280], in_=win_src[:, :, cb * 1280:(cb + 1) * 1280]),
              f"win{cb}", writes=["win"])

    A2 = Arena(nc)
    A2.off = big0
    qT = A2.alloc("qT", [128, 4, T], BF16)
    kT = A2.alloc("kT", [128, 4, T], BF16)
    k_tm = A2.alloc("k_tm", [128, NT, 512], BF16)
    v_tm = A2.alloc("v_tm", [128, NT, 512], BF16)
    gate = A2.alloc("gate", [128, NT, 512], BF16)
    uT = A2.alloc("uT", [128, 4, T], BF16)
    assert A2.off <= big0 + 96 * 1024
    stage_a_res = [P._r(n) for n in ("wm", "modrow", "bmod", "ang", "tq", "ti", "tf")]
    for nm in ("qT", "kT", "k_tm", "v_tm", "gate", "uT"):
        r_ = P._r(nm)
        for o in stage_a_res:
            r_.readers.extend(o.readers)
            if o.last_w is not None:
                r_.readers.append(o.last_w)

    r2_end = A.off
    SBA = Arena(nc); SBA.off = scratchB_off
    xts = [SBA.alloc(f"xt{i}", [128, D], F32) for i in range(2)]
    xsl = [SBA.alloc(f"xs{i}", [128, D], F32) for i in range(2)]
    ssl = [SBA.alloc(f"ss{i}", [128, 4], F32) for i in range(2)]
    hxT = [SBA.alloc(f"hxT{i}", [128, 8, 512], BF16) for i in range(2)]
    qrot = SBA.alloc("qrot", [128, 512], BF16)
    t1 = SBA.alloc("t1", [128, 256], F32)
    t2 = SBA.alloc("t2", [128, 256], F32)
    assert SBA.off <= scratchB_off + 37 * 1024 + 512, SBA.off - scratchB_off
    for nm_ in ("xt0", "xt1", "xs0", "xs1", "ss0a", "ss0b", "ss0c", "ss1a", "ss1b", "ss1c", "hxT0", "hxT1", "qrot", "t1", "t2"):
        r_ = P._r(nm_)
        o = P._r("wmB")
        r_.readers.extend(o.readers)
        if o.last_w is not None:
            r_.readers.append(o.last_w)
    psX = [psA[0], psA[1]]
    psP = [psA[2], psA[3], psA[4], psA[5]]
    psU = psA[6]
    psTq = nc.alloc_psum_tensor("psTq", [128, 1024], BF16)

    def norm_front(src_ap, pp):
        xt, xs_, s_ = xts[pp], xsl[pp], ssl[pp]
        P.dma("sp", lambda e: e.dma_start(out=xt[:, :], in_=src_ap), f"xin{pp}", writes=[f"xt{pp}"])
        P.c("act", lambda e: e.activation(out=xs_[:, :], in_=xt[:, :], func=AF.Square, accum_out=s_[:, 0:1]),
            reads=[f"xt{pp}"], writes=[f"xs{pp}", f"ss{pp}a"])
        P.c("act", lambda e: e.activation(out=s_[:, 1:2], in_=s_[:, 0:1], func=AF.Sqrt, scale=1.0 / D, bias=epst[:, 0:1]),
            reads=[f"ss{pp}a", "epst"], writes=[f"ss{pp}b"])
        P.c("dve", lambda e: e.reciprocal(out=s_[:, 2:3], in_=s_[:, 1:2]), reads=[f"ss{pp}b"], writes=[f"ss{pp}c"])
        P.c("dve", lambda e: e.tensor_scalar(out=xs_[:, :], in0=xt[:, :], scalar1=s_[:, 2:3], scalar2=None, op0=ALU.mult),
            reads=[f"xt{pp}", f"ss{pp}c"], writes=[f"xs{pp}"])

    def norm_back(pp, gvec, shvec_fn, dstT, col0, tag):
        xs_ = xsl[pp]
        for k in range(8):
            ps = psX[k // 4]
            P.c("pe", lambda e, ps=ps, k=k: e.transpose(ps[:, (k % 4) * 128:(k % 4 + 1) * 128], xs_[:, k * 128:(k + 1) * 128], ident_f[:, :]),
                reads=[f"xs{pp}", "ident_f"], writes=[f"ps{k // 4}"])
        for k in range(8):
            ps = psX[k // 4]
            P.c("act", lambda e, ps=ps, k=k: e.activation(out=dstT[:, k, col0:col0 + 128], in_=ps[:, (k % 4) * 128:(k % 4 + 1) * 128],
                                                        func=AF.Identity, scale=gvec[:, k:k + 1], bias=shvec_fn(k)),
                reads=[f"ps{k // 4}", "g1x", "g1c", "g2x", "modT"], writes=[tag])

    def rope(ps, Ct, St, i, dst_ap):
        pv = ps[:, :].rearrange("p (h j w) -> p h j w", h=4, w=2)
        dv = dst_ap.rearrange("p (h j w) -> p h j w", h=4, w=2)
        Cb = Ct[:, i, :].unsqueeze(1).broadcast_to([128, 4, 64])
        Sb = St[:, i, :].unsqueeze(1).broadcast_to([128, 4, 64])
        a = t1[:, :].rearrange("p (h j) -> p h j", h=4)
        b = t2[:, :].rearrange("p (h j) -> p h j", h=4)
        return pv, dv, Cb, Sb, a, b

    norm_front(x_d[0:128, :], 0)
    for grp in range(4):
        hT = hxT[grp % 2]
        for ti_ in range(4):
            i = grp * 4 + ti_
            if i + 1 < NT:
                norm_front(x_d[(i + 1) * 128:(i + 2) * 128, :], (i + 1) % 2)
            norm_back(i % 2, g1x, lambda k: modT[:, k, 0:1], hT, ti_ * 128, f"hxT{grp % 2}")
            for nb in range(4):
                ps = psP[nb]
                for k in range(8):
                    P.c("pe", lambda e, ps=ps, k=k, nb=nb, hT=hT, ti_=ti_: e.matmul(
                        ps[:, :], lhsT=hT[:, k, ti_ * 128:(ti_ + 1) * 128], rhs=win[:, k, 512 + nb * 512: 1024 + nb * 512],
                        start=(k == 0), stop=(k == 7)), reads=[f"hxT{grp % 2}", "win"], writes=[f"ps{2 + nb}"])
            for (nb, Ct, St, dstT, nm) in ((0, ropeC, ropeS, qT, "q"), (1, ropeCk, ropeSk, kT, "k")):
                ps = psP[nb]
                dst_ap = qrot[:, :] if nb == 0 else k_tm[:, i, :]
                dres = "qrot" if nb == 0 else "k_tm"
                pv, dv, Cb, Sb, a, b = rope(ps, Ct, St, i, dst_ap)
                rd = [f"ps{2 + nb}", "ropeC", "ropeS", "ropeCk", "ropeSk"]
                P.c("dve", lambda e, pv=pv, Cb=Cb, a=a: e.tensor_tensor(out=a, in0=pv[:, :, :, 0], in1=Cb, op=ALU.mult), reads=rd, writes=["t1"])
                P.c("dve", lambda e, pv=pv, Sb=Sb, b=b: e.tensor_tensor(out=b, in0=pv[:, :, :, 1], in1=Sb, op=ALU.mult), reads=rd, writes=["t2"])
                P.c("dve", lambda e, dv=dv, a=a, b=b: e.tensor_tensor(out=dv[:, :, :, 0], in0=a, in1=b, op=ALU.subtract),
                    reads=["t1", "t2"], writes=[dres])
                P.c("dve", lambda e, pv=pv, Sb=Sb, a=a: e.tensor_tensor(out=a, in0=pv[:, :, :, 0], in1=Sb, op=ALU.mult), reads=rd + [dres], writes=["t1"])
                P.c("dve", lambda e, pv=pv, Cb=Cb, b=b: e.tensor_tensor(out=b, in0=pv[:, :, :, 1], in1=Cb, op=ALU.mult), reads=rd + [dres], writes=["t2"])
                P.c("dve", lambda e, dv=dv, a=a, b=b: e.tensor_tensor(out=dv[:, :, :, 1], in0=a, in1=b, op=ALU.add),
                    reads=["t1", "t2"], writes=[dres])
                for h in range(4):
                    P.c("pe", lambda e, h=h, dst_ap=dst_ap: e.transpose(psTq[:, h * 128:(h + 1) * 128], dst_ap[:, h * 128:(h + 1) * 128], ident_b[:, :]),
                        reads=[dres, "ident_b"], writes=["psTq"])
                P.c("act", lambda e, dstT=dstT, i=i: e.activation(out=dstT[:, :, i * 128:(i + 1) * 128],
                                                                 in_=psTq[:, 0:512].rearrange("p (h t) -> p h t", h=4), func=AF.Copy),
                    reads=["psTq"], writes=[nm + "T"])
            P.c("act", lambda e, i=i: e.activation(out=v_tm[:, i, :], in_=psP[2][:, :], func=AF.Copy), reads=["ps4"], writes=["v_tm"])
            P.c("act", lambda e, i=i: e.activation(out=gate[:, i, :], in_=psP[3][:, :], func=AF.Silu), reads=["ps5"], writes=["gate"])
        for fc in range(4):
            for k in range(8):
                P.c("pe", lambda e, fc=fc, k=k, hT=hT: e.matmul(psU[:, :], lhsT=win[:, k, fc * 128:(fc + 1) * 128], rhs=hT[:, k, :],
                                                              start=(k == 0), stop=(k == 7)), reads=[f"hxT{grp % 2}", "win"], writes=["ps6"])
            P.c("dve", lambda e, fc=fc, grp=grp: e.tensor_copy(out=uT[:, fc, grp * 512:(grp + 1) * 512], in_=psU[:, :]),
                reads=["ps6"], writes=["uT"])

    stage("B")

    def alias(new_names, old_names):
        olds = [P._r(n) for n in old_names]
        for nm in new_names:
            r_ = P._r(nm)
            for o in olds:
                r_.readers.extend(o.readers)
                if o.last_w is not None:
                    r_.readers.append(o.last_w)

    spare_off = A.off
    kc_tm = A.alloc("kc_tm", [128, 2, 512], BF16)
    vc_tm = A.alloc("vc_tm", [128, 2, 512], BF16)
    ucT = A.alloc("ucT", [128, 4, 256], BF16)
    KSC = float(128.0 ** -0.5)
    hT = hxT[0]
    for j in range(2):
        norm_front(ctx_d[j * 128:(j + 1) * 128, :], j)
        norm_back(j, g1c, lambda k: modT[:, k, 1:2], hT, j * 128, "hxT0")
        for (nb, dst, nm) in ((1, kc_tm, "kc_tm"), (2, vc_tm, "vc_tm")):
            ps = psP[nb]
            for k in range(8):
                P.c("pe", lambda e, ps=ps, k=k, nb=nb, j=j: e.matmul(ps[:, :], lhsT=hT[:, k, j * 128:(j + 1) * 128],
                                                                  rhs=win[:, k, 512 + nb * 512: 1024 + nb * 512], start=(k == 0), stop=(k == 7)),
                    reads=["hxT0", "win"], writes=[f"ps{2 + nb}"])
            P.c("act", lambda e, ps=ps, dst=dst, j=j, nb=nb: e.activation(out=dst[:, j, :], in_=ps[:, :], func=AF.Copy, scale=(KSC if nb == 1 else 1.0)),
                reads=[f"ps{2 + nb}"], writes=[nm])
    for fc in range(4):
        for k in range(8):
            P.c("pe", lambda e, fc=fc, k=k: e.matmul(psU[:, 0:256], lhsT=win[:, k, fc * 128:(fc + 1) * 128], rhs=hT[:, k, 0:256],
                                                   start=(k == 0), stop=(k == 7)), reads=["hxT0", "win"], writes=["ps6"])
        P.c("dve", lambda e, fc=fc: e.tensor_copy(out=ucT[:, fc, :], in_=psU[:, 0:256]), reads=["ps6"], writes=["ucT"])

    stage("C")
    ldv = A.alloc("ldv", [128, 8], F32)
    msk = A.alloc("msk", [128, 8], F32)
    P.dma("sp", lambda e: e.dma_start(out=ldv[:, :], in_=ldv_d[:, :]), "small6", writes=["ldv"])
    P.dma("sp", lambda e: e.dma_start(out=msk[:, :], in_=msk_d[:, :]), "small7", writes=["msk"])
    AR = Arena(nc)
    AR.off = r2_base
    old_r2 = ["ropeC", "ropeS", "ropeCk", "ropeSk", "win", "xt0", "xt1", "xs0", "xs1", "ss0a", "ss0b", "ss0c", "ss1a", "ss1b", "ss1c", "hxT0", "hxT1", "qrot", "t1", "t2",
              "invf", "rowv", "colf", "rowb"]
    mixT = AR.alloc("mixT", [128, 8, T], BF16)
    Rb_store = AR.alloc("Rb_store", [128, NT, 512], BF16)
    xg_off = AR.off
    xg = AR.alloc("xg", [128, 4, 1024], F32)
    Dm = AR.alloc("Dm", [128, 4, 128], F32)
    XiF = AR.alloc("XiF", [128, 4, 128], F32)
    XiB = AR.alloc("XiB", [128, 4, 128], F32)
    dm = AR.alloc("dm", [128, 128], F32)
    tabs_off = AR.off
    dpos = AR.alloc("dpos", [128, 128], F32)
    dneg = AR.alloc("dneg", [128, 128], F32)
    mge = AR.alloc("mge", [128, 128], F32)
    mlt = AR.alloc("mlt", [128, 128], F32)
    e1 = AR.alloc("e1", [128, 128], F32)
    e2 = AR.alloc("e2", [128, 128], F32)
    tfree = AR.alloc("tfree", [128, 128], F32)
    pidx = AR.alloc("pidx", [128, 2], F32)
    zarg = AR.alloc("zarg", [128, 8], F32)
    Zeta = AR.alloc("Zeta", [128, 8], F32)
    G128 = AR.alloc("G128", [128, 8], F32)
    G2048 = AR.alloc("G2048", [128, 8], F32)
    nld = AR.alloc("nld", [128, 8], F32)
    ld128 = AR.alloc("ld128", [128, 8], F32)
    Rf = AR.alloc("Rf", [128, 4, 128], F32)
    Rb = AR.alloc("Rb", [128, 4, 128], F32)
    Rcf_off = AR.off
    Rcf = AR.alloc("Rcf", [128, 4, 128], F32)
    Rcb_off = AR.off
    Rcb = AR.alloc("Rcb", [128, 4, 128], F32)
    Rfbf_off = AR.off
    Rf_bf = AR.alloc("Rf_bf", [128, 4, 128], BF16)
    vzs = [AR.alloc(f"vz{i}", [128, 4, 128], BF16) for i in range(2)]
    scm = AR.alloc("scm", [128, 4, 128], BF16)
    qf = AR.alloc("qf", [128, 4, 128], BF16)
    qb = AR.alloc("qb", [128, 4, 128], BF16)
    yn = AR.alloc("yn", [128, 4, 128], F32)
    hacc_t = yn
    retx = AR.alloc("retx", [128, 512], BF16)
    bst = AR.alloc("bst", [128, 4, 6], F32)
    junk2 = AR.alloc("junk2", [128, 128], BF16)
    mv = AR.alloc("mv", [128, 4, 2], F32)
    rs = AR.alloc("rs", [128, 8], F32)
    assert AR.off <= r2_end, (AR.off, r2_end)
    new_r2 = ["mixT", "Rb_store", "xg", "Dm", "XiF", "XiB", "dm", "dpos", "dneg", "mge", "mlt", "e1", "e2", "tfree", "pidx", "zarg", "Zeta",
              "G128", "G2048", "nld", "ld128", "Rf", "Rb", "Rcf", "Rcb", "Rf_bf", "hacc_t", "vz0", "vz1", "scm", "qf", "qb", "yn", "retx", "bst", "bst2", "bst3", "junk2", "mv", "rs"]
    alias(new_r2, old_r2)

    P.c("pool", lambda e: e.iota(pidx[:, 0:1], pattern=[[0, 1]], base=0, channel_multiplier=1, allow_small_or_imprecise_dtypes=True), writes=["pidx"])
    P.c("pool", lambda e: e.iota(tfree[:, :], pattern=[[1, 128]], base=0, channel_multiplier=0, allow_small_or_imprecise_dtypes=True), writes=["tfree"])
    P.c("pool", lambda e: e.iota(dm[:, :], pattern=[[1, 128]], base=0, channel_multiplier=-1, allow_small_or_imprecise_dtypes=True), writes=["dm"])
    P.c("dve", lambda e: e.tensor_scalar(out=pidx[:, 1:2], in0=pidx[:, 0:1], scalar1=-1.0, scalar2=127.0, op0=ALU.mult, op1=ALU.add),
        reads=["pidx"], writes=["pidx"])
    P.c("dve", lambda e: e.tensor_scalar(out=zarg[:, 0:4], in0=ldv[:, 0:4], scalar1=pidx[:, 1:2], scalar2=None, op0=ALU.mult), reads=["ldv", "pidx"], writes=["zarg"])
    P.c("dve", lambda e: e.tensor_scalar(out=zarg[:, 4:8], in0=ldv[:, 4:8], scalar1=pidx[:, 0:1], scalar2=None, op0=ALU.mult), reads=["ldv", "pidx"], writes=["zarg"])
    P.c("act", lambda e: e.activation(out=Zeta[:, :], in_=zarg[:, :], func=AF.Exp), reads=["zarg"], writes=["Zeta"])
    P.c("act", lambda e: e.activation(out=G128[:, :], in_=ldv[:, :], func=AF.Exp, scale=128.0), reads=["ldv"], writes=["G128"])
    P.c("act", lambda e: e.activation(out=G2048[:, :], in_=ldv[:, :], func=AF.Exp, scale=2048.0), reads=["ldv"], writes=["G2048"])
    P.c("dve", lambda e: e.tensor_scalar(out=nld[:, :], in0=ldv[:, :], scalar1=-1.0, scalar2=None, op0=ALU.mult), reads=["ldv"], writes=["nld"])
    P.c("dve", lambda e: e.tensor_scalar(out=ld128[:, :], in0=ldv[:, :], scalar1=128.0, scalar2=None, op0=ALU.mult), reads=["ldv"], writes=["ld128"])
    P.c("dve", lambda e: e.tensor_scalar(out=dpos[:, :], in0=dm[:, :], scalar1=0.0, scalar2=None, op0=ALU.max), reads=["dm"], writes=["dpos"])
    P.c("dve", lambda e: e.tensor_scalar(out=dneg[:, :], in0=dm[:, :], scalar1=-1.0, scalar2=0.0, op0=ALU.mult, op1=ALU.max), reads=["dm"], writes=["dneg"])
    P.c("dve", lambda e: e.tensor_scalar(out=mge[:, :], in0=dm[:, :], scalar1=0.0, scalar2=None, op0=ALU.is_ge), reads=["dm"], writes=["mge"])
    P.c("dve", lambda e: e.tensor_scalar(out=mlt[:, :], in0=dm[:, :], scalar1=0.0, scalar2=None, op0=ALU.is_lt), reads=["dm"], writes=["mlt"])
    for h in range(4):
        P.c("act", lambda e, h=h: e.activation(out=XiF[:, h, :], in_=tfree[:, :], func=AF.Exp, scale=ldv[:, h:h + 1], bias=ldv[:, h:h + 1]),
            reads=["tfree", "ldv"], writes=["XiF"])
        P.c("act", lambda e, h=h: e.activation(out=XiB[:, h, :], in_=tfree[:, :], func=AF.Exp, scale=nld[:, 4 + h:5 + h], bias=ld128[:, 4 + h:5 + h]),
            reads=["tfree", "nld", "ld128"], writes=["XiB"])
        P.c("act", lambda e, h=h: e.activation(out=e1[:, :], in_=dpos[:, :], func=AF.Exp, scale=ldv[:, h:h + 1]), reads=["dpos", "ldv"], writes=["e1"])
        P.c("act", lambda e, h=h: e.activation(out=e2[:, :], in_=dneg[:, :], func=AF.Exp, scale=ldv[:, 4 + h:5 + h]), reads=["dneg", "ldv"], writes=["e2"])
        P.c("dve", lambda e: e.tensor_tensor(out=e1[:, :], in0=e1[:, :], in1=mge[:, :], op=ALU.mult), reads=["e1", "mge"], writes=["e1"])
        P.c("dve", lambda e: e.tensor_tensor(out=e2[:, :], in0=e2[:, :], in1=mlt[:, :], op=ALU.mult), reads=["e2", "mlt"], writes=["e2"])
        P.c("dve", lambda e, h=h: e.tensor_tensor(out=Dm[:, h, :], in0=e1[:, :], in1=e2[:, :], op=ALU.add), reads=["e1", "e2"], writes=["Dm"])

    stage("Dtab")
    psS, psO, psKV = psA[0], psA[1], psA[2]

    vz_extra = [nc.alloc_sbuf_tensor_at("vz2", [128, 4, 128], BF16, offset=tabs_off), nc.alloc_sbuf_tensor_at("vz3", [128, 4, 128], BF16, offset=tabs_off + 1024)]
    vz_tab = [[vzs[0], vz_extra[0]], [vzs[1], vz_extra[1]]]
    vz_nm = [["vz0", "vz2"], ["vz1", "vz3"]]
    kv_bank = [[(psA[2], "ps2"), (psA[4], "ps4")], [(psA[3], "ps3"), (psA[5], "ps5")]]

    def kv_front(ksrc, vsrc, d, kres, vres, lane, par):
        zb = Zeta[:, d * 4:(d + 1) * 4].unsqueeze(2).broadcast_to([128, 4, 128])
        vz_, vzn = vz_tab[lane % 2][par], vz_nm[lane % 2][par]
        pk_, pkn = kv_bank[lane % 2][par]
        if lane % 2 == 0:
            for h in range(4):
                P.c("act", lambda e, h=h: e.activation(out=vz_[:, h, :], in_=vsrc[:, h * 128:(h + 1) * 128], func=AF.Identity, scale=Zeta[:, d * 4 + h:d * 4 + h + 1]),
                    reads=[vres, "Zeta"], writes=[vzn])
        else:
            P.c("pool", lambda e: e.tensor_tensor(out=vz_[:, :, :], in0=vsrc.rearrange("p (h e) -> p h e", h=4), in1=zb, op=ALU.mult),
                reads=[vres, "Zeta"], writes=[vzn])
        for h in range(4):
            P.c("pe", lambda e, h=h: e.matmul(pk_[:, h * 128:(h + 1) * 128], lhsT=ksrc[:, h * 128:(h + 1) * 128], rhs=vz_[:, h, :], start=True, stop=True),
                reads=[kres, vzn], writes=[pkn])

    def kv_back(Rst, rname, d, lane, par):
        pk_, pkn = kv_bank[lane % 2][par]
        for h in range(4):
            P.c("dve", lambda e, h=h: e.scalar_tensor_tensor(out=Rst[:, h, :], in0=Rst[:, h, :], scalar=G128[:, d * 4 + h:d * 4 + h + 1],
                                                            in1=pk_[:, h * 128:(h + 1) * 128], op0=ALU.mult, op1=ALU.add),
                reads=[rname, "G128", pkn], writes=[rname])

    def kv_step(Rst, rname, ksrc, vsrc, d, kres, vres, lane=0):
        kv_front(ksrc, vsrc, d, kres, vres, lane, 0)
        kv_back(Rst, rname, d, lane, 0)

    def zero(t_, nm):
        P.c("pool", lambda e: e.memset(t_[:, :, :], 0.0), writes=[nm])

    alias(["vz2", "vz3"], ["dpos", "dneg", "mge", "mlt", "e1", "e2"])
    zero(Rf, "Rf"); zero(Rb, "Rb"); zero(Rcf, "Rcf"); zero(Rcb, "Rcb")
    def fronts_A(n_):
        kv_front(k_tm[:, n_, :], v_tm[:, n_, :], 0, "k_tm", "v_tm", 0, n_ % 2)
        kv_front(k_tm[:, NT - 1 - n_, :], v_tm[:, NT - 1 - n_, :], 1, "k_tm", "v_tm", 1, n_ % 2)

    fronts_A(0)
    for n_ in range(NT):
        if n_ + 1 < NT:
            fronts_A(n_ + 1)
        kv_back(Rf, "Rf", 0, 0, n_ % 2)
        kv_back(Rb, "Rb", 1, 1, n_ % 2)
    for n_ in range(2):
        kv_step(Rcf, "Rcf", kc_tm[:, n_, :], vc_tm[:, n_, :], 0, "kc_tm", "vc_tm", 0)
        kv_step(Rcb, "Rcb", kc_tm[:, 1 - n_, :], vc_tm[:, 1 - n_, :], 1, "kc_tm", "vc_tm", 1)

    stage("DpassA")
    xa_in = nc.dram_tensor("xa_in", [128, 1024], F32)
    xa_out = nc.dram_tensor("xa_out", [512, 1024], F32)
    P.dma("sp", lambda e: e.dma_start(out=xa_in[:, 0:512], in_=Rf[:, :, :].rearrange("p h e -> p (h e)")), "xa_st", reads=["Rf"], writes=["xa_in"])
    P.dma("sp", lambda e: e.dma_start(out=xa_in[:, 512:1024], in_=Rb[:, :, :].rearrange("p h e -> p (h e)")), "xa_st", reads=["Rb"], writes=["xa_in"])
    P.dma("pool", lambda e: e.collective_compute("AllGather", ALU.bypass, replica_groups=[[0, 1, 2, 3], [4, 5, 6, 7]],
                                                  ins=[xa_in.ap().opt()], outs=[xa_out.ap().opt()]),
          "ccA", reads=["xa_in"], writes=["xa_out"], inc=1)
    P.dma("sp", lambda e: e.dma_start(out=xg[:, :, :], in_=xa_out.ap().rearrange("(r p) n -> p r n", p=128)), "xa_ld", reads=["xa_out"], writes=["xg"])

    stage("Dxchg")

    def horner(acc, aname, d, order, mcol0):
        gb = G2048[:, d * 4:(d + 1) * 4].unsqueeze(2).broadcast_to([128, 4, 128])
        for n_, i in enumerate(order):
            P.c("dve", lambda e: e.tensor_tensor(out=hacc_t[:, :, :], in0=acc[:, :, :], in1=gb, op=ALU.mult), reads=[aname, "G2048"], writes=["hacc_t"])
            P.c("dve", lambda e, i=i: e.tensor_tensor(out=hacc_t[:, :, :], in0=hacc_t[:, :, :],
                                                     in1=xg[:, i, d * 512:(d + 1) * 512].rearrange("p (h e) -> p h e", h=4), op=ALU.add),
                reads=["hacc_t", "xg"], writes=["hacc_t"])
            P.c("dve", lambda e: e.tensor_tensor(out=hacc_t[:, :, :], in0=hacc_t[:, :, :], in1=acc[:, :, :], op=ALU.subtract),
                reads=["hacc_t", aname], writes=["hacc_t"])
            P.c("dve", lambda e, n_=n_: e.scalar_tensor_tensor(out=acc[:, :, :], in0=hacc_t[:, :, :], scalar=msk[:, mcol0 + n_:mcol0 + n_ + 1],
                                                              in1=acc[:, :, :], op0=ALU.mult, op1=ALU.add),
                reads=["hacc_t", aname, "msk"], writes=[aname])

    horner(Rcf, "Rcf", 0, [0, 1, 2], 0)
    horner(Rcb, "Rcb", 1, [3, 2, 1], 3)

    stage("Dhorner")
    alias(["yn"], ["hacc_t"])
    P.c("dve", lambda e: e.tensor_copy(out=Rb[:, :, :], in_=Rcb[:, :, :]), reads=["Rcb"], writes=["Rb"])
    P.c("dve", lambda e: e.tensor_copy(out=Rf[:, :, :], in_=Rcf[:, :, :]), reads=["Rcf"], writes=["Rf"])
    Rf_store = nc.alloc_sbuf_tensor_at("Rf_store", [128, NT, 512], BF16, offset=xg_off)
    alias(["Rf_store"], ["xg"])
    def fronts_B(n_):
        kv_front(k_tm[:, NT - 1 - n_, :], v_tm[:, NT - 1 - n_, :], 1, "k_tm", "v_tm", 0, n_ % 2)
        kv_front(k_tm[:, n_, :], v_tm[:, n_, :], 0, "k_tm", "v_tm", 1, n_ % 2)

    fronts_B(0)
    for n_ in range(NT):
        ib = NT - 1 - n_
        P.c("act", lambda e, ib=ib: e.activation(out=Rb_store[:, ib, :], in_=Rb[:, :, :].rearrange("p h e -> p (h e)"), func=AF.Copy),
            reads=["Rb"], writes=["Rb_store"])
        P.c("act", lambda e, n_=n_: e.activation(out=Rf_store[:, n_, :], in_=Rf[:, :, :].rearrange("p h e -> p (h e)"), func=AF.Copy),
            reads=["Rf"], writes=["Rf_store"])
        if n_ < NT - 1:
            if n_ + 1 < NT - 1:
                fronts_B(n_ + 1)
            kv_back(Rb, "Rb", 1, 0, n_ % 2)
            kv_back(Rf, "Rf", 0, 1, n_ % 2)
    stage("DBb")
    scm_l = [scm, nc.alloc_sbuf_tensor_at("scm1", [128, 4, 128], BF16, offset=tabs_off)]
    qf_l = [qf, nc.alloc_sbuf_tensor_at("qf1", [128, 4, 128], BF16, offset=tabs_off + 1024)]
    qb_l = [qb, nc.alloc_sbuf_tensor_at("qb1", [128, 4, 128], BF16, offset=tabs_off + 2048)]
    yn_l = [yn, nc.alloc_sbuf_tensor_at("yn1", [128, 4, 128], F32, offset=Rcf_off)]
    retx_l = [retx, nc.alloc_sbuf_tensor_at("retx1", [128, 512], BF16, offset=Rfbf_off)]
    bst_l = [bst, nc.alloc_sbuf_tensor_at("bst1", [128, 4, 6], F32, offset=Rcb_off)]
    mv_l = [mv, nc.alloc_sbuf_tensor_at("mv1", [128, 4, 2], F32, offset=Rcb_off + 128)]
    rs_l = [rs, nc.alloc_sbuf_tensor_at("rs1", [128, 8], F32, offset=Rcb_off + 192)]
    alias(["scm1", "qf1", "qb1"], ["dpos", "dneg", "mge", "mlt", "e1", "e2", "vz2", "vz3"])
    alias(["yn1"], ["Rcf"])
    alias(["bst1", "bst21", "bst31", "mv1", "rs1", "rs41"], ["Rcb"])
    psS_l = [(psA[0], "ps0"), (psA[6], "ps6")]
    psO_l = [(psA[1], "ps1"), (psA[4], "ps4")]

    def ret_front(i):
        pp = i % 2
        sfx = "" if pp == 0 else "1"
        tsl = slice(i * 128, (i + 1) * 128)
        pS, pSn = psS_l[pp]
        pO, pOn = psO_l[pp]
        scm_, qf_, qb_, yn_ = scm_l[pp], qf_l[pp], qb_l[pp], yn_l[pp]
        for h in range(4):
            P.c("pe", lambda e, h=h: e.matmul(pS[:, h * 128:(h + 1) * 128], lhsT=kT[:, h, tsl], rhs=qT[:, h, tsl], start=True, stop=True),
                reads=["kT", "qT"], writes=[pSn])
        P.c("dve", lambda e: e.tensor_tensor(out=scm_[:, :, :], in0=pS[:, :].rearrange("p (h t) -> p h t", h=4), in1=Dm[:, :, :], op=ALU.mult),
            reads=[pSn, "Dm"], writes=["scm" + sfx])
        P.c("pool", lambda e: e.tensor_tensor(out=qf_[:, :, :], in0=qT[:, :, tsl], in1=XiF[:, :, :], op=ALU.mult), reads=["qT", "XiF"], writes=["qf" + sfx])
        P.c("pool", lambda e: e.tensor_tensor(out=qb_[:, :, :], in0=qT[:, :, tsl], in1=XiB[:, :, :], op=ALU.mult), reads=["qT", "XiB"], writes=["qb" + sfx])
        for h in range(4):
            osl = slice(h * 128, (h + 1) * 128)
            P.c("pe", lambda e, h=h, osl=osl: e.matmul(pO[:, osl], lhsT=scm_[:, h, :], rhs=v_tm[:, i, osl], start=True, stop=False),
                reads=["scm" + sfx, "v_tm"], writes=[pOn])
            P.c("pe", lambda e, h=h, osl=osl: e.matmul(pO[:, osl], lhsT=qf_[:, h, :], rhs=Rf_store[:, i, osl], start=False, stop=False),
                reads=["qf" + sfx, "Rf_store"], writes=[pOn])
            P.c("pe", lambda e, h=h, osl=osl: e.matmul(pO[:, osl], lhsT=qb_[:, h, :], rhs=Rb_store[:, i, osl], start=False, stop=True),
                reads=["qb" + sfx, "Rb_store"], writes=[pOn])
        P.c("act", lambda e: e.activation(out=yn_[:, :, :], in_=pO[:, :].rearrange("p (h e) -> p h e", h=4), func=AF.Copy), reads=[pOn], writes=["yn" + sfx])
        if debug and i in (0, 15):
            dd = dout(f"d_rety{i}", [128, 512])
            P.dma("sp", lambda e, dd=dd: e.dma_start(out=dd[:, :], in_=yn_[:, :, :].rearrange("p h e -> p (h e)")), "dbg", reads=["yn" + sfx])

    def ret_back(i):
        pp = i % 2
        sfx = "" if pp == 0 else "1"
        tsl = slice(i * 128, (i + 1) * 128)
        yn_, retx_, bst_, mv_, rs_ = yn_l[pp], retx_l[pp], bst_l[pp], mv_l[pp], rs_l[pp]
        ynn, rxn = "yn" + sfx, "retx" + sfx
        P.c("dve", lambda e: e.tensor_reduce(out=bst_[:, 0, 0:4], in_=yn_[:, :, :], axis=mybir.AxisListType.X, op=ALU.add), reads=[ynn], writes=["bst" + sfx])
        for h in range(4):
            P.c("act", lambda e, h=h: e.activation(out=junk2[:, :], in_=yn_[:, h, :], func=AF.Square, accum_out=bst_[:, 1, h:h + 1]),
                reads=[ynn], writes=["junk2", "bst2" + sfx])
        P.c("dve", lambda e: e.tensor_scalar(out=mv_[:, :, 0], in0=bst_[:, 0, 0:4], scalar1=1.0 / 128.0, scalar2=None, op0=ALU.mult), reads=["bst" + sfx], writes=["mv" + sfx])
        P.c("dve", lambda e: e.tensor_tensor(out=bst_[:, 2, 0:4], in0=mv_[:, :, 0], in1=mv_[:, :, 0], op=ALU.mult), reads=["mv" + sfx], writes=["bst3" + sfx])
        P.c("dve", lambda e: e.scalar_tensor_tensor(out=mv_[:, :, 1], in0=bst_[:, 1, 0:4], scalar=1.0 / 128.0, in1=bst_[:, 2, 0:4], op0=ALU.mult, op1=ALU.subtract),
            reads=["bst2" + sfx, "bst3" + sfx, "mv" + sfx], writes=["mv" + sfx])
        P.c("act", lambda e: e.activation(out=rs_[:, 0:4], in_=mv_[:, :, 1], func=AF.Sqrt, bias=epst[:, 0:1]), reads=["mv" + sfx, "epst"], writes=["rs" + sfx])
        P.c("dve", lambda e: e.reciprocal(out=rs_[:, 4:8], in_=rs_[:, 0:4]), reads=["rs" + sfx], writes=["rs4" + sfx])
        for h in range(4):
            P.c("dve", lambda e, h=h: e.tensor_scalar(out=yn_[:, h, :], in0=yn_[:, h, :], scalar1=mv_[:, h, 0:1], scalar2=rs_[:, 4 + h:5 + h],
                                                     op0=ALU.subtract, op1=ALU.mult), reads=[ynn, "mv" + sfx, "rs4" + sfx], writes=[ynn])
        P.c("dve", lambda e: e.tensor_tensor(out=retx_[:, :], in0=yn_[:, :, :].rearrange("p h e -> p (h e)"), in1=gate[:, i, :], op=ALU.mult),
            reads=[ynn, "gate"], writes=[rxn])
        for h in range(4):
            P.c("pe", lambda e, h=h: e.transpose(psTq[:, h * 128:(h + 1) * 128], retx_[:, h * 128:(h + 1) * 128], ident_b[:, :]),
                reads=[rxn, "ident_b"], writes=["psTq"])
        P.c("act", lambda e: e.activation(out=mixT[:, 4:8, tsl], in_=psTq[:, 0:512].rearrange("p (h t) -> p h t", h=4), func=AF.Copy),
            reads=["psTq"], writes=["mixT"])

    ret_front(0)
    for i in range(NT):
        if i + 1 < NT:
            ret_front(i + 1)
        ret_back(i)

    if debug:
        for (nm, tns) in (("d_rcf", Rcf), ("d_rcb", Rcb)):
            dd = dout(nm, [128, 512])
            P.dma("sp", lambda e, dd=dd, tns=tns: e.dma_start(out=dd[:, :], in_=tns[:, :, :].rearrange("p h e -> p (h e)")), "dbg", reads=["Rcf", "Rcb"])
        dd = nc.dram_tensor("d_mixT", [128, 8 * T], BF16, kind="ExternalOutput").ap()
        dbg["d_mixT"] = dd
        P.dma("sp", lambda e, dd=dd: e.dma_start(out=dd[:, :], in_=mixT[:, :, :].rearrange("p a b -> p (a b)")), "dbg", reads=["mixT"])

    stage("ret")
    d_in = {}
    for (nm, shp) in (("lamre", [128, 64]), ("lamim", [128, 64]), ("lstep", [128, 64]), ("bre", [128, 32, 16]), ("bim", [128, 32, 16]),
                      ("cre", [128, 32, 16]), ("cim", [128, 32, 16]), ("d128", [128, 32, 16]), ("bglu", [128, 4])):
        d_in[nm] = din("s5_" + nm, shp)
    d_in["wglu"] = din("w_glu", [512, 512])
    _s5(nc, P, debug, stage, dout, dbg, alias, big0, r2_base, r2_end, spare_off, uT, ucT, mixT, psA, psTq, ident_f, ident_b, ones_f, negpi, msk, d_in)
    stage("s5")
    wout_d = din("w_out", [D, D])
    fnw_d = din("fnw_b", [128, D])
    cw_d = din("cw", [128, 66])
    cb_d = din("cb", [128, 22])
    oh_d = din("oh", [128, 8])
    wup_d = din("w_up", [D, 5632])
    wdn_d = din("w_down", [2816, D])
    SP = Arena(nc); SP.off = spare_off
    fnw = SP.alloc("fnw", [128, D], F32)
    cw = SP.alloc("cw", [128, 22, 3], F32)
    cb = SP.alloc("cb", [128, 22], F32)
    oh = SP.alloc("oh", [128, 8], F32)
    hxe = SP.alloc("hxe", [128, 16], F32)
    xgC = SP.alloc("xgC", [128, 4, 16], F32)
    halo = SP.alloc("halo", [128, 16], F32)
    ss2 = SP.alloc("ss2", [128, 4], F32)
    assert SP.off <= spare_off + 6 * 1024, SP.off - spare_off
    alias(["fnw", "cw", "cb", "oh", "hxe", "xgC", "halo", "ss2a", "ss2b", "ss2c"], ["kc_tm", "vc_tm", "ucT", "e1lo", "e1hi", "e2lo", "e2hi", "s5tab"])
    P.dma("sp", lambda e: e.dma_start(out=fnw[:, :], in_=fnw_d[:, :]), "small8", writes=["fnw"])
    P.dma("sp", lambda e: e.dma_start(out=cw[:, :, :], in_=cw_d.rearrange("p (j w) -> p j w", w=3)), "small9", writes=["cw"])
    P.dma("sp", lambda e: e.dma_start(out=cb[:, :], in_=cb_d[:, :]), "small10", writes=["cb"])
    P.dma("sp", lambda e: e.dma_start(out=oh[:, :], in_=oh_d[:, :]), "small11", writes=["oh"])
    FS = Arena(nc); FS.off = big0
    x_mid = FS.alloc("x_mid", [128, NT, D], F32)
    wout = FS.alloc("wout", [128, 8, D], BF16)
    xf = [FS.alloc(f"xf{i}", [128, D], F32) for i in range(2)]
    assert FS.off <= big0 + 96 * 1024
    s_dead = ["qT", "kT", "k_tm", "v_tm", "gate", "uT", "XR", "QN", "s5tab", "L1lo", "L1hi", "e1lo", "e1hi", "e2lo", "e2hi", "tmpM", "Z", "Zc", "Ysb",
              "Fall", "Fctx", "hacc", "hacc_bf", "htmp", "xgB", "D2", "s5gT", "wglu", "sgt", "Xw0", "Xw1", "TAw0", "TAw1", "TBw0", "TBw1",
              "XAw0", "XAw1", "XBw0", "XBw1"] + [f"Rr{i}" for i in range(36)]
    alias(["x_mid", "wout", "xf0", "xf1"], s_dead)
    P.dma("pool", lambda e: e.dma_start(out=wout[:, :, :], in_=wout_d.rearrange("(k p) n -> p k n", p=128)), "wout", writes=["wout"])
    for k in range(8):
        P.c("pool", lambda e, k=k: e.tensor_tensor(out=wout[:, k, :], in0=wout[:, k, :], in1=bc2[:, :], op=ALU.mult), reads=["wout", "bc2"], writes=["wout"])
    for i in range(NT):
        xt_ = xf[i % 2]
        P.dma("sp", lambda e, xt_=xt_, i=i: e.dma_start(out=xt_[:, :], in_=x_d[i * 128:(i + 1) * 128, :]), f"xf{i % 2}", writes=[f"xf{i % 2}"])
        for hh in range(2):
            ps = psA[hh]
            for k in range(8):
                P.c("pe", lambda e, ps=ps, k=k, hh=hh, i=i: e.matmul(ps[:, :], lhsT=mixT[:, k, i * 128:(i + 1) * 128], rhs=wout[:, k, hh * 512:(hh + 1) * 512],
                                                                  start=(k == 0), stop=(k == 7)), reads=["mixT", "wout"], writes=[f"ps{hh}"])
            P.c("dve", lambda e, ps=ps, hh=hh, i=i, xt_=xt_: e.tensor_tensor(out=x_mid[:, i, hh * 512:(hh + 1) * 512], in0=ps[:, :],
                                                                           in1=xt_[:, hh * 512:(hh + 1) * 512], op=ALU.add),
                reads=[f"ps{hh}", f"xf{i % 2}"], writes=[f"x_mid{i}"])
    if debug:
        dd = dout("d_xmid", [128, NT * D])
        P.dma("sp", lambda e, dd=dd: e.dma_start(out=dd[:, :], in_=x_mid[:, :, :].rearrange("p a b -> p (a b)")), "dbg", reads=[f"x_mid{i}" for i in range(NT)])
    stage("F")

    GR = Arena(nc); GR.off = r2_base
    hx2T = GR.alloc("hx2T", [128, 8, T + 2], BF16)
    NJ = 4
    wa = GR.alloc("wa", [128, NJ, 8, 128], BF16)
    wg = GR.alloc("wg", [128, NJ, 8, 128], BF16)
    wdn = GR.alloc("wdn", [128, NJ, D], BF16)
    gbuf2 = [GR.alloc(f"gbuf{i}", [128, T + 2], F32) for i in range(2)]
    abuf2 = [GR.alloc(f"abuf{i}", [128, T], BF16) for i in range(2)]
    glb2 = [GR.alloc(f"glb{i}", [128, T], BF16) for i in range(2)]
    assert GR.off <= r2_end, (GR.off, r2_end)
    xs2 = nc.alloc_sbuf_tensor_at("xs2", [128, D], F32, offset=GR.off - 8192)
    junk3 = nc.alloc_sbuf_tensor_at("junk3", [128, D], BF16, offset=GR.off - 4096)
    gnames = ["hx2T"] + [f"{n}{i}" for n in ("wa", "wg", "wdn") for i in range(NJ)] + ["gbuf0", "gbuf1", "abuf0", "abuf1", "glb0", "glb1", "xs2", "junk3"]
    alias(gnames, ["mixT", "WRb", "WRf", "PV", "Mbf", "Eblk", "Macc", "L1lo", "L1hi"])
    HS = Arena(nc); HS.off = big0 + NT * D * 4
    hid = HS.alloc("hid", [128, NJ, T], BF16)
    ub2 = [HS.alloc(f"ub{i}", [128, T], F32) for i in range(2)]
    assert HS.off <= big0 + 96 * 1024
    alias([f"hid{i}" for i in range(NJ)] + ["ub0", "ub1"], ["wout", "xf0", "xf1"])

    xs2b = nc.alloc_sbuf_tensor_at("xs2b", [128, D], F32, offset=GR.off - 12288)
    ss2b = SP.alloc("ss2b", [128, 4], F32)
    xs2s = [(xs2, "xs2", ss2, "ss2"), (xs2b, "xs2b", ss2b, "ss2b")]

    def norm2_front(i, pp):
        xs_, xn_, s_, sn_ = xs2s[pp]
        P.c("act", lambda e: e.activation(out=junk3[:, :], in_=x_mid[:, i, :], func=AF.Square, accum_out=s_[:, 0:1]),
            reads=[f"x_mid{i}"], writes=["junk3", sn_ + "a"])
        P.c("act", lambda e: e.activation(out=s_[:, 1:2], in_=s_[:, 0:1], func=AF.Sqrt, scale=1.0 / D, bias=epst[:, 0:1]), reads=[sn_ + "a", "epst"], writes=[sn_ + "b"])
        P.c("dve", lambda e: e.reciprocal(out=s_[:, 2:3], in_=s_[:, 1:2]), reads=[sn_ + "b"], writes=[sn_ + "c"])
        P.c("dve", lambda e: e.tensor_scalar(out=xs_[:, :], in0=x_mid[:, i, :], scalar1=s_[:, 2:3], scalar2=None, op0=ALU.mult),
            reads=[f"x_mid{i}", sn_ + "c"], writes=[xn_])

    def norm2_back(i, pp):
        xs_, xn_, s_, sn_ = xs2s[pp]
        for k in range(8):
            ps = psA[k // 4]
            P.c("pe", lambda e, ps=ps, k=k: e.transpose(ps[:, (k % 4) * 128:(k % 4 + 1) * 128], xs_[:, k * 128:(k + 1) * 128], ident_f[:, :]),
                reads=[xn_, "ident_f"], writes=[f"ps{k // 4}"])
        for k in range(8):
            ps = psA[k // 4]
            P.c("act", lambda e, ps=ps, k=k: e.activation(out=hx2T[:, k, 1 + i * 128: 1 + (i + 1) * 128], in_=ps[:, (k % 4) * 128:(k % 4 + 1) * 128],
                                                        func=AF.Identity, scale=g2x[:, k:k + 1], bias=modT[:, 16 + k, 0:1]),
                reads=[f"ps{k // 4}", "g2x", "modT"], writes=["hx2T"])

    wup_src = wup_d.rearrange("(k p) n -> p k n", p=128)
    passes = [(0, 4), (4, 8), (8, 12), (12, 16), (16, 19), (19, 22)]

    def load_up(j, sl):
        P.dma("pool", lambda e: e.dma_start(out=wa[:, sl, :, :], in_=wup_src[:, :, j * 128:(j + 1) * 128]), f"wa{sl}", writes=[f"wa{sl}"])
        P.dma("pool", lambda e: e.dma_start(out=wg[:, sl, :, :], in_=wup_src[:, :, 2816 + j * 128: 2816 + (j + 1) * 128]), f"wg{sl}", writes=[f"wg{sl}"])

    def load_dn(j, sl):
        P.dma("pool", lambda e: e.dma_start(out=wdn[:, sl, :], in_=wdn_d[j * 128:(j + 1) * 128, :]), f"wdn{sl}", writes=[f"wdn{sl}"])

    def scale_dn(sl):
        P.c("pool", lambda e: e.tensor_tensor(out=wdn[:, sl, :], in0=wdn[:, sl, :], in1=bc5[:, :], op=ALU.mult), reads=[f"wdn{sl}", "bc5"], writes=[f"wdn{sl}"])

    def chunk_tail(t_):
        glb, ub, abuf, jj, gln, ubn, abn = t_
        P.c("act", lambda e: e.activation(out=glb[:, :], in_=ub[:, :], func=AF.Gelu_apprx_tanh), reads=[ubn], writes=[gln])
        P.c("dve", lambda e: e.tensor_tensor(out=hid[:, jj, :], in0=glb[:, :], in1=abuf[:, :], op=ALU.mult), reads=[gln, abn], writes=[f"hid{jj}"])

    for sl in range(passes[0][1]):
        load_up(sl, sl)
    for jj_ in range(passes[0][1] - passes[0][0]):
        load_dn(passes[0][0] + jj_, jj_)
    order = [0, NT - 1] + list(range(1, NT - 1))
    norm2_front(order[0], 0)
    for n_, i in enumerate(order):
        if n_ + 1 < len(order):
            norm2_front(order[n_ + 1], (n_ + 1) % 2)
        norm2_back(i, n_ % 2)
        if n_ == 1:
            P.c("dve", lambda e: e.tensor_copy(out=hxe[:, 0:8], in_=hx2T[:, :, 1]), reads=["hx2T"], writes=["hxe"])
            P.c("dve", lambda e: e.tensor_copy(out=hxe[:, 8:16], in_=hx2T[:, :, T]), reads=["hx2T"], writes=["hxe"])
            xc_in = nc.dram_tensor("xc_in", [128, 16], F32)
            xc_out = nc.dram_tensor("xc_out", [512, 16], F32)
            P.dma("sp", lambda e: e.dma_start(out=xc_in[:, :], in_=hxe[:, :]), "xc_st", reads=["hxe"], writes=["xc_in"])
            P.dma("pool", lambda e: e.collective_compute("AllGather", ALU.bypass, replica_groups=[[0, 1, 2, 3], [4, 5, 6, 7]],
                                                          ins=[xc_in.ap().opt()], outs=[xc_out.ap().opt()]),
                  "ccC", reads=["xc_in"], writes=["xc_out"], inc=1)
            P.dma("sp", lambda e: e.dma_start(out=xgC[:, :, :], in_=xc_out.ap().rearrange("(r p) n -> p r n", p=128)), "xc_ld", reads=["xc_out"], writes=["xgC"])
    P.c("pool", lambda e: e.memset(halo[:, :], 0.0), writes=["halo"])
    for r_ in range(4):
        P.c("dve", lambda e, r_=r_: e.scalar_tensor_tensor(out=halo[:, 0:8], in0=xgC[:, r_, 8:16], scalar=oh[:, r_:r_ + 1], in1=halo[:, 0:8],
                                                          op0=ALU.mult, op1=ALU.add), reads=["xgC", "oh", "halo"], writes=["halo"])
        P.c("dve", lambda e, r_=r_: e.scalar_tensor_tensor(out=halo[:, 8:16], in0=xgC[:, r_, 0:8], scalar=oh[:, 4 + r_:5 + r_], in1=halo[:, 8:16],
                                                          op0=ALU.mult, op1=ALU.add), reads=["xgC", "oh", "halo"], writes=["halo"])
    P.c("dve", lambda e: e.tensor_copy(out=hx2T[:, :, 0], in_=halo[:, 0:8]), reads=["halo"], writes=["hx2T"])
    P.c("dve", lambda e: e.tensor_copy(out=hx2T[:, :, T + 1], in_=halo[:, 8:16]), reads=["halo"], writes=["hx2T"])
    stage("G0")

    alias(["glb1"], ["xs2", "junk3"])
    alias(["glb0"], ["xs2"])
    alias(["abuf1"], ["xs2b"])
    cnum = 0
    pend_tail = []
    for pi, (j0, j1) in enumerate(passes):
        nj = j1 - j0
        if pi > 0:
            for jj in range(nj):
                load_dn(j0 + jj, jj)
        for jj in range(nj):
            j = j0 + jj
            pb = cnum % 2
            cnum += 1
            gbuf, abuf, glb, ub = gbuf2[pb], abuf2[pb], glb2[pb], ub2[pb]
            gbn, abn, gln, ubn = f"gbuf{pb}", f"abuf{pb}", f"glb{pb}", f"ub{pb}"
            for tg in range(4):
                psa, psg = (psA[2], psA[3]) if tg % 2 == 0 else (psA[4], psA[5])
                pan, pgn = ("ps2", "ps3") if tg % 2 == 0 else ("ps4", "ps5")
                csl = slice(1 + tg * 512, 1 + (tg + 1) * 512)
                for k in range(8):
                    P.c("pe", lambda e, psa=psa, k=k, jj=jj, csl=csl: e.matmul(psa[:, :], lhsT=wa[:, jj, k, :], rhs=hx2T[:, k, csl], start=(k == 0), stop=(k == 7)),
                        reads=[f"wa{jj}", "hx2T"], writes=[pan])
                for k in range(8):
                    P.c("pe", lambda e, psg=psg, k=k, jj=jj, csl=csl: e.matmul(psg[:, :], lhsT=wg[:, jj, k, :], rhs=hx2T[:, k, csl], start=(k == 0), stop=(k == 7)),
                        reads=[f"wg{jj}", "hx2T"], writes=[pgn])
                P.c("act", lambda e, psa=psa, tg=tg, abuf=abuf: e.activation(out=abuf[:, tg * 512:(tg + 1) * 512], in_=psa[:, :], func=AF.Copy), reads=[pan], writes=[abn])
                P.c("act", lambda e, psg=psg, csl=csl, gbuf=gbuf: e.activation(out=gbuf[:, csl], in_=psg[:, :], func=AF.Copy), reads=[pgn], writes=[gbn])
            for k in range(8):
                P.c("pe", lambda e, k=k, jj=jj: e.matmul(psA[6][:, 0:2], lhsT=wg[:, jj, k, :], rhs=hx2T[:, k, 0:T + 2:T + 1], start=(k == 0), stop=(k == 7)),
                    reads=[f"wg{jj}", "hx2T"], writes=["ps6"])
            P.c("act", lambda e, gbuf=gbuf: e.activation(out=gbuf[:, 0:T + 2:T + 1], in_=psA[6][:, 0:2], func=AF.Copy), reads=["ps6"], writes=[gbn])
            while len(pend_tail) > 0:
                chunk_tail(pend_tail.pop(0))
            if pi + 1 < len(passes) and jj < passes[pi + 1][1] - passes[pi + 1][0]:
                load_up(passes[pi + 1][0] + jj, jj)
            scale_dn(jj)
            P.c("dve", lambda e, j=j, ub=ub, gbuf=gbuf: e.tensor_scalar(out=ub[:, :], in0=gbuf[:, 1:T + 1], scalar1=cw[:, j, 1:2], scalar2=cb[:, j:j + 1], op0=ALU.mult, op1=ALU.add),
                reads=[gbn, "cw", "cb"], writes=[ubn])
            P.c("dve", lambda e, j=j, ub=ub, gbuf=gbuf: e.scalar_tensor_tensor(out=ub[:, :], in0=gbuf[:, 0:T], scalar=cw[:, j, 0:1], in1=ub[:, :], op0=ALU.mult, op1=ALU.add),
                reads=[gbn, "cw", ubn], writes=[ubn])
            P.c("dve", lambda e, j=j, ub=ub, gbuf=gbuf: e.scalar_tensor_tensor(out=ub[:, :], in0=gbuf[:, 2:T + 2], scalar=cw[:, j, 2:3], in1=ub[:, :], op0=ALU.mult, op1=ALU.add),
                reads=[gbn, "cw", ubn], writes=[ubn])
            pend_tail.append((glb, ub, abuf, jj, gln, ubn, abn))
        while len(pend_tail) > 0:
            chunk_tail(pend_tail.pop(0))
        for i in range(NT):
            for hh in range(2):
                ps = psA[hh]
                for jj in range(nj):
                    P.c("pe", lambda e, ps=ps, jj=jj, i=i, hh=hh, nj=nj: e.matmul(ps[:, :], lhsT=hid[:, jj, i * 128:(i + 1) * 128], rhs=wdn[:, jj, hh * 512:(hh + 1) * 512],
                                                                               start=(jj == 0), stop=(jj == nj - 1)), reads=[f"hid{jj}", f"wdn{jj}"], writes=[f"ps{hh}"])
                P.c("dve", lambda e, ps=ps, i=i, hh=hh: e.tensor_tensor(out=x_mid[:, i, hh * 512:(hh + 1) * 512], in0=ps[:, :],
                                                                      in1=x_mid[:, i, hh * 512:(hh + 1) * 512], op=ALU.add),
                    reads=[f"ps{hh}", f"x_mid{i}"], writes=[f"x_mid{i}"])
    stage("G1")
    alias(["junk3"], ["glb1"])
    def fin_front(i):
        s_, sn_ = (ss2, "ss2") if i % 2 == 0 else (ss2b, "ss2b")
        P.c("act", lambda e: e.activation(out=junk3[:, :], in_=x_mid[:, i, :], func=AF.Square, accum_out=s_[:, 0:1]),
            reads=[f"x_mid{i}"], writes=["junk3", sn_ + "a"])
        P.c("act", lambda e: e.activation(out=s_[:, 1:2], in_=s_[:, 0:1], func=AF.Sqrt, scale=1.0 / D, bias=epst[:, 0:1]), reads=[sn_ + "a", "epst"], writes=[sn_ + "b"])
        P.c("dve", lambda e: e.reciprocal(out=s_[:, 2:3], in_=s_[:, 1:2]), reads=[sn_ + "b"], writes=[sn_ + "c"])

    fin_front(0)
    for i in range(NT):
        if i + 1 < NT:
            fin_front(i + 1)
        s_, sn_ = (ss2, "ss2") if i % 2 == 0 else (ss2b, "ss2b")
        P.c("dve", lambda e, i=i, s_=s_: e.scalar_tensor_tensor(out=x_mid[:, i, :], in0=x_mid[:, i, :], scalar=s_[:, 2:3], in1=fnw[:, :], op0=ALU.mult, op1=ALU.mult),
            reads=[f"x_mid{i}", sn_ + "c", "fnw"], writes=[f"x_mid{i}"])
        P.dma("sp", lambda e, i=i: e.dma_start(out=out_d[i * 128:(i + 1) * 128, :], in_=x_mid[:, i, :]), "xout", reads=[f"x_mid{i}"])
    stage("end")


def _s5(nc, P, debug, stage, dout, dbg, alias, big0, r2_base, r2_end, spare_off, uT, ucT, mixT, psA, psTq, ident_f, ident_b, ones_f, negpi, msk, d_in):
    TWO_PI = 2.0 * np.pi
    AS = Arena(nc)
    AS.off = big0
    S_END = big0 + 80 * 1024
    XR = AS.alloc("XR", [128, 32, 128], F32)
    QN = AS.alloc("QN", [128, 32, 128], F32)
    Macc = nc.alloc_sbuf_tensor_at("Macc", [128, 32, 128], F32, offset=r2_base)
    sm0 = AS.off
    V1 = AS.alloc("V1", [128, 13, 64], F32); V2 = AS.alloc("V2", [128, 13, 64], F32)
    bglu = AS.alloc("bglu", [128, 4], F32)
    dead0 = AS.off

    def f32t(name, w):
        return AS.alloc(name, [128, w], F32)
    lamre = f32t("lamre", 64); lamim = f32t("lamim", 64); lstep = f32t("lstep", 64)
    dtt = f32t("dtt", 64); rho = f32t("rho", 64); tht = f32t("tht", 64); mag = f32t("mag", 64)
    sn = f32t("sn", 64); cs = f32t("cs", 64); ta = f32t("ta", 64); tb = f32t("tb", 64); tc = f32t("tc", 64)
    tiI = AS.alloc("tiI", [128, 64], I32)
    kre = f32t("kre", 64); kim = f32t("kim", 64)
    PWre = AS.alloc("PWre", [128, 9, 64], F32); PWim = AS.alloc("PWim", [128, 9, 64], F32)
    NPre = AS.alloc("NPre", [128, 8, 64], F32); NPim = AS.alloc("NPim", [128, 8, 64], F32)
    HPre = AS.alloc("HPre", [128, 13, 64], F32); HPim = AS.alloc("HPim", [128, 13, 64], F32)
    bre = AS.alloc("bre", [128, 32, 16], F32); bim = AS.alloc("bim", [128, 32, 16], F32)
    cre = AS.alloc("cre", [128, 32, 16], F32); cim = AS.alloc("cim", [128, 32, 16], F32)
    Cr = AS.alloc("Cr", [128, 32, 16], F32)
    d128 = AS.alloc("d128", [128, 32, 16], F32)
    BBre = AS.alloc("BBre", [128, 64, 16], F32); BBim = AS.alloc("BBim", [128, 64, 16], F32)
    e1 = nc.alloc_sbuf_tensor_at("s5e1", [128, 32, 16], F32, offset=spare_off)
    e2 = nc.alloc_sbuf_tensor_at("s5e2", [128, 32, 16], F32, offset=spare_off + 2048)
    maskF = AS.alloc("maskF", [128, 128], F32); maskB = AS.alloc("maskB", [128, 128], F32)
    tmpM = AS.alloc("tmpM", [128, 128], F32)
    WRf = nc.alloc_sbuf_tensor_at("WRf", [128, 32, 128], F32, offset=r2_base + 8 * T * 2 + 56 * 1024 - 16 * 1024)
    assert AS.off <= S_END, (AS.off, S_END)
    AR2 = Arena(nc)
    AR2.off = r2_base + 8 * T * 2
    WR = AR2.alloc("WR", [128, 64, 128], BF16)
    PV = AR2.alloc("PV", [128, 64, 128], BF16)
    Mbf = AR2.alloc("Mbf", [128, 32, 128], BF16)
    Eblk = AR2.alloc("Eblk", [128, 64, 128], BF16)
    assert AR2.off <= r2_end, (AR2.off, r2_end)
    s5names = ["XR", "QN", "Macc", "s5tab", "WRb", "WRf", "PV", "Mbf", "Eblk", "s5tmp", "L1lo", "L1hi", "e1lo", "e2lo", "e1hi", "e2hi"]
    alias(s5names, ["qT", "kT", "k_tm", "v_tm", "gate", "Rb_store", "Rf_store", "scm1", "qf1", "qb1", "yn1", "retx1", "bst1", "bst21", "bst31", "mv1", "rs1", "rs41", "xg", "Dm", "XiF", "XiB", "dm", "dpos", "dneg", "mge", "mlt", "e1", "e2", "tfree",
                    "pidx", "zarg", "Zeta", "G128", "G2048", "nld", "ld128", "Rf", "Rb", "Rcf", "Rcb", "Rf_bf", "hacc_t", "vz0", "vz1", "scm", "qf", "qb", "yn",
                    "retx", "bst", "bst2", "bst3", "junk2", "mv", "rs", "rs4", "kc_tm", "vc_tm"])
    TAB = "s5tab"

    for (t_, nm) in ((lamre, "lamre"), (lamim, "lamim"), (lstep, "lstep"), (bre, "bre"), (bim, "bim"), (cre, "cre"), (cim, "cim"), (d128, "d128"), (bglu, "bglu")):
        src_ap = d_in[nm]
        if len(t_.shape) == 3:
            P.dma("sp", lambda e, t_=t_, src_ap=src_ap: e.dma_start(out=t_[:, :, :], in_=src_ap), "s5ld", writes=[TAB])
        else:
            P.dma("sp", lambda e, t_=t_, src_ap=src_ap: e.dma_start(out=t_[:, :], in_=src_ap), "s5ld", writes=[TAB])

    cnt = [0]

    def ew():
        cnt[0] += 1
        return "dve" if cnt[0] % 2 else "pool"

    def tt(out, a, b, op, eng=None, rd=(TAB,), wr=(TAB,)):
        P.c(eng or ew(), lambda e: e.tensor_tensor(out=out, in0=a, in1=b, op=op), reads=list(rd), writes=list(wr))

    def ts(out, a, s1, s2, op0, op1=None, eng="dve", rd=(TAB,), wr=(TAB,)):
        if op1 is None:
            P.c(eng, lambda e: e.tensor_scalar(out=out, in0=a, scalar1=s1, scalar2=None, op0=op0), reads=list(rd), writes=list(wr))
        else:
            P.c(eng, lambda e: e.tensor_scalar(out=out, in0=a, scalar1=s1, scalar2=s2, op0=op0, op1=op1), reads=list(rd), writes=list(wr))

    def sin_of(dst, src_, off):
        ts(ta[:, :], src_, 1.0 / TWO_PI, off, ALU.mult, ALU.add)
        P.c("dve", lambda e: e.tensor_copy(out=tiI[:, :], in_=ta[:, :]), reads=[TAB], writes=[TAB])
        P.c("dve", lambda e: e.tensor_copy(out=tb[:, :], in_=tiI[:, :]), reads=[TAB], writes=[TAB])
        tt(ta[:, :], ta[:, :], tb[:, :], ALU.subtract, eng="dve")
        ts(tb[:, :], ta[:, :], 0.0, None, ALU.is_lt)
        tt(ta[:, :], ta[:, :], tb[:, :], ALU.add, eng="dve")
        P.c("act", lambda e: e.activation(out=dst, in_=ta[:, :], func=AF.Sin, scale=TWO_PI, bias=negpi[:, 0:1]), reads=[TAB, "negpi"], writes=[TAB])

    def cmul(ore, oim, are, aim, bre_, bim_, w=64):
        t1, t2 = ta[:, 0:w], tb[:, 0:w]
        tt(t1, are, bre_, ALU.mult, eng="dve"); tt(t2, aim, bim_, ALU.mult, eng="dve"); tt(ore, t1, t2, ALU.subtract, eng="dve")
        tt(t1, are, bim_, ALU.mult, eng="dve"); tt(t2, aim, bre_, ALU.mult, eng="dve"); tt(oim, t1, t2, ALU.add, eng="dve")

    P.c("act", lambda e: e.activation(out=dtt[:, :], in_=lstep[:, :], func=AF.Exp), reads=[TAB], writes=[TAB])
    tt(rho[:, :], lamre[:, :], dtt[:, :], ALU.mult, eng="dve")
    tt(tht[:, :], lamim[:, :], dtt[:, :], ALU.mult, eng="dve")
    P.c("act", lambda e: e.activation(out=mag[:, :], in_=rho[:, :], func=AF.Exp), reads=[TAB], writes=[TAB])
    sin_of(sn[:, :], tht[:, :], 0.5)
    sin_of(cs[:, :], tht[:, :], 0.75)
    P.c("pool", lambda e: e.memset(PWre[:, 0, :], 1.0), reads=[TAB], writes=[TAB])
    P.c("pool", lambda e: e.memset(PWim[:, 0, :], 0.0), reads=[TAB], writes=[TAB])
    tt(PWre[:, 1, :], mag[:, :], cs[:, :], ALU.mult, eng="dve")
    tt(PWim[:, 1, :], mag[:, :], sn[:, :], ALU.mult, eng="dve")
    ts(tc[:, :], PWre[:, 1, :], -1.0, None, ALU.add)
    tt(kre[:, :], tc[:, :], lamre[:, :], ALU.mult, eng="dve"); tt(ta[:, :], PWim[:, 1, :], lamim[:, :], ALU.mult, eng="dve")
    tt(kre[:, :], kre[:, :], ta[:, :], ALU.add, eng="dve")
    tt(kim[:, :], PWim[:, 1, :], lamre[:, :], ALU.mult, eng="dve"); tt(ta[:, :], tc[:, :], lamim[:, :], ALU.mult, eng="dve")
    tt(kim[:, :], kim[:, :], ta[:, :], ALU.subtract, eng="dve")
    tt(ta[:, :], lamre[:, :], lamre[:, :], ALU.mult, eng="dve"); tt(tb[:, :], lamim[:, :], lamim[:, :], ALU.mult, eng="dve")
    tt(ta[:, :], ta[:, :], tb[:, :], ALU.add, eng="dve")
    P.c("dve", lambda e: e.reciprocal(out=tb[:, :], in_=ta[:, :]), reads=[TAB], writes=[TAB])
    tt(kre[:, :], kre[:, :], tb[:, :], ALU.mult, eng="dve"); tt(kim[:, :], kim[:, :], tb[:, :], ALU.mult, eng="dve")
    for d in range(2):
        cs_ = slice(d * 32, (d + 1) * 32)
        kr = kre[:, cs_].unsqueeze(2).broadcast_to([128, 32, 16]); ki = kim[:, cs_].unsqueeze(2).broadcast_to([128, 32, 16])
        tt(e1[:, :, :], kr, bre[:, :, :], ALU.mult); tt(e2[:, :, :], ki, bim[:, :, :], ALU.mult)
        tt(BBre[:, cs_, :], e1[:, :, :], e2[:, :, :], ALU.subtract, eng="dve")
        tt(e1[:, :, :], kr, bim[:, :, :], ALU.mult); tt(e2[:, :, :], ki, bre[:, :, :], ALU.mult)
        tt(BBim[:, cs_, :], e1[:, :, :], e2[:, :, :], ALU.add, eng="dve")
    P.c("dve", lambda e: e.tensor_copy(out=Cr[0:64, :, :], in_=cre[0:64, :, :]), reads=[TAB], writes=[TAB])
    ts(Cr[64:128, :, :], cim[64:128, :, :], -1.0, None, ALU.mult)
    for e_ in range(2, 9):
        cmul(PWre[:, e_, :], PWim[:, e_, :], PWre[:, e_ - 1, :], PWim[:, e_ - 1, :], PWre[:, 1, :], PWim[:, 1, :])
    P.c("pool", lambda e: e.memset(NPre[:, 0, :], 1.0), reads=[TAB], writes=[TAB])
    P.c("pool", lambda e: e.memset(NPim[:, 0, :], 0.0), reads=[TAB], writes=[TAB])
    tt(ta[:, :], PWre[:, 1, :], PWre[:, 1, :], ALU.mult, eng="dve"); tt(tb[:, :], PWim[:, 1, :], PWim[:, 1, :], ALU.mult, eng="dve")
    tt(ta[:, :], ta[:, :], tb[:, :], ALU.add, eng="dve")
    P.c("dve", lambda e: e.reciprocal(out=tc[:, :], in_=ta[:, :]), reads=[TAB], writes=[TAB])
    tt(NPre[:, 1, :], PWre[:, 1, :], tc[:, :], ALU.mult, eng="dve")
    tt(NPim[:, 1, :], PWim[:, 1, :], tc[:, :], ALU.mult, eng="dve")
    ts(NPim[:, 1, :], NPim[:, 1, :], -1.0, None, ALU.mult)
    for e_ in range(2, 8):
        cmul(NPre[:, e_, :], NPim[:, e_, :], NPre[:, e_ - 1, :], NPim[:, e_ - 1, :], NPre[:, 1, :], NPim[:, 1, :])
    P.c("dve", lambda e: e.tensor_copy(out=HPre[:, 0, :], in_=PWre[:, 8, :]), reads=[TAB], writes=[TAB])
    P.c("dve", lambda e: e.tensor_copy(out=HPim[:, 0, :], in_=PWim[:, 8, :]), reads=[TAB], writes=[TAB])
    for k in range(4):
        b0 = 3 * k
        cmul(HPre[:, b0 + 1, :], HPim[:, b0 + 1, :], HPre[:, b0, :], HPim[:, b0, :], HPre[:, b0, :], HPim[:, b0, :])
        cmul(HPre[:, b0 + 2, :], HPim[:, b0 + 2, :], HPre[:, b0 + 1, :], HPim[:, b0 + 1, :], HPre[:, b0, :], HPim[:, b0, :])
        cmul(HPre[:, b0 + 3, :], HPim[:, b0 + 3, :], HPre[:, b0 + 1, :], HPim[:, b0 + 1, :], HPre[:, b0 + 1, :], HPim[:, b0 + 1, :])
    P.c("dve", lambda e: e.tensor_copy(out=V1[0:64, :, :], in_=HPre[0:64, :, :]), reads=[TAB], writes=[TAB])
    ts(V1[64:128, :, :], HPim[64:128, :, :], -1.0, None, ALU.mult)
    P.c("dve", lambda e: e.tensor_copy(out=V2[0:64, :, :], in_=HPim[0:64, :, :]), reads=[TAB], writes=[TAB])
    P.c("dve", lambda e: e.tensor_copy(out=V2[64:128, :, :], in_=HPre[64:128, :, :]), reads=[TAB], writes=[TAB])
    P.c("pool", lambda e: e.iota(maskF[:, :], pattern=[[1, 8], [0, 16]], base=0, channel_multiplier=0, allow_small_or_imprecise_dtypes=True),
        reads=[TAB], writes=[TAB])
    P.c("pool", lambda e: e.iota(tiI[:, 0:1], pattern=[[0, 1]], base=0, channel_multiplier=1), reads=[TAB], writes=[TAB])
    P.c("dve", lambda e: e.tensor_single_scalar(out=tiI[:, 1:2], in_=tiI[:, 0:1], scalar=4, op=ALU.arith_shift_right), reads=[TAB], writes=[TAB])
    P.c("dve", lambda e: e.tensor_copy(out=ta[:, 0:1], in_=tiI[:, 1:2]), reads=[TAB], writes=[TAB])
    ts(maskB[:, :], maskF[:, :], ta[:, 0:1], None, ALU.is_le)
    ts(maskF[:, :], maskF[:, :], ta[:, 0:1], None, ALU.is_ge)
    if debug:
        for (nm, t_, w) in (("d_PWre", PWre, 9 * 64), ("d_PWim", PWim, 9 * 64), ("d_HPre", HPre, 13 * 64), ("d_HPim", HPim, 13 * 64),
                            ("d_BBre", BBre, 64 * 16), ("d_BBim", BBim, 64 * 16), ("d_NPre", NPre, 8 * 64)):
            dd = dout(nm, [128, w])
            P.dma("sp", lambda e, dd=dd, t_=t_: e.dma_start(out=dd[:, :], in_=t_[:, :, :].rearrange("p a b -> p (a b)")), "dbg", reads=[TAB])
        dd = dout("d_maskF", [128, 128])
        P.dma("sp", lambda e, dd=dd: e.dma_start(out=dd[:, :], in_=maskF[:, :]), "dbg", reads=[TAB])
    stage("S5tab")

    XR4 = XR[:, :, :].rearrange("p g (s q) -> p g s q", s=8)
    QN4 = QN[:, :, :].rearrange("p g (s q) -> p g s q", s=8)
    WRf4 = WRf[:, :, :].rearrange("p g (s q) -> p g s q", s=8)
    LO, HI = slice(0, 64), slice(64, 128)
    psM = [psA[0], psA[1]]
    psPV = psA[2]

    P.c("dve", lambda e: e.tensor_copy(out=e1[HI, :, :], in_=BBre[HI, 0:32, :]), reads=[TAB], writes=["e1hi"])
    P.c("dve", lambda e: e.tensor_copy(out=e2[HI, :, :], in_=BBre[HI, 32:64, :]), reads=[TAB], writes=["e2hi"])
    P.c("dve", lambda e: e.tensor_copy(out=BBre[HI, :, :], in_=BBim[HI, :, :]), reads=[TAB, "e1hi", "e2hi"], writes=[TAB])
    P.c("dve", lambda e: e.tensor_scalar(out=BBim[HI, 0:32, :], in0=e1[HI, :, :], scalar1=-1.0, scalar2=None, op0=ALU.mult), reads=[TAB, "e1hi"], writes=[TAB])
    P.c("dve", lambda e: e.tensor_scalar(out=BBim[HI, 32:64, :], in0=e2[HI, :, :], scalar1=-1.0, scalar2=None, op0=ALU.mult), reads=[TAB, "e2hi"], writes=[TAB])
    P.c("dve", lambda e: e.tensor_copy(out=cim[HI, :, :], in_=cre[HI, :, :]), reads=[TAB], writes=[TAB])

    def ctab(dst, dname, Are, Aim, e_idx, d, X1, X2, eng, ta_, tan):
        cs_ = slice(d * 32, (d + 1) * 32)
        ar = Are[:, e_idx, cs_].unsqueeze(2).broadcast_to([128, 32, 16])
        ai = Aim[:, e_idx, cs_].unsqueeze(2).broadcast_to([128, 32, 16])
        P.c(eng, lambda e: e.tensor_tensor(out=dst, in0=ar, in1=X1, op=ALU.mult), reads=[TAB], writes=[dname])
        P.c(eng, lambda e: e.tensor_tensor(out=ta_, in0=ai, in1=X2, op=ALU.mult), reads=[TAB], writes=[tan])
        P.c(eng, lambda e: e.tensor_tensor(out=dst, in0=dst, in1=ta_, op=ALU.subtract), reads=[dname, tan], writes=[dname])

    for d in range(2):
        cs_ = slice(d * 32, (d + 1) * 32)
        for s_ in range(8):
            eb = (7 - s_) if d == 0 else s_
            ctab(XR4[:, :, s_, :], "L1lo", PWre, PWim, eb, d, BBre[:, cs_, :], BBim[:, cs_, :], "dve", e1[:, :, :], "e1lo")
            ew_ = (s_ + 1) if d == 0 else (8 - s_)
            ctab(WRf4[:, :, s_, :], "WRf", PWre, PWim, ew_, d, Cr[:, :, :], cim[:, :, :], "dve", e1[:, :, :], "e1lo")
            en = (7 - s_) if d == 0 else s_
            ctab(QN4[:, :, s_, :], "L1hi", NPre, NPim, en, d, Cr[:, :, :], cim[:, :, :], "pool", e2[:, :, :], "e2lo")
        P.c("act", lambda e, cs_=cs_: e.activation(out=WR[:, cs_, :], in_=WRf[:, :, :], func=AF.Copy), reads=["WRf"], writes=["WRb"])
        for g in range(32):
            ps = psM[g % 2]
            P.c("pe", lambda e, ps=ps, g=g: e.matmul(ps[:, 0:128], lhsT=XR[:, g, :], rhs=QN[:, g, :], start=True, stop=True),
                reads=["L1lo", "L1hi"], writes=[f"ps{g % 2}"])
            if d == 0:
                P.c("dve", lambda e, ps=ps, g=g: e.tensor_tensor(out=Macc[:, g, :], in0=ps[:, 0:128], in1=maskF[:, :], op=ALU.mult),
                    reads=[f"ps{g % 2}", TAB], writes=["Macc"])
            else:
                P.c("dve", lambda e, ps=ps, g=g: e.tensor_tensor(out=tmpM[:, :], in0=ps[:, 0:128], in1=maskB[:, :], op=ALU.mult),
                    reads=[f"ps{g % 2}", TAB], writes=["tmpM"])
                P.c("dve", lambda e, g=g: e.tensor_tensor(out=Macc[:, g, :], in0=Macc[:, g, :], in1=tmpM[:, :], op=ALU.add),
                    reads=["tmpM", "Macc"], writes=["Macc"])
        for g4 in range(8):
            for j in range(4):
                g = g4 * 4 + j
                P.c("pe", lambda e, g=g, j=j: e.transpose(psPV[:, j * 128:(j + 1) * 128], XR[:, g, :], ident_f[:, :]),
                    reads=["L1lo", "L1hi", "ident_f"], writes=["ps2"])
            P.c("act", lambda e, g4=g4, d=d: e.activation(out=PV[:, d * 32 + g4 * 4: d * 32 + g4 * 4 + 4, :],
                                                        in_=psPV[:, :].rearrange("p (j m) -> p j m", j=4), func=AF.Copy),
                reads=["ps2"], writes=["PV"])
    idb = ident_f[:, :].rearrange("p (t q) -> p t q", t=8).unsqueeze(1).broadcast_to([128, 32, 8, 16])
    d4 = d128[:, :, :].unsqueeze(2).broadcast_to([128, 32, 8, 16])
    P.c("dve", lambda e: e.tensor_tensor(out=QN4, in0=idb, in1=d4, op=ALU.mult), reads=[TAB, "ident_f", "L1lo", "L1hi"], writes=["L1lo", "L1hi"])
    P.c("dve", lambda e: e.tensor_tensor(out=Mbf[:, :, :], in0=Macc[:, :, :], in1=QN[:, :, :], op=ALU.add), reads=["Macc", "L1lo", "L1hi"], writes=["Mbf"])
    if debug:
        dd = nc.dram_tensor("d_Mbf", [128, 32 * 128], BF16, kind="ExternalOutput").ap(); dbg["d_Mbf"] = dd
        P.dma("sp", lambda e, dd=dd: e.dma_start(out=dd[:, :], in_=Mbf[:, :, :].rearrange("p a b -> p (a b)")), "dbg", reads=["Mbf"])
        dd = nc.dram_tensor("d_PV", [128, 64 * 128], BF16, kind="ExternalOutput").ap(); dbg["d_PV"] = dd
        P.dma("sp", lambda e, dd=dd: e.dma_start(out=dd[:, :], in_=PV[:, :, :].rearrange("p a b -> p (a b)")), "dbg", reads=["PV"])
        dd = nc.dram_tensor("d_WR", [128, 64 * 128], BF16, kind="ExternalOutput").ap(); dbg["d_WR"] = dd
        P.dma("sp", lambda e, dd=dd: e.dma_start(out=dd[:, :], in_=WR[:, :, :].rearrange("p a b -> p (a b)")), "dbg", reads=["WRb"])
    stage("S5l1")

    B1 = Arena(nc); B1.off = big0
    Z = B1.alloc("Z", [128, 32, 256], BF16)
    Zc = B1.alloc("Zc", [128, 32, 32], BF16)
    Ysb = B1.alloc("Ysb", [128, 8, 256], BF16)
    WV = 4
    Xw = [B1.alloc(f"Xw{i}", [128, WV, 288], BF16) for i in range(2)]
    TAw = [B1.alloc(f"TAw{i}", [128, WV, 80], BF16) for i in range(2)]
    TBw = [B1.alloc(f"TBw{i}", [128, WV, 80], BF16) for i in range(2)]
    Fall = B1.alloc("Fall", [128, 64], F32)
    Fctx = B1.alloc("Fctx", [128, 64], F32)
    hacc = B1.alloc("hacc", [128, 64], F32)
    hacc_bf = B1.alloc("hacc_bf", [128, 64], BF16)
    htmp = B1.alloc("htmp", [128, 64], F32)
    xgB = B1.alloc("xgB", [128, 4, 64], F32)
    D2 = B1.alloc("D2", [128, 64], F32)
    assert B1.off <= big0 + 32 * 1024, B1.off - big0
    B2 = Arena(nc); B2.off = dead0
    s5gT = B2.alloc("s5gT", [128, 4, T], BF16)
    wglu = B2.alloc("wglu", [128, 4, 512], BF16)
    NR = 36
    Rring = [B2.alloc(f"Rr{i}", [128, 128], BF16) for i in range(NR)]
    sgt = B2.alloc("sgt", [128, 512], BF16)
    XAw = [B2.alloc(f"XAw{i}", [128, WV, 256], BF16) for i in range(2)]
    XBw = [B2.alloc(f"XBw{i}", [128, WV, 256], BF16) for i in range(2)]
    assert B2.off <= S_END, (B2.off, S_END)
    post = ["Z", "Zc", "Ysb", "Fall", "Fctx", "hacc", "hacc_bf", "htmp", "xgB", "D2", "s5gT", "wglu", "sgt"] + \
           [f"{n}{i}" for n in ("Xw", "TAw", "TBw", "XAw", "XBw") for i in range(2)] + [f"Rr{i}" for i in range(NR)]
    alias(post, ["L1lo", "L1hi", "e1lo", "e1hi", "e2lo", "e2hi", "tmpM", TAB])
    P.dma("pool", lambda e: e.dma_start(out=wglu[:, :, :], in_=d_in["wglu"].rearrange("(k p) n -> p k n", p=128)), "wglu", reads=[TAB], writes=["wglu"])
    P.c("dve", lambda e: e.tensor_tensor(out=D2[:, :], in0=ident_f[:, 0:64], in1=ident_f[:, 64:128], op=ALU.add), reads=["ident_f", TAB], writes=["D2"])
    P.c("pool", lambda e: e.memset(Eblk[:, :, :], 0.0), reads=[], writes=["Eblk", "WRf"])
    for a_ in range(8):
        for b_ in range(8):
            P.c("pool", lambda e, a_=a_, b_=b_: e.affine_select(out=Eblk[:, a_ * 8 + b_, 16 * b_:16 * b_ + 16], in_=ones_f[:, 0:16], pattern=[[1, 16]],
                                                              compare_op=ALU.is_equal, fill=0.0, base=16 * a_, channel_multiplier=-1),
                reads=["ones_f"], writes=["Eblk"])
    psZ = [psA[3], psA[4]]
    ev = [0]

    def evac(out, in_, reads, writes, eng=None):
        ev[0] += 1
        if eng is None:
            eng = "act" if ev[0] % 2 else "dve"
        if eng == "act":
            P.c("act", lambda e: e.activation(out=out, in_=in_, func=AF.Copy), reads=reads, writes=writes)
        else:
            P.c("dve", lambda e: e.tensor_copy(out=out, in_=in_), reads=reads, writes=writes)

    for g in range(32):
        fc, g8 = g // 8, g % 8
        ps = psZ[g % 2]
        for s_ in range(8):
            P.c("pe", lambda e, ps=ps, fc=fc, g8=g8, s_=s_: e.matmul(ps[:, 0:256], lhsT=Eblk[:, g8 * 8 + s_, :], rhs=uT[:, fc, s_::8],
                                                                    start=(s_ == 0), stop=(s_ == 7)), reads=["Eblk", "uT"], writes=[f"ps{3 + g % 2}"])
        for s_ in range(8):
            P.c("pe", lambda e, ps=ps, fc=fc, g8=g8, s_=s_: e.matmul(ps[:, 256:288], lhsT=Eblk[:, g8 * 8 + s_, :], rhs=ucT[:, fc, s_::8],
                                                                    start=(s_ == 0), stop=(s_ == 7)), reads=["Eblk", "ucT"], writes=[f"ps{3 + g % 2}"])
        en_ = "act" if g % 2 else "dve"
        evac(Z[:, g, :], ps[:, 0:256], [f"ps{3 + g % 2}"], ["Z"], eng=en_)
        evac(Zc[:, g, :], ps[:, 256:288], [f"ps{3 + g % 2}"], ["Zc"], eng=en_)

    rr = [0]
    reng = ["dve", "pool"]

    def build_R(dst, dname, hp_idx, q):
        for half, Vt in ((0, V1), (1, V2)):
            rr[0] += 1
            eng = reng[rr[0] % 2]
            o_ = dst[:, half * 64:(half + 1) * 64]
            sc = Vt[:, hp_idx, q:q + 1]
            if eng == "act":
                P.c("act", lambda e, o_=o_, sc=sc: e.activation(out=o_, in_=D2[:, :], func=AF.Identity, scale=sc), reads=["D2", TAB], writes=[dname])
            elif eng == "dve":
                P.c("dve", lambda e, o_=o_, sc=sc: e.tensor_scalar(out=o_, in0=D2[:, :], scalar1=sc, scalar2=None, op0=ALU.mult), reads=["D2", TAB], writes=[dname])
            else:
                P.c("pool", lambda e, o_=o_, sc=sc: e.tensor_scalar(out=o_, in0=D2[:, :], scalar1=sc, scalar2=0.0, op0=ALU.mult, op1=ALU.add),
                    reads=["D2", TAB], writes=[dname])

    ring = [0]

    def get_R(hp_idx, q):
        slot = ring[0] % NR
        ring[0] += 1
        build_R(Rring[slot][:, :], f"Rr{slot}", hp_idx, q)
        return Rring[slot], f"Rr{slot}"

    def tree_wave(w):
        par = w % 2
        qs = [w * WV + c_ for c_ in range(WV)]
        d = qs[0] // 32
        bankL = [psA[1 + 2 * par], psA[2 + 2 * par]]
        bnL = [f"ps{1 + 2 * par}", f"ps{2 + 2 * par}"]
        psC, pcn = psA[5 + par], f"ps{5 + par}"
        for c_, q in enumerate(qs):
            g = q % 32
            bk, bn = bankL[c_ // 2], bnL[c_ // 2]
            P.c("pe", lambda e, bk=bk, q=q, g=g, c_=c_: e.matmul(bk[:, (c_ % 2) * 256:(c_ % 2 + 1) * 256], lhsT=PV[:, q, :], rhs=Z[:, g, :], start=True, stop=True),
                reads=["PV", "Z"], writes=[bn])
            P.c("pe", lambda e, q=q, g=g, c_=c_, psC=psC: e.matmul(psC[:, c_ * 32:(c_ + 1) * 32], lhsT=PV[:, q, :], rhs=Zc[:, g, :], start=True, stop=True),
                reads=["PV", "Zc"], writes=[pcn])
        yield
        xw, xwn = Xw[par], f"Xw{par}"
        for h_ in range(2):
            evac(xw[:, 2 * h_:2 * h_ + 2, 0:256], bankL[h_][:, :].rearrange("p (c n) -> p c n", c=2), [bnL[h_]], [xwn], eng="act")
        evac(xw[:, :, 256:288], psC[:, 0:WV * 32].rearrange("p (c n) -> p c n", c=WV), [pcn], [xwn], eng="act")
        yield
        cur, cname = xw, xwn
        loc_off, loc_n, ctx_off, ctx_n = 0, 256, 256, 32
        bufs = [(TAw[par], f"TAw{par}"), (TBw[par], f"TBw{par}")]
        tb_ = par * 256
        Rw_next = [[None] + [get_R(j - 1, q) for j in (1, 2, 3)] for q in qs]
        for k in range(4):
            nxt, nname = bufs[k % 2]
            lev = []
            for (off, n_, is_ctx) in ((loc_off, loc_n, False), (ctx_off, ctx_n, True)):
                if n_ <= 1:
                    continue
                rad = 4 if n_ >= 4 else n_
                no = n_ // rad
                base = 128 if not is_ctx else 384
                lev.append((off, n_, is_ctx, rad, no, base))
            Rw = Rw_next
            for c_, q in enumerate(qs):
                for (off, n_, is_ctx, rad, no, base) in lev:
                    ocol = base + c_ * no
                    bk_, bkn_ = (psC, pcn)
                    for jj in range(rad):
                        pw = (rad - 1 - jj) if d == 0 else jj
                        lt, ltn = (ident_b, "ident_b") if pw == 0 else Rw[c_][pw]
                        P.c("pe", lambda e, bk_=bk_, lt=lt, cur=cur, c_=c_, off=off, jj=jj, rad=rad, n_=n_, ocol=ocol, no=no: e.matmul(
                            bk_[:, ocol:ocol + no], lhsT=lt[:, :], rhs=cur[:, c_, off + jj:off + n_:rad], start=(jj == 0), stop=(jj == rad - 1)),
                            reads=[ltn, cname], writes=[bkn_])
            if k < 3:
                Rw_next = [[None] + [get_R(3 * (k + 1) + j - 1, q) for j in (1, 2, 3)] for q in qs]
            yield
            for (off, n_, is_ctx, rad, no, base) in lev:
                bk_, bkn_ = (psC, pcn)
                if no == 1:
                    dstF, dn = (Fctx, "Fctx") if is_ctx else (Fall, "Fall")
                    P.c("act", lambda e, bk_=bk_, dstF=dstF, base=base, q0=qs[0]: e.activation(out=dstF[:, q0:q0 + WV], in_=bk_[:, base:base + WV], func=AF.Copy),
                        reads=[bkn_], writes=[dn])
                else:
                    o2 = 64 if is_ctx else 0
                    evac(nxt[:, :, o2:o2 + no], bk_[:, base:base + WV * no].rearrange("p (c n) -> p c n", c=WV), [bkn_], [nname], eng="act")
            cur, cname = nxt, nname
            loc_off, loc_n = 0, (loc_n // 4 if loc_n > 1 else 0)
            ctx_off, ctx_n = 64, (ctx_n // (4 if ctx_n >= 4 else ctx_n) if ctx_n > 1 else 0)

    def lockstep(gens):
        gens = list(gens)
        while gens:
            for g_ in list(gens):
                try:
                    next(g_)
                except StopIteration:
                    gens.remove(g_)

    def rolling(gens, depth=2, stagger=2):
        it = iter(gens)
        active = []
        pending = next(it, None)
        while active or pending is not None:
            if pending is not None and len(active) < depth and (not active or active[-1][1] >= stagger):
                active.append([pending, 0])
                pending = next(it, None)
            for ent in list(active):
                try:
                    next(ent[0])
                    ent[1] += 1
                except StopIteration:
                    active.remove(ent)

    rolling([tree_wave(w) for w in range(64 // WV)])
    if debug:
        for (nm, t_) in (("d_Fall", Fall), ("d_Fctx", Fctx)):
            dd = dout(nm, [128, 64])
            P.dma("sp", lambda e, dd=dd, t_=t_: e.dma_start(out=dd[:, :], in_=t_[:, :]), "dbg", reads=["Fall", "Fctx"])
    stage("S5tree")

    xb_in = nc.dram_tensor("xb_in", [128, 64], F32)
    xb_out = nc.dram_tensor("xb_out", [512, 64], F32)
    P.dma("sp", lambda e: e.dma_start(out=xb_in[:, :], in_=Fall[:, :]), "xb_st", reads=["Fall"], writes=["xb_in"])
    P.dma("pool", lambda e: e.collective_compute("AllGather", ALU.bypass, replica_groups=[[0, 1, 2, 3], [4, 5, 6, 7]],
                                                  ins=[xb_in.ap().opt()], outs=[xb_out.ap().opt()]),
          "ccB", reads=["xb_in"], writes=["xb_out"], inc=1)
    P.dma("sp", lambda e: e.dma_start(out=xgB[:, :, :], in_=xb_out.ap().rearrange("(r p) n -> p r n", p=128)), "xb_ld", reads=["xb_out"], writes=["xgB"])
    P.c("dve", lambda e: e.tensor_copy(out=hacc[:, :], in_=Fctx[:, :]), reads=["Fctx"], writes=["hacc"])
    psH = psA[1]
    for n_ in range(3):
        P.c("act", lambda e: e.activation(out=hacc_bf[:, :], in_=hacc[:, :], func=AF.Copy), reads=["hacc"], writes=["hacc_bf"])
        for q in range(64):
            Rt, Rn = get_R(12, q)
            P.c("pe", lambda e, q=q, Rt=Rt: e.matmul(psH[:, q:q + 1], lhsT=Rt[:, :], rhs=hacc_bf[:, q:q + 1], start=True, stop=True),
                reads=[Rn, "hacc_bf"], writes=["ps1"])
        for (cs_, rank, mcol) in ((slice(0, 32), n_, n_), (slice(32, 64), 3 - n_, 3 + n_)):
            P.c("dve", lambda e, cs_=cs_, rank=rank: e.tensor_tensor(out=htmp[:, cs_], in0=psH[:, cs_], in1=xgB[:, rank, cs_], op=ALU.add),
                reads=["ps1", "xgB"], writes=["htmp"])
            P.c("dve", lambda e, cs_=cs_: e.tensor_tensor(out=htmp[:, cs_], in0=htmp[:, cs_], in1=hacc[:, cs_], op=ALU.subtract),
                reads=["htmp", "hacc"], writes=["htmp"])
            P.c("dve", lambda e, cs_=cs_, mcol=mcol: e.scalar_tensor_tensor(out=hacc[:, cs_], in0=htmp[:, cs_], scalar=msk[:, mcol:mcol + 1],
                                                                           in1=hacc[:, cs_], op0=ALU.mult, op1=ALU.add),
                reads=["htmp", "hacc", "msk"], writes=["hacc"])
    P.c("act", lambda e: e.activation(out=hacc_bf[:, :], in_=hacc[:, :], func=AF.Copy), reads=["hacc"], writes=["hacc_bf"])
    stage("S5xchg")

    psY = psA[0]
    psS2 = [psA[5], psA[6]]
    def ks_wave(fc, gp):
        w = fc * 4 + gp
        par = w % 2
        g0 = fc * 8 + gp * 2
        qs = [g0, g0 + 1, 32 + g0, 33 + g0]
        banks = [psA[1 + 2 * par], psA[2 + 2 * par]]
        bns = [f"ps{1 + 2 * par}", f"ps{2 + 2 * par}"]
        for c_, q in enumerate(qs):
            P.c("pe", lambda e, c_=c_, q=q, bk=banks[c_ // 2]: e.matmul(bk[:, (c_ % 2) * 256:(c_ % 2 + 1) * 256], lhsT=PV[:, q, :], rhs=Z[:, q % 32, :],
                                                   start=True, stop=True), reads=["PV", "Z"], writes=[bns[c_ // 2]])
        yield
        xa, xan, xb_, xbn = XAw[par], f"XAw{par}", XBw[par], f"XBw{par}"
        evac(xa[:, 0:2, 1:256], banks[0][:, :].rearrange("p (c n) -> p c n", c=2)[:, :, 0:255], [bns[0]], [xan], eng="act")
        evac(xa[:, 2:4, 0:255], banks[1][:, :].rearrange("p (c n) -> p c n", c=2)[:, :, 1:256], [bns[1]], [xan], eng="act")
        P.c("dve", lambda e, xa=xa, g0=g0: e.tensor_copy(out=xa[:, 0:2, 0], in_=hacc_bf[:, g0:g0 + 2]), reads=["hacc_bf"], writes=[xan])
        P.c("dve", lambda e, xa=xa, g0=g0: e.tensor_copy(out=xa[:, 2:4, 255], in_=hacc_bf[:, 32 + g0:34 + g0]), reads=["hacc_bf"], writes=[xan])
        yield
        cur, cn, oth, on = xa, xan, xb_, xbn
        Rw_next = [[get_R(j - 1, q) for j in (1, 2, 3)] for q in qs]
        for k in range(4):
            Rw = Rw_next
            for c_, q in enumerate(qs):
                pk, pkn = banks[c_ // 2], bns[c_ // 2]
                cb_ = (c_ % 2) * 256
                P.c("pe", lambda e, pk=pk, cb_=cb_, cur=cur, c_=c_: e.matmul(pk[:, cb_:cb_ + 256], lhsT=ident_b[:, :], rhs=cur[:, c_, :], start=True, stop=False),
                    reads=["ident_b", cn], writes=[pkn])
                for j in (1, 2, 3):
                    sh = j * (4 ** k)
                    Rt, Rn = Rw[c_][j - 1]
                    if c_ < 2:
                        P.c("pe", lambda e, pk=pk, cb_=cb_, Rt=Rt, cur=cur, c_=c_, sh=sh, j=j: e.matmul(
                            pk[:, cb_ + sh:cb_ + 256], lhsT=Rt[:, :], rhs=cur[:, c_, 0:256 - sh], start=False, stop=(j == 3)), reads=[Rn, cn], writes=[pkn])
                    else:
                        P.c("pe", lambda e, pk=pk, cb_=cb_, Rt=Rt, cur=cur, c_=c_, sh=sh, j=j: e.matmul(
                            pk[:, cb_:cb_ + 256 - sh], lhsT=Rt[:, :], rhs=cur[:, c_, sh:256], start=False, stop=(j == 3)), reads=[Rn, cn], writes=[pkn])
            if k < 3:
                Rw_next = [[get_R(3 * (k + 1) + j - 1, q) for j in (1, 2, 3)] for q in qs]
            yield
            evac(oth[:, 0:2, :], banks[0][:, :].rearrange("p (c n) -> p c n", c=2), [bns[0]], [on], eng="act")
            evac(oth[:, 2:4, :], banks[1][:, :].rearrange("p (c n) -> p c n", c=2), [bns[1]], [on], eng="act")
            cur, cn, oth, on = oth, on, cur, cn
            yield
        for c_ in range(2):
            g = g0 + c_
            ysl = slice(c_ * 256, (c_ + 1) * 256)
            P.c("pe", lambda e, g=g, ysl=ysl: e.matmul(psY[:, ysl], lhsT=Mbf[:, g, :], rhs=Z[:, g, :], start=True, stop=False), reads=["Mbf", "Z"], writes=["ps0"])
            P.c("pe", lambda e, g=g, ysl=ysl, cur=cur, c_=c_: e.matmul(psY[:, ysl], lhsT=WR[:, g, :], rhs=cur[:, c_, :], start=False, stop=False),
                reads=["WRb", cn], writes=["ps0"])
            P.c("pe", lambda e, g=g, ysl=ysl, cur=cur, c_=c_: e.matmul(psY[:, ysl], lhsT=WR[:, 32 + g, :], rhs=cur[:, 2 + c_, :], start=False, stop=True),
                reads=["WRb", cn], writes=["ps0"])
        evac(Ysb[:, 2 * gp:2 * gp + 2, :], psY[:, :].rearrange("p (c n) -> p c n", c=2), ["ps0"], ["Ysb"], eng="act")

    for fc in range(4):
        rolling([ks_wave(fc, gp) for gp in range(4)])
        for s_ in range(8):
            ps = psS2[s_ % 2]
            for g8 in range(8):
                P.c("pe", lambda e, ps=ps, s_=s_, g8=g8: e.matmul(ps[:, 0:256], lhsT=Eblk[:, s_ * 8 + g8, :], rhs=Ysb[:, g8, :],
                                                                 start=(g8 == 0), stop=(g8 == 7)), reads=["Eblk", "Ysb"], writes=[f"ps{5 + s_ % 2}"])
            P.c("act", lambda e, ps=ps, s_=s_, fc=fc: e.activation(out=s5gT[:, fc, s_::8], in_=ps[:, 0:256], func=AF.Gelu_apprx_tanh),
                reads=[f"ps{5 + s_ % 2}"], writes=["s5gT"])
    stage("S5y")
    psG = [psA[0], psA[1]]
    for tb_ in range(4):
        tsl = slice(tb_ * 512, (tb_ + 1) * 512)
        for oc in range(4):
            ps = psG[oc % 2]
            for kc in range(4):
                P.c("pe", lambda e, ps=ps, kc=kc, oc=oc, tsl=tsl: e.matmul(ps[:, :], lhsT=wglu[:, kc, oc * 128:(oc + 1) * 128], rhs=s5gT[:, kc, tsl],
                                                                          start=(kc == 0), stop=(kc == 3)), reads=["wglu", "s5gT"], writes=[f"ps{oc % 2}"])
            P.c("act", lambda e, ps=ps, oc=oc: e.activation(out=sgt[:, :], in_=ps[:, :], func=AF.Sigmoid, bias=bglu[:, oc:oc + 1]),
                reads=[f"ps{oc % 2}", TAB], writes=["sgt"])
            P.c("dve", lambda e, oc=oc, tsl=tsl: e.tensor_tensor(out=mixT[:, oc, tsl], in0=s5gT[:, oc, tsl], in1=sgt[:, :], op=ALU.mult),
                reads=["s5gT", "sgt", "Macc"], writes=["mixT", "Macc"])
    if debug:
        dd = nc.dram_tensor("d_s5gT", [128, 4 * T], BF16, kind="ExternalOutput").ap(); dbg["d_s5gT"] = dd
        P.dma("sp", lambda e, dd=dd: e.dma_start(out=dd[:, :], in_=s5gT[:, :, :].rearrange("p a b -> p (a b)")), "dbg", reads=["s5gT"])
        dd = nc.dram_tensor("d_mixT2", [128, 8 * T], BF16, kind="ExternalOutput").ap(); dbg["d_mixT2"] = dd
        P.dma("sp", lambda e, dd=dd: e.dma_start(out=dd[:, :], in_=mixT[:, :, :].rearrange("p a b -> p (a b)")), "dbg", reads=["mixT"])
    stage("S5glu")
    return None


def _s5_host(inputs):
    def dup(a):
        return np.ascontiguousarray(np.concatenate([a, a], 0)).astype(np.float32)
    lre = np.concatenate([inputs["s5_lambda_re_f"][0], inputs["s5_lambda_re_b"][0]], 0)
    lim = np.concatenate([inputs["s5_lambda_im_f"][0], inputs["s5_lambda_im_b"][0]], 0)
    lst = np.concatenate([inputs["s5_log_step_f"][0], inputs["s5_log_step_b"][0]], 0)
    return {
        "s5_lamre": dup(lre.T), "s5_lamim": dup(lim.T),
        "s5_lstep": np.ascontiguousarray(np.broadcast_to(lst[None, :], (128, 64))).astype(np.float32),
        "s5_bre": dup(inputs["s5_b_re"][0].transpose(1, 0, 2)), "s5_bim": dup(inputs["s5_b_im"][0].transpose(1, 0, 2)),
        "s5_cre": dup(inputs["s5_c_re"][0].transpose(2, 0, 1)), "s5_cim": dup(inputs["s5_c_im"][0].transpose(2, 0, 1)),
        "s5_d128": np.ascontiguousarray(np.broadcast_to(inputs["s5_d"][0].reshape(1, 32, 16), (128, 32, 16))).astype(np.float32),
        "s5_bglu": np.ascontiguousarray(inputs["s5_b_glu"][0].reshape(4, 128).T).astype(np.float32),
        "w_glu": np.ascontiguousarray(inputs["s5_w_glu"][0]).astype(np.float32),
    }


def make_inputs(inputs):
    x = np.asarray(inputs["x"], np.float32)
    per = []
    for r in range(NCORES):
        b, seg = r // 4, r % 4
        cv = np.stack([inputs["c"][b].reshape(8, 128).T, inputs["c_ctx"].reshape(8, 128).T], -1)
        m = {
            "x_loc": np.ascontiguousarray(x[b, seg * T:(seg + 1) * T]),
            "cvec": np.ascontiguousarray(cv.reshape(128, 16)).astype(np.float32),
            "w_mod": np.ascontiguousarray(inputs["w_mod"][0]),
            "b_mod2": np.ascontiguousarray(np.broadcast_to(inputs["b_mod"][0][None, :], (2, 6 * D))).astype(np.float32),
            "n1w": np.ascontiguousarray(inputs["norm1_w"][0].reshape(8, 128).T),
            "n2w": np.ascontiguousarray(inputs["norm2_w"][0].reshape(8, 128).T),
            "w_in": np.ascontiguousarray(inputs["w_in"][0]),
            "segf": np.full((128, 1), float(seg), np.float32),
            "ctxb": np.ascontiguousarray(inputs["ctx"][b]).astype(np.float32),
            "ldv": np.ascontiguousarray(np.broadcast_to(np.concatenate([inputs["ret_log_decay_f"][0], inputs["ret_log_decay_b"][0]])[None, :], (128, 8))).astype(np.float32),
            **_s5_host(inputs),
            "w_out": np.ascontiguousarray(inputs["w_out"][0]).astype(np.float32),
            "fnw_b": np.ascontiguousarray(np.broadcast_to(inputs["final_norm_w"][None, :], (128, D))).astype(np.float32),
            "cw": np.ascontiguousarray(inputs["conv_w"][0].reshape(3, 22, 128).transpose(2, 1, 0).reshape(128, 66)).astype(np.float32),
            "cb": np.ascontiguousarray(inputs["conv_b"][0].reshape(22, 128).T).astype(np.float32),
            "oh": np.ascontiguousarray(np.broadcast_to(np.array([float(r_ == seg - 1) for r_ in range(4)] + [float(r_ == seg + 1) for r_ in range(4)],
                                                                 np.float32)[None, :], (128, 8))),
            "w_up": np.ascontiguousarray(inputs["w_up"][0]).astype(np.float32),
            "w_down": np.ascontiguousarray(inputs["w_down"][0]).astype(np.float32),
            "msk": np.ascontiguousarray(np.broadcast_to(np.array([0 < seg, 1 < seg, 2 < seg, 3 > seg, 2 > seg, 1 > seg, 0, 0], np.float32)[None, :], (128, 8))),
        }
        per.append(m)
    return per


def kernel(**inputs):
    nc, _ = build(debug=False)
    per = make_inputs(inputs)
    res = run_bass_kernel_spmd(nc, per, core_ids=list(range(NCORES)))
    out = np.zeros((2, 4 * T, D), np.float32)
    for r in range(NCORES):
        b, seg = r // 4, r % 4
        out[b, seg * T:(seg + 1) * T] = res.results[r]["out"]
    return out
```
